# Optimizing a Trainium2 kernel written in Bass

```python
import jax, jax.numpy as jnp
from jax import lax
import numpy as np

D_MODEL = 1024
BATCH = 32
SEQ = 2048
DEPTH = 2

N_MIXERS = 2
N_A = (DEPTH + 1) // 2
N_B = DEPTH // 2
D_RNN = D_MODEL
LRU_BLOCKS = 16
LRU_BW = D_RNN // LRU_BLOCKS
LRU_CONV = 4
LRU_C = 8.0
RWKV_N = 64
RWKV_H = D_MODEL // RWKV_N
R_DECAY = 64
R_AAA = 64
R_GATE = 160
GN_EPS = 64e-5
D_FF = 3 * D_MODEL
FFN_CONV = 3
RMS_EPS = 1e-6

kernel_name = 'hybrid_rglru_rwkv7_convffn'


def _rmsnorm(x, g):
    xf = x.astype(jnp.float32)
    y = xf * lax.rsqrt(jnp.mean(xf * xf, axis=-1, keepdims=True) + RMS_EPS)
    return (y * g.astype(jnp.float32)).astype(x.dtype)


def _causal_dwconv(x, w, b):
    k_width, seq = w.shape[0], x.shape[1]
    xp = jnp.pad(x, ((0, 0), (k_width - 1, 0), (0, 0)))
    out = b
    for j in range(k_width):
        out = out + xp[:, j:j + seq] * w[j]
    return out


def _lru_combine(c1, c2):
    a1, b1 = c1
    a2, b2 = c2
    return a1 * a2, a2 * b1 + b2


def _rglru_block(x, norm, w_in, b_in, conv_w, conv_b, gate_w, gate_b, lam, w_out, b_out):
    bsz, seq, _ = x.shape
    h = _rmsnorm(x, norm)
    u = jnp.einsum('btd,de->bte', h, w_in) + b_in
    y_branch = jax.nn.gelu(u[..., :D_RNN], approximate=True)
    xr = _causal_dwconv(u[..., D_RNN:], conv_w, conv_b)
    xb = xr.reshape(bsz, seq, LRU_BLOCKS, LRU_BW)
    gates = jax.nn.sigmoid(jnp.einsum('btnc,gncd->gbtnd', xb, gate_w) + gate_b[:, None, None])
    r_gate = gates[0].reshape(bsz, seq, D_RNN).astype(jnp.float32)
    i_gate = gates[1].reshape(bsz, seq, D_RNN).astype(jnp.float32)
    log_a = -LRU_C * r_gate * jax.nn.softplus(-lam.astype(jnp.float32))
    a = jnp.exp(log_a)
    mult = jnp.sqrt(-jnp.expm1(2.0 * log_a))
    bterm = mult * (i_gate * xr.astype(jnp.float32))
    _, hs = lax.associative_scan(_lru_combine, (a, bterm), axis=1)
    out = hs.astype(x.dtype) * y_branch
    return jnp.einsum('bte,ed->btd', out, w_out) + b_out


def _rwkv7_scan(r, w, k, v, aa, bb):
    bsz, _, nh, n = r.shape

    def step(S, inp):
        r_t, w_t, k_t, v_t, a_t, b_t = inp
        sa = jnp.einsum('bhij,bhj->bhi', S, a_t)
        S = S * w_t[:, :, None, :] + sa[..., None] * b_t[:, :, None, :] + v_t[..., :, None] * k_t[..., None, :]
        y = jnp.einsum('bhij,bhj->bhi', S, r_t)
        return S, y

    s0 = jnp.zeros((bsz, nh, n, n), jnp.float32)
    xs = tuple(t.transpose(1, 0, 2, 3) for t in (r, w, k, v, aa, bb))
    _, ys = lax.scan(step, s0, xs)
    return ys.transpose(1, 0, 2, 3)


def _rwkv7_block(x, norm, mix, w_rkv, w0, w1, w2, a0, a1, a2, g1, g2, k_k, k_a, r_k, ln_w, ln_b, w_out):
    bsz, seq, d = x.shape
    f32 = jnp.float32
    h = _rmsnorm(x, norm)
    xx = jnp.pad(h, ((0, 0), (1, 0), (0, 0)))[:, :-1] - h
    xs_rkv = jnp.stack([h + xx * mix[0], h + xx * mix[1], h + xx * mix[2]])
    rkv = jnp.einsum('sbtd,sde->sbte', xs_rkv, w_rkv)
    r, k, v = rkv[0], rkv[1], rkv[2]
    xw = h + xx * mix[3]
    xa = h + xx * mix[4]
    xg = h + xx * mix[5]
    w = -jax.nn.softplus(-(w0 + jnp.tanh(xw @ w1) @ w2).astype(f32)) - 0.5
    decay = jnp.exp(-jnp.exp(w))
    a = jax.nn.sigmoid((a0 + (xa @ a1) @ a2).astype(f32))
    g = jax.nn.sigmoid(xg @ g1) @ g2
    kf = k.astype(f32)
    kk = (kf * k_k).reshape(bsz, seq, RWKV_H, RWKV_N)
    kk = kk / jnp.maximum(jnp.linalg.norm(kk, axis=-1, keepdims=True), 1e-12)
    kf = kf * (1.0 + (a - 1.0) * k_a)
    hs = lambda t: t.reshape(bsz, seq, RWKV_H, RWKV_N)
    rh, kh, vh = hs(r.astype(f32)), hs(kf), hs(v.astype(f32))
    ah = hs(a)
    y = _rwkv7_scan(rh, hs(decay), kh, vh, -kk, kk * ah)
    mu = jnp.mean(y, axis=-1, keepdims=True)
    var = jnp.mean(jnp.square(y - mu), axis=-1, keepdims=True)
    y = ((y - mu) * lax.rsqrt(var + GN_EPS)).reshape(bsz, seq, d) * ln_w + ln_b
    bonus = jnp.sum(rh * kh * r_k, axis=-1, keepdims=True) * vh
    y = (y + bonus.reshape(bsz, seq, d)).astype(x.dtype)
    return jnp.einsum('btd,de->bte', y * g, w_out)


def _conv_ffn(x, norm, w_up, conv_w, conv_b, w_down):
    h = _rmsnorm(x, norm)
    u = jnp.einsum('btd,df->btf', h, w_up)
    gate = _causal_dwconv(u[..., :D_FF], conv_w, conv_b)
    hid = jax.nn.gelu(gate, approximate=True) * u[..., D_FF:]
    return jnp.einsum('btf,fd->btd', hid, w_down)


def setup_inputs(seed: int = 0) -> dict:
    key = jax.random.key(seed)
    ks = iter(jax.random.split(key, 48))
    f32 = jnp.float32
    nrm = lambda shape, scale: jax.random.normal(next(ks), shape, f32) * scale
    uni = lambda shape, lo, hi: jax.random.uniform(next(ks), shape, f32, lo, hi)
    d = D_MODEL
    u_a = uni((N_A, D_RNN), 0.9, 0.999)
    a_init = u_a ** (1.0 / LRU_C)
    return {
        'x': nrm((BATCH, SEQ, d), 1.0),
        'lru_norm': 1.0 + nrm((N_A, d), 0.02),
        'lru_w_in': nrm((N_A, d, 2 * D_RNN), d ** -0.5),
        'lru_b_in': nrm((N_A, 2 * D_RNN), 0.01),
        'lru_conv_w': nrm((N_A, LRU_CONV, D_RNN), LRU_CONV ** -0.5),
        'lru_conv_b': nrm((N_A, D_RNN), 0.01),
        'lru_gate_w': nrm((N_A, 2, LRU_BLOCKS, LRU_BW, LRU_BW), LRU_BW ** -0.5),
        'lru_gate_b': nrm((N_A, 2, LRU_BLOCKS, LRU_BW), 0.01),
        'lru_lambda': jnp.log(a_init) - jnp.log1p(-a_init),
        'lru_w_out': nrm((N_A, D_RNN, d), D_RNN ** -0.5),
        'lru_b_out': nrm((N_A, d), 0.01),
        'rwkv_norm': 1.0 + nrm((N_B, d), 0.02),
        'rwkv_mix': uni((N_B, 6, d), 0.0, 1.0),
        'rwkv_w_rkv': nrm((N_B, 3, d, d), d ** -0.5),
        'rwkv_w0': uni((N_B, d), -5.0, -1.0),
        'rwkv_w1': nrm((N_B, d, R_DECAY), d ** -0.5),
        'rwkv_w2': nrm((N_B, R_DECAY, d), 0.1 * R_DECAY ** -0.5),
        'rwkv_a0': nrm((N_B, d), 0.1),
        'rwkv_a1': nrm((N_B, d, R_AAA), d ** -0.5),
        'rwkv_a2': nrm((N_B, R_AAA, d), 0.1 * R_AAA ** -0.5),
        'rwkv_g1': nrm((N_B, d, R_GATE), d ** -0.5),
        'rwkv_g2': nrm((N_B, R_GATE, d), R_GATE ** -0.5),
        'rwkv_k_k': 0.85 + nrm((N_B, d), 0.05),
        'rwkv_k_a': 1.0 + nrm((N_B, d), 0.05),
        'rwkv_r_k': nrm((N_B, RWKV_H, RWKV_N), 0.1),
        'rwkv_ln_w': 1.0 + nrm((N_B, d), 0.02),
        'rwkv_ln_b': nrm((N_B, d), 0.01),
        'rwkv_w_out': nrm((N_B, d, d), d ** -0.5),
        'ffn_norm': 1.0 + nrm((DEPTH, d), 0.02),
        'ffn_w_up': nrm((DEPTH, d, 2 * D_FF), d ** -0.5),
        'ffn_conv_w': nrm((DEPTH, FFN_CONV, D_FF), FFN_CONV ** -0.5),
        'ffn_conv_b': nrm((DEPTH, D_FF), 0.01),
        'ffn_w_down': nrm((DEPTH, D_FF, d), D_FF ** -0.5),
        'final_norm': 1.0 + nrm((d,), 0.02),
    }


def reference(x, lru_norm, lru_w_in, lru_b_in, lru_conv_w, lru_conv_b, lru_gate_w, lru_gate_b, lru_lambda, lru_w_out, lru_b_out,
              rwkv_norm, rwkv_mix, rwkv_w_rkv, rwkv_w0, rwkv_w1, rwkv_w2, rwkv_a0, rwkv_a1, rwkv_a2, rwkv_g1, rwkv_g2,
              rwkv_k_k, rwkv_k_a, rwkv_r_k, rwkv_ln_w, rwkv_ln_b, rwkv_w_out,
              ffn_norm, ffn_w_up, ffn_conv_w, ffn_conv_b, ffn_w_down, final_norm):
    for layer in range(DEPTH):
        j = layer // N_MIXERS
        if layer % N_MIXERS == 0:
            x = x + _rglru_block(x, lru_norm[j], lru_w_in[j], lru_b_in[j], lru_conv_w[j], lru_conv_b[j],
                                 lru_gate_w[j], lru_gate_b[j], lru_lambda[j], lru_w_out[j], lru_b_out[j])
        else:
            x = x + _rwkv7_block(x, rwkv_norm[j], rwkv_mix[j], rwkv_w_rkv[j], rwkv_w0[j], rwkv_w1[j], rwkv_w2[j],
                                 rwkv_a0[j], rwkv_a1[j], rwkv_a2[j], rwkv_g1[j], rwkv_g2[j], rwkv_k_k[j], rwkv_k_a[j],
                                 rwkv_r_k[j], rwkv_ln_w[j], rwkv_ln_b[j], rwkv_w_out[j])
        x = x + _conv_ffn(x, ffn_norm[layer], ffn_w_up[layer], ffn_conv_w[layer], ffn_conv_b[layer], ffn_w_down[layer])
    return _rmsnorm(x, final_norm)
```

```python
import numpy as np
from contextlib import ExitStack
import concourse.bass as bass
import concourse.mybir as mybir
from concourse.bass_utils import run_bass_kernel_spmd

F32 = mybir.dt.float32
BF16 = mybir.dt.bfloat16
AF = mybir.ActivationFunctionType
ALU = mybir.AluOpType
AX = mybir.AxisListType

D = 1024
KC = 8
TT = 512
NS = 4
DFF = 3072
NJ = 24
NW = 5
CH = 128
RMS_EPS = 1e-6
GN_EPS = 64e-5
N_CORES = 8
SAME_ENGINE_SYNC = True

VEC_SPECS = [
    ("lru_norm", 8), ("lru_b_in", 16), ("lru_conv_w", 32), ("lru_conv_b", 8), ("lru_gate_b", 16), ("lru_lambda", 8),
    ("rwkv_norm", 8), ("rwkv_mix", 48), ("rwkv_w0", 8), ("rwkv_a0", 8), ("rwkv_k_k", 8), ("rwkv_k_a", 8), ("rwkv_r_k", 8),
    ("ffn_norm", 16), ("ffn_conv_w", 144), ("ffn_conv_b", 48),
]
VOFF = {}
_o = 0
for _n, _w in VEC_SPECS:
    VOFF[_n] = _o
    _o += _w
NV = _o


def _fm(v):
    v = np.asarray(v, dtype=np.float32)
    return np.ascontiguousarray(v.reshape(-1, 128).T)


def make_vecs(inp):
    cols = [_fm(inp[n]) for n, _ in VEC_SPECS]
    out = np.concatenate(cols, axis=1)
    assert out.shape == (128, NV), out.shape
    return np.ascontiguousarray(out)


class DmaSem:
    def __init__(self, sem):
        self.sem = sem
        self.val = 0


class Sched:
    ENG = ("pe", "act", "dve", "pool", "sp")
    EPOCH = 30000

    def __init__(self, nc, es):
        self.nc = nc
        self.es = es
        self.nsem = 0
        self.prog = {e: [] for e in self.ENG}
        self.esem = {}
        self.cnt = {}
        self.own = {e: set() for e in self.ENG}
        for e in self.ENG:
            self._new_epoch(e)
        self.waited = {e: {} for e in self.ENG}
        self.lastw = {}
        self.readers = {}
        self.nops = 0

    def new_sem(self, name):
        self.nsem += 1
        return self.es.enter_context(self.nc.semaphore(f"{name}{self.nsem}"))

    def new_dsem(self, name="d"):
        return DmaSem(self.new_sem(name))

    def _new_epoch(self, e):
        s = self.new_sem("e" + e)
        self.esem[e] = s
        self.cnt[e] = 0
        self.own[e].add(id(s))

    def op(self, eng, fn, reads=(), writes=(), dsem=None):
        deps = {}

        def add(c):
            if c is None:
                return
            k = id(c[0])
            if k not in deps or deps[k][1] < c[1]:
                deps[k] = c

        for t in reads:
            add(self.lastw.get(t))
        for t in writes:
            add(self.lastw.get(t))
            for c in self.readers.get(t, {}).values():
                add(c)
        for k, (s, v) in deps.items():
            if k in self.own[eng] and dsem is None and (eng == "pe" or not SAME_ENGINE_SYNC):
                continue
            if self.waited[eng].get(k, 0) < v:
                self.prog[eng].append(("w", s, v))
                self.waited[eng][k] = v
        if dsem is None:
            if self.cnt[eng] >= self.EPOCH:
                self._new_epoch(eng)
            self.cnt[eng] += 1
            comp = (self.esem[eng], self.cnt[eng])
            inc = 1
        else:
            dsem.val += 16
            comp = (dsem.sem, dsem.val)
            inc = 16
        self.prog[eng].append(("o", fn, comp[0], inc))
        for t in reads:
            self.readers.setdefault(t, {})[id(comp[0])] = comp
        for t in writes:
            self.lastw[t] = comp
            self.readers[t] = {}
        self.nops += 1
        return comp

    def drain(self, eng):
        if self.cnt[eng] > 0:
            self.prog[eng].append(("w", self.esem[eng], self.cnt[eng]))

    def final_wait(self, eng, comp):
        self.prog[eng].append(("w", comp[0], comp[1]))

    def emit(self, block):
        def mk(e):
            def f(h):
                for it in self.prog[e]:
                    if it[0] == "w":
                        h.wait_ge(it[1], it[2])
                    else:
                        it[1](h).then_inc(it[2], it[3])
            return f

        block.tensor(mk("pe"))
        block.scalar(mk("act"))
        block.vector(mk("dve"))
        block.gpsimd(mk("pool"))
        block.sync(mk("sp"))


def build(n_seq=4, T=2048, stages="ABCDE"):
    assert T % TT == 0
    n_tiles = T // TT
    nc = bass.Bass("TRN2", target_bir_lowering=False)
    es = ExitStack()
    S = Sched(nc, es)

    def dram(name, shape, dt=F32, kind="ExternalInput"):
        return nc.dram_tensor(name, list(shape), dt, kind=kind).ap()

    def sb(name, shape, dt=F32):
        return es.enter_context(nc.sbuf_tensor("s_" + name, list(shape), dt))

    x_d = dram("x", [n_seq, T, D])
    out_d = dram("out", [n_seq, T, D], kind="ExternalOutput")
    vecs_d = dram("vecs", [128, NV])
    bigw = {
        "lru_w_in": (D, 2048), "lru_w_out": (D, D),
        "ffn_w_up0": (D, 2 * DFF), "ffn_w_up1": (D, 2 * DFF), "ffn_w_dn0": (DFF, D), "ffn_w_dn1": (DFF, D),
        "rwkv_w_r": (D, D), "rwkv_w_k": (D, D), "rwkv_w_v": (D, D), "rwkv_w_out": (D, D),
    }
    w_d = {n: dram(n, s) for n, s in bigw.items()}
    wb_d = {n: dram("b_" + n, s, BF16, kind="Internal") for n, s in bigw.items()}
    gate_w_d = dram("lru_gate_w", [2, 16, 64, 64])
    b_out_d = dram("lru_b_out", [1, D])
    w1_d = dram("rwkv_w1", [D, 64]); a1_d = dram("rwkv_a1", [D, 64]); g1_d = dram("rwkv_g1", [D, 160])
    w2_d = dram("rwkv_w2", [64, D]); a2_d = dram("rwkv_a2", [64, D]); g2_d = dram("rwkv_g2", [160, D])
    lnw_d = dram("rwkv_ln_w", [D]); lnb_d = dram("rwkv_ln_b", [D]); fin_d = dram("final_norm", [D])

    vecs = sb("vecs", [128, NV])
    ident = sb("ident", [128, 128], BF16)
    ones_row = sb("ones_row", [1, 128], BF16)
    bout_row = sb("bout_row", [1, D], BF16)
    gateW = sb("gateW", [128, 16, 128], BF16)
    cneg = sb("cneg", [128, 8])
    gfin_b = sb("gfin_b", [128, D])
    HAS_C = "C" in stages
    if HAS_C:
        lnw_b = sb("lnw_b", [128, D]); lnb_b = sb("lnb_b", [128, D])
        w1b = sb("w1b", [128, KC, 64], BF16); a1b = sb("a1b", [128, KC, 64], BF16); g1b = sb("g1b", [128, KC, 160], BF16)
        w2b = sb("w2b", [64, D], BF16); a2b = sb("a2b", [64, D], BF16); g2b = sb("g2b", [128, 2, D], BF16)
        mUs = sb("mUs", [128, 128], BF16); mUi = sb("mUi", [128, 128], BF16); mLs = sb("mLs", [128, 128], BF16)
        sel2 = sb("sel2", [128, 2], BF16); blkones = sb("blkones", [128, 128], BF16)
        rmask = sb("rmask", [128, TT]); omka = sb("omka", [128, 8])
        tanhw = sb("tanhw", [64, TT], BF16); a1o = sb("a1o", [64, TT], BF16); sgT = sb("sgT", [128, 2, TT], BF16)
        WC = sb("WC", [128, 8, 4]); dtok = sb("dtok", [128, 4, 16]); hcar = sb("hcar", [128, 8], BF16)
        Hst = sb("Hst", [128, 8, 64]); Hbd = sb("Hbd", [128, 8, 128], BF16)
        st_a = sb("st_a", [128, 16]); st_b = sb("st_b", [128, 16]); st_c = sb("st_c", [128, 16])
    xres = [sb("xres0", [128, NS, D])]
    hT = sb("hT", [128, KC, TT], BF16)
    xn = [sb(f"xn{i}", [128, D], BF16) for i in range(2)]
    sqj = xn[0]
    ss = sb("ss", [128, NS]); rstd = sb("rstd", [128, NS])
    wring = [sb(f"wring{i}", [128, 2048], BF16) for i in range(NW)]
    wsem = [S.new_dsem("w") for _ in range(NW)]
    hstate = sb("hstate", [128, 8])
    upre = sb("upre", [128, 8, TT + 3])
    fcar = [sb(f"fcar{l}", [128, NJ, 2]) for l in range(2)]
    NBLK = 46
    arena = sb("arena", [128, NBLK * 1024], BF16)
    ps = [es.enter_context(nc.psum_tensor(f"ps{i}", [128, 512], F32)) for i in range(8)]

    def V(name, c, n=1):
        o = VOFF[name] + c
        return vecs[:, o:o + n]

    class LB:
        def __init__(self, b0, nb, dt, inner=None):
            self.b0, self.nb, self.dt = b0, nb, dt
            a = arena[:, b0 * 1024:(b0 + nb) * 1024]
            if dt == F32:
                a = a.bitcast(F32)
            self.flat = a
            self.per = 1024 if dt == BF16 else 512
        def blk(self, i, n=1):
            return self.flat[:, i * self.per:(i + n) * self.per]
        def tok(self, i=None, n=1):
            if i is None:
                return [("ar", self.b0 + k) for k in range(self.nb)]
            return [("ar", self.b0 + i + k) for k in range(n)]

    bank_ctr = [0]

    def bank():
        b = bank_ctr[0] % 8
        bank_ctr[0] += 1
        return b

    wi = [0]

    def wload(name, src_ap, view):
        i = wi[0] % NW
        wi[0] += 1
        dst = view(wring[i])
        S.op("sp", lambda e, dst=dst, src_ap=src_ap: e.dma_start(out=dst, in_=src_ap),
             reads=[("wd", name)], writes=[("w", i)], dsem=wsem[i])
        return dst, ("w", i)

    cs_by_eng = {"sp": S.new_dsem("c"), "pool": S.new_dsem("cp")}
    const_toks = []

    def cload(eng, out_ap, in_ap, tok):
        S.op(eng, lambda e: e.dma_start(out=out_ap, in_=in_ap), reads=[], writes=[tok], dsem=cs_by_eng[eng])
        const_toks.append((tok, eng))

    cload("sp", vecs[:], vecs_d[:, :], "vecs")
    cload("sp", gfin_b[:], fin_d.partition_broadcast(128), "gfin_b")
    cload("pool", bout_row[:], b_out_d[:, :], "bout_row")
    S.op("pool", lambda e: e.memset(gateW[:], 0.0), writes=["gateW"])
    for g in range(2):
        for par in range(2):
            src = gate_w_d[g, par::2].rearrange("n c d -> c n d")
            dst = gateW[par * 64:(par + 1) * 64, g * 8:(g + 1) * 8, par * 64:(par + 1) * 64]
            cload("pool", dst, src, "gateW")
    if HAS_C:
        cload("sp", lnw_b[:], lnw_d.partition_broadcast(128), "lnw_b")
        cload("sp", lnb_b[:], lnb_d.partition_broadcast(128), "lnb_b")
        cload("pool", w1b[:], w1_d.rearrange("(k p) r -> p k r", p=128), "w1b")
        cload("pool", a1b[:], a1_d.rearrange("(k p) r -> p k r", p=128), "a1b")
        cload("pool", g1b[:], g1_d.rearrange("(k p) r -> p k r", p=128), "g1b")
        cload("pool", w2b[:], w2_d[:, :], "w2b")
        cload("pool", a2b[:], a2_d[:, :], "a2b")
        cload("pool", g2b[:, 0, :], g2_d[0:128, :], "g2b")
        cload("pool", g2b[0:32, 1, :], g2_d[128:160, :], "g2b")
    for t, eng in set(const_toks):
        S.lastw[t] = (cs_by_eng[eng].sem, cs_by_eng[eng].val)
    if HAS_C:
        for m, cmp_, cm, pat in ((mUs, ALU.is_gt, -1, 1), (mUi, ALU.is_ge, -1, 1), (mLs, ALU.is_gt, 1, -1)):
            tk = "mask%d" % id(m)
            S.op("pool", lambda e, m=m: e.memset(m[:], 1.0), writes=[tk])
            S.op("pool", lambda e, m=m, cmp_=cmp_, cm=cm, pat=pat: e.affine_select(out=m[:], in_=m[:], pattern=[[pat, 128]], compare_op=cmp_,
                                                                               fill=0.0, base=0, channel_multiplier=cm), reads=[tk], writes=[tk])
        S.op("pool", lambda e: e.memset(sel2[:], 0.0), writes=["sel2"])
        S.op("pool", lambda e: e.memset(sel2[0:64, 0:1], 1.0), reads=["sel2"], writes=["sel2"])
        S.op("pool", lambda e: e.memset(sel2[64:128, 1:2], 1.0), reads=["sel2"], writes=["sel2"])
        S.op("pool", lambda e: e.memset(blkones[:], 0.0), writes=["blkones"])
        S.op("pool", lambda e: e.memset(blkones[0:64, 0:64], 1.0), reads=["blkones"], writes=["blkones"])
        S.op("pool", lambda e: e.memset(blkones[64:128, 64:128], 1.0), reads=["blkones"], writes=["blkones"])
        S.op("pool", lambda e: e.memset(rmask[:], 1.0), writes=["rmask"])
        S.op("pool", lambda e: e.memset(rmask[:].rearrange("p (c t) -> p c t", t=CH)[:, :, 0:1], 0.0), reads=["rmask"], writes=["rmask"])
        S.op("dve", lambda e: e.tensor_scalar(out=omka[:], in0=V("rwkv_k_a", 0, 8), scalar1=-1.0, scalar2=1.0, op0=ALU.mult, op1=ALU.add),
             reads=["vecs"], writes=["omka"])
    S.op("pool", lambda e: e.memset(ident[:], 0.0), writes=["ident"])
    S.op("pool", lambda e: e.affine_select(out=ident[:], in_=ident[:], pattern=[[-1, 128]], compare_op=ALU.not_equal,
                                           fill=1.0, base=0, channel_multiplier=1), reads=["ident"], writes=["ident"])
    S.op("pool", lambda e: e.memset(ones_row[:], 1.0), writes=["ones_row"])
    S.op("act", lambda e: e.activation(out=cneg[:], in_=V("lru_lambda", 0, 8), func=AF.Exp, scale=-1.0), reads=["vecs"], writes=["cneg"])
    S.op("act", lambda e: e.activation(out=cneg[:], in_=cneg[:], func=AF.Ln, bias=1.0), reads=["cneg"], writes=["cneg"])
    S.op("act", lambda e: e.mul(out=cneg[:], in_=cneg[:], mul=-8.0), reads=["cneg"], writes=["cneg"])

    used = []
    if "A" in stages:
        used += ["lru_w_in", "lru_w_out"]
    if "B" in stages:
        used += ["ffn_w_up0", "ffn_w_dn0"]
    if "C" in stages:
        used += ["rwkv_w_r", "rwkv_w_k", "rwkv_w_v", "rwkv_w_out"]
    if "D" in stages:
        used += ["ffn_w_up1", "ffn_w_dn1"]
    for n in used:
        rows, cols = bigw[n]
        ds = S.new_dsem("k")
        step = max(32, (1 << 18) // cols)
        for r0 in range(0, rows, step):
            r1 = min(rows, r0 + step)
            S.op("pool", lambda e, n=n, r0=r0, r1=r1: e.dma_start(out=wb_d[n][r0:r1, :], in_=w_d[n][r0:r1, :]),
                 writes=[("wd", n)], dsem=ds)
        S.lastw[("wd", n)] = (ds.sem, ds.val)

    def rmsnorm_hT(xr_, xtok, gname, goff):
        for s in range(NS):
            S.op("act", lambda e, s=s: e.activation(out=sqj[:], in_=xr_[:, s, :], func=AF.Square, accum_out=ss[:, s:s + 1]),
                 reads=[(xtok, s)], writes=[("xn", 0), "ss"])
        S.op("dve", lambda e: e.tensor_scalar(out=rstd[:], in0=ss[:], scalar1=1.0 / D, scalar2=RMS_EPS, op0=ALU.mult, op1=ALU.add),
             reads=["ss"], writes=["rstd"])
        S.op("act", lambda e: e.activation(out=rstd[:], in_=rstd[:], func=AF.Sqrt), reads=["rstd"], writes=["rstd"])
        S.op("dve", lambda e: e.reciprocal(out=rstd[:], in_=rstd[:]), reads=["rstd"], writes=["rstd"])
        gb = V(gname, goff, 8).unsqueeze(2).to_broadcast([128, KC, 128])
        for s in range(NS):
            xb = xn[s % 2]
            xbt = ("xn", s % 2)
            S.op("act", lambda e, s=s, xb=xb: e.activation(out=xb[:], in_=xr_[:, s, :], func=AF.Copy, scale=rstd[:, s:s + 1]),
                 reads=[(xtok, s), "rstd"], writes=[xbt])
            b = bank()
            psb = ps[b][:].bitcast(BF16).rearrange("p (k t) -> p k t", k=KC)
            for kc in range(KC):
                S.op("pe", lambda e, kc=kc, xb=xb, psb=psb: e.transpose(psb[:, kc, :], xb[:, kc * 128:(kc + 1) * 128], ident[:]),
                     reads=[xbt, "ident"], writes=[("ps", b)])
            S.op("dve", lambda e, s=s, psb=psb: e.tensor_tensor(out=hT[:, :, s * 128:(s + 1) * 128], in0=psb, in1=gb, op=ALU.mult),
                 reads=[("ps", b), "vecs"], writes=["hT"])

    def out_proj(xr_, xtok, wname, nk, actT, act_toks, bias=False, evac=None):
        kpl = 4 if nk >= 4 else nk
        for nh in range(2):
            banks = [bank() for _ in range(NS)]
            for k0 in range(0, nk, kpl):
                src = wb_d[wname][k0 * 128:(k0 + kpl) * 128, nh * 512:(nh + 1) * 512].rearrange("(k p) e -> p k e", p=128)
                wsl, wtok = wload(wname, src, lambda t: t[:, 0:kpl * 512].rearrange("p (k e) -> p k e", k=kpl))
                for kk in range(kpl):
                    kc = k0 + kk
                    for s in range(NS):
                        last = (kc == nk - 1) and not bias
                        S.op("pe", lambda e, kc=kc, s=s, kk=kk, wsl=wsl, last=last, b=banks[s]:
                             e.matmul(ps[b][:], lhsT=actT(kc, s), rhs=wsl[:, kk, :], start=(kc == 0), stop=last),
                             reads=[wtok] + act_toks(kc), writes=[("ps", banks[s])])
            for s in range(NS):
                if bias:
                    S.op("pe", lambda e, s=s, nh=nh, b=banks[s]: e.matmul(ps[b][:], lhsT=ones_row[0:1, :], rhs=bout_row[0:1, nh * 512:(nh + 1) * 512],
                                                                        start=False, stop=True),
                         reads=["ones_row", "bout_row"], writes=[("ps", banks[s])])
                if evac is not None:
                    evac(s, nh, banks[s])
                    continue
                S.op("dve", lambda e, s=s, nh=nh, b=banks[s]: e.tensor_tensor(out=xr_[:, s, nh * 512:(nh + 1) * 512], in0=ps[b][:],
                                                                            in1=xr_[:, s, nh * 512:(nh + 1) * 512], op=ALU.add),
                     reads=[("ps", banks[s]), (xtok, s)], writes=[(xtok, s)])

    def stage_A(xr_, xtok, first_tile):
        yb = LB(0, 4, BF16); xr = LB(4, 8, F32)
        rg = LB(12, 8, F32); ig = LB(20, 8, F32); t1 = LB(28, 8, F32); xrb = LB(28, 4, BF16)
        if first_tile:
            S.op("pool", lambda e: e.memset(upre[:, :, 0:3], 0.0), writes=["upre_c"])
            S.op("pool", lambda e: e.memset(hstate[:], 0.0), writes=["hstate"])
        rmsnorm_hT(xr_, xtok, "lru_norm", 0)
        for q in range(8):
            src = wb_d["lru_w_in"][:, q * 256:(q + 1) * 256].rearrange("(k p) e -> p k e", p=128)
            wsl, wtok = wload("lru_w_in", src, lambda t: t[:].rearrange("p (k e) -> p k e", k=KC))
            for ee in range(2):
                c = 2 * q + ee
                b = bank()
                for kc in range(KC):
                    S.op("pe", lambda e, kc=kc, ee=ee, wsl=wsl, b=b: e.matmul(ps[b][:], lhsT=wsl[:, kc, ee * 128:(ee + 1) * 128], rhs=hT[:, kc, :],
                                                                          start=(kc == 0), stop=(kc == KC - 1)),
                         reads=[wtok, "hT"], writes=[("ps", b)])
                if c < 8:
                    S.op("act", lambda e, c=c, b=b: e.activation(out=yb.flat[:, c * 512:(c + 1) * 512], in_=ps[b][:], func=AF.Gelu_apprx_tanh,
                                                               bias=V("lru_b_in", c)),
                         reads=[("ps", b), "vecs"], writes=yb.tok(c // 2))
                else:
                    cc = c - 8
                    S.op("act", lambda e, cc=cc, b=b: e.activation(out=upre[:, cc, 3:3 + TT], in_=ps[b][:], func=AF.Identity, bias=V("lru_b_in", 8 + cc)),
                         reads=[("ps", b), "vecs"], writes=[("upre", cc)])
        for cc in range(8):
            o = xr.blk(cc)
            rd = [("upre", cc), "upre_c", "vecs"]
            S.op("pool", lambda e, cc=cc, o=o: e.tensor_scalar(out=o, in0=upre[:, cc, 0:TT], scalar1=V("lru_conv_w", 0 * 8 + cc), scalar2=V("lru_conv_b", cc),
                                                             op0=ALU.mult, op1=ALU.add), reads=rd, writes=xr.tok(cc))
            for j in range(1, 4):
                S.op("dve", lambda e, cc=cc, o=o, j=j: e.scalar_tensor_tensor(out=o, in0=upre[:, cc, j:j + TT], scalar=V("lru_conv_w", j * 8 + cc), in1=o,
                                                                             op0=ALU.mult, op1=ALU.add), reads=rd + xr.tok(cc), writes=xr.tok(cc))
            S.op("pool", lambda e, cc=cc, o=o: e.tensor_copy(out=xrb.flat[:, cc * 512:(cc + 1) * 512], in_=o), reads=xr.tok(cc), writes=xrb.tok(cc // 2))
        S.op("pool", lambda e: e.tensor_copy(out=upre[:, :, 0:3], in_=upre[:, :, TT:TT + 3]), reads=[("upre", c) for c in range(8)], writes=["upre_c"])
        for cc in range(8):
            for g, dst in ((0, rg), (1, ig)):
                b = bank()
                S.op("pe", lambda e, cc=cc, g=g, b=b: e.matmul(ps[b][:], lhsT=gateW[:, g * 8 + cc, :], rhs=xrb.flat[:, cc * 512:(cc + 1) * 512], start=True, stop=True),
                     reads=["gateW"] + xrb.tok(cc // 2), writes=[("ps", b)])
                S.op("act", lambda e, cc=cc, g=g, b=b, dst=dst: e.activation(out=dst.blk(cc), in_=ps[b][:], func=AF.Sigmoid, bias=V("lru_gate_b", g * 8 + cc)),
                     reads=[("ps", b), "vecs"], writes=dst.tok(cc))
        for cc in range(8):
            S.op("act", lambda e, cc=cc: e.activation(out=rg.blk(cc), in_=rg.blk(cc), func=AF.Exp, scale=cneg[:, cc:cc + 1]),
                 reads=rg.tok(cc) + ["cneg"], writes=rg.tok(cc))
            S.op("dve", lambda e, cc=cc: e.tensor_tensor(out=t1.blk(cc), in0=rg.blk(cc), in1=rg.blk(cc), op=ALU.mult), reads=rg.tok(cc), writes=t1.tok(cc))
            S.op("pool", lambda e, cc=cc: e.tensor_tensor(out=ig.blk(cc), in0=ig.blk(cc), in1=xr.blk(cc), op=ALU.mult), reads=ig.tok(cc) + xr.tok(cc), writes=ig.tok(cc))
        for cc in range(8):
            S.op("act", lambda e, cc=cc: e.activation(out=t1.blk(cc), in_=t1.blk(cc), func=AF.Sqrt, scale=-1.0, bias=1.0), reads=t1.tok(cc), writes=t1.tok(cc))
            S.op("dve", lambda e, cc=cc: e.tensor_tensor(out=ig.blk(cc), in0=ig.blk(cc), in1=t1.blk(cc), op=ALU.mult), reads=ig.tok(cc) + t1.tok(cc), writes=ig.tok(cc))
            S.op("dve", lambda e, cc=cc: e.tensor_tensor_scan(out=t1.blk(cc), data0=rg.blk(cc), data1=ig.blk(cc), initial=hstate[:, cc:cc + 1], op0=ALU.mult, op1=ALU.add),
                 reads=rg.tok(cc) + ig.tok(cc) + ["hstate"], writes=t1.tok(cc))
            S.op("dve", lambda e, cc=cc: e.tensor_copy(out=hstate[:, cc:cc + 1], in_=t1.blk(cc)[:, TT - 1:TT]), reads=t1.tok(cc), writes=["hstate"])
            S.op("dve", lambda e, cc=cc: e.tensor_tensor(out=yb.flat[:, cc * 512:(cc + 1) * 512], in0=t1.blk(cc), in1=yb.flat[:, cc * 512:(cc + 1) * 512], op=ALU.mult),
                 reads=t1.tok(cc) + yb.tok(cc // 2), writes=yb.tok(cc // 2))
        out_proj(xr_, xtok, "lru_w_out", 8, lambda kc, s: yb.flat[:, kc * 512 + s * 128: kc * 512 + (s + 1) * 128], lambda kc: yb.tok(kc // 2), bias=True)

    def stage_F(xr_, xtok, l, first_tile):
        hid = LB(0, 12, BF16)
        gpre = [sbuf_gpre[0], sbuf_gpre[1]]
        gc = LB(12, 2, F32); gg = LB(14, 2, F32)
        if first_tile:
            S.op("pool", lambda e: e.memset(fcar[l][:], 0.0), writes=[("fcar", l)])
        rmsnorm_hT(xr_, xtok, "ffn_norm", l * 8)
        up = f"ffn_w_up{l}"
        for jj in range(NJ // 2):
            srcg = wb_d[up][:, jj * 256:(jj + 1) * 256].rearrange("(k p) e -> p k e", p=128)
            srcu = wb_d[up][:, DFF + jj * 256:DFF + (jj + 1) * 256].rearrange("(k p) e -> p k e", p=128)
            wg, wgt = wload(up, srcg, lambda t: t[:].rearrange("p (k e) -> p k e", k=KC))
            wu, wut = wload(up, srcu, lambda t: t[:].rearrange("p (k e) -> p k e", k=KC))
            for ee in range(2):
                j = 2 * jj + ee
                r = j % 2
                bg = bank(); bu = bank()
                for kc in range(KC):
                    S.op("pe", lambda e, kc=kc, ee=ee, wg=wg, bg=bg: e.matmul(ps[bg][:], lhsT=wg[:, kc, ee * 128:(ee + 1) * 128], rhs=hT[:, kc, :], start=(kc == 0), stop=(kc == KC - 1)),
                         reads=[wgt, "hT"], writes=[("ps", bg)])
                for kc in range(KC):
                    S.op("pe", lambda e, kc=kc, ee=ee, wu=wu, bu=bu: e.matmul(ps[bu][:], lhsT=wu[:, kc, ee * 128:(ee + 1) * 128], rhs=hT[:, kc, :], start=(kc == 0), stop=(kc == KC - 1)),
                         reads=[wut, "hT"], writes=[("ps", bu)])
                gp = gpre[r]; gpt = ("gpre", r)
                S.op("pool", lambda e, gp=gp, j=j: e.tensor_copy(out=gp[:, 0:2], in_=fcar[l][:, j, :]), reads=[("fcar", l)], writes=[gpt])
                S.op("act", lambda e, gp=gp, bg=bg: e.activation(out=gp[:, 2:2 + TT], in_=ps[bg][:], func=AF.Identity), reads=[("ps", bg)], writes=[gpt])
                S.op("pool", lambda e, gp=gp, j=j: e.tensor_copy(out=fcar[l][:, j, :], in_=gp[:, TT:TT + 2]), reads=[gpt], writes=[("fcar", l)])
                co = VOFF["ffn_conv_w"] + l * 72
                S.op("pool", lambda e, gp=gp, j=j, r=r, co=co: e.tensor_scalar(out=gc.blk(r), in0=gp[:, 0:TT], scalar1=vecs[:, co + j:co + j + 1], scalar2=None, op0=ALU.mult),
                     reads=[gpt, "vecs"], writes=gc.tok(r))
                for tap in (1, 2):
                    S.op("dve", lambda e, gp=gp, j=j, r=r, co=co, tap=tap: e.scalar_tensor_tensor(out=gc.blk(r), in0=gp[:, tap:tap + TT], scalar=vecs[:, co + tap * 24 + j:co + tap * 24 + j + 1],
                                                                                               in1=gc.blk(r), op0=ALU.mult, op1=ALU.add),
                         reads=[gpt, "vecs"] + gc.tok(r), writes=gc.tok(r))
                S.op("act", lambda e, j=j, r=r: e.activation(out=gg.blk(r), in_=gc.blk(r), func=AF.Gelu_apprx_tanh, bias=V("ffn_conv_b", l * 24 + j)),
                     reads=gc.tok(r) + ["vecs"], writes=gg.tok(r))
                S.op("dve", lambda e, j=j, r=r, bu=bu: e.tensor_tensor(out=hid.flat[:, j * 512:(j + 1) * 512], in0=ps[bu][:], in1=gg.blk(r), op=ALU.mult),
                     reads=[("ps", bu)] + gg.tok(r), writes=hid.tok(j // 2))
        out_proj(xr_, xtok, f"ffn_w_dn{l}", NJ, lambda kc, s: hid.flat[:, kc * 512 + s * 128: kc * 512 + (s + 1) * 128], lambda kc: hid.tok(kc // 2))

    sbuf_gpre = [sb(f"gpre{i}", [128, TT + 2]) for i in range(2)]

    KAPPA = 0.6065306597126334

    def stage_C(xr_, xtok, first_tile):
        AT = LB(0, 4, BF16); BT = LB(4, 4, BF16); KT = LB(8, 4, BF16); RT = LB(12, 4, BF16)
        Vt = LB(16, 4, BF16); Kh = LB(20, 4, BF16); Bh = LB(24, 4, BF16)
        xx = LB(28, 4, BF16); xs = LB(32, 4, BF16)

        def v3(lb):
            return lb.flat.rearrange("p (k t) -> p k t", k=KC)

        def v4(lb):
            return lb.flat.rearrange("p (c f) -> p c f", c=NS)

        def mm(out, lhsT, rhs, rd, b, start=True, stop=True):
            S.op("pe", lambda e: e.matmul(out, lhsT=lhsT, rhs=rhs, start=start, stop=stop), reads=rd, writes=[("ps", b)])

        if first_tile:
            S.op("pool", lambda e: e.memset(hcar[:], 0.0), writes=["hcar"])
            S.op("pool", lambda e: e.memset(Hst[:], 0.0), writes=["Hst"])
            S.op("pool", lambda e: e.memset(Hbd[:], 0.0), writes=["Hbd"])
        rmsnorm_hT(xr_, xtok, "rwkv_norm", 0)
        xx3 = v3(xx); xs3 = v3(xs)
        S.op("dve", lambda e: e.tensor_tensor(out=xx3[:, :, 1:TT], in0=hT[:, :, 0:TT - 1], in1=hT[:, :, 1:TT], op=ALU.subtract), reads=["hT"], writes=xx.tok())
        S.op("dve", lambda e: e.tensor_tensor(out=xx3[:, :, 0:1], in0=hcar[:].unsqueeze(2), in1=hT[:, :, 0:1], op=ALU.subtract), reads=["hT", "hcar"], writes=xx.tok())
        S.op("pool", lambda e: e.tensor_copy(out=hcar[:].unsqueeze(2), in_=hT[:, :, TT - 1:TT]), reads=["hT"], writes=["hcar"])

        def make_xs(i):
            for kc in range(KC):
                S.op("dve", lambda e, kc=kc: e.scalar_tensor_tensor(out=xs3[:, kc, :], in0=xx3[:, kc, :], scalar=V("rwkv_mix", i * 8 + kc), in1=hT[:, kc, :], op0=ALU.mult, op1=ALU.add),
                     reads=xx.tok(kc // 2) + ["hT", "vecs"], writes=xs.tok(kc // 2))

        def lora1(wt, wtok, c0, c1, func, out_ap, out_tok):
            M = c1 - c0
            b = bank()
            for kc in range(KC):
                mm(ps[b][0:M, :], wt[:, kc, c0:c1], xs3[:, kc, :], [wtok] + xs.tok(kc // 2), b, start=(kc == 0), stop=(kc == KC - 1))
            S.op("act", lambda e: e.activation(out=out_ap, in_=ps[b][0:M, :], func=func), reads=[("ps", b)], writes=[out_tok])

        make_xs(3); lora1(w1b, "w1b", 0, 64, AF.Tanh, tanhw[:], "tanhw")
        make_xs(4); lora1(a1b, "a1b", 0, 64, AF.Copy, a1o[:], "a1o")
        make_xs(5); lora1(g1b, "g1b", 0, 128, AF.Sigmoid, sgT[:, 0, :], "sgT"); lora1(g1b, "g1b", 128, 160, AF.Sigmoid, sgT[0:32, 1, :], "sgT")

        def proj_fm(i, wname, dst):
            make_xs(i)
            for q in range(4):
                src = wb_d[wname][:, q * 256:(q + 1) * 256].rearrange("(k p) e -> p k e", p=128)
                wsl, wtok = wload(wname, src, lambda t: t[:].rearrange("p (k e) -> p k e", k=KC))
                for ee in range(2):
                    p = 2 * q + ee
                    b = bank()
                    for kc in range(KC):
                        mm(ps[b][:], wsl[:, kc, ee * 128:(ee + 1) * 128], xs3[:, kc, :], [wtok] + xs.tok(kc // 2), b, start=(kc == 0), stop=(kc == KC - 1))
                    S.op("act", lambda e, p=p, b=b: e.activation(out=dst.flat[:, p * 512:(p + 1) * 512], in_=ps[b][:], func=AF.Copy), reads=[("ps", b)], writes=dst.tok(p // 2))

        proj_fm(0, "rwkv_w_r", RT)
        proj_fm(1, "rwkv_w_k", KT)
        make_xs(2)
        Vt4 = v4(Vt); Kh4 = v4(Kh); Bh4 = v4(Bh)
        out_proj(None, None, "rwkv_w_v", 8, lambda kc, s: xs3[:, kc, s * 128:(s + 1) * 128], lambda kc: xs.tok(kc // 2),
                 evac=lambda s, nh, b: S.op("act", lambda e: e.activation(out=Vt4[:, s, nh * 512:(nh + 1) * 512], in_=ps[b][:], func=AF.Copy),
                                            reads=[("ps", b)], writes=Vt.tok(s)))

        TKB = LB(44, 1, BF16); TRQ = LB(45, 1, BF16)
        TK = TKB.flat[:, 0:512]; TB = TKB.flat[:, 512:1024]; TR = TRQ.flat[:, 0:512]; KSQ = TRQ.flat[:, 512:1024]
        c4 = lambda ap: ap.rearrange("p (c t) -> p c t", t=CH)
        def pair_prep(p):
            tb = 28 + 8 * (p % 2)
            Tt = [LB(tb + i, 1, F32) for i in range(8)]
            T0, T1, T2, T3, T4, T5, T6 = [Tt[i].flat for i in range(7)]
            k0, k1, k2, k3, k4, k5, k6 = [Tt[i].tok() for i in range(7)]
            kt = KT.flat[:, p * 512:(p + 1) * 512]; rt = RT.flat[:, p * 512:(p + 1) * 512]
            at = AT.flat[:, p * 512:(p + 1) * 512]; bt = BT.flat[:, p * 512:(p + 1) * 512]
            ktk = KT.tok(p // 2); rtk = RT.tok(p // 2); atk = AT.tok(p // 2); btk = BT.tok(p // 2)
            pc = slice(p * 128, (p + 1) * 128)
            b = bank()
            mm(ps[b][:], w2b[0:64, pc], tanhw[:], ["w2b", "tanhw"], b)
            S.op("act", lambda e, b=b: e.activation(out=T0, in_=ps[b][:], func=AF.Sigmoid, bias=V("rwkv_w0", p)), reads=[("ps", b), "vecs"], writes=k0)
            S.op("dve", lambda e: e.tensor_tensor_scan(out=T1, data0=rmask[:], data1=T0, initial=0.0, op0=ALU.mult, op1=ALU.add), reads=k0 + ["rmask"], writes=k1)
            S.op("pool", lambda e: e.tensor_tensor(out=T0, in0=T1, in1=T0, op=ALU.subtract), reads=k0 + k1, writes=k0)
            S.op("act", lambda e: e.activation(out=T2, in_=T1, func=AF.Exp, scale=-KAPPA), reads=k1, writes=k2)
            S.op("act", lambda e: e.activation(out=T0, in_=T0, func=AF.Exp, scale=-KAPPA), reads=k0, writes=k0)
            S.op("act", lambda e: e.activation(out=T1, in_=T1, func=AF.Exp, scale=KAPPA), reads=k1, writes=k1)
            S.op("pool", lambda e: e.tensor_copy(out=WC[:, p, :], in_=c4(T2)[:, :, CH - 1]), reads=k2, writes=["WC"])
            b = bank()
            mm(ps[b][:], a2b[0:64, pc], a1o[:], ["a2b", "a1o"], b)
            S.op("act", lambda e, b=b: e.activation(out=T3, in_=ps[b][:], func=AF.Sigmoid, bias=V("rwkv_a0", p)), reads=[("ps", b), "vecs"], writes=k3)
            S.op("pool", lambda e: e.tensor_scalar(out=T4, in0=kt, scalar1=V("rwkv_k_k", p), scalar2=None, op0=ALU.mult), reads=ktk + ["vecs"], writes=k4)
            S.op("pool", lambda e: e.tensor_tensor(out=KSQ, in0=T4, in1=T4, op=ALU.mult), reads=k4, writes=TRQ.tok())
            b = bank()
            mm(ps[b][:], blkones[:], KSQ, ["blkones"] + TRQ.tok(), b)
            S.op("act", lambda e, b=b: e.activation(out=T6, in_=ps[b][:], func=AF.Sqrt), reads=[("ps", b)], writes=k6)
            S.op("dve", lambda e: e.tensor_scalar(out=T6, in0=T6, scalar1=1e-12, scalar2=None, op0=ALU.max), reads=k6, writes=k6)
            S.op("dve", lambda e: e.reciprocal(out=T6, in_=T6), reads=k6, writes=k6)
            S.op("dve", lambda e: e.tensor_tensor(out=T4, in0=T4, in1=T6, op=ALU.mult), reads=k4 + k6, writes=k4)
            S.op("pool", lambda e: e.tensor_scalar(out=T5, in0=T3, scalar1=V("rwkv_k_a", p), scalar2=omka[:, p:p + 1], op0=ALU.mult, op1=ALU.add), reads=k3 + ["vecs", "omka"], writes=k5)
            S.op("pool", lambda e: e.tensor_tensor(out=T5, in0=T5, in1=kt, op=ALU.mult), reads=k5 + ktk, writes=k5)
            S.op("dve", lambda e: e.scalar_tensor_tensor(out=at, in0=T4, scalar=-1.0, in1=T0, op0=ALU.mult, op1=ALU.mult), reads=k4 + k0, writes=atk)
            S.op("dve", lambda e: e.tensor_tensor(out=kt, in0=T5, in1=T1, op=ALU.mult), reads=k5 + k1, writes=ktk)
            wcb = WC[:, p, :].unsqueeze(2).to_broadcast([128, 4, CH])
            S.op("pool", lambda e: e.tensor_tensor(out=c4(TK), in0=c4(kt), in1=wcb, op=ALU.mult), reads=ktk + ["WC"], writes=TKB.tok())
            S.op("pool", lambda e: e.tensor_tensor(out=T4, in0=T4, in1=T3, op=ALU.mult), reads=k4 + k3, writes=k4)
            S.op("dve", lambda e: e.tensor_tensor(out=bt, in0=T4, in1=T1, op=ALU.mult), reads=k4 + k1, writes=btk)
            S.op("pool", lambda e: e.tensor_tensor(out=c4(TB), in0=c4(bt), in1=wcb, op=ALU.mult), reads=btk + ["WC"], writes=TKB.tok())
            S.op("dve", lambda e: e.scalar_tensor_tensor(out=TR, in0=rt, scalar=V("rwkv_r_k", p), in1=T5, op0=ALU.mult, op1=ALU.mult), reads=rtk + k5 + ["vecs"], writes=TRQ.tok())
            S.op("dve", lambda e: e.tensor_tensor(out=rt, in0=rt, in1=T2, op=ALU.mult), reads=rtk + k2, writes=rtk)
            b = bank()
            for c in range(NS):
                mm(ps[b][:, c * 2:(c + 1) * 2], TR[:, c * 128:(c + 1) * 128], sel2[:], TRQ.tok() + ["sel2"], b)
            S.op("act", lambda e, b=b: e.activation(out=dtok[:, :, 2 * p:2 * p + 2], in_=ps[b][:, 0:8].rearrange("p (c e) -> p c e", e=2), func=AF.Copy),
                 reads=[("ps", b)], writes=["dtok"])
            for src, d4, dlb in ((TK, Kh4, Kh), (TB, Bh4, Bh)):
                b = bank()
                psb = ps[b][:].bitcast(BF16)[:, 0:512].rearrange("p (c f) -> p c f", c=NS)
                for c in range(NS):
                    S.op("pe", lambda e, c=c, psb=psb, src=src: e.transpose(psb[:, c, :], src[:, c * 128:(c + 1) * 128], ident[:]), reads=TKB.tok() + ["ident"], writes=[("ps", b)])
                S.op("act", lambda e, psb=psb, d4=d4: e.activation(out=d4[:, :, pc], in_=psb, func=AF.Copy), reads=[("ps", b)], writes=dlb.tok())

        for p_ in range(8):
            pair_prep(p_)
        Xb = LB(40, 1, BF16); Ub = LB(41, 1, BF16); yq = LB(42, 2, F32); ysq = LB(44, 2, F32); ygb = LB(44, 1, BF16)
        PAD = LB(40, 4, BF16); BTB = LB(44, 2, BF16); PADALL = LB(40, 6, BF16)
        pad5 = PAD.flat.rearrange("p (q j t) -> p q j t", q=8, j=4)
        btb4 = BTB.flat.rearrange("p (q j t) -> p q j t", q=8, j=2)
        AT3 = v3(AT); BT3 = v3(BT); KT3 = v3(KT); RT3 = v3(RT)
        h8 = lambda ap: ap.rearrange("p (h v) -> p h v", v=64)
        mb2 = lambda m: m[:].unsqueeze(1).to_broadcast([128, 2, 128])
        mb4 = lambda m: m[:].unsqueeze(1).to_broadcast([128, 4, 128])
        def scan_chunk(c):
            cc = slice(c * 128, (c + 1) * 128)
            S.op("pool", lambda e: e.memset(PADALL.flat, 0.0), writes=PADALL.tok())
            for e_ in range(2):
                rows = slice(64 * e_, 64 * e_ + 64)
                S.op("pool", lambda e, rows=rows, e_=e_: e.tensor_copy(out=pad5[rows, :, e_, :], in_=AT3[rows, :, cc]), reads=AT.tok() + PAD.tok(), writes=PAD.tok())
                S.op("pool", lambda e, rows=rows, e_=e_: e.tensor_copy(out=pad5[rows, :, 2 + e_, :], in_=RT3[rows, :, cc]), reads=RT.tok() + PAD.tok(), writes=PAD.tok())
                S.op("pool", lambda e, rows=rows, e_=e_: e.tensor_copy(out=btb4[rows, :, e_, :], in_=BT3[rows, :, cc]), reads=BT.tok() + BTB.tok(), writes=BTB.tok())
            Q = []
            for q in range(4):
                QL = LB(28 + 3 * q, 3, BF16)
                hv = lambda i, QL=QL: QL.flat[:, i * 512:(i + 1) * 512].rearrange("p (h t) -> p h t", h=4)
                Aak, Ark, Arb, P = hv(0), hv(1), hv(2), hv(3)
                PTST = QL.flat[:, 2048:3072].rearrange("p (h x) -> p h x", h=4)
                PT = PTST[:, :, 0:128]; ST = PTST[:, :, 128:256]
                tA, tB, tC = QL.tok(0), QL.tok(1), QL.tok(2)
                Q.append((Aak, Ark, Arb, P, PTST, PT, ST, tA, tB, tC))
                b3 = bank()
                for pp in range(2):
                    p = 2 * q + pp
                    hs2 = slice(2 * pp, 2 * pp + 2)
                    b1 = bank(); b2 = bank()
                    padp = PAD.flat[:, p * 512:(p + 1) * 512]
                    mm(ps[b1][:], BT3[:, p, cc], padp, BT.tok(p // 2) + PAD.tok(), b1)
                    mm(ps[b2][:], KT3[:, p, cc], padp, KT.tok(p // 2) + PAD.tok(), b2)
                    mm(ps[b3][:, pp * 256:(pp + 1) * 256], AT3[:, p, cc], BTB.flat[:, p * 256:(p + 1) * 256], AT.tok(p // 2) + BTB.tok(), b3)
                    v2 = lambda b, half: ps[b][:, half * 256:(half + 1) * 256].rearrange("p (h t) -> p h t", h=2)
                    for (dst, b, half, m, tk) in ((PT, b1, 0, mUs, tC), (Arb, b1, 1, mUi, tB), (Aak, b2, 0, mUs, tA), (Ark, b2, 1, mUi, tA)):
                        S.op("dve", lambda e, dst=dst, b=b, half=half, m=m, hs2=hs2: e.tensor_tensor(out=dst[:, hs2, :], in0=v2(b, half), in1=mb2(m), op=ALU.mult),
                             reads=[("ps", b), "mask%d" % id(m)], writes=tk)
                S.op("dve", lambda e, P=P, b3=b3: e.tensor_tensor(out=P, in0=ps[b3][:].rearrange("p (h t) -> p h t", h=4), in1=mb4(mLs), op=ALU.mult),
                     reads=[("ps", b3), "mask%d" % id(mLs)], writes=tB)
                S.op("pool", lambda e, ST=ST, PT=PT: e.tensor_tensor(out=ST, in0=PT, in1=mb4(ident), op=ALU.add), reads=tC + ["ident"], writes=tC)
            for i in range(7):
                for q in range(4):
                    Aak, Ark, Arb, P, PTST, PT, ST, tA, tB, tC = Q[q]
                    p4 = lambda b: ps[b][:].rearrange("p (h t) -> p h t", h=4)
                    if i < 6:
                        bP = bank()
                        for hq in range(4):
                            mm(ps[bP][:, hq * 128:(hq + 1) * 128], PT[:, hq, :], P[:, hq, :], tB + tC, bP)
                    if i == 0:
                        bT = bank()
                        for hq in range(4):
                            mm(ps[bT][:, hq * 128:(hq + 1) * 128], P[:, hq, :], PT[:, hq, :], tB + tC, bT)
                    elif i < 6:
                        bT2 = [bank(), bank()]
                        for hq in range(4):
                            mm(ps[bT2[hq // 2]][:, (hq % 2) * 256:(hq % 2 + 1) * 256], P[:, hq, :], PTST[:, hq, :], tB + tC, bT2[hq // 2])
                    else:
                        bS = bank()
                        for hq in range(4):
                            mm(ps[bS][:, hq * 128:(hq + 1) * 128], P[:, hq, :], ST[:, hq, :], tB + tC, bS)
                    if i < 6:
                        S.op("act", lambda e, P=P, bP=bP: e.activation(out=P, in_=p4(bP), func=AF.Copy), reads=[("ps", bP)], writes=tB)
                    if i == 0:
                        S.op("act", lambda e, PT=PT, bT=bT: e.activation(out=PT, in_=p4(bT), func=AF.Copy), reads=[("ps", bT)], writes=tC)
                    elif i < 6:
                        for k in range(2):
                            pv = ps[bT2[k]][:].rearrange("p (h x) -> p h x", h=2)
                            S.op("act", lambda e, PT=PT, pv=pv, k=k: e.activation(out=PT[:, 2 * k:2 * k + 2, :], in_=pv[:, :, 0:128], func=AF.Copy), reads=[("ps", bT2[k])], writes=tC)
                            S.op("dve", lambda e, ST=ST, pv=pv, k=k: e.tensor_tensor(out=ST[:, 2 * k:2 * k + 2, :], in0=pv[:, :, 128:256], in1=ST[:, 2 * k:2 * k + 2, :], op=ALU.add),
                                 reads=[("ps", bT2[k])] + tC, writes=tC)
                    else:
                        S.op("dve", lambda e, ST=ST, bS=bS: e.tensor_tensor(out=ST, in0=p4(bS), in1=ST, op=ALU.add), reads=[("ps", bS)] + tC, writes=tC)
            bX = [bank(), bank()]
            for p in range(8):
                bx = bX[p // 4]
                for e_ in range(2):
                    h = 2 * p + e_
                    q, hq = divmod(h, 4)
                    Aak, Ark, Arb, P, PTST, PT, ST, tA, tB, tC = Q[q]
                    o = ps[bx][:, (h % 8) * 64:(h % 8 + 1) * 64]
                    mm(o, AT3[:, p, cc], Hbd[:, p, e_ * 64:(e_ + 1) * 64], AT.tok(p // 2) + ["Hbd"], bx, start=True, stop=False)
                    mm(o, Aak[:, hq, :], Vt4[:, c, h * 64:(h + 1) * 64], tA + Vt.tok(c), bx, start=False, stop=True)
            for k in range(2):
                S.op("act", lambda e, k=k: e.activation(out=Xb.flat[:, k * 512:(k + 1) * 512], in_=ps[bX[k]][:], func=AF.Copy), reads=[("ps", bX[k])], writes=Xb.tok())
            bU = [bank(), bank()]
            for h in range(16):
                q, hq = divmod(h, 4)
                Aak, Ark, Arb, P, PTST, PT, ST, tA, tB, tC = Q[q]
                mm(ps[bU[h // 8]][:, (h % 8) * 64:(h % 8 + 1) * 64], ST[:, hq, :], Xb.flat[:, h * 64:(h + 1) * 64], tC + Xb.tok(), bU[h // 8])
            for k in range(2):
                S.op("act", lambda e, k=k: e.activation(out=Ub.flat[:, k * 512:(k + 1) * 512], in_=ps[bU[k]][:], func=AF.Copy), reads=[("ps", bU[k])], writes=Ub.tok())
            bY = [bank(), bank()]
            for p in range(8):
                by = bY[p // 4]
                for e_ in range(2):
                    h = 2 * p + e_
                    q, hq = divmod(h, 4)
                    Aak, Ark, Arb, P, PTST, PT, ST, tA, tB, tC = Q[q]
                    o = ps[by][:, (h % 8) * 64:(h % 8 + 1) * 64]
                    mm(o, RT3[:, p, cc], Hbd[:, p, e_ * 64:(e_ + 1) * 64], RT.tok(p // 2) + ["Hbd"], by, start=True, stop=False)
                    mm(o, Ark[:, hq, :], Vt4[:, c, h * 64:(h + 1) * 64], tA + Vt.tok(c), by, start=False, stop=False)
                    mm(o, Arb[:, hq, :], Ub.flat[:, h * 64:(h + 1) * 64], tB + Ub.tok(), by, start=False, stop=True)
            bH = [bank(), bank()]
            for p in range(8):
                o = ps[bH[p // 4]][:, (p % 4) * 128:(p % 4 + 1) * 128]
                mm(o, Kh4[:, c, p * 128:(p + 1) * 128], Vt4[:, c, p * 128:(p + 1) * 128], Kh.tok(c) + Vt.tok(c), bH[p // 4], start=True, stop=False)
                mm(o, Bh4[:, c, p * 128:(p + 1) * 128], Ub.flat[:, p * 128:(p + 1) * 128], Bh.tok(c) + Ub.tok(), bH[p // 4], start=False, stop=True)
            for e_ in range(2):
                rows = slice(64 * e_, 64 * e_ + 64)
                S.op("pool", lambda e, rows=rows: e.tensor_tensor(out=Hst[rows, :, :], in0=Hst[rows, :, :], in1=WC[rows, :, c:c + 1].to_broadcast([64, 8, 64]), op=ALU.mult),
                     reads=["Hst", "WC"], writes=["Hst"])
                for k in range(2):
                    pv = ps[bH[k]][rows, :].rearrange("p (q f) -> p q f", q=4)[:, :, e_ * 64:(e_ + 1) * 64]
                    S.op("dve", lambda e, rows=rows, pv=pv, k=k: e.tensor_tensor(out=Hst[rows, 4 * k:4 * k + 4, :], in0=pv, in1=Hst[rows, 4 * k:4 * k + 4, :], op=ALU.add),
                         reads=[("ps", bH[k]), "Hst"], writes=["Hst"])
            for e_ in range(2):
                rows = slice(64 * e_, 64 * e_ + 64)
                S.op("pool", lambda e, rows=rows, e_=e_: e.tensor_copy(out=Hbd[rows, :, e_ * 64:(e_ + 1) * 64], in_=Hst[rows, :, :]), reads=["Hst", "Hbd"], writes=["Hbd"])
            yq3 = yq.flat.rearrange("p (h v) -> p h v", v=64); ysq3 = ysq.flat.rearrange("p (h v) -> p h v", v=64)
            for k in range(2):
                S.op("dve", lambda e, k=k: e.tensor_reduce(out=st_a[:, 8 * k:8 * k + 8], in_=h8(ps[bY[k]][:]), axis=AX.X, op=ALU.add), reads=[("ps", bY[k])], writes=["st_a"])
                S.op("act", lambda e, k=k: e.activation(out=ysq.flat[:, k * 512:(k + 1) * 512], in_=ps[bY[k]][:], func=AF.Square), reads=[("ps", bY[k])], writes=ysq.tok(k))
                S.op("dve", lambda e, k=k: e.tensor_reduce(out=st_b[:, 8 * k:8 * k + 8], in_=ysq3[:, 8 * k:8 * k + 8, :], axis=AX.X, op=ALU.add), reads=ysq.tok(k), writes=["st_b"])
            S.op("dve", lambda e: e.tensor_scalar(out=st_a[:], in0=st_a[:], scalar1=1.0 / 64, scalar2=None, op0=ALU.mult), reads=["st_a"], writes=["st_a"])
            S.op("dve", lambda e: e.tensor_tensor(out=st_c[:], in0=st_a[:], in1=st_a[:], op=ALU.mult), reads=["st_a"], writes=["st_c"])
            S.op("dve", lambda e: e.scalar_tensor_tensor(out=st_b[:], in0=st_b[:], scalar=1.0 / 64, in1=st_c[:], op0=ALU.mult, op1=ALU.subtract), reads=["st_b", "st_c"], writes=["st_b"])
            S.op("act", lambda e: e.activation(out=st_b[:], in_=st_b[:], func=AF.Sqrt, bias=GN_EPS), reads=["st_b"], writes=["st_b"])
            S.op("dve", lambda e: e.reciprocal(out=st_b[:], in_=st_b[:]), reads=["st_b"], writes=["st_b"])
            for k in range(2):
                hs_ = slice(8 * k, 8 * k + 8)
                S.op("dve", lambda e, k=k, hs_=hs_: e.tensor_tensor(out=yq3[:, hs_, :], in0=h8(ps[bY[k]][:]), in1=st_a[:, hs_].unsqueeze(2).to_broadcast([128, 8, 64]), op=ALU.subtract),
                     reads=[("ps", bY[k]), "st_a"], writes=yq.tok(k))
                S.op("pool", lambda e, hs_=hs_: e.tensor_tensor(out=yq3[:, hs_, :], in0=yq3[:, hs_, :], in1=st_b[:, hs_].unsqueeze(2).to_broadcast([128, 8, 64]), op=ALU.mult),
                     reads=yq.tok(k) + ["st_b"], writes=yq.tok(k))
            S.op("pool", lambda e: e.tensor_tensor(out=yq.flat, in0=yq.flat, in1=lnw_b[:], op=ALU.mult), reads=yq.tok() + ["lnw_b"], writes=yq.tok())
            S.op("pool", lambda e: e.tensor_tensor(out=yq.flat, in0=yq.flat, in1=lnb_b[:], op=ALU.add), reads=yq.tok() + ["lnb_b"], writes=yq.tok())
            S.op("pool", lambda e: e.tensor_tensor(out=ysq3, in0=h8(Vt4[:, c, :]), in1=dtok[:, c, :].unsqueeze(2).to_broadcast([128, 16, 64]), op=ALU.mult),
                 reads=Vt.tok(c) + ["dtok"], writes=ysq.tok())
            S.op("pool", lambda e: e.tensor_tensor(out=yq.flat, in0=yq.flat, in1=ysq.flat, op=ALU.add), reads=yq.tok() + ysq.tok(), writes=yq.tok())
            bG = [bank(), bank()]
            for nh in range(2):
                mm(ps[bG[nh]][:], sgT[:, 0, cc], g2b[:, 0, nh * 512:(nh + 1) * 512], ["sgT", "g2b"], bG[nh], start=True, stop=False)
                mm(ps[bG[nh]][:], sgT[0:32, 1, cc], g2b[0:32, 1, nh * 512:(nh + 1) * 512], ["sgT", "g2b"], bG[nh], start=False, stop=True)
                S.op("dve", lambda e, nh=nh: e.tensor_tensor(out=ygb.flat[:, nh * 512:(nh + 1) * 512], in0=ps[bG[nh]][:], in1=yq.flat[:, nh * 512:(nh + 1) * 512], op=ALU.mult),
                     reads=[("ps", bG[nh])] + yq.tok(nh) + ysq.tok(), writes=ygb.tok())
            b = bank()
            psb = ps[b][:].bitcast(BF16).rearrange("p (k t) -> p k t", k=KC)
            for kc in range(KC):
                S.op("pe", lambda e, kc=kc, psb=psb: e.transpose(psb[:, kc, :], ygb.flat[:, kc * 128:(kc + 1) * 128], ident[:]), reads=ygb.tok() + ["ident"], writes=[("ps", b)])
            S.op("act", lambda e, psb=psb: e.activation(out=hT[:, :, cc], in_=psb, func=AF.Copy), reads=[("ps", b)], writes=["hT"])
        for c_ in range(NS):
            scan_chunk(c_)
        out_proj(xr_, xtok, "rwkv_w_out", 8, lambda kc, s: hT[:, kc, s * 128:(s + 1) * 128], lambda kc: ["hT"])

    def stage_E(xr_, xtok, seq, t0, ost):
        for s in range(NS):
            S.op("act", lambda e, s=s: e.activation(out=sqj[:], in_=xr_[:, s, :], func=AF.Square, accum_out=ss[:, s:s + 1]),
                 reads=[(xtok, s)], writes=[("xn", 0), "ss"])
        S.op("dve", lambda e: e.tensor_scalar(out=rstd[:], in0=ss[:], scalar1=1.0 / D, scalar2=RMS_EPS, op0=ALU.mult, op1=ALU.add), reads=["ss"], writes=["rstd"])
        S.op("act", lambda e: e.activation(out=rstd[:], in_=rstd[:], func=AF.Sqrt), reads=["rstd"], writes=["rstd"])
        S.op("dve", lambda e: e.reciprocal(out=rstd[:], in_=rstd[:]), reads=["rstd"], writes=["rstd"])
        for s in range(NS):
            S.op("dve", lambda e, s=s: e.scalar_tensor_tensor(out=xr_[:, s, :], in0=xr_[:, s, :], scalar=rstd[:, s:s + 1], in1=gfin_b[:], op0=ALU.mult, op1=ALU.mult),
                 reads=[(xtok, s), "rstd", "gfin_b"], writes=[(xtok, s)])
        dst = out_d[seq, t0:t0 + TT, :].rearrange("(s p) d -> p s d", p=128)
        return S.op("sp", lambda e: e.dma_start(out=dst, in_=xr_[:]), reads=[(xtok, s) for s in range(NS)], writes=[], dsem=ost)

    xsem = [S.new_dsem("x") for _ in range(2)]
    osem = [S.new_dsem("o") for _ in range(2)]
    tiles = [(q, t) for q in range(n_seq) for t in range(n_tiles)]

    def xload(i):
        q, t = tiles[i]
        buf = xres[0]
        src = x_d[q, t * TT:(t + 1) * TT, :].rearrange("(s p) d -> p s d", p=128)
        S.op("sp", lambda e: e.dma_start(out=buf[:], in_=src), reads=[], writes=[("x0", s) for s in range(NS)], dsem=xsem[0])

    last_out = []
    for i, (q, t) in enumerate(tiles):
        xload(i)
        xr_ = xres[0]
        xtok = "x0"
        first = (t == 0)
        if "A" in stages:
            stage_A(xr_, xtok, first)
        if "B" in stages:
            stage_F(xr_, xtok, 0, first)
        if "C" in stages:
            stage_C(xr_, xtok, first)
        if "D" in stages:
            stage_F(xr_, xtok, 1, first)
        c = stage_E(xr_, xtok, q, t * TT, osem[0])
        last_out.append(c)
    for c in last_out[-2:]:
        S.final_wait("sp", c)

    with nc.Block() as block:
        S.emit(block)
    es.close()
    return nc, S


def kernel(**inputs):
    inp = {k: np.asarray(v) for k, v in inputs.items()}
    x = np.ascontiguousarray(inp["x"], dtype=np.float32)
    B, T, _ = x.shape
    n_seq = B // N_CORES
    nc, _ = build(n_seq=n_seq, T=T)
    shared = host_shared(inp)
    in_maps = []
    for c in range(N_CORES):
        m = dict(shared)
        m["x"] = np.ascontiguousarray(x[c * n_seq:(c + 1) * n_seq])
        in_maps.append(m)
    res = run_bass_kernel_spmd(nc, in_maps, core_ids=list(range(N_CORES)))
    return np.concatenate([r["out"] for r in res.results], axis=0)


def host_shared(inp):
    f = lambda a: np.ascontiguousarray(np.asarray(a, dtype=np.float32))
    return {
        "vecs": make_vecs(inp),
        "lru_w_in": f(inp["lru_w_in"][0]), "lru_w_out": f(inp["lru_w_out"][0]),
        "ffn_w_up0": f(inp["ffn_w_up"][0]), "ffn_w_up1": f(inp["ffn_w_up"][1]),
        "ffn_w_dn0": f(inp["ffn_w_down"][0]), "ffn_w_dn1": f(inp["ffn_w_down"][1]),
        "rwkv_w_r": f(inp["rwkv_w_rkv"][0, 0]), "rwkv_w_k": f(inp["rwkv_w_rkv"][0, 1]), "rwkv_w_v": f(inp["rwkv_w_rkv"][0, 2]),
        "rwkv_w_out": f(inp["rwkv_w_out"][0]),
        "lru_gate_w": f(inp["lru_gate_w"][0]), "lru_b_out": f(inp["lru_b_out"]),
        "rwkv_w1": f(inp["rwkv_w1"][0]), "rwkv_a1": f(inp["rwkv_a1"][0]), "rwkv_g1": f(inp["rwkv_g1"][0]),
        "rwkv_w2": f(inp["rwkv_w2"][0]), "rwkv_a2": f(inp["rwkv_a2"][0]), "rwkv_g2": f(inp["rwkv_g2"][0]),
        "rwkv_ln_w": f(inp["rwkv_ln_w"][0]), "rwkv_ln_b": f(inp["rwkv_ln_b"][0]), "final_norm": f(inp["final_norm"]),
    }
```

```python
import numpy as np
from contextlib import ExitStack
import concourse.bass as bass
import concourse.mybir as mybir
from concourse.bass_utils import run_bass_kernel_spmd

F32 = mybir.dt.float32
BF16 = mybir.dt.bfloat16
AF = mybir.ActivationFunctionType
ALU = mybir.AluOpType
AX = mybir.AxisListType

D = 1024
KC = 8
TT = 512
NS = 4
DFF = 3072
NJ = 24
NW = 5
CH = 128
RMS_EPS = 1e-6
GN_EPS = 64e-5
N_CORES = 8
SAME_ENGINE_SYNC = True

VEC_SPECS = [
    ("lru_norm", 8), ("lru_b_in", 16), ("lru_conv_w", 32), ("lru_conv_b", 8), ("lru_gate_b", 16), ("lru_lambda", 8),
    ("rwkv_norm", 8), ("rwkv_mix", 48), ("rwkv_w0", 8), ("rwkv_a0", 8), ("rwkv_k_k", 8), ("rwkv_k_a", 8), ("rwkv_r_k", 8),
    ("ffn_norm", 16), ("ffn_conv_w", 144), ("ffn_conv_b", 48),
]
VOFF = {}
_o = 0
for _n, _w in VEC_SPECS:
    VOFF[_n] = _o
    _o += _w
NV = _o


def _fm(v):
    v = np.asarray(v, dtype=np.float32)
    return np.ascontiguousarray(v.reshape(-1, 128).T)


def make_vecs(inp):
    cols = [_fm(inp[n]) for n, _ in VEC_SPECS]
    out = np.concatenate(cols, axis=1)
    assert out.shape == (128, NV), out.shape
    return np.ascontiguousarray(out)


class DmaSem:
    def __init__(self, sem):
        self.sem = sem
        self.val = 0


class Sched:
    ENG = ("pe", "act", "dve", "pool", "sp")
    EPOCH = 30000

    def __init__(self, nc, es):
        self.nc = nc
        self.es = es
        self.nsem = 0
        self.prog = {e: [] for e in self.ENG}
        self.esem = {}
        self.cnt = {}
        self.own = {e: set() for e in self.ENG}
        for e in self.ENG:
            self._new_epoch(e)
        self.waited = {e: {} for e in self.ENG}
        self.lastw = {}
        self.readers = {}
        self.nops = 0

    def new_sem(self, name):
        self.nsem += 1
        return self.es.enter_context(self.nc.semaphore(f"{name}{self.nsem}"))

    def new_dsem(self, name="d"):
        return DmaSem(self.new_sem(name))

    def _new_epoch(self, e):
        s = self.new_sem("e" + e)
        self.esem[e] = s
        self.cnt[e] = 0
        self.own[e].add(id(s))

    def op(self, eng, fn, reads=(), writes=(), dsem=None):
        deps = {}

        def add(c):
            if c is None:
                return
            k = id(c[0])
            if k not in deps or deps[k][1] < c[1]:
                deps[k] = c

        for t in reads:
            add(self.lastw.get(t))
        for t in writes:
            add(self.lastw.get(t))
            for c in self.readers.get(t, {}).values():
                add(c)
        for k, (s, v) in deps.items():
            if k in self.own[eng] and dsem is None and (eng == "pe" or not SAME_ENGINE_SYNC):
                continue
            if self.waited[eng].get(k, 0) < v:
                self.prog[eng].append(("w", s, v))
                self.waited[eng][k] = v
        if dsem is None:
            if self.cnt[eng] >= self.EPOCH:
                self._new_epoch(eng)
            self.cnt[eng] += 1
            comp = (self.esem[eng], self.cnt[eng])
            inc = 1
        else:
            dsem.val += 16
            comp = (dsem.sem, dsem.val)
            inc = 16
        self.prog[eng].append(("o", fn, comp[0], inc))
        for t in reads:
            self.readers.setdefault(t, {})[id(comp[0])] = comp
        for t in writes:
            self.lastw[t] = comp
            self.readers[t] = {}
        self.nops += 1
        return comp

    def drain(self, eng):
        if self.cnt[eng] > 0:
            self.prog[eng].append(("w", self.esem[eng], self.cnt[eng]))

    def final_wait(self, eng, comp):
        self.prog[eng].append(("w", comp[0], comp[1]))

    def emit(self, block):
        def mk(e):
            def f(h):
                for it in self.prog[e]:
                    if it[0] == "w":
                        h.wait_ge(it[1], it[2])
                    else:
                        it[1](h).then_inc(it[2], it[3])
            return f

        block.tensor(mk("pe"))
        block.scalar(mk("act"))
        block.vector(mk("dve"))
        block.gpsimd(mk("pool"))
        block.sync(mk("sp"))


def build(n_seq=4, T=2048, stages="ABCDE"):
    assert T % TT == 0
    n_tiles = T // TT
    nc = bass.Bass("TRN2", target_bir_lowering=False)
    es = ExitStack()
    S = Sched(nc, es)

    def dram(name, shape, dt=F32, kind="ExternalInput"):
        return nc.dram_tensor(name, list(shape), dt, kind=kind).ap()

    def sb(name, shape, dt=F32):
        return es.enter_context(nc.sbuf_tensor("s_" + name, list(shape), dt))

    x_d = dram("x", [n_seq, T, D])
    out_d = dram("out", [n_seq, T, D], kind="ExternalOutput")
    vecs_d = dram("vecs", [128, NV])
    bigw = {
        "lru_w_in": (D, 2048), "lru_w_out": (D, D),
        "ffn_w_up0": (D, 2 * DFF), "ffn_w_up1": (D, 2 * DFF), "ffn_w_dn0": (DFF, D), "ffn_w_dn1": (DFF, D),
        "rwkv_w_r": (D, D), "rwkv_w_k": (D, D), "rwkv_w_v": (D, D), "rwkv_w_out": (D, D),
    }
    w_d = {n: dram(n, s) for n, s in bigw.items()}
    wb_d = {n: dram("b_" + n, s, BF16, kind="Internal") for n, s in bigw.items()}
    gate_w_d = dram("lru_gate_w", [2, 16, 64, 64])
    b_out_d = dram("lru_b_out", [1, D])
    w1_d = dram("rwkv_w1", [D, 64]); a1_d = dram("rwkv_a1", [D, 64]); g1_d = dram("rwkv_g1", [D, 160])
    w2_d = dram("rwkv_w2", [64, D]); a2_d = dram("rwkv_a2", [64, D]); g2_d = dram("rwkv_g2", [160, D])
    lnw_d = dram("rwkv_ln_w", [D]); lnb_d = dram("rwkv_ln_b", [D]); fin_d = dram("final_norm", [D])

    vecs = sb("vecs", [128, NV])
    ident = sb("ident", [128, 128], BF16)
    ones_row = sb("ones_row", [1, 128], BF16)
    bout_row = sb("bout_row", [1, D], BF16)
    gateW = sb("gateW", [128, 16, 128], BF16)
    cneg = sb("cneg", [128, 8])
    gfin_b = sb("gfin_b", [128, D])
    HAS_C = "C" in stages
    if HAS_C:
        lnw_b = sb("lnw_b", [128, D]); lnb_b = sb("lnb_b", [128, D])
        w1b = sb("w1b", [128, KC, 64], BF16); a1b = sb("a1b", [128, KC, 64], BF16); g1b = sb("g1b", [128, KC, 160], BF16)
        w2b = sb("w2b", [64, D], BF16); a2b = sb("a2b", [64, D], BF16); g2b = sb("g2b", [128, 2, D], BF16)
        mUs = sb("mUs", [128, 128], BF16); mUi = sb("mUi", [128, 128], BF16); mLs = sb("mLs", [128, 128], BF16)
        sel2 = sb("sel2", [128, 2], BF16); blkones = sb("blkones", [128, 128], BF16)
        rmask = sb("rmask", [128, TT]); omka = sb("omka", [128, 8])
        tanhw = sb("tanhw", [64, TT], BF16); a1o = sb("a1o", [64, TT], BF16); sgT = sb("sgT", [128, 2, TT], BF16)
        WC = sb("WC", [128, 8, 4]); dtok = sb("dtok", [128, 4, 16]); hcar = sb("hcar", [128, 8], BF16)
        Hst = sb("Hst", [128, 8, 64]); Hbd = sb("Hbd", [128, 8, 128], BF16)
        st_a = sb("st_a", [128, 16]); st_b = sb("st_b", [128, 16]); st_c = sb("st_c", [128, 16])
    xres = [sb("xres0", [128, NS, D])]
    hT = sb("hT", [128, KC, TT], BF16)
    xn = [sb(f"xn{i}", [128, D], BF16) for i in range(2)]
    sqj = xn[0]
    ss = sb("ss", [128, NS]); rstd = sb("rstd", [128, NS])
    wring = [sb(f"wring{i}", [128, 2048], BF16) for i in range(NW)]
    wsem = [S.new_dsem("w") for _ in range(NW)]
    hstate = sb("hstate", [128, 8])
    upre = sb("upre", [128, 8, TT + 3])
    fcar = [sb(f"fcar{l}", [128, NJ, 2]) for l in range(2)]
    NBLK = 46
    arena = sb("arena", [128, NBLK * 1024], BF16)
    ps = [es.enter_context(nc.psum_tensor(f"ps{i}", [128, 512], F32)) for i in range(8)]

    def V(name, c, n=1):
        o = VOFF[name] + c
        return vecs[:, o:o + n]

    class LB:
        def __init__(self, b0, nb, dt, inner=None):
            self.b0, self.nb, self.dt = b0, nb, dt
            a = arena[:, b0 * 1024:(b0 + nb) * 1024]
            if dt == F32:
                a = a.bitcast(F32)
            self.flat = a
            self.per = 1024 if dt == BF16 else 512
        def blk(self, i, n=1):
            return self.flat[:, i * self.per:(i + n) * self.per]
        def tok(self, i=None, n=1):
            if i is None:
                return [("ar", self.b0 + k) for k in range(self.nb)]
            return [("ar", self.b0 + i + k) for k in range(n)]

    bank_ctr = [0]

    def bank():
        b = bank_ctr[0] % 8
        bank_ctr[0] += 1
        return b

    wi = [0]

    def wload(name, src_ap, view):
        i = wi[0] % NW
        wi[0] += 1
        dst = view(wring[i])
        S.op("sp", lambda e, dst=dst, src_ap=src_ap: e.dma_start(out=dst, in_=src_ap),
             reads=[("wd", name)], writes=[("w", i)], dsem=wsem[i])
        return dst, ("w", i)

    cs_by_eng = {"sp": S.new_dsem("c"), "pool": S.new_dsem("cp")}
    const_toks = []

    def cload(eng, out_ap, in_ap, tok):
        S.op(eng, lambda e: e.dma_start(out=out_ap, in_=in_ap), reads=[], writes=[tok], dsem=cs_by_eng[eng])
        const_toks.append((tok, eng))

    cload("sp", vecs[:], vecs_d[:, :], "vecs")
    cload("sp", gfin_b[:], fin_d.partition_broadcast(128), "gfin_b")
    cload("pool", bout_row[:], b_out_d[:, :], "bout_row")
    S.op("pool", lambda e: e.memset(gateW[:], 0.0), writes=["gateW"])
    for g in range(2):
        for par in range(2):
            src = gate_w_d[g, par::2].rearrange("n c d -> c n d")
            dst = gateW[par * 64:(par + 1) * 64, g * 8:(g + 1) * 8, par * 64:(par + 1) * 64]
            cload("pool", dst, src, "gateW")
    if HAS_C:
        cload("sp", lnw_b[:], lnw_d.partition_broadcast(128), "lnw_b")
        cload("sp", lnb_b[:], lnb_d.partition_broadcast(128), "lnb_b")
        cload("pool", w1b[:], w1_d.rearrange("(k p) r -> p k r", p=128), "w1b")
        cload("pool", a1b[:], a1_d.rearrange("(k p) r -> p k r", p=128), "a1b")
        cload("pool", g1b[:], g1_d.rearrange("(k p) r -> p k r", p=128), "g1b")
        cload("pool", w2b[:], w2_d[:, :], "w2b")
        cload("pool", a2b[:], a2_d[:, :], "a2b")
        cload("pool", g2b[:, 0, :], g2_d[0:128, :], "g2b")
        cload("pool", g2b[0:32, 1, :], g2_d[128:160, :], "g2b")
    for t, eng in set(const_toks):
        S.lastw[t] = (cs_by_eng[eng].sem, cs_by_eng[eng].val)
    if HAS_C:
        for m, cmp_, cm, pat in ((mUs, ALU.is_gt, -1, 1), (mUi, ALU.is_ge, -1, 1), (mLs, ALU.is_gt, 1, -1)):
            tk = "mask%d" % id(m)
            S.op("pool", lambda e, m=m: e.memset(m[:], 1.0), writes=[tk])
            S.op("pool", lambda e, m=m, cmp_=cmp_, cm=cm, pat=pat: e.affine_select(out=m[:], in_=m[:], pattern=[[pat, 128]], compare_op=cmp_,
                                                                               fill=0.0, base=0, channel_multiplier=cm), reads=[tk], writes=[tk])
        S.op("pool", lambda e: e.memset(sel2[:], 0.0), writes=["sel2"])
        S.op("pool", lambda e: e.memset(sel2[0:64, 0:1], 1.0), reads=["sel2"], writes=["sel2"])
        S.op("pool", lambda e: e.memset(sel2[64:128, 1:2], 1.0), reads=["sel2"], writes=["sel2"])
        S.op("pool", lambda e: e.memset(blkones[:], 0.0), writes=["blkones"])
        S.op("pool", lambda e: e.memset(blkones[0:64, 0:64], 1.0), reads=["blkones"], writes=["blkones"])
        S.op("pool", lambda e: e.memset(blkones[64:128, 64:128], 1.0), reads=["blkones"], writes=["blkones"])
        S.op("pool", lambda e: e.memset(rmask[:], 1.0), writes=["rmask"])
        S.op("pool", lambda e: e.memset(rmask[:].rearrange("p (c t) -> p c t", t=CH)[:, :, 0:1], 0.0), reads=["rmask"], writes=["rmask"])
        S.op("dve", lambda e: e.tensor_scalar(out=omka[:], in0=V("rwkv_k_a", 0, 8), scalar1=-1.0, scalar2=1.0, op0=ALU.mult, op1=ALU.add),
             reads=["vecs"], writes=["omka"])
    S.op("pool", lambda e: e.memset(ident[:], 0.0), writes=["ident"])
    S.op("pool", lambda e: e.affine_select(out=ident[:], in_=ident[:], pattern=[[-1, 128]], compare_op=ALU.not_equal,
                                           fill=1.0, base=0, channel_multiplier=1), reads=["ident"], writes=["ident"])
    S.op("pool", lambda e: e.memset(ones_row[:], 1.0), writes=["ones_row"])
    S.op("act", lambda e: e.activation(out=cneg[:], in_=V("lru_lambda", 0, 8), func=AF.Exp, scale=-1.0), reads=["vecs"], writes=["cneg"])
    S.op("act", lambda e: e.activation(out=cneg[:], in_=cneg[:], func=AF.Ln, bias=1.0), reads=["cneg"], writes=["cneg"])
    S.op("act", lambda e: e.mul(out=cneg[:], in_=cneg[:], mul=-8.0), reads=["cneg"], writes=["cneg"])

    used = []
    if "A" in stages:
        used += ["lru_w_in", "lru_w_out"]
    if "B" in stages:
        used += ["ffn_w_up0", "ffn_w_dn0"]
    if "C" in stages:
        used += ["rwkv_w_r", "rwkv_w_k", "rwkv_w_v", "rwkv_w_out"]
    if "D" in stages:
        used += ["ffn_w_up1", "ffn_w_dn1"]
    for n in used:
        rows, cols = bigw[n]
        ds = S.new_dsem("k")
        step = max(32, (1 << 18) // cols)
        for r0 in range(0, rows, step):
            r1 = min(rows, r0 + step)
            S.op("pool", lambda e, n=n, r0=r0, r1=r1: e.dma_start(out=wb_d[n][r0:r1, :], in_=w_d[n][r0:r1, :]),
                 writes=[("wd", n)], dsem=ds)
        S.lastw[("wd", n)] = (ds.sem, ds.val)

    def rmsnorm_hT(xr_, xtok, gname, goff):
        for s in range(NS):
            S.op("act", lambda e, s=s: e.activation(out=sqj[:], in_=xr_[:, s, :], func=AF.Square, accum_out=ss[:, s:s + 1]),
                 reads=[(xtok, s)], writes=[("xn", 0), "ss"])
        S.op("dve", lambda e: e.tensor_scalar(out=rstd[:], in0=ss[:], scalar1=1.0 / D, scalar2=RMS_EPS, op0=ALU.mult, op1=ALU.add),
             reads=["ss"], writes=["rstd"])
        S.op("act", lambda e: e.activation(out=rstd[:], in_=rstd[:], func=AF.Sqrt), reads=["rstd"], writes=["rstd"])
        S.op("dve", lambda e: e.reciprocal(out=rstd[:], in_=rstd[:]), reads=["rstd"], writes=["rstd"])
        gb = V(gname, goff, 8).unsqueeze(2).to_broadcast([128, KC, 128])
        for s in range(NS):
            xb = xn[s % 2]
            xbt = ("xn", s % 2)
            S.op("act", lambda e, s=s, xb=xb: e.activation(out=xb[:], in_=xr_[:, s, :], func=AF.Copy, scale=rstd[:, s:s + 1]),
                 reads=[(xtok, s), "rstd"], writes=[xbt])
            b = bank()
            psb = ps[b][:].bitcast(BF16).rearrange("p (k t) -> p k t", k=KC)
            for kc in range(KC):
                S.op("pe", lambda e, kc=kc, xb=xb, psb=psb: e.transpose(psb[:, kc, :], xb[:, kc * 128:(kc + 1) * 128], ident[:]),
                     reads=[xbt, "ident"], writes=[("ps", b)])
            S.op("dve", lambda e, s=s, psb=psb: e.tensor_tensor(out=hT[:, :, s * 128:(s + 1) * 128], in0=psb, in1=gb, op=ALU.mult),
                 reads=[("ps", b), "vecs"], writes=["hT"])

    def out_proj(xr_, xtok, wname, nk, actT, act_toks, bias=False, evac=None):
        kpl = 4 if nk >= 4 else nk
        for nh in range(2):
            banks = [bank() for _ in range(NS)]
            for k0 in range(0, nk, kpl):
                src = wb_d[wname][k0 * 128:(k0 + kpl) * 128, nh * 512:(nh + 1) * 512].rearrange("(k p) e -> p k e", p=128)
                wsl, wtok = wload(wname, src, lambda t: t[:, 0:kpl * 512].rearrange("p (k e) -> p k e", k=kpl))
                for kk in range(kpl):
                    kc = k0 + kk
                    for s in range(NS):
                        last = (kc == nk - 1) and not bias
                        S.op("pe", lambda e, kc=kc, s=s, kk=kk, wsl=wsl, last=last, b=banks[s]:
                             e.matmul(ps[b][:], lhsT=actT(kc, s), rhs=wsl[:, kk, :], start=(kc == 0), stop=last),
                             reads=[wtok] + act_toks(kc), writes=[("ps", banks[s])])
            for s in range(NS):
                if bias:
                    S.op("pe", lambda e, s=s, nh=nh, b=banks[s]: e.matmul(ps[b][:], lhsT=ones_row[0:1, :], rhs=bout_row[0:1, nh * 512:(nh + 1) * 512],
                                                                        start=False, stop=True),
                         reads=["ones_row", "bout_row"], writes=[("ps", banks[s])])
                if evac is not None:
                    evac(s, nh, banks[s])
                    continue
                S.op("dve", lambda e, s=s, nh=nh, b=banks[s]: e.tensor_tensor(out=xr_[:, s, nh * 512:(nh + 1) * 512], in0=ps[b][:],
                                                                            in1=xr_[:, s, nh * 512:(nh + 1) * 512], op=ALU.add),
                     reads=[("ps", banks[s]), (xtok, s)], writes=[(xtok, s)])

    def stage_A(xr_, xtok, first_tile):
        yb = LB(0, 4, BF16); xr = LB(4, 8, F32)
        rg = LB(12, 8, F32); ig = LB(20, 8, F32); t1 = LB(28, 8, F32); xrb = LB(28, 4, BF16)
        if first_tile:
            S.op("pool", lambda e: e.memset(upre[:, :, 0:3], 0.0), writes=["upre_c"])
            S.op("pool", lambda e: e.memset(hstate[:], 0.0), writes=["hstate"])
        rmsnorm_hT(xr_, xtok, "lru_norm", 0)
        for q in range(8):
            src = wb_d["lru_w_in"][:, q * 256:(q + 1) * 256].rearrange("(k p) e -> p k e", p=128)
            wsl, wtok = wload("lru_w_in", src, lambda t: t[:].rearrange("p (k e) -> p k e", k=KC))
            for ee in range(2):
                c = 2 * q + ee
                b = bank()
                for kc in range(KC):
                    S.op("pe", lambda e, kc=kc, ee=ee, wsl=wsl, b=b: e.matmul(ps[b][:], lhsT=wsl[:, kc, ee * 128:(ee + 1) * 128], rhs=hT[:, kc, :],
                                                                          start=(kc == 0), stop=(kc == KC - 1)),
                         reads=[wtok, "hT"], writes=[("ps", b)])
                if c < 8:
                    S.op("act", lambda e, c=c, b=b: e.activation(out=yb.flat[:, c * 512:(c + 1) * 512], in_=ps[b][:], func=AF.Gelu_apprx_tanh,
                                                               bias=V("lru_b_in", c)),
                         reads=[("ps", b), "vecs"], writes=yb.tok(c // 2))
                else:
                    cc = c - 8
                    S.op("act", lambda e, cc=cc, b=b: e.activation(out=upre[:, cc, 3:3 + TT], in_=ps[b][:], func=AF.Identity, bias=V("lru_b_in", 8 + cc)),
                         reads=[("ps", b), "vecs"], writes=[("upre", cc)])
        for cc in range(8):
            o = xr.blk(cc)
            rd = [("upre", cc), "upre_c", "vecs"]
            S.op("act", lambda e, cc=cc, o=o: e.activation(out=o, in_=upre[:, cc, 0:TT], func=AF.Identity, scale=V("lru_conv_w", 0 * 8 + cc), bias=V("lru_conv_b", cc)),
                 reads=rd, writes=xr.tok(cc))
            for j in range(1, 4):
                S.op("dve", lambda e, cc=cc, o=o, j=j: e.scalar_tensor_tensor(out=o, in0=upre[:, cc, j:j + TT], scalar=V("lru_conv_w", j * 8 + cc), in1=o,
                                                                             op0=ALU.mult, op1=ALU.add), reads=rd + xr.tok(cc), writes=xr.tok(cc))
            S.op("act", lambda e, cc=cc, o=o: e.activation(out=xrb.flat[:, cc * 512:(cc + 1) * 512], in_=o, func=AF.Copy), reads=xr.tok(cc), writes=xrb.tok(cc // 2))
        S.op("pool", lambda e: e.tensor_copy(out=upre[:, :, 0:3], in_=upre[:, :, TT:TT + 3]), reads=[("upre", c) for c in range(8)], writes=["upre_c"])
        for cc in range(8):
            for g, dst in ((0, rg), (1, ig)):
                b = bank()
                S.op("pe", lambda e, cc=cc, g=g, b=b: e.matmul(ps[b][:], lhsT=gateW[:, g * 8 + cc, :], rhs=xrb.flat[:, cc * 512:(cc + 1) * 512], start=True, stop=True),
                     reads=["gateW"] + xrb.tok(cc // 2), writes=[("ps", b)])
                S.op("act", lambda e, cc=cc, g=g, b=b, dst=dst: e.activation(out=dst.blk(cc), in_=ps[b][:], func=AF.Sigmoid, bias=V("lru_gate_b", g * 8 + cc)),
                     reads=[("ps", b), "vecs"], writes=dst.tok(cc))
        for cc in range(8):
            S.op("act", lambda e, cc=cc: e.activation(out=rg.blk(cc), in_=rg.blk(cc), func=AF.Exp, scale=cneg[:, cc:cc + 1]),
                 reads=rg.tok(cc) + ["cneg"], writes=rg.tok(cc))
            S.op("dve", lambda e, cc=cc: e.tensor_tensor(out=t1.blk(cc), in0=rg.blk(cc), in1=rg.blk(cc), op=ALU.mult), reads=rg.tok(cc), writes=t1.tok(cc))
            S.op("dve", lambda e, cc=cc: e.tensor_tensor(out=ig.blk(cc), in0=ig.blk(cc), in1=xr.blk(cc), op=ALU.mult), reads=ig.tok(cc) + xr.tok(cc), writes=ig.tok(cc))
        for cc in range(8):
            S.op("act", lambda e, cc=cc: e.activation(out=t1.blk(cc), in_=t1.blk(cc), func=AF.Sqrt, scale=-1.0, bias=1.0), reads=t1.tok(cc), writes=t1.tok(cc))
            S.op("dve", lambda e, cc=cc: e.tensor_tensor(out=ig.blk(cc), in0=ig.blk(cc), in1=t1.blk(cc), op=ALU.mult), reads=ig.tok(cc) + t1.tok(cc), writes=ig.tok(cc))
            S.op("dve", lambda e, cc=cc: e.tensor_tensor_scan(out=t1.blk(cc), data0=rg.blk(cc), data1=ig.blk(cc), initial=hstate[:, cc:cc + 1], op0=ALU.mult, op1=ALU.add),
                 reads=rg.tok(cc) + ig.tok(cc) + ["hstate"], writes=t1.tok(cc))
            S.op("dve", lambda e, cc=cc: e.tensor_copy(out=hstate[:, cc:cc + 1], in_=t1.blk(cc)[:, TT - 1:TT]), reads=t1.tok(cc), writes=["hstate"])
            S.op("dve", lambda e, cc=cc: e.tensor_tensor(out=yb.flat[:, cc * 512:(cc + 1) * 512], in0=t1.blk(cc), in1=yb.flat[:, cc * 512:(cc + 1) * 512], op=ALU.mult),
                 reads=t1.tok(cc) + yb.tok(cc // 2), writes=yb.tok(cc // 2))
        out_proj(xr_, xtok, "lru_w_out", 8, lambda kc, s: yb.flat[:, kc * 512 + s * 128: kc * 512 + (s + 1) * 128], lambda kc: yb.tok(kc // 2), bias=True)

    def stage_F(xr_, xtok, l, first_tile):
        hid = LB(0, 12, BF16)
        gpre = [sbuf_gpre[0], sbuf_gpre[1]]
        gc = LB(12, 2, F32); gg = LB(14, 2, F32)
        if first_tile:
            S.op("pool", lambda e: e.memset(fcar[l][:], 0.0), writes=[("fcar", l)])
        rmsnorm_hT(xr_, xtok, "ffn_norm", l * 8)
        up = f"ffn_w_up{l}"
        for jj in range(NJ // 2):
            srcg = wb_d[up][:, jj * 256:(jj + 1) * 256].rearrange("(k p) e -> p k e", p=128)
            srcu = wb_d[up][:, DFF + jj * 256:DFF + (jj + 1) * 256].rearrange("(k p) e -> p k e", p=128)
            wg, wgt = wload(up, srcg, lambda t: t[:].rearrange("p (k e) -> p k e", k=KC))
            wu, wut = wload(up, srcu, lambda t: t[:].rearrange("p (k e) -> p k e", k=KC))
            for ee in range(2):
                j = 2 * jj + ee
                r = j % 2
                bg = bank(); bu = bank()
                for kc in range(KC):
                    S.op("pe", lambda e, kc=kc, ee=ee, wg=wg, bg=bg: e.matmul(ps[bg][:], lhsT=wg[:, kc, ee * 128:(ee + 1) * 128], rhs=hT[:, kc, :], start=(kc == 0), stop=(kc == KC - 1)),
                         reads=[wgt, "hT"], writes=[("ps", bg)])
                for kc in range(KC):
                    S.op("pe", lambda e, kc=kc, ee=ee, wu=wu, bu=bu: e.matmul(ps[bu][:], lhsT=wu[:, kc, ee * 128:(ee + 1) * 128], rhs=hT[:, kc, :], start=(kc == 0), stop=(kc == KC - 1)),
                         reads=[wut, "hT"], writes=[("ps", bu)])
                gp = gpre[r]; gpt = ("gpre", r)
                S.op("pool", lambda e, gp=gp, j=j: e.tensor_copy(out=gp[:, 0:2], in_=fcar[l][:, j, :]), reads=[("fcar", l)], writes=[gpt])
                S.op("act", lambda e, gp=gp, bg=bg: e.activation(out=gp[:, 2:2 + TT], in_=ps[bg][:], func=AF.Identity), reads=[("ps", bg)], writes=[gpt])
                S.op("pool", lambda e, gp=gp, j=j: e.tensor_copy(out=fcar[l][:, j, :], in_=gp[:, TT:TT + 2]), reads=[gpt], writes=[("fcar", l)])
                co = VOFF["ffn_conv_w"] + l * 72
                S.op("act", lambda e, j=j, r=r, co=co, bg=bg: e.activation(out=gc.blk(r), in_=ps[bg][:], func=AF.Copy, scale=vecs[:, co + 2 * 24 + j:co + 2 * 24 + j + 1]),
                     reads=[("ps", bg), "vecs"], writes=gc.tok(r))
                for tap in (0, 1):
                    S.op("dve", lambda e, gp=gp, j=j, r=r, co=co, tap=tap: e.scalar_tensor_tensor(out=gc.blk(r), in0=gp[:, tap:tap + TT], scalar=vecs[:, co + tap * 24 + j:co + tap * 24 + j + 1],
                                                                                               in1=gc.blk(r), op0=ALU.mult, op1=ALU.add),
                         reads=[gpt, "vecs"] + gc.tok(r), writes=gc.tok(r))
                S.op("act", lambda e, j=j, r=r: e.activation(out=gg.blk(r), in_=gc.blk(r), func=AF.Gelu_apprx_tanh, bias=V("ffn_conv_b", l * 24 + j)),
                     reads=gc.tok(r) + ["vecs"], writes=gg.tok(r))
                S.op("dve", lambda e, j=j, r=r, bu=bu: e.tensor_tensor(out=hid.flat[:, j * 512:(j + 1) * 512], in0=ps[bu][:], in1=gg.blk(r), op=ALU.mult),
                     reads=[("ps", bu)] + gg.tok(r), writes=hid.tok(j // 2))
        out_proj(xr_, xtok, f"ffn_w_dn{l}", NJ, lambda kc, s: hid.flat[:, kc * 512 + s * 128: kc * 512 + (s + 1) * 128], lambda kc: hid.tok(kc // 2))

    sbuf_gpre = [sb(f"gpre{i}", [128, TT + 2]) for i in range(2)]

    KAPPA = 0.6065306597126334

    def stage_C(xr_, xtok, first_tile):
        AT = LB(0, 4, BF16); BT = LB(4, 4, BF16); KT = LB(8, 4, BF16); RT = LB(12, 4, BF16)
        Vt = LB(16, 4, BF16); Kh = LB(20, 4, BF16); Bh = LB(24, 4, BF16)
        xx = LB(28, 4, BF16); xs = LB(32, 4, BF16)

        def v3(lb):
            return lb.flat.rearrange("p (k t) -> p k t", k=KC)

        def v4(lb):
            return lb.flat.rearrange("p (c f) -> p c f", c=NS)

        def mm(out, lhsT, rhs, rd, b, start=True, stop=True):
            S.op("pe", lambda e: e.matmul(out, lhsT=lhsT, rhs=rhs, start=start, stop=stop), reads=rd, writes=[("ps", b)])

        if first_tile:
            S.op("pool", lambda e: e.memset(hcar[:], 0.0), writes=["hcar"])
            S.op("pool", lambda e: e.memset(Hst[:], 0.0), writes=["Hst"])
            S.op("pool", lambda e: e.memset(Hbd[:], 0.0), writes=["Hbd"])
        rmsnorm_hT(xr_, xtok, "rwkv_norm", 0)
        xx3 = v3(xx); xs3 = v3(xs)
        S.op("dve", lambda e: e.tensor_tensor(out=xx3[:, :, 1:TT], in0=hT[:, :, 0:TT - 1], in1=hT[:, :, 1:TT], op=ALU.subtract), reads=["hT"], writes=xx.tok())
        S.op("dve", lambda e: e.tensor_tensor(out=xx3[:, :, 0:1], in0=hcar[:].unsqueeze(2), in1=hT[:, :, 0:1], op=ALU.subtract), reads=["hT", "hcar"], writes=xx.tok())
        S.op("pool", lambda e: e.tensor_copy(out=hcar[:].unsqueeze(2), in_=hT[:, :, TT - 1:TT]), reads=["hT"], writes=["hcar"])

        def make_xs(i):
            for kc in range(KC):
                S.op("dve", lambda e, kc=kc: e.scalar_tensor_tensor(out=xs3[:, kc, :], in0=xx3[:, kc, :], scalar=V("rwkv_mix", i * 8 + kc), in1=hT[:, kc, :], op0=ALU.mult, op1=ALU.add),
                     reads=xx.tok(kc // 2) + ["hT", "vecs"], writes=xs.tok(kc // 2))

        def lora1(wt, wtok, c0, c1, func, out_ap, out_tok):
            M = c1 - c0
            b = bank()
            for kc in range(KC):
                mm(ps[b][0:M, :], wt[:, kc, c0:c1], xs3[:, kc, :], [wtok] + xs.tok(kc // 2), b, start=(kc == 0), stop=(kc == KC - 1))
            S.op("act", lambda e: e.activation(out=out_ap, in_=ps[b][0:M, :], func=func), reads=[("ps", b)], writes=[out_tok])

        make_xs(3); lora1(w1b, "w1b", 0, 64, AF.Tanh, tanhw[:], "tanhw")
        make_xs(4); lora1(a1b, "a1b", 0, 64, AF.Copy, a1o[:], "a1o")
        make_xs(5); lora1(g1b, "g1b", 0, 128, AF.Sigmoid, sgT[:, 0, :], "sgT"); lora1(g1b, "g1b", 128, 160, AF.Sigmoid, sgT[0:32, 1, :], "sgT")

        make_xs(2)
        Vt4 = v4(Vt); Kh4 = v4(Kh); Bh4 = v4(Bh)
        out_proj(None, None, "rwkv_w_v", 8, lambda kc, s: xs3[:, kc, s * 128:(s + 1) * 128], lambda kc: xs.tok(kc // 2),
                 evac=lambda s, nh, b: S.op("act", lambda e: e.activation(out=Vt4[:, s, nh * 512:(nh + 1) * 512], in_=ps[b][:], func=AF.Copy),
                                            reads=[("ps", b)], writes=Vt.tok(s)))
        make_xs(0)
        for kc in range(KC):
            S.op("dve", lambda e, kc=kc: e.scalar_tensor_tensor(out=xx3[:, kc, :], in0=xx3[:, kc, :], scalar=V("rwkv_mix", 1 * 8 + kc), in1=hT[:, kc, :], op0=ALU.mult, op1=ALU.add),
                 reads=xx.tok(kc // 2) + ["hT", "vecs"], writes=xx.tok(kc // 2))

        TKB = LB(44, 1, BF16); TRQ = LB(45, 1, BF16)
        TK = TKB.flat[:, 0:512]; TB = TKB.flat[:, 512:1024]; TR = TRQ.flat[:, 0:512]; KSQ = TRQ.flat[:, 512:1024]
        c4 = lambda ap: ap.rearrange("p (c t) -> p c t", t=CH)
        wslabs = {}

        def pair_prep(p):
            Tt = [LB(36 + i, 1, F32) for i in range(8)]
            T0, T1, T2, T3, T4, T5, T6 = [Tt[i].flat for i in range(7)]
            k0, k1, k2, k3, k4, k5, k6 = [Tt[i].tok() for i in range(7)]
            kt = KT.flat[:, p * 512:(p + 1) * 512]; rt = RT.flat[:, p * 512:(p + 1) * 512]
            at = AT.flat[:, p * 512:(p + 1) * 512]; bt = BT.flat[:, p * 512:(p + 1) * 512]
            ktk = KT.tok(p // 2); rtk = RT.tok(p // 2); atk = AT.tok(p // 2); btk = BT.tok(p // 2)
            pc = slice(p * 128, (p + 1) * 128)
            ee = p % 2
            if ee == 0:
                for nm in ("rwkv_w_k", "rwkv_w_r"):
                    src = wb_d[nm][:, (p // 2) * 256:(p // 2 + 1) * 256].rearrange("(k p) e -> p k e", p=128)
                    wslabs[nm] = wload(nm, src, lambda t: t[:].rearrange("p (k e) -> p k e", k=KC))
            b = bank()
            mm(ps[b][:], w2b[0:64, pc], tanhw[:], ["w2b", "tanhw"], b)
            S.op("act", lambda e: e.activation(out=T0, in_=ps[b][:], func=AF.Sigmoid, bias=V("rwkv_w0", p)), reads=[("ps", b), "vecs"], writes=k0)
            S.op("dve", lambda e: e.tensor_tensor_scan(out=T1, data0=rmask[:], data1=T0, initial=0.0, op0=ALU.mult, op1=ALU.add), reads=k0 + ["rmask"], writes=k1)
            S.op("pool", lambda e: e.tensor_tensor(out=T0, in0=T1, in1=T0, op=ALU.subtract), reads=k0 + k1, writes=k0)
            S.op("act", lambda e: e.activation(out=T2, in_=T1, func=AF.Exp, scale=-KAPPA), reads=k1, writes=k2)
            S.op("act", lambda e: e.activation(out=T0, in_=T0, func=AF.Exp, scale=-KAPPA), reads=k0, writes=k0)
            S.op("act", lambda e: e.activation(out=T1, in_=T1, func=AF.Exp, scale=KAPPA), reads=k1, writes=k1)
            S.op("pool", lambda e: e.tensor_copy(out=WC[:, p, :], in_=c4(T2)[:, :, CH - 1]), reads=k2, writes=["WC"])
            b2 = bank()
            mm(ps[b2][:], a2b[0:64, pc], a1o[:], ["a2b", "a1o"], b2)
            S.op("act", lambda e: e.activation(out=T3, in_=ps[b2][:], func=AF.Sigmoid, bias=V("rwkv_a0", p)), reads=[("ps", b2), "vecs"], writes=k3)
            bK = bank(); bR = bank()
            wk, wkt = wslabs["rwkv_w_k"]; wr, wrt = wslabs["rwkv_w_r"]
            for kc in range(KC):
                mm(ps[bK][:], wk[:, kc, ee * 128:(ee + 1) * 128], xx3[:, kc, :], [wkt] + xx.tok(kc // 2), bK, start=(kc == 0), stop=(kc == KC - 1))
            for kc in range(KC):
                mm(ps[bR][:], wr[:, kc, ee * 128:(ee + 1) * 128], xs3[:, kc, :], [wrt] + xs.tok(kc // 2), bR, start=(kc == 0), stop=(kc == KC - 1))
            S.op("act", lambda e: e.activation(out=T4, in_=ps[bK][:], func=AF.Copy, scale=V("rwkv_k_k", p)), reads=[("ps", bK), "vecs"], writes=k4)
            S.op("pool", lambda e: e.tensor_tensor(out=KSQ, in0=T4, in1=T4, op=ALU.mult), reads=k4, writes=TRQ.tok())
            b3 = bank()
            mm(ps[b3][:], blkones[:], KSQ, ["blkones"] + TRQ.tok(), b3)
            S.op("act", lambda e: e.activation(out=T6, in_=ps[b3][:], func=AF.Sqrt), reads=[("ps", b3)], writes=k6)
            S.op("dve", lambda e: e.tensor_scalar(out=T6, in0=T6, scalar1=1e-12, scalar2=None, op0=ALU.max), reads=k6, writes=k6)
            S.op("dve", lambda e: e.reciprocal(out=T6, in_=T6), reads=k6, writes=k6)
            S.op("dve", lambda e: e.tensor_tensor(out=T4, in0=T4, in1=T6, op=ALU.mult), reads=k4 + k6, writes=k4)
            S.op("act", lambda e: e.activation(out=T5, in_=T3, func=AF.Identity, scale=V("rwkv_k_a", p), bias=omka[:, p:p + 1]), reads=k3 + ["vecs", "omka"], writes=k5)
            S.op("dve", lambda e: e.tensor_tensor(out=T5, in0=ps[bK][:], in1=T5, op=ALU.mult), reads=k5 + [("ps", bK)], writes=k5)
            S.op("dve", lambda e: e.scalar_tensor_tensor(out=at, in0=T4, scalar=-1.0, in1=T0, op0=ALU.mult, op1=ALU.mult), reads=k4 + k0, writes=atk)
            S.op("dve", lambda e: e.tensor_tensor(out=kt, in0=T5, in1=T1, op=ALU.mult), reads=k5 + k1, writes=ktk)
            wcb = WC[:, p, :].unsqueeze(2).to_broadcast([128, 4, CH])
            S.op("pool", lambda e: e.tensor_tensor(out=c4(TK), in0=c4(kt), in1=wcb, op=ALU.mult), reads=ktk + ["WC"], writes=TKB.tok())
            S.op("pool", lambda e: e.tensor_tensor(out=T4, in0=T4, in1=T3, op=ALU.mult), reads=k4 + k3, writes=k4)
            S.op("dve", lambda e: e.tensor_tensor(out=bt, in0=T4, in1=T1, op=ALU.mult), reads=k4 + k1, writes=btk)
            S.op("pool", lambda e: e.tensor_tensor(out=c4(TB), in0=c4(bt), in1=wcb, op=ALU.mult), reads=btk + ["WC"], writes=TKB.tok())
            S.op("dve", lambda e: e.scalar_tensor_tensor(out=TR, in0=ps[bR][:], scalar=V("rwkv_r_k", p), in1=T5, op0=ALU.mult, op1=ALU.mult), reads=[("ps", bR)] + k5 + ["vecs"], writes=TRQ.tok())
            S.op("dve", lambda e: e.tensor_tensor(out=rt, in0=ps[bR][:], in1=T2, op=ALU.mult), reads=[("ps", bR)] + k2, writes=rtk)
            b4 = bank()
            for c in range(NS):
                mm(ps[b4][:, c * 2:(c + 1) * 2], TR[:, c * 128:(c + 1) * 128], sel2[:], TRQ.tok() + ["sel2"], b4)
            S.op("act", lambda e: e.activation(out=dtok[:, :, 2 * p:2 * p + 2], in_=ps[b4][:, 0:8].rearrange("p (c e) -> p c e", e=2), func=AF.Copy),
                 reads=[("ps", b4)], writes=["dtok"])
            for src, d4, dlb in ((TK, Kh4, Kh), (TB, Bh4, Bh)):
                b5 = bank()
                psb = ps[b5][:].bitcast(BF16)[:, 0:512].rearrange("p (c f) -> p c f", c=NS)
                for c in range(NS):
                    S.op("pe", lambda e, c=c, psb=psb, src=src: e.transpose(psb[:, c, :], src[:, c * 128:(c + 1) * 128], ident[:]), reads=TKB.tok() + ["ident"], writes=[("ps", b5)])
                S.op("act", lambda e, psb=psb, d4=d4: e.activation(out=d4[:, :, pc], in_=psb, func=AF.Copy), reads=[("ps", b5)], writes=dlb.tok())

        for p_ in range(8):
            pair_prep(p_)
        Xb = LB(40, 1, BF16); Ub = LB(41, 1, BF16); yq = LB(42, 2, F32); ysq = LB(44, 2, F32); ygb = LB(44, 1, BF16)
        PAD = LB(40, 4, BF16); BTB = LB(44, 2, BF16); PADALL = LB(40, 6, BF16)
        pad5 = PAD.flat.rearrange("p (q j t) -> p q j t", q=8, j=4)
        btb4 = BTB.flat.rearrange("p (q j t) -> p q j t", q=8, j=2)
        AT3 = v3(AT); BT3 = v3(BT); KT3 = v3(KT); RT3 = v3(RT)
        h8 = lambda ap: ap.rearrange("p (h v) -> p h v", v=64)
        mb2 = lambda m: m[:].unsqueeze(1).to_broadcast([128, 2, 128])
        mb4 = lambda m: m[:].unsqueeze(1).to_broadcast([128, 4, 128])
        def scan_chunk(c):
            cc = slice(c * 128, (c + 1) * 128)
            S.op("dve", lambda e: e.memset(PADALL.flat, 0.0), writes=PADALL.tok())
            for e_ in range(2):
                rows = slice(64 * e_, 64 * e_ + 64)
                S.op("dve", lambda e, rows=rows, e_=e_: e.tensor_copy(out=pad5[rows, :, e_, :], in_=AT3[rows, :, cc]), reads=AT.tok() + PAD.tok(), writes=PAD.tok())
                S.op("act", lambda e, rows=rows, e_=e_: e.activation(out=pad5[rows, :, 2 + e_, :], in_=RT3[rows, :, cc], func=AF.Copy), reads=RT.tok() + PAD.tok(), writes=PAD.tok())
                S.op("dve", lambda e, rows=rows, e_=e_: e.tensor_copy(out=btb4[rows, :, e_, :], in_=BT3[rows, :, cc]), reads=BT.tok() + BTB.tok(), writes=BTB.tok())
            Q = []
            for q in range(4):
                QL = LB(28 + 3 * q, 3, BF16)
                hv = lambda i, QL=QL: QL.flat[:, i * 512:(i + 1) * 512].rearrange("p (h t) -> p h t", h=4)
                Aak, Ark, Arb, P = hv(0), hv(1), hv(2), hv(3)
                PTST = QL.flat[:, 2048:3072].rearrange("p (h x) -> p h x", h=4)
                PT = PTST[:, :, 0:128]; ST = PTST[:, :, 128:256]
                tA, tB, tC = QL.tok(0), QL.tok(1), QL.tok(2)
                Q.append((Aak, Ark, Arb, P, PTST, PT, ST, tA, tB, tC))
                b3 = bank()
                for pp in range(2):
                    p = 2 * q + pp
                    hs2 = slice(2 * pp, 2 * pp + 2)
                    b1 = bank(); b2 = bank()
                    padp = PAD.flat[:, p * 512:(p + 1) * 512]
                    mm(ps[b1][:], BT3[:, p, cc], padp, BT.tok(p // 2) + PAD.tok(), b1)
                    mm(ps[b2][:], KT3[:, p, cc], padp, KT.tok(p // 2) + PAD.tok(), b2)
                    mm(ps[b3][:, pp * 256:(pp + 1) * 256], AT3[:, p, cc], BTB.flat[:, p * 256:(p + 1) * 256], AT.tok(p // 2) + BTB.tok(), b3)
                    v2 = lambda b, half: ps[b][:, half * 256:(half + 1) * 256].rearrange("p (h t) -> p h t", h=2)
                    for (dst, b, half, m, tk) in ((PT, b1, 0, mUs, tC), (Arb, b1, 1, mUi, tB), (Aak, b2, 0, mUs, tA), (Ark, b2, 1, mUi, tA)):
                        S.op("dve", lambda e, dst=dst, b=b, half=half, m=m, hs2=hs2: e.tensor_tensor(out=dst[:, hs2, :], in0=v2(b, half), in1=mb2(m), op=ALU.mult),
                             reads=[("ps", b), "mask%d" % id(m)], writes=tk)
                S.op("dve", lambda e, P=P, b3=b3: e.tensor_tensor(out=P, in0=ps[b3][:].rearrange("p (h t) -> p h t", h=4), in1=mb4(mLs), op=ALU.mult),
                     reads=[("ps", b3), "mask%d" % id(mLs)], writes=tB)
                S.op("pool", lambda e, ST=ST, PT=PT: e.tensor_tensor(out=ST, in0=PT, in1=mb4(ident), op=ALU.add), reads=tC + ["ident"], writes=tC)
            for i in range(7):
                for q in range(4):
                    Aak, Ark, Arb, P, PTST, PT, ST, tA, tB, tC = Q[q]
                    p4 = lambda b: ps[b][:].rearrange("p (h t) -> p h t", h=4)
                    if i < 6:
                        bP = bank()
                        for hq in range(4):
                            mm(ps[bP][:, hq * 128:(hq + 1) * 128], PT[:, hq, :], P[:, hq, :], tB + tC, bP)
                    if i == 0:
                        bT = bank()
                        for hq in range(4):
                            mm(ps[bT][:, hq * 128:(hq + 1) * 128], P[:, hq, :], PT[:, hq, :], tB + tC, bT)
                    elif i < 6:
                        bT2 = [bank(), bank()]
                        for hq in range(4):
                            mm(ps[bT2[hq // 2]][:, (hq % 2) * 256:(hq % 2 + 1) * 256], P[:, hq, :], PTST[:, hq, :], tB + tC, bT2[hq // 2])
                    else:
                        bS = bank()
                        for hq in range(4):
                            mm(ps[bS][:, hq * 128:(hq + 1) * 128], P[:, hq, :], ST[:, hq, :], tB + tC, bS)
                    if i < 6:
                        S.op("act", lambda e, P=P, bP=bP: e.activation(out=P, in_=p4(bP), func=AF.Copy), reads=[("ps", bP)], writes=tB)
                    if i == 0:
                        S.op("act", lambda e, PT=PT, bT=bT: e.activation(out=PT, in_=p4(bT), func=AF.Copy), reads=[("ps", bT)], writes=tC)
                    elif i < 6:
                        for k in range(2):
                            pv = ps[bT2[k]][:].rearrange("p (h x) -> p h x", h=2)
                            S.op("act", lambda e, PT=PT, pv=pv, k=k: e.activation(out=PT[:, 2 * k:2 * k + 2, :], in_=pv[:, :, 0:128], func=AF.Copy), reads=[("ps", bT2[k])], writes=tC)
                            S.op("dve", lambda e, ST=ST, pv=pv, k=k: e.tensor_tensor(out=ST[:, 2 * k:2 * k + 2, :], in0=pv[:, :, 128:256], in1=ST[:, 2 * k:2 * k + 2, :], op=ALU.add),
                                 reads=[("ps", bT2[k])] + tC, writes=tC)
                    else:
                        S.op("dve", lambda e, ST=ST, bS=bS: e.tensor_tensor(out=ST, in0=p4(bS), in1=ST, op=ALU.add), reads=[("ps", bS)] + tC, writes=tC)
            bX = [bank(), bank()]
            for p in range(8):
                bx = bX[p // 4]
                for e_ in range(2):
                    h = 2 * p + e_
                    q, hq = divmod(h, 4)
                    Aak, Ark, Arb, P, PTST, PT, ST, tA, tB, tC = Q[q]
                    o = ps[bx][:, (h % 8) * 64:(h % 8 + 1) * 64]
                    mm(o, AT3[:, p, cc], Hbd[:, p, e_ * 64:(e_ + 1) * 64], AT.tok(p // 2) + ["Hbd"], bx, start=True, stop=False)
                    mm(o, Aak[:, hq, :], Vt4[:, c, h * 64:(h + 1) * 64], tA + Vt.tok(c), bx, start=False, stop=True)
            for k in range(2):
                S.op("act", lambda e, k=k: e.activation(out=Xb.flat[:, k * 512:(k + 1) * 512], in_=ps[bX[k]][:], func=AF.Copy), reads=[("ps", bX[k])], writes=Xb.tok())
            bU = [bank(), bank()]
            for h in range(16):
                q, hq = divmod(h, 4)
                Aak, Ark, Arb, P, PTST, PT, ST, tA, tB, tC = Q[q]
                mm(ps[bU[h // 8]][:, (h % 8) * 64:(h % 8 + 1) * 64], ST[:, hq, :], Xb.flat[:, h * 64:(h + 1) * 64], tC + Xb.tok(), bU[h // 8])
            for k in range(2):
                S.op("act", lambda e, k=k: e.activation(out=Ub.flat[:, k * 512:(k + 1) * 512], in_=ps[bU[k]][:], func=AF.Copy), reads=[("ps", bU[k])], writes=Ub.tok())
            bY = [bank(), bank()]
            for p in range(8):
                by = bY[p // 4]
                for e_ in range(2):
                    h = 2 * p + e_
                    q, hq = divmod(h, 4)
                    Aak, Ark, Arb, P, PTST, PT, ST, tA, tB, tC = Q[q]
                    o = ps[by][:, (h % 8) * 64:(h % 8 + 1) * 64]
                    mm(o, RT3[:, p, cc], Hbd[:, p, e_ * 64:(e_ + 1) * 64], RT.tok(p // 2) + ["Hbd"], by, start=True, stop=False)
                    mm(o, Ark[:, hq, :], Vt4[:, c, h * 64:(h + 1) * 64], tA + Vt.tok(c), by, start=False, stop=False)
                    mm(o, Arb[:, hq, :], Ub.flat[:, h * 64:(h + 1) * 64], tB + Ub.tok(), by, start=False, stop=True)
            bH = [bank(), bank()]
            for p in range(8):
                o = ps[bH[p // 4]][:, (p % 4) * 128:(p % 4 + 1) * 128]
                mm(o, Kh4[:, c, p * 128:(p + 1) * 128], Vt4[:, c, p * 128:(p + 1) * 128], Kh.tok(c) + Vt.tok(c), bH[p // 4], start=True, stop=False)
                mm(o, Bh4[:, c, p * 128:(p + 1) * 128], Ub.flat[:, p * 128:(p + 1) * 128], Bh.tok(c) + Ub.tok(), bH[p // 4], start=False, stop=True)
            for e_ in range(2):
                rows = slice(64 * e_, 64 * e_ + 64)
                S.op("pool", lambda e, rows=rows: e.tensor_tensor(out=Hst[rows, :, :], in0=Hst[rows, :, :], in1=WC[rows, :, c:c + 1].to_broadcast([64, 8, 64]), op=ALU.mult),
                     reads=["Hst", "WC"], writes=["Hst"])
                for k in range(2):
                    pv = ps[bH[k]][rows, :].rearrange("p (q f) -> p q f", q=4)[:, :, e_ * 64:(e_ + 1) * 64]
                    S.op("dve", lambda e, rows=rows, pv=pv, k=k: e.tensor_tensor(out=Hst[rows, 4 * k:4 * k + 4, :], in0=pv, in1=Hst[rows, 4 * k:4 * k + 4, :], op=ALU.add),
                         reads=[("ps", bH[k]), "Hst"], writes=["Hst"])
            for e_ in range(2):
                rows = slice(64 * e_, 64 * e_ + 64)
                S.op("pool", lambda e, rows=rows, e_=e_: e.tensor_copy(out=Hbd[rows, :, e_ * 64:(e_ + 1) * 64], in_=Hst[rows, :, :]), reads=["Hst", "Hbd"], writes=["Hbd"])
            yq3 = yq.flat.rearrange("p (h v) -> p h v", v=64); ysq3 = ysq.flat.rearrange("p (h v) -> p h v", v=64)
            for k in range(2):
                S.op("dve", lambda e, k=k: e.tensor_reduce(out=st_a[:, 8 * k:8 * k + 8], in_=h8(ps[bY[k]][:]), axis=AX.X, op=ALU.add), reads=[("ps", bY[k])], writes=["st_a"])
                S.op("act", lambda e, k=k: e.activation(out=ysq.flat[:, k * 512:(k + 1) * 512], in_=ps[bY[k]][:], func=AF.Square), reads=[("ps", bY[k])], writes=ysq.tok(k))
                S.op("dve", lambda e, k=k: e.tensor_reduce(out=st_b[:, 8 * k:8 * k + 8], in_=ysq3[:, 8 * k:8 * k + 8, :], axis=AX.X, op=ALU.add), reads=ysq.tok(k), writes=["st_b"])
            S.op("dve", lambda e: e.tensor_scalar(out=st_a[:], in0=st_a[:], scalar1=1.0 / 64, scalar2=None, op0=ALU.mult), reads=["st_a"], writes=["st_a"])
            S.op("dve", lambda e: e.tensor_tensor(out=st_c[:], in0=st_a[:], in1=st_a[:], op=ALU.mult), reads=["st_a"], writes=["st_c"])
            S.op("dve", lambda e: e.scalar_tensor_tensor(out=st_b[:], in0=st_b[:], scalar=1.0 / 64, in1=st_c[:], op0=ALU.mult, op1=ALU.subtract), reads=["st_b", "st_c"], writes=["st_b"])
            S.op("act", lambda e: e.activation(out=st_b[:], in_=st_b[:], func=AF.Sqrt, bias=GN_EPS), reads=["st_b"], writes=["st_b"])
            S.op("dve", lambda e: e.reciprocal(out=st_b[:], in_=st_b[:]), reads=["st_b"], writes=["st_b"])
            for k in range(2):
                hs_ = slice(8 * k, 8 * k + 8)
                S.op("dve", lambda e, k=k, hs_=hs_: e.tensor_tensor(out=yq3[:, hs_, :], in0=h8(ps[bY[k]][:]), in1=st_a[:, hs_].unsqueeze(2).to_broadcast([128, 8, 64]), op=ALU.subtract),
                     reads=[("ps", bY[k]), "st_a"], writes=yq.tok(k))
                S.op("pool", lambda e, hs_=hs_: e.tensor_tensor(out=yq3[:, hs_, :], in0=yq3[:, hs_, :], in1=st_b[:, hs_].unsqueeze(2).to_broadcast([128, 8, 64]), op=ALU.mult),
                     reads=yq.tok(k) + ["st_b"], writes=yq.tok(k))
            S.op("pool", lambda e: e.tensor_tensor(out=yq.flat, in0=yq.flat, in1=lnw_b[:], op=ALU.mult), reads=yq.tok() + ["lnw_b"], writes=yq.tok())
            S.op("pool", lambda e: e.tensor_tensor(out=yq.flat, in0=yq.flat, in1=lnb_b[:], op=ALU.add), reads=yq.tok() + ["lnb_b"], writes=yq.tok())
            S.op("pool", lambda e: e.tensor_tensor(out=ysq3, in0=h8(Vt4[:, c, :]), in1=dtok[:, c, :].unsqueeze(2).to_broadcast([128, 16, 64]), op=ALU.mult),
                 reads=Vt.tok(c) + ["dtok"], writes=ysq.tok())
            S.op("pool", lambda e: e.tensor_tensor(out=yq.flat, in0=yq.flat, in1=ysq.flat, op=ALU.add), reads=yq.tok() + ysq.tok(), writes=yq.tok())
            bG = [bank(), bank()]
            for nh in range(2):
                mm(ps[bG[nh]][:], sgT[:, 0, cc], g2b[:, 0, nh * 512:(nh + 1) * 512], ["sgT", "g2b"], bG[nh], start=True, stop=False)
                mm(ps[bG[nh]][:], sgT[0:32, 1, cc], g2b[0:32, 1, nh * 512:(nh + 1) * 512], ["sgT", "g2b"], bG[nh], start=False, stop=True)
                S.op("dve", lambda e, nh=nh: e.tensor_tensor(out=ygb.flat[:, nh * 512:(nh + 1) * 512], in0=ps[bG[nh]][:], in1=yq.flat[:, nh * 512:(nh + 1) * 512], op=ALU.mult),
                     reads=[("ps", bG[nh])] + yq.tok(nh) + ysq.tok(), writes=ygb.tok())
            b = bank()
            psb = ps[b][:].bitcast(BF16).rearrange("p (k t) -> p k t", k=KC)
            for kc in range(KC):
                S.op("pe", lambda e, kc=kc, psb=psb: e.transpose(psb[:, kc, :], ygb.flat[:, kc * 128:(kc + 1) * 128], ident[:]), reads=ygb.tok() + ["ident"], writes=[("ps", b)])
            S.op("act", lambda e, psb=psb: e.activation(out=hT[:, :, cc], in_=psb, func=AF.Copy), reads=[("ps", b)], writes=["hT"])
        for c_ in range(NS):
            scan_chunk(c_)
        out_proj(xr_, xtok, "rwkv_w_out", 8, lambda kc, s: hT[:, kc, s * 128:(s + 1) * 128], lambda kc: ["hT"])

    def stage_E(xr_, xtok, seq, t0, ost):
        for s in range(NS):
            S.op("act", lambda e, s=s: e.activation(out=sqj[:], in_=xr_[:, s, :], func=AF.Square, accum_out=ss[:, s:s + 1]),
                 reads=[(xtok, s)], writes=[("xn", 0), "ss"])
        S.op("dve", lambda e: e.tensor_scalar(out=rstd[:], in0=ss[:], scalar1=1.0 / D, scalar2=RMS_EPS, op0=ALU.mult, op1=ALU.add), reads=["ss"], writes=["rstd"])
        S.op("act", lambda e: e.activation(out=rstd[:], in_=rstd[:], func=AF.Sqrt), reads=["rstd"], writes=["rstd"])
        S.op("dve", lambda e: e.reciprocal(out=rstd[:], in_=rstd[:]), reads=["rstd"], writes=["rstd"])
        for s in range(NS):
            S.op("dve", lambda e, s=s: e.scalar_tensor_tensor(out=xr_[:, s, :], in0=xr_[:, s, :], scalar=rstd[:, s:s + 1], in1=gfin_b[:], op0=ALU.mult, op1=ALU.mult),
                 reads=[(xtok, s), "rstd", "gfin_b"], writes=[(xtok, s)])
        dst = out_d[seq, t0:t0 + TT, :].rearrange("(s p) d -> p s d", p=128)
        return S.op("sp", lambda e: e.dma_start(out=dst, in_=xr_[:]), reads=[(xtok, s) for s in range(NS)], writes=[], dsem=ost)

    xsem = [S.new_dsem("x") for _ in range(2)]
    osem = [S.new_dsem("o") for _ in range(2)]
    tiles = [(q, t) for q in range(n_seq) for t in range(n_tiles)]

    def xload(i):
        q, t = tiles[i]
        buf = xres[0]
        src = x_d[q, t * TT:(t + 1) * TT, :].rearrange("(s p) d -> p s d", p=128)
        S.op("sp", lambda e: e.dma_start(out=buf[:], in_=src), reads=[], writes=[("x0", s) for s in range(NS)], dsem=xsem[0])

    last_out = []
    for i, (q, t) in enumerate(tiles):
        xload(i)
        xr_ = xres[0]
        xtok = "x0"
        first = (t == 0)
        if "A" in stages:
            stage_A(xr_, xtok, first)
        if "B" in stages:
            stage_F(xr_, xtok, 0, first)
        if "C" in stages:
            stage_C(xr_, xtok, first)
        if "D" in stages:
            stage_F(xr_, xtok, 1, first)
        c = stage_E(xr_, xtok, q, t * TT, osem[0])
        last_out.append(c)
    for c in last_out[-2:]:
        S.final_wait("sp", c)

    with nc.Block() as block:
        S.emit(block)
    es.close()
    return nc, S


def kernel(**inputs):
    inp = {k: np.asarray(v) for k, v in inputs.items()}
    x = np.ascontiguousarray(inp["x"], dtype=np.float32)
    B, T, _ = x.shape
    n_seq = B // N_CORES
    nc, _ = build(n_seq=n_seq, T=T)
    shared = host_shared(inp)
    in_maps = []
    for c in range(N_CORES):
        m = dict(shared)
        m["x"] = np.ascontiguousarray(x[c * n_seq:(c + 1) * n_seq])
        in_maps.append(m)
    res = run_bass_kernel_spmd(nc, in_maps, core_ids=list(range(N_CORES)))
    return np.concatenate([r["out"] for r in res.results], axis=0)


def host_shared(inp):
    f = lambda a: np.ascontiguousarray(np.asarray(a, dtype=np.float32))
    return {
        "vecs": make_vecs(inp),
        "lru_w_in": f(inp["lru_w_in"][0]), "lru_w_out": f(inp["lru_w_out"][0]),
        "ffn_w_up0": f(inp["ffn_w_up"][0]), "ffn_w_up1": f(inp["ffn_w_up"][1]),
        "ffn_w_dn0": f(inp["ffn_w_down"][0]), "ffn_w_dn1": f(inp["ffn_w_down"][1]),
        "rwkv_w_r": f(inp["rwkv_w_rkv"][0, 0]), "rwkv_w_k": f(inp["rwkv_w_rkv"][0, 1]), "rwkv_w_v": f(inp["rwkv_w_rkv"][0, 2]),
        "rwkv_w_out": f(inp["rwkv_w_out"][0]),
        "lru_gate_w": f(inp["lru_gate_w"][0]), "lru_b_out": f(inp["lru_b_out"]),
        "rwkv_w1": f(inp["rwkv_w1"][0]), "rwkv_a1": f(inp["rwkv_a1"][0]), "rwkv_g1": f(inp["rwkv_g1"][0]),
        "rwkv_w2": f(inp["rwkv_w2"][0]), "rwkv_a2": f(inp["rwkv_a2"][0]), "rwkv_g2": f(inp["rwkv_g2"][0]),
        "rwkv_ln_w": f(inp["rwkv_ln_w"][0]), "rwkv_ln_b": f(inp["rwkv_ln_b"][0]), "final_norm": f(inp["final_norm"]),
    }
```

```python
import numpy as np
from contextlib import ExitStack
import concourse.bass as bass
import concourse.mybir as mybir
from concourse.bass_utils import run_bass_kernel_spmd

F32 = mybir.dt.float32
BF16 = mybir.dt.bfloat16
AF = mybir.ActivationFunctionType
ALU = mybir.AluOpType
AX = mybir.AxisListType

D = 1024
KC = 8
TT = 512
NS = 4
DFF = 3072
NJ = 24
NW = 5
CH = 128
RMS_EPS = 1e-6
GN_EPS = 64e-5
N_CORES = 8
SAME_ENGINE_SYNC = True

VEC_SPECS = [
    ("lru_norm", 8), ("lru_b_in", 16), ("lru_conv_w", 32), ("lru_conv_b", 8), ("lru_gate_b", 16), ("lru_lambda", 8),
    ("rwkv_norm", 8), ("rwkv_mix", 48), ("rwkv_w0", 8), ("rwkv_a0", 8), ("rwkv_k_k", 8), ("rwkv_k_a", 8), ("rwkv_r_k", 8),
    ("ffn_norm", 16), ("ffn_conv_w", 144), ("ffn_conv_b", 48),
]
VOFF = {}
_o = 0
for _n, _w in VEC_SPECS:
    VOFF[_n] = _o
    _o += _w
NV = _o


def _fm(v):
    v = np.asarray(v, dtype=np.float32)
    return np.ascontiguousarray(v.reshape(-1, 128).T)


def make_vecs(inp):
    cols = [_fm(inp[n]) for n, _ in VEC_SPECS]
    out = np.concatenate(cols, axis=1)
    assert out.shape == (128, NV), out.shape
    return np.ascontiguousarray(out)


class DmaSem:
    def __init__(self, sem):
        self.sem = sem
        self.val = 0


class Sched:
    ENG = ("pe", "act", "dve", "pool", "sp")
    EPOCH = 30000

    def __init__(self, nc, es):
        self.nc = nc
        self.es = es
        self.nsem = 0
        self.prog = {e: [] for e in self.ENG}
        self.esem = {}
        self.cnt = {}
        self.own = {e: set() for e in self.ENG}
        for e in self.ENG:
            self._new_epoch(e)
        self.waited = {e: {} for e in self.ENG}
        self.lastw = {}
        self.readers = {}
        self.nops = 0

    def new_sem(self, name):
        self.nsem += 1
        return self.es.enter_context(self.nc.semaphore(f"{name}{self.nsem}"))

    def new_dsem(self, name="d"):
        return DmaSem(self.new_sem(name))

    def _new_epoch(self, e):
        s = self.new_sem("e" + e)
        self.esem[e] = s
        self.cnt[e] = 0
        self.own[e].add(id(s))

    def op(self, eng, fn, reads=(), writes=(), dsem=None):
        deps = {}

        def add(c):
            if c is None:
                return
            k = id(c[0])
            if k not in deps or deps[k][1] < c[1]:
                deps[k] = c

        for t in reads:
            add(self.lastw.get(t))
        for t in writes:
            add(self.lastw.get(t))
            for c in self.readers.get(t, {}).values():
                add(c)
        for k, (s, v) in deps.items():
            if k in self.own[eng] and dsem is None and (eng == "pe" or not SAME_ENGINE_SYNC):
                continue
            if self.waited[eng].get(k, 0) < v:
                self.prog[eng].append(("w", s, v))
                self.waited[eng][k] = v
        if dsem is None:
            if self.cnt[eng] >= self.EPOCH:
                self._new_epoch(eng)
            self.cnt[eng] += 1
            comp = (self.esem[eng], self.cnt[eng])
            inc = 1
        else:
            dsem.val += 16
            comp = (dsem.sem, dsem.val)
            inc = 16
        self.prog[eng].append(("o", fn, comp[0], inc))
        for t in reads:
            self.readers.setdefault(t, {})[id(comp[0])] = comp
        for t in writes:
            self.lastw[t] = comp
            self.readers[t] = {}
        self.nops += 1
        return comp

    def drain(self, eng):
        if self.cnt[eng] > 0:
            self.prog[eng].append(("w", self.esem[eng], self.cnt[eng]))

    def final_wait(self, eng, comp):
        self.prog[eng].append(("w", comp[0], comp[1]))

    def emit(self, block):
        def mk(e):
            def f(h):
                for it in self.prog[e]:
                    if it[0] == "w":
                        h.wait_ge(it[1], it[2])
                    else:
                        it[1](h).then_inc(it[2], it[3])
            return f

        block.tensor(mk("pe"))
        block.scalar(mk("act"))
        block.vector(mk("dve"))
        block.gpsimd(mk("pool"))
        block.sync(mk("sp"))


def build(n_seq=4, T=2048, stages="ABCDE"):
    assert T % TT == 0
    n_tiles = T // TT
    nc = bass.Bass("TRN2", target_bir_lowering=False)
    es = ExitStack()
    S = Sched(nc, es)

    def dram(name, shape, dt=F32, kind="ExternalInput"):
        return nc.dram_tensor(name, list(shape), dt, kind=kind).ap()

    def sb(name, shape, dt=F32):
        return es.enter_context(nc.sbuf_tensor("s_" + name, list(shape), dt))

    x_d = dram("x", [n_seq, T, D])
    out_d = dram("out", [n_seq, T, D], kind="ExternalOutput")
    vecs_d = dram("vecs", [128, NV])
    bigw = {
        "lru_w_in": (D, 2048), "lru_w_out": (D, D),
        "ffn_w_up0": (D, 2 * DFF), "ffn_w_up1": (D, 2 * DFF), "ffn_w_dn0": (DFF, D), "ffn_w_dn1": (DFF, D),
        "rwkv_w_r": (D, D), "rwkv_w_k": (D, D), "rwkv_w_v": (D, D), "rwkv_w_out": (D, D),
    }
    w_d = {n: dram(n, s) for n, s in bigw.items()}
    wb_d = {n: dram("b_" + n, s, BF16, kind="Internal") for n, s in bigw.items()}
    gate_w_d = dram("lru_gate_w", [2, 16, 64, 64])
    b_out_d = dram("lru_b_out", [1, D])
    w1_d = dram("rwkv_w1", [D, 64]); a1_d = dram("rwkv_a1", [D, 64]); g1_d = dram("rwkv_g1", [D, 160])
    w2_d = dram("rwkv_w2", [64, D]); a2_d = dram("rwkv_a2", [64, D]); g2_d = dram("rwkv_g2", [160, D])
    lnw_d = dram("rwkv_ln_w", [D]); lnb_d = dram("rwkv_ln_b", [D]); fin_d = dram("final_norm", [D])

    vecs = sb("vecs", [128, NV])
    ident = sb("ident", [128, 128], BF16)
    ones_row = sb("ones_row", [1, 128], BF16)
    bout_row = sb("bout_row", [1, D], BF16)
    gateW = sb("gateW", [128, 16, 128], BF16)
    cneg = sb("cneg", [128, 8])
    gfin_b = sb("gfin_b", [128, D])
    HAS_C = "C" in stages
    if HAS_C:
        lnw_b = sb("lnw_b", [128, D]); lnb_b = sb("lnb_b", [128, D])
        w1b = sb("w1b", [128, KC, 64], BF16); a1b = sb("a1b", [128, KC, 64], BF16); g1b = sb("g1b", [128, KC, 160], BF16)
        w2b = sb("w2b", [64, D], BF16); a2b = sb("a2b", [64, D], BF16); g2b = sb("g2b", [128, 2, D], BF16)
        mUs = sb("mUs", [128, 128], BF16); mUi = sb("mUi", [128, 128], BF16); mLs = sb("mLs", [128, 128], BF16)
        sel2 = sb("sel2", [128, 2], BF16); blkones = sb("blkones", [128, 128], BF16)
        rmask = sb("rmask", [128, TT]); omka = sb("omka", [128, 8])
        tanhw = sb("tanhw", [64, TT], BF16); a1o = sb("a1o", [64, TT], BF16); sgT = sb("sgT", [128, 2, TT], BF16)
        WC = sb("WC", [128, 8, 4]); dtok = sb("dtok", [128, 4, 16]); hcar = sb("hcar", [128, 8], BF16)
        Hst = sb("Hst", [128, 8, 64]); Hbd = sb("Hbd", [128, 8, 128], BF16)
        st_a = sb("st_a", [128, 16]); st_b = sb("st_b", [128, 16]); st_c = sb("st_c", [128, 16])
    xres = [sb("xres0", [128, NS, D])]
    hT = sb("hT", [128, KC, TT], BF16)
    xn = [sb(f"xn{i}", [128, D], BF16) for i in range(2)]
    sqj = xn[0]
    ss = sb("ss", [128, NS]); rstd = sb("rstd", [128, NS])
    wring = [sb(f"wring{i}", [128, 2048], BF16) for i in range(NW)]
    wsem = [S.new_dsem("w") for _ in range(NW)]
    hstate = sb("hstate", [128, 8])
    ucar = sb("ucar", [128, 8, 3])
    fcar = [sb(f"fcar{l}", [128, NJ, 2]) for l in range(2)]
    NBLK = 54
    arena = sb("arena", [128, NBLK * 1024], BF16)
    ps = [es.enter_context(nc.psum_tensor(f"ps{i}", [128, 512], F32)) for i in range(8)]

    def V(name, c, n=1):
        o = VOFF[name] + c
        return vecs[:, o:o + n]

    class LB:
        def __init__(self, b0, nb, dt, inner=None):
            self.b0, self.nb, self.dt = b0, nb, dt
            a = arena[:, b0 * 1024:(b0 + nb) * 1024]
            if dt == F32:
                a = a.bitcast(F32)
            self.flat = a
            self.per = 1024 if dt == BF16 else 512
        def blk(self, i, n=1):
            return self.flat[:, i * self.per:(i + n) * self.per]
        def tok(self, i=None, n=1):
            if i is None:
                return [("ar", self.b0 + k) for k in range(self.nb)]
            return [("ar", self.b0 + i + k) for k in range(n)]

    bank_ctr = [0]

    bank_list = [list(range(8))]

    def bank():
        bl = bank_list[0]
        b = bl[bank_ctr[0] % len(bl)]
        bank_ctr[0] += 1
        return b

    wi = [0]

    def wload(name, src_ap, view):
        i = wi[0] % NW
        wi[0] += 1
        dst = view(wring[i])
        S.op("sp", lambda e, dst=dst, src_ap=src_ap: e.dma_start(out=dst, in_=src_ap),
             reads=[("wd", name)], writes=[("w", i)], dsem=wsem[i])
        return dst, ("w", i)

    cs_by_eng = {"sp": S.new_dsem("c"), "pool": S.new_dsem("cp")}
    const_toks = []

    def cload(eng, out_ap, in_ap, tok):
        S.op(eng, lambda e: e.dma_start(out=out_ap, in_=in_ap), reads=[], writes=[tok], dsem=cs_by_eng[eng])
        const_toks.append((tok, eng))

    cload("sp", vecs[:], vecs_d[:, :], "vecs")
    cload("sp", gfin_b[:], fin_d.partition_broadcast(128), "gfin_b")
    cload("pool", bout_row[:], b_out_d[:, :], "bout_row")
    S.op("pool", lambda e: e.memset(gateW[:], 0.0), writes=["gateW"])
    for g in range(2):
        for par in range(2):
            src = gate_w_d[g, par::2].rearrange("n c d -> c n d")
            dst = gateW[par * 64:(par + 1) * 64, g * 8:(g + 1) * 8, par * 64:(par + 1) * 64]
            cload("pool", dst, src, "gateW")
    if HAS_C:
        cload("sp", lnw_b[:], lnw_d.partition_broadcast(128), "lnw_b")
        cload("sp", lnb_b[:], lnb_d.partition_broadcast(128), "lnb_b")
        cload("pool", w1b[:], w1_d.rearrange("(k p) r -> p k r", p=128), "w1b")
        cload("pool", a1b[:], a1_d.rearrange("(k p) r -> p k r", p=128), "a1b")
        cload("pool", g1b[:], g1_d.rearrange("(k p) r -> p k r", p=128), "g1b")
        cload("pool", w2b[:], w2_d[:, :], "w2b")
        cload("pool", a2b[:], a2_d[:, :], "a2b")
        cload("pool", g2b[:, 0, :], g2_d[0:128, :], "g2b")
        cload("pool", g2b[0:32, 1, :], g2_d[128:160, :], "g2b")
    for t, eng in set(const_toks):
        S.lastw[t] = (cs_by_eng[eng].sem, cs_by_eng[eng].val)
    if HAS_C:
        for m, cmp_, cm, pat in ((mUs, ALU.is_gt, -1, 1), (mUi, ALU.is_ge, -1, 1), (mLs, ALU.is_gt, 1, -1)):
            tk = "mask%d" % id(m)
            S.op("pool", lambda e, m=m: e.memset(m[:], 1.0), writes=[tk])
            S.op("pool", lambda e, m=m, cmp_=cmp_, cm=cm, pat=pat: e.affine_select(out=m[:], in_=m[:], pattern=[[pat, 128]], compare_op=cmp_,
                                                                               fill=0.0, base=0, channel_multiplier=cm), reads=[tk], writes=[tk])
        S.op("pool", lambda e: e.memset(sel2[:], 0.0), writes=["sel2"])
        S.op("pool", lambda e: e.memset(sel2[0:64, 0:1], 1.0), reads=["sel2"], writes=["sel2"])
        S.op("pool", lambda e: e.memset(sel2[64:128, 1:2], 1.0), reads=["sel2"], writes=["sel2"])
        S.op("pool", lambda e: e.memset(blkones[:], 0.0), writes=["blkones"])
        S.op("pool", lambda e: e.memset(blkones[0:64, 0:64], 1.0), reads=["blkones"], writes=["blkones"])
        S.op("pool", lambda e: e.memset(blkones[64:128, 64:128], 1.0), reads=["blkones"], writes=["blkones"])
        S.op("pool", lambda e: e.memset(rmask[:], 1.0), writes=["rmask"])
        S.op("pool", lambda e: e.memset(rmask[:].rearrange("p (c t) -> p c t", t=CH)[:, :, 0:1], 0.0), reads=["rmask"], writes=["rmask"])
        S.op("dve", lambda e: e.tensor_scalar(out=omka[:], in0=V("rwkv_k_a", 0, 8), scalar1=-1.0, scalar2=1.0, op0=ALU.mult, op1=ALU.add),
             reads=["vecs"], writes=["omka"])
    S.op("pool", lambda e: e.memset(ident[:], 0.0), writes=["ident"])
    S.op("pool", lambda e: e.affine_select(out=ident[:], in_=ident[:], pattern=[[-1, 128]], compare_op=ALU.not_equal,
                                           fill=1.0, base=0, channel_multiplier=1), reads=["ident"], writes=["ident"])
    S.op("pool", lambda e: e.memset(ones_row[:], 1.0), writes=["ones_row"])
    S.op("act", lambda e: e.activation(out=cneg[:], in_=V("lru_lambda", 0, 8), func=AF.Exp, scale=-1.0), reads=["vecs"], writes=["cneg"])
    S.op("act", lambda e: e.activation(out=cneg[:], in_=cneg[:], func=AF.Ln, bias=1.0), reads=["cneg"], writes=["cneg"])
    S.op("act", lambda e: e.mul(out=cneg[:], in_=cneg[:], mul=-8.0), reads=["cneg"], writes=["cneg"])

    used = []
    if "A" in stages:
        used += ["lru_w_in", "lru_w_out"]
    if "B" in stages:
        used += ["ffn_w_up0", "ffn_w_dn0"]
    if "C" in stages:
        used += ["rwkv_w_r", "rwkv_w_k", "rwkv_w_v", "rwkv_w_out"]
    if "D" in stages:
        used += ["ffn_w_up1", "ffn_w_dn1"]
    for n in used:
        rows, cols = bigw[n]
        ds = S.new_dsem("k")
        step = max(32, (1 << 18) // cols)
        for r0 in range(0, rows, step):
            r1 = min(rows, r0 + step)
            S.op("pool", lambda e, n=n, r0=r0, r1=r1: e.dma_start(out=wb_d[n][r0:r1, :], in_=w_d[n][r0:r1, :]),
                 writes=[("wd", n)], dsem=ds)
        S.lastw[("wd", n)] = (ds.sem, ds.val)

    def rmsnorm_hT(xr_, xtok, gname, goff):
        for s in range(NS):
            S.op("act", lambda e, s=s: e.activation(out=sqj[:], in_=xr_[:, s, :], func=AF.Square, accum_out=ss[:, s:s + 1]),
                 reads=[(xtok, s)], writes=[("xn", 0), "ss"])
        S.op("dve", lambda e: e.tensor_scalar(out=rstd[:], in0=ss[:], scalar1=1.0 / D, scalar2=RMS_EPS, op0=ALU.mult, op1=ALU.add),
             reads=["ss"], writes=["rstd"])
        S.op("act", lambda e: e.activation(out=rstd[:], in_=rstd[:], func=AF.Sqrt), reads=["rstd"], writes=["rstd"])
        S.op("dve", lambda e: e.reciprocal(out=rstd[:], in_=rstd[:]), reads=["rstd"], writes=["rstd"])
        gb = V(gname, goff, 8).unsqueeze(2).to_broadcast([128, KC, 128])
        for s in range(NS):
            xb = xn[s % 2]
            xbt = ("xn", s % 2)
            S.op("act", lambda e, s=s, xb=xb: e.activation(out=xb[:], in_=xr_[:, s, :], func=AF.Copy, scale=rstd[:, s:s + 1]),
                 reads=[(xtok, s), "rstd"], writes=[xbt])
            b = bank()
            psb = ps[b][:].bitcast(BF16).rearrange("p (k t) -> p k t", k=KC)
            for kc in range(KC):
                S.op("pe", lambda e, kc=kc, xb=xb, psb=psb: e.transpose(psb[:, kc, :], xb[:, kc * 128:(kc + 1) * 128], ident[:]),
                     reads=[xbt, "ident"], writes=[("ps", b)])
            S.op("dve", lambda e, s=s, psb=psb: e.tensor_tensor(out=hT[:, :, s * 128:(s + 1) * 128], in0=psb, in1=gb, op=ALU.mult),
                 reads=[("ps", b), "vecs"], writes=["hT"])

    def out_proj(xr_, xtok, wname, nk, actT, act_toks, bias=False, evac=None):
        kpl = 4 if nk >= 4 else nk
        for nh in range(2):
            banks = [bank() for _ in range(NS)]
            for k0 in range(0, nk, kpl):
                src = wb_d[wname][k0 * 128:(k0 + kpl) * 128, nh * 512:(nh + 1) * 512].rearrange("(k p) e -> p k e", p=128)
                wsl, wtok = wload(wname, src, lambda t: t[:, 0:kpl * 512].rearrange("p (k e) -> p k e", k=kpl))
                for kk in range(kpl):
                    kc = k0 + kk
                    for s in range(NS):
                        last = (kc == nk - 1) and not bias
                        S.op("pe", lambda e, kc=kc, s=s, kk=kk, wsl=wsl, last=last, b=banks[s]:
                             e.matmul(ps[b][:], lhsT=actT(kc, s), rhs=wsl[:, kk, :], start=(kc == 0), stop=last),
                             reads=[wtok] + act_toks(kc), writes=[("ps", banks[s])])
            for s in range(NS):
                if bias:
                    S.op("pe", lambda e, s=s, nh=nh, b=banks[s]: e.matmul(ps[b][:], lhsT=ones_row[0:1, :], rhs=bout_row[0:1, nh * 512:(nh + 1) * 512],
                                                                        start=False, stop=True),
                         reads=["ones_row", "bout_row"], writes=[("ps", banks[s])])
                if evac is not None:
                    evac(s, nh, banks[s])
                    continue
                S.op("dve", lambda e, s=s, nh=nh, b=banks[s]: e.tensor_tensor(out=xr_[:, s, nh * 512:(nh + 1) * 512], in0=ps[b][:],
                                                                            in1=xr_[:, s, nh * 512:(nh + 1) * 512], op=ALU.add),
                     reads=[("ps", banks[s]), (xtok, s)], writes=[(xtok, s)])

    def stage_A(xr_, xtok, first_tile):
        yb = LB(0, 4, BF16); xr = LB(4, 8, F32)
        rg = LB(12, 8, F32); ig = LB(20, 8, F32); t1 = LB(28, 8, F32); xrb = LB(28, 4, BF16)
        UP = LB(36, 9, F32)
        upre = UP.flat[:, 0:8 * (TT + 3)].rearrange("p (c t) -> p c t", c=8)
        uptok = lambda cc: UP.tok((cc * (TT + 3)) // 512, ((cc + 1) * (TT + 3) - 1) // 512 - (cc * (TT + 3)) // 512 + 1)
        if first_tile:
            S.op("pool", lambda e: e.memset(ucar[:], 0.0), writes=["ucar"])
            S.op("pool", lambda e: e.memset(hstate[:], 0.0), writes=["hstate"])
        S.op("pool", lambda e: e.tensor_copy(out=upre[:, :, 0:3], in_=ucar[:]), reads=["ucar"], writes=UP.tok())
        rmsnorm_hT(xr_, xtok, "lru_norm", 0)
        for q in range(8):
            src = wb_d["lru_w_in"][:, q * 256:(q + 1) * 256].rearrange("(k p) e -> p k e", p=128)
            wsl, wtok = wload("lru_w_in", src, lambda t: t[:].rearrange("p (k e) -> p k e", k=KC))
            for ee in range(2):
                c = 2 * q + ee
                b = bank()
                for kc in range(KC):
                    S.op("pe", lambda e, kc=kc, ee=ee, wsl=wsl, b=b: e.matmul(ps[b][:], lhsT=wsl[:, kc, ee * 128:(ee + 1) * 128], rhs=hT[:, kc, :],
                                                                          start=(kc == 0), stop=(kc == KC - 1)),
                         reads=[wtok, "hT"], writes=[("ps", b)])
                if c < 8:
                    S.op("act", lambda e, c=c, b=b: e.activation(out=yb.flat[:, c * 512:(c + 1) * 512], in_=ps[b][:], func=AF.Gelu_apprx_tanh,
                                                               bias=V("lru_b_in", c)),
                         reads=[("ps", b), "vecs"], writes=yb.tok(c // 2))
                else:
                    cc = c - 8
                    S.op("act", lambda e, cc=cc, b=b: e.activation(out=upre[:, cc, 3:3 + TT], in_=ps[b][:], func=AF.Identity, bias=V("lru_b_in", 8 + cc)),
                         reads=[("ps", b), "vecs"], writes=uptok(cc))
        for cc in range(8):
            o = xr.blk(cc)
            rd = uptok(cc) + ["vecs"]
            S.op("act", lambda e, cc=cc, o=o: e.activation(out=o, in_=upre[:, cc, 0:TT], func=AF.Identity, scale=V("lru_conv_w", 0 * 8 + cc), bias=V("lru_conv_b", cc)),
                 reads=rd, writes=xr.tok(cc))
            for j in range(1, 4):
                S.op("dve", lambda e, cc=cc, o=o, j=j: e.scalar_tensor_tensor(out=o, in0=upre[:, cc, j:j + TT], scalar=V("lru_conv_w", j * 8 + cc), in1=o,
                                                                             op0=ALU.mult, op1=ALU.add), reads=rd + xr.tok(cc), writes=xr.tok(cc))
            S.op("act", lambda e, cc=cc, o=o: e.activation(out=xrb.flat[:, cc * 512:(cc + 1) * 512], in_=o, func=AF.Copy), reads=xr.tok(cc), writes=xrb.tok(cc // 2))
        S.op("pool", lambda e: e.tensor_copy(out=ucar[:], in_=upre[:, :, TT:TT + 3]), reads=UP.tok(), writes=["ucar"])
        for cc in range(8):
            for g, dst in ((0, rg), (1, ig)):
                b = bank()
                S.op("pe", lambda e, cc=cc, g=g, b=b: e.matmul(ps[b][:], lhsT=gateW[:, g * 8 + cc, :], rhs=xrb.flat[:, cc * 512:(cc + 1) * 512], start=True, stop=True),
                     reads=["gateW"] + xrb.tok(cc // 2), writes=[("ps", b)])
                S.op("act", lambda e, cc=cc, g=g, b=b, dst=dst: e.activation(out=dst.blk(cc), in_=ps[b][:], func=AF.Sigmoid, bias=V("lru_gate_b", g * 8 + cc)),
                     reads=[("ps", b), "vecs"], writes=dst.tok(cc))
        for cc in range(8):
            S.op("act", lambda e, cc=cc: e.activation(out=rg.blk(cc), in_=rg.blk(cc), func=AF.Exp, scale=cneg[:, cc:cc + 1]),
                 reads=rg.tok(cc) + ["cneg"], writes=rg.tok(cc))
            S.op("dve", lambda e, cc=cc: e.tensor_tensor(out=t1.blk(cc), in0=rg.blk(cc), in1=rg.blk(cc), op=ALU.mult), reads=rg.tok(cc), writes=t1.tok(cc))
            S.op("dve", lambda e, cc=cc: e.tensor_tensor(out=ig.blk(cc), in0=ig.blk(cc), in1=xr.blk(cc), op=ALU.mult), reads=ig.tok(cc) + xr.tok(cc), writes=ig.tok(cc))
        for cc in range(8):
            S.op("act", lambda e, cc=cc: e.activation(out=t1.blk(cc), in_=t1.blk(cc), func=AF.Sqrt, scale=-1.0, bias=1.0), reads=t1.tok(cc), writes=t1.tok(cc))
            S.op("dve", lambda e, cc=cc: e.tensor_tensor(out=ig.blk(cc), in0=ig.blk(cc), in1=t1.blk(cc), op=ALU.mult), reads=ig.tok(cc) + t1.tok(cc), writes=ig.tok(cc))
            S.op("dve", lambda e, cc=cc: e.tensor_tensor_scan(out=t1.blk(cc), data0=rg.blk(cc), data1=ig.blk(cc), initial=hstate[:, cc:cc + 1], op0=ALU.mult, op1=ALU.add),
                 reads=rg.tok(cc) + ig.tok(cc) + ["hstate"], writes=t1.tok(cc))
            S.op("dve", lambda e, cc=cc: e.tensor_copy(out=hstate[:, cc:cc + 1], in_=t1.blk(cc)[:, TT - 1:TT]), reads=t1.tok(cc), writes=["hstate"])
            S.op("dve", lambda e, cc=cc: e.tensor_tensor(out=yb.flat[:, cc * 512:(cc + 1) * 512], in0=t1.blk(cc), in1=yb.flat[:, cc * 512:(cc + 1) * 512], op=ALU.mult),
                 reads=t1.tok(cc) + yb.tok(cc // 2), writes=yb.tok(cc // 2))
        out_proj(xr_, xtok, "lru_w_out", 8, lambda kc, s: yb.flat[:, kc * 512 + s * 128: kc * 512 + (s + 1) * 128], lambda kc: yb.tok(kc // 2), bias=True)

    def stage_F(xr_, xtok, l, first_tile):
        hid = LB(0, 12, BF16)
        gpre = [sbuf_gpre[0], sbuf_gpre[1]]
        gc = LB(12, 2, F32); gg = LB(14, 2, F32)
        if first_tile:
            S.op("pool", lambda e: e.memset(fcar[l][:], 0.0), writes=[("fcar", l)])
        rmsnorm_hT(xr_, xtok, "ffn_norm", l * 8)
        up = f"ffn_w_up{l}"
        for jj in range(NJ // 2):
            srcg = wb_d[up][:, jj * 256:(jj + 1) * 256].rearrange("(k p) e -> p k e", p=128)
            srcu = wb_d[up][:, DFF + jj * 256:DFF + (jj + 1) * 256].rearrange("(k p) e -> p k e", p=128)
            wg, wgt = wload(up, srcg, lambda t: t[:].rearrange("p (k e) -> p k e", k=KC))
            wu, wut = wload(up, srcu, lambda t: t[:].rearrange("p (k e) -> p k e", k=KC))
            for ee in range(2):
                j = 2 * jj + ee
                r = j % 2
                bg = bank(); bu = bank()
                for kc in range(KC):
                    S.op("pe", lambda e, kc=kc, ee=ee, wg=wg, bg=bg: e.matmul(ps[bg][:], lhsT=wg[:, kc, ee * 128:(ee + 1) * 128], rhs=hT[:, kc, :], start=(kc == 0), stop=(kc == KC - 1)),
                         reads=[wgt, "hT"], writes=[("ps", bg)])
                for kc in range(KC):
                    S.op("pe", lambda e, kc=kc, ee=ee, wu=wu, bu=bu: e.matmul(ps[bu][:], lhsT=wu[:, kc, ee * 128:(ee + 1) * 128], rhs=hT[:, kc, :], start=(kc == 0), stop=(kc == KC - 1)),
                         reads=[wut, "hT"], writes=[("ps", bu)])
                gp = gpre[r]; gpt = ("gpre", r)
                S.op("pool", lambda e, gp=gp, j=j: e.tensor_copy(out=gp[:, 0:2], in_=fcar[l][:, j, :]), reads=[("fcar", l)], writes=[gpt])
                S.op("act", lambda e, gp=gp, bg=bg: e.activation(out=gp[:, 2:2 + TT], in_=ps[bg][:], func=AF.Identity), reads=[("ps", bg)], writes=[gpt])
                S.op("pool", lambda e, gp=gp, j=j: e.tensor_copy(out=fcar[l][:, j, :], in_=gp[:, TT:TT + 2]), reads=[gpt], writes=[("fcar", l)])
                co = VOFF["ffn_conv_w"] + l * 72
                S.op("act", lambda e, j=j, r=r, co=co, bg=bg: e.activation(out=gc.blk(r), in_=ps[bg][:], func=AF.Copy, scale=vecs[:, co + 2 * 24 + j:co + 2 * 24 + j + 1]),
                     reads=[("ps", bg), "vecs"], writes=gc.tok(r))
                for tap in (0, 1):
                    S.op("dve", lambda e, gp=gp, j=j, r=r, co=co, tap=tap: e.scalar_tensor_tensor(out=gc.blk(r), in0=gp[:, tap:tap + TT], scalar=vecs[:, co + tap * 24 + j:co + tap * 24 + j + 1],
                                                                                               in1=gc.blk(r), op0=ALU.mult, op1=ALU.add),
                         reads=[gpt, "vecs"] + gc.tok(r), writes=gc.tok(r))
                S.op("act", lambda e, j=j, r=r: e.activation(out=gg.blk(r), in_=gc.blk(r), func=AF.Gelu_apprx_tanh, bias=V("ffn_conv_b", l * 24 + j)),
                     reads=gc.tok(r) + ["vecs"], writes=gg.tok(r))
                S.op("dve", lambda e, j=j, r=r, bu=bu: e.tensor_tensor(out=hid.flat[:, j * 512:(j + 1) * 512], in0=ps[bu][:], in1=gg.blk(r), op=ALU.mult),
                     reads=[("ps", bu)] + gg.tok(r), writes=hid.tok(j // 2))
        out_proj(xr_, xtok, f"ffn_w_dn{l}", NJ, lambda kc, s: hid.flat[:, kc * 512 + s * 128: kc * 512 + (s + 1) * 128], lambda kc: hid.tok(kc // 2))

    sbuf_gpre = [sb(f"gpre{i}", [128, TT + 2]) for i in range(2)]

    KAPPA = 0.6065306597126334

    def stage_C(xr_, xtok, first_tile):
        AT = LB(0, 4, BF16); BT = LB(4, 4, BF16); KT = LB(8, 4, BF16); RT = LB(12, 4, BF16)
        Vt = LB(16, 4, BF16); Kh = LB(20, 4, BF16); Bh = LB(24, 4, BF16)
        xx = LB(28, 4, BF16); xs = LB(32, 4, BF16)

        def v3(lb):
            return lb.flat.rearrange("p (k t) -> p k t", k=KC)

        def v4(lb):
            return lb.flat.rearrange("p (c f) -> p c f", c=NS)

        def mm(out, lhsT, rhs, rd, b, start=True, stop=True):
            S.op("pe", lambda e: e.matmul(out, lhsT=lhsT, rhs=rhs, start=start, stop=stop), reads=rd, writes=[("ps", b)])

        if first_tile:
            S.op("pool", lambda e: e.memset(hcar[:], 0.0), writes=["hcar"])
            S.op("pool", lambda e: e.memset(Hst[:], 0.0), writes=["Hst"])
            S.op("pool", lambda e: e.memset(Hbd[:], 0.0), writes=["Hbd"])
        rmsnorm_hT(xr_, xtok, "rwkv_norm", 0)
        xx3 = v3(xx); xs3 = v3(xs)
        S.op("dve", lambda e: e.tensor_tensor(out=xx3[:, :, 1:TT], in0=hT[:, :, 0:TT - 1], in1=hT[:, :, 1:TT], op=ALU.subtract), reads=["hT"], writes=xx.tok())
        S.op("dve", lambda e: e.tensor_tensor(out=xx3[:, :, 0:1], in0=hcar[:].unsqueeze(2), in1=hT[:, :, 0:1], op=ALU.subtract), reads=["hT", "hcar"], writes=xx.tok())
        S.op("pool", lambda e: e.tensor_copy(out=hcar[:].unsqueeze(2), in_=hT[:, :, TT - 1:TT]), reads=["hT"], writes=["hcar"])

        def make_xs(i):
            for kc in range(KC):
                S.op("dve", lambda e, kc=kc: e.scalar_tensor_tensor(out=xs3[:, kc, :], in0=xx3[:, kc, :], scalar=V("rwkv_mix", i * 8 + kc), in1=hT[:, kc, :], op0=ALU.mult, op1=ALU.add),
                     reads=xx.tok(kc // 2) + ["hT", "vecs"], writes=xs.tok(kc // 2))

        def lora1(wt, wtok, c0, c1, func, out_ap, out_tok):
            M = c1 - c0
            b = bank()
            for kc in range(KC):
                mm(ps[b][0:M, :], wt[:, kc, c0:c1], xs3[:, kc, :], [wtok] + xs.tok(kc // 2), b, start=(kc == 0), stop=(kc == KC - 1))
            S.op("act", lambda e: e.activation(out=out_ap, in_=ps[b][0:M, :], func=func), reads=[("ps", b)], writes=[out_tok])

        make_xs(3); lora1(w1b, "w1b", 0, 64, AF.Tanh, tanhw[:], "tanhw")
        make_xs(4); lora1(a1b, "a1b", 0, 64, AF.Copy, a1o[:], "a1o")
        make_xs(5); lora1(g1b, "g1b", 0, 128, AF.Sigmoid, sgT[:, 0, :], "sgT"); lora1(g1b, "g1b", 128, 160, AF.Sigmoid, sgT[0:32, 1, :], "sgT")

        make_xs(2)
        Vt4 = v4(Vt); Kh4 = v4(Kh); Bh4 = v4(Bh)
        out_proj(None, None, "rwkv_w_v", 8, lambda kc, s: xs3[:, kc, s * 128:(s + 1) * 128], lambda kc: xs.tok(kc // 2),
                 evac=lambda s, nh, b: S.op("act", lambda e: e.activation(out=Vt4[:, s, nh * 512:(nh + 1) * 512], in_=ps[b][:], func=AF.Copy),
                                            reads=[("ps", b)], writes=Vt.tok(s)))
        make_xs(0)
        for kc in range(KC):
            S.op("dve", lambda e, kc=kc: e.scalar_tensor_tensor(out=xx3[:, kc, :], in0=xx3[:, kc, :], scalar=V("rwkv_mix", 1 * 8 + kc), in1=hT[:, kc, :], op0=ALU.mult, op1=ALU.add),
                 reads=xx.tok(kc // 2) + ["hT", "vecs"], writes=xx.tok(kc // 2))

        c4 = lambda ap: ap.rearrange("p (c t) -> p c t", t=CH)
        wslabs = {}

        def pair_bufs(p):
            tb = 36 if p % 2 == 0 else 45
            Tt = [LB(tb + i, 1, F32) for i in range(7)]
            return Tt, LB(tb + 7, 1, BF16), LB(tb + 8, 1, BF16)

        def prep1(p):
            Tt, TKB, TRQ = pair_bufs(p)
            T0, T1, T2, T3 = [Tt[i].flat for i in range(4)]
            k0, k1, k2, k3 = [Tt[i].tok() for i in range(4)]
            pc = slice(p * 128, (p + 1) * 128)
            ee = p % 2
            if ee == 0:
                for nm in ("rwkv_w_k", "rwkv_w_r"):
                    src = wb_d[nm][:, (p // 2) * 256:(p // 2 + 1) * 256].rearrange("(k p) e -> p k e", p=128)
                    wslabs[nm] = wload(nm, src, lambda t: t[:].rearrange("p (k e) -> p k e", k=KC))
            b = bank(); b2 = bank()
            mm(ps[b][:], w2b[0:64, pc], tanhw[:], ["w2b", "tanhw"], b)
            mm(ps[b2][:], a2b[0:64, pc], a1o[:], ["a2b", "a1o"], b2)
            S.op("act", lambda e: e.activation(out=T0, in_=ps[b][:], func=AF.Sigmoid, bias=V("rwkv_w0", p)), reads=[("ps", b), "vecs"], writes=k0)
            S.op("act", lambda e: e.activation(out=T3, in_=ps[b2][:], func=AF.Sigmoid, bias=V("rwkv_a0", p)), reads=[("ps", b2), "vecs"], writes=k3)
            S.op("dve", lambda e: e.tensor_tensor_scan(out=T1, data0=rmask[:], data1=T0, initial=0.0, op0=ALU.mult, op1=ALU.add), reads=k0 + ["rmask"], writes=k1)
            S.op("pool", lambda e: e.tensor_tensor(out=T0, in0=T1, in1=T0, op=ALU.subtract), reads=k0 + k1, writes=k0)
            S.op("act", lambda e: e.activation(out=T2, in_=T1, func=AF.Exp, scale=-KAPPA), reads=k1, writes=k2)
            S.op("act", lambda e: e.activation(out=T0, in_=T0, func=AF.Exp, scale=-KAPPA), reads=k0, writes=k0)
            S.op("act", lambda e: e.activation(out=T1, in_=T1, func=AF.Exp, scale=KAPPA), reads=k1, writes=k1)
            S.op("pool", lambda e: e.tensor_copy(out=WC[:, p, :], in_=c4(T2)[:, :, CH - 1]), reads=k2, writes=["WC"])
            bK, bR = (0, 1) if p % 2 == 0 else (2, 3)
            wk, wkt = wslabs["rwkv_w_k"]; wr, wrt = wslabs["rwkv_w_r"]
            for kc in range(KC):
                mm(ps[bK][:], wk[:, kc, ee * 128:(ee + 1) * 128], xx3[:, kc, :], [wkt] + xx.tok(kc // 2), bK, start=(kc == 0), stop=(kc == KC - 1))
            for kc in range(KC):
                mm(ps[bR][:], wr[:, kc, ee * 128:(ee + 1) * 128], xs3[:, kc, :], [wrt] + xs.tok(kc // 2), bR, start=(kc == 0), stop=(kc == KC - 1))
            return bK, bR

        def prep2(p, bK, bR):
            Tt, TKB, TRQ = pair_bufs(p)
            T0, T1, T2, T3, T4, T5, T6 = [Tt[i].flat for i in range(7)]
            k0, k1, k2, k3, k4, k5, k6 = [Tt[i].tok() for i in range(7)]
            TK = TKB.flat[:, 0:512]; TB = TKB.flat[:, 512:1024]; TR = TRQ.flat[:, 0:512]; KSQ = TRQ.flat[:, 512:1024]
            kt = KT.flat[:, p * 512:(p + 1) * 512]; rt = RT.flat[:, p * 512:(p + 1) * 512]
            at = AT.flat[:, p * 512:(p + 1) * 512]; bt = BT.flat[:, p * 512:(p + 1) * 512]
            ktk = KT.tok(p // 2); rtk = RT.tok(p // 2); atk = AT.tok(p // 2); btk = BT.tok(p // 2)
            pc = slice(p * 128, (p + 1) * 128)
            S.op("act", lambda e: e.activation(out=T4, in_=ps[bK][:], func=AF.Copy, scale=V("rwkv_k_k", p)), reads=[("ps", bK), "vecs"], writes=k4)
            S.op("pool", lambda e: e.tensor_tensor(out=KSQ, in0=T4, in1=T4, op=ALU.mult), reads=k4, writes=TRQ.tok())
            b3 = bank()
            mm(ps[b3][:], blkones[:], KSQ, ["blkones"] + TRQ.tok(), b3)
            S.op("act", lambda e: e.activation(out=T6, in_=ps[b3][:], func=AF.Sqrt), reads=[("ps", b3)], writes=k6)
            S.op("act", lambda e: e.activation(out=T5, in_=T3, func=AF.Identity, scale=V("rwkv_k_a", p), bias=omka[:, p:p + 1]), reads=k3 + ["vecs", "omka"], writes=k5)
            S.op("dve", lambda e: e.tensor_tensor(out=T5, in0=ps[bK][:], in1=T5, op=ALU.mult), reads=k5 + [("ps", bK)], writes=k5)
            S.op("dve", lambda e: e.tensor_tensor(out=kt, in0=T5, in1=T1, op=ALU.mult), reads=k5 + k1, writes=ktk)
            wcb = WC[:, p, :].unsqueeze(2).to_broadcast([128, 4, CH])
            S.op("pool", lambda e: e.tensor_tensor(out=c4(TK), in0=c4(kt), in1=wcb, op=ALU.mult), reads=ktk + ["WC"], writes=TKB.tok())
            S.op("dve", lambda e: e.scalar_tensor_tensor(out=TR, in0=ps[bR][:], scalar=V("rwkv_r_k", p), in1=T5, op0=ALU.mult, op1=ALU.mult), reads=[("ps", bR)] + k5 + ["vecs"], writes=TRQ.tok())
            S.op("dve", lambda e: e.tensor_tensor(out=rt, in0=ps[bR][:], in1=T2, op=ALU.mult), reads=[("ps", bR)] + k2, writes=rtk)
            S.op("dve", lambda e: e.tensor_scalar(out=T6, in0=T6, scalar1=1e-12, scalar2=None, op0=ALU.max), reads=k6, writes=k6)
            S.op("dve", lambda e: e.reciprocal(out=T6, in_=T6), reads=k6, writes=k6)
            S.op("dve", lambda e: e.tensor_tensor(out=T4, in0=T4, in1=T6, op=ALU.mult), reads=k4 + k6, writes=k4)
            S.op("dve", lambda e: e.scalar_tensor_tensor(out=at, in0=T4, scalar=-1.0, in1=T0, op0=ALU.mult, op1=ALU.mult), reads=k4 + k0, writes=atk)
            S.op("pool", lambda e: e.tensor_tensor(out=T4, in0=T4, in1=T3, op=ALU.mult), reads=k4 + k3, writes=k4)
            S.op("dve", lambda e: e.tensor_tensor(out=bt, in0=T4, in1=T1, op=ALU.mult), reads=k4 + k1, writes=btk)
            S.op("pool", lambda e: e.tensor_tensor(out=c4(TB), in0=c4(bt), in1=wcb, op=ALU.mult), reads=btk + ["WC"], writes=TKB.tok())
            b4 = bank()
            for c in range(NS):
                mm(ps[b4][:, c * 2:(c + 1) * 2], TR[:, c * 128:(c + 1) * 128], sel2[:], TRQ.tok() + ["sel2"], b4)
            S.op("act", lambda e: e.activation(out=dtok[:, :, 2 * p:2 * p + 2], in_=ps[b4][:, 0:8].rearrange("p (c e) -> p c e", e=2), func=AF.Copy),
                 reads=[("ps", b4)], writes=["dtok"])
            for src, d4, dlb in ((TK, Kh4, Kh), (TB, Bh4, Bh)):
                b5 = bank()
                psb = ps[b5][:].bitcast(BF16)[:, 0:512].rearrange("p (c f) -> p c f", c=NS)
                for c in range(NS):
                    S.op("pe", lambda e, c=c, psb=psb, src=src: e.transpose(psb[:, c, :], src[:, c * 128:(c + 1) * 128], ident[:]), reads=TKB.tok() + ["ident"], writes=[("ps", b5)])
                S.op("act", lambda e, psb=psb, d4=d4: e.activation(out=d4[:, :, pc], in_=psb, func=AF.Copy), reads=[("ps", b5)], writes=dlb.tok())

        bank_list[0] = [4, 5, 6, 7]
        ctx = {0: prep1(0)}
        for p_ in range(8):
            if p_ + 1 < 8:
                ctx[p_ + 1] = prep1(p_ + 1)
            prep2(p_, *ctx[p_])
        bank_list[0] = list(range(8))
        Xb = LB(40, 1, BF16); Ub = LB(41, 1, BF16); yq = LB(42, 2, F32); ysq = LB(44, 2, F32); ygb = LB(44, 1, BF16)
        PAD = LB(40, 4, BF16); BTB = LB(44, 2, BF16); PADALL = LB(40, 6, BF16)
        pad5 = PAD.flat.rearrange("p (q j t) -> p q j t", q=8, j=4)
        btb4 = BTB.flat.rearrange("p (q j t) -> p q j t", q=8, j=2)
        AT3 = v3(AT); BT3 = v3(BT); KT3 = v3(KT); RT3 = v3(RT)
        h8 = lambda ap: ap.rearrange("p (h v) -> p h v", v=64)
        mb2 = lambda m: m[:].unsqueeze(1).to_broadcast([128, 2, 128])
        mb4 = lambda m: m[:].unsqueeze(1).to_broadcast([128, 4, 128])
        def scan_chunk(c):
            cc = slice(c * 128, (c + 1) * 128)
            S.op("dve", lambda e: e.memset(PADALL.flat, 0.0), writes=PADALL.tok())
            for e_ in range(2):
                rows = slice(64 * e_, 64 * e_ + 64)
                S.op("dve", lambda e, rows=rows, e_=e_: e.tensor_copy(out=pad5[rows, :, e_, :], in_=AT3[rows, :, cc]), reads=AT.tok() + PAD.tok(), writes=PAD.tok())
                S.op("act", lambda e, rows=rows, e_=e_: e.activation(out=pad5[rows, :, 2 + e_, :], in_=RT3[rows, :, cc], func=AF.Copy), reads=RT.tok() + PAD.tok(), writes=PAD.tok())
                S.op("dve", lambda e, rows=rows, e_=e_: e.tensor_copy(out=btb4[rows, :, e_, :], in_=BT3[rows, :, cc]), reads=BT.tok() + BTB.tok(), writes=BTB.tok())
            Q = []
            for q in range(4):
                QL = LB(28 + 3 * q, 3, BF16)
                hv = lambda i, QL=QL: QL.flat[:, i * 512:(i + 1) * 512].rearrange("p (h t) -> p h t", h=4)
                Aak, Ark, Arb, P = hv(0), hv(1), hv(2), hv(3)
                PTST = QL.flat[:, 2048:3072].rearrange("p (h x) -> p h x", h=4)
                PT = PTST[:, :, 0:128]; ST = PTST[:, :, 128:256]
                tA, tB, tC = QL.tok(0), QL.tok(1), QL.tok(2)
                Q.append((Aak, Ark, Arb, P, PTST, PT, ST, tA, tB, tC))
                b3 = bank()
                for pp in range(2):
                    p = 2 * q + pp
                    hs2 = slice(2 * pp, 2 * pp + 2)
                    b1 = bank(); b2 = bank()
                    padp = PAD.flat[:, p * 512:(p + 1) * 512]
                    mm(ps[b1][:], BT3[:, p, cc], padp, BT.tok(p // 2) + PAD.tok(), b1)
                    mm(ps[b2][:], KT3[:, p, cc], padp, KT.tok(p // 2) + PAD.tok(), b2)
                    mm(ps[b3][:, pp * 256:(pp + 1) * 256], AT3[:, p, cc], BTB.flat[:, p * 256:(p + 1) * 256], AT.tok(p // 2) + BTB.tok(), b3)
                    v2 = lambda b, half: ps[b][:, half * 256:(half + 1) * 256].rearrange("p (h t) -> p h t", h=2)
                    for (dst, b, half, m, tk) in ((PT, b1, 0, mUs, tC), (Arb, b1, 1, mUi, tB), (Aak, b2, 0, mUs, tA), (Ark, b2, 1, mUi, tA)):
                        S.op("dve", lambda e, dst=dst, b=b, half=half, m=m, hs2=hs2: e.tensor_tensor(out=dst[:, hs2, :], in0=v2(b, half), in1=mb2(m), op=ALU.mult),
                             reads=[("ps", b), "mask%d" % id(m)], writes=tk)
                S.op("dve", lambda e, P=P, b3=b3: e.tensor_tensor(out=P, in0=ps[b3][:].rearrange("p (h t) -> p h t", h=4), in1=mb4(mLs), op=ALU.mult),
                     reads=[("ps", b3), "mask%d" % id(mLs)], writes=tB)
                S.op("pool", lambda e, ST=ST, PT=PT: e.tensor_tensor(out=ST, in0=PT, in1=mb4(ident), op=ALU.add), reads=tC + ["ident"], writes=tC)
            for i in range(7):
                for q in range(4):
                    Aak, Ark, Arb, P, PTST, PT, ST, tA, tB, tC = Q[q]
                    p4 = lambda b: ps[b][:].rearrange("p (h t) -> p h t", h=4)
                    if i < 6:
                        bP = bank()
                        for hq in range(4):
                            mm(ps[bP][:, hq * 128:(hq + 1) * 128], PT[:, hq, :], P[:, hq, :], tB + tC, bP)
                    if i == 0:
                        bT = bank()
                        for hq in range(4):
                            mm(ps[bT][:, hq * 128:(hq + 1) * 128], P[:, hq, :], PT[:, hq, :], tB + tC, bT)
                    elif i < 6:
                        bT2 = [bank(), bank()]
                        for hq in range(4):
                            o = ps[bT2[hq // 2]]
                            c0 = (hq % 2) * 256
                            mm(o[:, c0:c0 + 128], P[:, hq, :], PT[:, hq, :], tB + tC, bT2[hq // 2])
                            mm(o[:, c0 + 128:c0 + 256], P[:, hq, :], ST[:, hq, :], tB + tC, bT2[hq // 2], start=True, stop=False)
                            mm(o[:, c0 + 128:c0 + 256], ident[:, :], ST[:, hq, :], tC + ["ident"], bT2[hq // 2], start=False, stop=True)
                    else:
                        bS = bank()
                        for hq in range(4):
                            mm(ps[bS][:, hq * 128:(hq + 1) * 128], P[:, hq, :], ST[:, hq, :], tB + tC, bS, start=True, stop=False)
                            mm(ps[bS][:, hq * 128:(hq + 1) * 128], ident[:, :], ST[:, hq, :], tC + ["ident"], bS, start=False, stop=True)
                    if i < 6:
                        S.op("act", lambda e, P=P, bP=bP: e.activation(out=P, in_=p4(bP), func=AF.Copy), reads=[("ps", bP)], writes=tB)
                    if i == 0:
                        S.op("dve", lambda e, PT=PT, bT=bT: e.tensor_copy(out=PT, in_=p4(bT)), reads=[("ps", bT)], writes=tC)
                    elif i < 6:
                        for k in range(2):
                            pv = ps[bT2[k]][:].rearrange("p (h x) -> p h x", h=2)
                            if k == 0:
                                S.op("dve", lambda e, PTST=PTST, pv=pv, k=k: e.tensor_copy(out=PTST[:, 2 * k:2 * k + 2, :], in_=pv), reads=[("ps", bT2[k])], writes=tC)
                            else:
                                S.op("act", lambda e, PTST=PTST, pv=pv, k=k: e.activation(out=PTST[:, 2 * k:2 * k + 2, :], in_=pv, func=AF.Copy), reads=[("ps", bT2[k])], writes=tC)
                    else:
                        S.op("dve", lambda e, ST=ST, bS=bS: e.tensor_copy(out=ST, in_=p4(bS)), reads=[("ps", bS)], writes=tC)
            bX = [bank(), bank()]
            for p in range(8):
                bx = bX[p // 4]
                for e_ in range(2):
                    h = 2 * p + e_
                    q, hq = divmod(h, 4)
                    Aak, Ark, Arb, P, PTST, PT, ST, tA, tB, tC = Q[q]
                    o = ps[bx][:, (h % 8) * 64:(h % 8 + 1) * 64]
                    mm(o, AT3[:, p, cc], Hbd[:, p, e_ * 64:(e_ + 1) * 64], AT.tok(p // 2) + ["Hbd"], bx, start=True, stop=False)
                    mm(o, Aak[:, hq, :], Vt4[:, c, h * 64:(h + 1) * 64], tA + Vt.tok(c), bx, start=False, stop=True)
            for k in range(2):
                S.op("act", lambda e, k=k: e.activation(out=Xb.flat[:, k * 512:(k + 1) * 512], in_=ps[bX[k]][:], func=AF.Copy), reads=[("ps", bX[k])], writes=Xb.tok())
            bU = [bank(), bank()]
            for h in range(16):
                q, hq = divmod(h, 4)
                Aak, Ark, Arb, P, PTST, PT, ST, tA, tB, tC = Q[q]
                mm(ps[bU[h // 8]][:, (h % 8) * 64:(h % 8 + 1) * 64], ST[:, hq, :], Xb.flat[:, h * 64:(h + 1) * 64], tC + Xb.tok(), bU[h // 8])
            for k in range(2):
                S.op("act", lambda e, k=k: e.activation(out=Ub.flat[:, k * 512:(k + 1) * 512], in_=ps[bU[k]][:], func=AF.Copy), reads=[("ps", bU[k])], writes=Ub.tok())
            bY = [bank(), bank()]
            for p in range(8):
                by = bY[p // 4]
                for e_ in range(2):
                    h = 2 * p + e_
                    q, hq = divmod(h, 4)
                    Aak, Ark, Arb, P, PTST, PT, ST, tA, tB, tC = Q[q]
                    o = ps[by][:, (h % 8) * 64:(h % 8 + 1) * 64]
                    mm(o, RT3[:, p, cc], Hbd[:, p, e_ * 64:(e_ + 1) * 64], RT.tok(p // 2) + ["Hbd"], by, start=True, stop=False)
                    mm(o, Ark[:, hq, :], Vt4[:, c, h * 64:(h + 1) * 64], tA + Vt.tok(c), by, start=False, stop=False)
                    mm(o, Arb[:, hq, :], Ub.flat[:, h * 64:(h + 1) * 64], tB + Ub.tok(), by, start=False, stop=True)
            bH = [bank(), bank()]
            for p in range(8):
                o = ps[bH[p // 4]][:, (p % 4) * 128:(p % 4 + 1) * 128]
                mm(o, Kh4[:, c, p * 128:(p + 1) * 128], Vt4[:, c, p * 128:(p + 1) * 128], Kh.tok(c) + Vt.tok(c), bH[p // 4], start=True, stop=False)
                mm(o, Bh4[:, c, p * 128:(p + 1) * 128], Ub.flat[:, p * 128:(p + 1) * 128], Bh.tok(c) + Ub.tok(), bH[p // 4], start=False, stop=True)
            for e_ in range(2):
                rows = slice(64 * e_, 64 * e_ + 64)
                S.op("pool", lambda e, rows=rows: e.tensor_tensor(out=Hst[rows, :, :], in0=Hst[rows, :, :], in1=WC[rows, :, c:c + 1].to_broadcast([64, 8, 64]), op=ALU.mult),
                     reads=["Hst", "WC"], writes=["Hst"])
                for k in range(2):
                    pv = ps[bH[k]][rows, :].rearrange("p (q f) -> p q f", q=4)[:, :, e_ * 64:(e_ + 1) * 64]
                    S.op("dve", lambda e, rows=rows, pv=pv, k=k: e.tensor_tensor(out=Hst[rows, 4 * k:4 * k + 4, :], in0=pv, in1=Hst[rows, 4 * k:4 * k + 4, :], op=ALU.add),
                         reads=[("ps", bH[k]), "Hst"], writes=["Hst"])
            for e_ in range(2):
                rows = slice(64 * e_, 64 * e_ + 64)
                S.op("act", lambda e, rows=rows, e_=e_: e.activation(out=Hbd[rows, :, e_ * 64:(e_ + 1) * 64], in_=Hst[rows, :, :], func=AF.Copy), reads=["Hst", "Hbd"], writes=["Hbd"])
            yq3 = yq.flat.rearrange("p (h v) -> p h v", v=64); ysq3 = ysq.flat.rearrange("p (h v) -> p h v", v=64)
            for k in range(2):
                S.op("dve", lambda e, k=k: e.tensor_reduce(out=st_a[:, 8 * k:8 * k + 8], in_=h8(ps[bY[k]][:]), axis=AX.X, op=ALU.add), reads=[("ps", bY[k])], writes=["st_a"])
                S.op("act", lambda e, k=k: e.activation(out=ysq.flat[:, k * 512:(k + 1) * 512], in_=ps[bY[k]][:], func=AF.Square), reads=[("ps", bY[k])], writes=ysq.tok(k))
                S.op("dve", lambda e, k=k: e.tensor_reduce(out=st_b[:, 8 * k:8 * k + 8], in_=ysq3[:, 8 * k:8 * k + 8, :], axis=AX.X, op=ALU.add), reads=ysq.tok(k), writes=["st_b"])
            S.op("dve", lambda e: e.tensor_scalar(out=st_a[:], in0=st_a[:], scalar1=1.0 / 64, scalar2=None, op0=ALU.mult), reads=["st_a"], writes=["st_a"])
            S.op("dve", lambda e: e.tensor_tensor(out=st_c[:], in0=st_a[:], in1=st_a[:], op=ALU.mult), reads=["st_a"], writes=["st_c"])
            S.op("dve", lambda e: e.scalar_tensor_tensor(out=st_b[:], in0=st_b[:], scalar=1.0 / 64, in1=st_c[:], op0=ALU.mult, op1=ALU.subtract), reads=["st_b", "st_c"], writes=["st_b"])
            S.op("act", lambda e: e.activation(out=st_b[:], in_=st_b[:], func=AF.Sqrt, bias=GN_EPS), reads=["st_b"], writes=["st_b"])
            S.op("dve", lambda e: e.reciprocal(out=st_b[:], in_=st_b[:]), reads=["st_b"], writes=["st_b"])
            for k in range(2):
                hs_ = slice(8 * k, 8 * k + 8)
                S.op("dve", lambda e, k=k, hs_=hs_: e.tensor_tensor(out=yq3[:, hs_, :], in0=h8(ps[bY[k]][:]), in1=st_a[:, hs_].unsqueeze(2).to_broadcast([128, 8, 64]), op=ALU.subtract),
                     reads=[("ps", bY[k]), "st_a"], writes=yq.tok(k))
                S.op("dve", lambda e, hs_=hs_: e.tensor_tensor(out=yq3[:, hs_, :], in0=yq3[:, hs_, :], in1=st_b[:, hs_].unsqueeze(2).to_broadcast([128, 8, 64]), op=ALU.mult),
                     reads=yq.tok(k) + ["st_b"], writes=yq.tok(k))
            S.op("pool", lambda e: e.tensor_tensor(out=ysq3, in0=h8(Vt4[:, c, :]), in1=dtok[:, c, :].unsqueeze(2).to_broadcast([128, 16, 64]), op=ALU.mult),
                 reads=Vt.tok(c) + ["dtok"], writes=ysq.tok())
            S.op("pool", lambda e: e.tensor_tensor(out=ysq.flat, in0=ysq.flat, in1=lnb_b[:], op=ALU.add), reads=ysq.tok() + ["lnb_b"], writes=ysq.tok())
            S.op("pool", lambda e: e.tensor_tensor(out=yq.flat, in0=yq.flat, in1=lnw_b[:], op=ALU.mult), reads=yq.tok() + ["lnw_b"], writes=yq.tok())
            S.op("dve", lambda e: e.tensor_tensor(out=yq.flat, in0=yq.flat, in1=ysq.flat, op=ALU.add), reads=yq.tok() + ysq.tok(), writes=yq.tok())
            bG = [bank(), bank()]
            for nh in range(2):
                mm(ps[bG[nh]][:], sgT[:, 0, cc], g2b[:, 0, nh * 512:(nh + 1) * 512], ["sgT", "g2b"], bG[nh], start=True, stop=False)
                mm(ps[bG[nh]][:], sgT[0:32, 1, cc], g2b[0:32, 1, nh * 512:(nh + 1) * 512], ["sgT", "g2b"], bG[nh], start=False, stop=True)
                S.op("dve", lambda e, nh=nh: e.tensor_tensor(out=ygb.flat[:, nh * 512:(nh + 1) * 512], in0=ps[bG[nh]][:], in1=yq.flat[:, nh * 512:(nh + 1) * 512], op=ALU.mult),
                     reads=[("ps", bG[nh])] + yq.tok(nh) + ysq.tok(), writes=ygb.tok())
            b = bank()
            psb = ps[b][:].bitcast(BF16).rearrange("p (k t) -> p k t", k=KC)
            for kc in range(KC):
                S.op("pe", lambda e, kc=kc, psb=psb: e.transpose(psb[:, kc, :], ygb.flat[:, kc * 128:(kc + 1) * 128], ident[:]), reads=ygb.tok() + ["ident"], writes=[("ps", b)])
            S.op("act", lambda e, psb=psb: e.activation(out=hT[:, :, cc], in_=psb, func=AF.Copy), reads=[("ps", b)], writes=["hT"])
        for c_ in range(NS):
            scan_chunk(c_)
        out_proj(xr_, xtok, "rwkv_w_out", 8, lambda kc, s: hT[:, kc, s * 128:(s + 1) * 128], lambda kc: ["hT"])

    def stage_E(xr_, xtok, seq, t0, ost):
        for s in range(NS):
            S.op("act", lambda e, s=s: e.activation(out=sqj[:], in_=xr_[:, s, :], func=AF.Square, accum_out=ss[:, s:s + 1]),
                 reads=[(xtok, s)], writes=[("xn", 0), "ss"])
        S.op("dve", lambda e: e.tensor_scalar(out=rstd[:], in0=ss[:], scalar1=1.0 / D, scalar2=RMS_EPS, op0=ALU.mult, op1=ALU.add), reads=["ss"], writes=["rstd"])
        S.op("act", lambda e: e.activation(out=rstd[:], in_=rstd[:], func=AF.Sqrt), reads=["rstd"], writes=["rstd"])
        S.op("dve", lambda e: e.reciprocal(out=rstd[:], in_=rstd[:]), reads=["rstd"], writes=["rstd"])
        for s in range(NS):
            S.op("dve", lambda e, s=s: e.scalar_tensor_tensor(out=xr_[:, s, :], in0=xr_[:, s, :], scalar=rstd[:, s:s + 1], in1=gfin_b[:], op0=ALU.mult, op1=ALU.mult),
                 reads=[(xtok, s), "rstd", "gfin_b"], writes=[(xtok, s)])
        dst = out_d[seq, t0:t0 + TT, :].rearrange("(s p) d -> p s d", p=128)
        return S.op("sp", lambda e: e.dma_start(out=dst, in_=xr_[:]), reads=[(xtok, s) for s in range(NS)], writes=[], dsem=ost)

    xsem = [S.new_dsem("x") for _ in range(2)]
    osem = [S.new_dsem("o") for _ in range(2)]
    tiles = [(q, t) for q in range(n_seq) for t in range(n_tiles)]

    def xload(i):
        q, t = tiles[i]
        buf = xres[0]
        src = x_d[q, t * TT:(t + 1) * TT, :].rearrange("(s p) d -> p s d", p=128)
        S.op("sp", lambda e: e.dma_start(out=buf[:], in_=src), reads=[], writes=[("x0", s) for s in range(NS)], dsem=xsem[0])

    last_out = []
    for i, (q, t) in enumerate(tiles):
        xload(i)
        xr_ = xres[0]
        xtok = "x0"
        first = (t == 0)
        if "A" in stages:
            stage_A(xr_, xtok, first)
        if "B" in stages:
            stage_F(xr_, xtok, 0, first)
        if "C" in stages:
            stage_C(xr_, xtok, first)
        if "D" in stages:
            stage_F(xr_, xtok, 1, first)
        c = stage_E(xr_, xtok, q, t * TT, osem[0])
        last_out.append(c)
    for c in last_out[-2:]:
        S.final_wait("sp", c)

    with nc.Block() as block:
        S.emit(block)
    es.close()
    return nc, S


def kernel(**inputs):
    inp = {k: np.asarray(v) for k, v in inputs.items()}
    x = np.ascontiguousarray(inp["x"], dtype=np.float32)
    B, T, _ = x.shape
    n_seq = B // N_CORES
    nc, _ = build(n_seq=n_seq, T=T)
    shared = host_shared(inp)
    in_maps = []
    for c in range(N_CORES):
        m = dict(shared)
        m["x"] = np.ascontiguousarray(x[c * n_seq:(c + 1) * n_seq])
        in_maps.append(m)
    res = run_bass_kernel_spmd(nc, in_maps, core_ids=list(range(N_CORES)))
    return np.concatenate([r["out"] for r in res.results], axis=0)


def host_shared(inp):
    f = lambda a: np.ascontiguousarray(np.asarray(a, dtype=np.float32))
    return {
        "vecs": make_vecs(inp),
        "lru_w_in": f(inp["lru_w_in"][0]), "lru_w_out": f(inp["lru_w_out"][0]),
        "ffn_w_up0": f(inp["ffn_w_up"][0]), "ffn_w_up1": f(inp["ffn_w_up"][1]),
        "ffn_w_dn0": f(inp["ffn_w_down"][0]), "ffn_w_dn1": f(inp["ffn_w_down"][1]),
        "rwkv_w_r": f(inp["rwkv_w_rkv"][0, 0]), "rwkv_w_k": f(inp["rwkv_w_rkv"][0, 1]), "rwkv_w_v": f(inp["rwkv_w_rkv"][0, 2]),
        "rwkv_w_out": f(inp["rwkv_w_out"][0]),
        "lru_gate_w": f(inp["lru_gate_w"][0]), "lru_b_out": f(inp["lru_b_out"]),
        "rwkv_w1": f(inp["rwkv_w1"][0]), "rwkv_a1": f(inp["rwkv_a1"][0]), "rwkv_g1": f(inp["rwkv_g1"][0]),
        "rwkv_w2": f(inp["rwkv_w2"][0]), "rwkv_a2": f(inp["rwkv_a2"][0]), "rwkv_g2": f(inp["rwkv_g2"][0]),
        "rwkv_ln_w": f(inp["rwkv_ln_w"][0]), "rwkv_ln_b": f(inp["rwkv_ln_b"][0]), "final_norm": f(inp["final_norm"]),
    }
```

```python
import numpy as np
from contextlib import ExitStack
import concourse.bass as bass
import concourse.mybir as mybir
from concourse.bass_utils import run_bass_kernel_spmd

F32 = mybir.dt.float32
BF16 = mybir.dt.bfloat16
AF = mybir.ActivationFunctionType
ALU = mybir.AluOpType
AX = mybir.AxisListType

D = 1024
KC = 8
TT = 512
NS = 4
DFF = 3072
NJ = 24
NW = 5
CH = 128
RMS_EPS = 1e-6
GN_EPS = 64e-5
N_CORES = 8
SAME_ENGINE_SYNC = True

VEC_SPECS = [
    ("lru_norm", 8), ("lru_b_in", 16), ("lru_conv_w", 32), ("lru_conv_b", 8), ("lru_gate_b", 16), ("lru_lambda", 8),
    ("rwkv_norm", 8), ("rwkv_mix", 48), ("rwkv_w0", 8), ("rwkv_a0", 8), ("rwkv_k_k", 8), ("rwkv_k_a", 8), ("rwkv_r_k", 8),
    ("ffn_norm", 16), ("ffn_conv_w", 144), ("ffn_conv_b", 48),
]
VOFF = {}
_o = 0
for _n, _w in VEC_SPECS:
    VOFF[_n] = _o
    _o += _w
NV = _o


def _fm(v):
    v = np.asarray(v, dtype=np.float32)
    return np.ascontiguousarray(v.reshape(-1, 128).T)


def make_vecs(inp):
    cols = [_fm(inp[n]) for n, _ in VEC_SPECS]
    out = np.concatenate(cols, axis=1)
    assert out.shape == (128, NV), out.shape
    return np.ascontiguousarray(out)


class DmaSem:
    def __init__(self, sem):
        self.sem = sem
        self.val = 0


class Sched:
    ENG = ("pe", "act", "dve", "pool", "sp")
    EPOCH = 30000

    def __init__(self, nc, es):
        self.nc = nc
        self.es = es
        self.nsem = 0
        self.prog = {e: [] for e in self.ENG}
        self.esem = {}
        self.cnt = {}
        self.own = {e: set() for e in self.ENG}
        for e in self.ENG:
            self._new_epoch(e)
        self.waited = {e: {} for e in self.ENG}
        self.lastw = {}
        self.readers = {}
        self.nops = 0

    def new_sem(self, name):
        self.nsem += 1
        return self.es.enter_context(self.nc.semaphore(f"{name}{self.nsem}"))

    def new_dsem(self, name="d"):
        return DmaSem(self.new_sem(name))

    def _new_epoch(self, e):
        s = self.new_sem("e" + e)
        self.esem[e] = s
        self.cnt[e] = 0
        self.own[e].add(id(s))

    def op(self, eng, fn, reads=(), writes=(), dsem=None):
        deps = {}

        def add(c):
            if c is None:
                return
            k = id(c[0])
            if k not in deps or deps[k][1] < c[1]:
                deps[k] = c

        for t in reads:
            add(self.lastw.get(t))
        for t in writes:
            add(self.lastw.get(t))
            for c in self.readers.get(t, {}).values():
                add(c)
        for k, (s, v) in deps.items():
            if k in self.own[eng] and dsem is None and (eng == "pe" or not SAME_ENGINE_SYNC):
                continue
            if self.waited[eng].get(k, 0) < v:
                self.prog[eng].append(("w", s, v))
                self.waited[eng][k] = v
        if dsem is None:
            if self.cnt[eng] >= self.EPOCH:
                self._new_epoch(eng)
            self.cnt[eng] += 1
            comp = (self.esem[eng], self.cnt[eng])
            inc = 1
        else:
            dsem.val += 16
            comp = (dsem.sem, dsem.val)
            inc = 16
        self.prog[eng].append(("o", fn, comp[0], inc))
        for t in reads:
            self.readers.setdefault(t, {})[id(comp[0])] = comp
        for t in writes:
            self.lastw[t] = comp
            self.readers[t] = {}
        self.nops += 1
        return comp

    def drain(self, eng):
        if self.cnt[eng] > 0:
            self.prog[eng].append(("w", self.esem[eng], self.cnt[eng]))

    def final_wait(self, eng, comp):
        self.prog[eng].append(("w", comp[0], comp[1]))

    def emit(self, block):
        def mk(e):
            def f(h):
                for it in self.prog[e]:
                    if it[0] == "w":
                        h.wait_ge(it[1], it[2])
                    else:
                        it[1](h).then_inc(it[2], it[3])
            return f

        block.tensor(mk("pe"))
        block.scalar(mk("act"))
        block.vector(mk("dve"))
        block.gpsimd(mk("pool"))
        block.sync(mk("sp"))


def build(n_seq=4, T=2048, stages="ABCDE"):
    assert T % TT == 0
    n_tiles = T // TT
    nc = bass.Bass("TRN2", target_bir_lowering=False)
    es = ExitStack()
    S = Sched(nc, es)

    def dram(name, shape, dt=F32, kind="ExternalInput"):
        return nc.dram_tensor(name, list(shape), dt, kind=kind).ap()

    def sb(name, shape, dt=F32):
        return es.enter_context(nc.sbuf_tensor("s_" + name, list(shape), dt))

    x_d = dram("x", [n_seq, T, D])
    out_d = dram("out", [n_seq, T, D], kind="ExternalOutput")
    vecs_d = dram("vecs", [128, NV])
    bigw = {
        "lru_w_in": (D, 2048), "lru_w_out": (D, D),
        "ffn_w_up0": (D, 2 * DFF), "ffn_w_up1": (D, 2 * DFF), "ffn_w_dn0": (DFF, D), "ffn_w_dn1": (DFF, D),
        "rwkv_w_r": (D, D), "rwkv_w_k": (D, D), "rwkv_w_v": (D, D), "rwkv_w_out": (D, D),
    }
    w_d = {n: dram(n, s) for n, s in bigw.items()}
    wb_d = {n: dram("b_" + n, s, BF16, kind="Internal") for n, s in bigw.items()}
    gate_w_d = dram("lru_gate_w", [2, 16, 64, 64])
    b_out_d = dram("lru_b_out", [1, D])
    w1_d = dram("rwkv_w1", [D, 64]); a1_d = dram("rwkv_a1", [D, 64]); g1_d = dram("rwkv_g1", [D, 160])
    w2_d = dram("rwkv_w2", [64, D]); a2_d = dram("rwkv_a2", [64, D]); g2_d = dram("rwkv_g2", [160, D])
    lnw_d = dram("rwkv_ln_w", [D]); lnb_d = dram("rwkv_ln_b", [D]); fin_d = dram("final_norm", [D])

    vecs = sb("vecs", [128, NV])
    ident = sb("ident", [128, 128], BF16)
    ones_row = sb("ones_row", [1, 128], BF16)
    bout_row = sb("bout_row", [1, D], BF16)
    gateW = sb("gateW", [128, 16, 128], BF16)
    cneg = sb("cneg", [128, 8])
    gfin_b = sb("gfin_b", [128, D])
    HAS_C = "C" in stages
    if HAS_C:
        lnw_b = sb("lnw_b", [128, D]); lnb_b = sb("lnb_b", [128, D])
        w1b = sb("w1b", [128, KC, 64], BF16); a1b = sb("a1b", [128, KC, 64], BF16); g1b = sb("g1b", [128, KC, 160], BF16)
        w2b = sb("w2b", [64, D], BF16); a2b = sb("a2b", [64, D], BF16); g2b = sb("g2b", [128, 2, D], BF16)
        mUs = sb("mUs", [128, 128], BF16); mUi = sb("mUi", [128, 128], BF16); mLs = sb("mLs", [128, 128], BF16)
        sel2 = sb("sel2", [128, 2], BF16); blkones = sb("blkones", [128, 128], BF16)
        rmask = sb("rmask", [128, TT]); omka = sb("omka", [128, 8])
        tanhw = sb("tanhw", [64, TT], BF16); a1o = sb("a1o", [64, TT], BF16); sgT = sb("sgT", [128, 2, TT], BF16)
        WC = sb("WC", [128, 8, 4]); dtok = sb("dtok", [128, 4, 16]); hcar = sb("hcar", [128, 8], BF16)
        Hst = sb("Hst", [128, 8, 64]); Hbd = sb("Hbd", [128, 8, 128], BF16)
        st_a = sb("st_a", [128, 16]); st_b = sb("st_b", [128, 16]); st_c = sb("st_c", [128, 16])
    xres = [sb("xres0", [128, NS, D])]
    hT = sb("hT", [128, KC, TT], BF16)
    xn = [sb(f"xn{i}", [128, D], BF16) for i in range(2)]
    sqj = xn[0]
    ss = sb("ss", [128, NS]); rstd = sb("rstd", [128, NS])
    wring = [sb(f"wring{i}", [128, 2048], BF16) for i in range(NW)]
    wsem = [S.new_dsem("w") for _ in range(NW)]
    hstate = sb("hstate", [128, 8])
    ucar = sb("ucar", [128, 8, 3])
    fcar = [sb(f"fcar{l}", [128, NJ, 2]) for l in range(2)]
    NBLK = 54
    arena = sb("arena", [128, NBLK * 1024], BF16)
    ps = [es.enter_context(nc.psum_tensor(f"ps{i}", [128, 512], F32)) for i in range(8)]

    def V(name, c, n=1):
        o = VOFF[name] + c
        return vecs[:, o:o + n]

    class LB:
        def __init__(self, b0, nb, dt, inner=None):
            self.b0, self.nb, self.dt = b0, nb, dt
            a = arena[:, b0 * 1024:(b0 + nb) * 1024]
            if dt == F32:
                a = a.bitcast(F32)
            self.flat = a
            self.per = 1024 if dt == BF16 else 512
        def blk(self, i, n=1):
            return self.flat[:, i * self.per:(i + n) * self.per]
        def tok(self, i=None, n=1):
            if i is None:
                return [("ar", self.b0 + k) for k in range(self.nb)]
            return [("ar", self.b0 + i + k) for k in range(n)]

    bank_ctr = [0]

    bank_list = [list(range(8))]

    def bank():
        bl = bank_list[0]
        b = bl[bank_ctr[0] % len(bl)]
        bank_ctr[0] += 1
        return b

    wi = [0]

    def wload(name, src_ap, view):
        i = wi[0] % NW
        wi[0] += 1
        dst = view(wring[i])
        S.op("sp", lambda e, dst=dst, src_ap=src_ap: e.dma_start(out=dst, in_=src_ap),
             reads=[("wd", name)], writes=[("w", i)], dsem=wsem[i])
        return dst, ("w", i)

    cs_by_eng = {"sp": S.new_dsem("c"), "pool": S.new_dsem("cp")}
    const_toks = []

    def cload(eng, out_ap, in_ap, tok):
        S.op(eng, lambda e: e.dma_start(out=out_ap, in_=in_ap), reads=[], writes=[tok], dsem=cs_by_eng[eng])
        const_toks.append((tok, eng))

    cload("sp", vecs[:], vecs_d[:, :], "vecs")
    cload("sp", gfin_b[:], fin_d.partition_broadcast(128), "gfin_b")
    cload("pool", bout_row[:], b_out_d[:, :], "bout_row")
    S.op("pool", lambda e: e.memset(gateW[:], 0.0), writes=["gateW"])
    for g in range(2):
        for par in range(2):
            src = gate_w_d[g, par::2].rearrange("n c d -> c n d")
            dst = gateW[par * 64:(par + 1) * 64, g * 8:(g + 1) * 8, par * 64:(par + 1) * 64]
            cload("pool", dst, src, "gateW")
    if HAS_C:
        cload("sp", lnw_b[:], lnw_d.partition_broadcast(128), "lnw_b")
        cload("sp", lnb_b[:], lnb_d.partition_broadcast(128), "lnb_b")
        cload("pool", w1b[:], w1_d.rearrange("(k p) r -> p k r", p=128), "w1b")
        cload("pool", a1b[:], a1_d.rearrange("(k p) r -> p k r", p=128), "a1b")
        cload("pool", g1b[:], g1_d.rearrange("(k p) r -> p k r", p=128), "g1b")
        cload("pool", w2b[:], w2_d[:, :], "w2b")
        cload("pool", a2b[:], a2_d[:, :], "a2b")
        cload("pool", g2b[:, 0, :], g2_d[0:128, :], "g2b")
        cload("pool", g2b[0:32, 1, :], g2_d[128:160, :], "g2b")
    for t, eng in set(const_toks):
        S.lastw[t] = (cs_by_eng[eng].sem, cs_by_eng[eng].val)
    if HAS_C:
        for m, cmp_, cm, pat in ((mUs, ALU.is_gt, -1, 1), (mUi, ALU.is_ge, -1, 1), (mLs, ALU.is_gt, 1, -1)):
            tk = "mask%d" % id(m)
            S.op("pool", lambda e, m=m: e.memset(m[:], 1.0), writes=[tk])
            S.op("pool", lambda e, m=m, cmp_=cmp_, cm=cm, pat=pat: e.affine_select(out=m[:], in_=m[:], pattern=[[pat, 128]], compare_op=cmp_,
                                                                               fill=0.0, base=0, channel_multiplier=cm), reads=[tk], writes=[tk])
        S.op("pool", lambda e: e.memset(sel2[:], 0.0), writes=["sel2"])
        S.op("pool", lambda e: e.memset(sel2[0:64, 0:1], 1.0), reads=["sel2"], writes=["sel2"])
        S.op("pool", lambda e: e.memset(sel2[64:128, 1:2], 1.0), reads=["sel2"], writes=["sel2"])
        S.op("pool", lambda e: e.memset(blkones[:], 0.0), writes=["blkones"])
        S.op("pool", lambda e: e.memset(blkones[0:64, 0:64], 1.0), reads=["blkones"], writes=["blkones"])
        S.op("pool", lambda e: e.memset(blkones[64:128, 64:128], 1.0), reads=["blkones"], writes=["blkones"])
        S.op("pool", lambda e: e.memset(rmask[:], 1.0), writes=["rmask"])
        S.op("pool", lambda e: e.memset(rmask[:].rearrange("p (c t) -> p c t", t=CH)[:, :, 0:1], 0.0), reads=["rmask"], writes=["rmask"])
        S.op("dve", lambda e: e.tensor_scalar(out=omka[:], in0=V("rwkv_k_a", 0, 8), scalar1=-1.0, scalar2=1.0, op0=ALU.mult, op1=ALU.add),
             reads=["vecs"], writes=["omka"])
    S.op("pool", lambda e: e.memset(ident[:], 0.0), writes=["ident"])
    S.op("pool", lambda e: e.affine_select(out=ident[:], in_=ident[:], pattern=[[-1, 128]], compare_op=ALU.not_equal,
                                           fill=1.0, base=0, channel_multiplier=1), reads=["ident"], writes=["ident"])
    S.op("pool", lambda e: e.memset(ones_row[:], 1.0), writes=["ones_row"])
    S.op("act", lambda e: e.activation(out=cneg[:], in_=V("lru_lambda", 0, 8), func=AF.Exp, scale=-1.0), reads=["vecs"], writes=["cneg"])
    S.op("act", lambda e: e.activation(out=cneg[:], in_=cneg[:], func=AF.Ln, bias=1.0), reads=["cneg"], writes=["cneg"])
    S.op("act", lambda e: e.mul(out=cneg[:], in_=cneg[:], mul=-8.0), reads=["cneg"], writes=["cneg"])

    cast_groups = {"A": ["lru_w_in", "lru_w_out"], "B": ["ffn_w_up0", "ffn_w_dn0"],
                   "C": ["rwkv_w_k", "rwkv_w_r", "rwkv_w_v", "rwkv_w_out"], "D": ["ffn_w_up1", "ffn_w_dn1"]}
    cast_order = [g for g in "ABCD" if g in stages]
    cast_done = []

    def cast_group(g):
        prev = [("wd", n) for n in cast_done]
        for n in cast_groups[g]:
            rows, cols = bigw[n]
            ds = S.new_dsem("k")
            step = max(32, (1 << 18) // cols)
            for r0 in range(0, rows, step):
                r1 = min(rows, r0 + step)
                S.op("pool", lambda e, n=n, r0=r0, r1=r1: e.dma_start(out=wb_d[n][r0:r1, :], in_=w_d[n][r0:r1, :]),
                     reads=prev, writes=[("wd", n)], dsem=ds)
            S.lastw[("wd", n)] = (ds.sem, ds.val)
        cast_done.extend(cast_groups[g])

    def cast_next():
        if len(cast_done) < sum(len(cast_groups[g]) for g in cast_order):
            k = 0
            for g in cast_order:
                if cast_groups[g][0] not in cast_done:
                    cast_group(g)
                    return

    cast_next()

    def rmsnorm_hT(xr_, xtok, gname, goff):
        for s in range(NS):
            S.op("act", lambda e, s=s: e.activation(out=sqj[:], in_=xr_[:, s, :], func=AF.Square, accum_out=ss[:, s:s + 1]),
                 reads=[(xtok, s)], writes=[("xn", 0), "ss"])
        S.op("dve", lambda e: e.tensor_scalar(out=rstd[:], in0=ss[:], scalar1=1.0 / D, scalar2=RMS_EPS, op0=ALU.mult, op1=ALU.add),
             reads=["ss"], writes=["rstd"])
        S.op("act", lambda e: e.activation(out=rstd[:], in_=rstd[:], func=AF.Sqrt), reads=["rstd"], writes=["rstd"])
        S.op("dve", lambda e: e.reciprocal(out=rstd[:], in_=rstd[:]), reads=["rstd"], writes=["rstd"])
        gb = V(gname, goff, 8).unsqueeze(2).to_broadcast([128, KC, 128])
        for s in range(NS):
            xb = xn[s % 2]
            xbt = ("xn", s % 2)
            S.op("act", lambda e, s=s, xb=xb: e.activation(out=xb[:], in_=xr_[:, s, :], func=AF.Copy, scale=rstd[:, s:s + 1]),
                 reads=[(xtok, s), "rstd"], writes=[xbt])
            b = bank()
            psb = ps[b][:].bitcast(BF16).rearrange("p (k t) -> p k t", k=KC)
            for kc in range(KC):
                S.op("pe", lambda e, kc=kc, xb=xb, psb=psb: e.transpose(psb[:, kc, :], xb[:, kc * 128:(kc + 1) * 128], ident[:]),
                     reads=[xbt, "ident"], writes=[("ps", b)])
            S.op("dve", lambda e, s=s, psb=psb: e.tensor_tensor(out=hT[:, :, s * 128:(s + 1) * 128], in0=psb, in1=gb, op=ALU.mult),
                 reads=[("ps", b), "vecs"], writes=["hT"])

    def out_proj(xr_, xtok, wname, nk, actT, act_toks, bias=False, evac=None):
        kpl = 4 if nk >= 4 else nk
        for nh in range(2):
            banks = [bank() for _ in range(NS)]
            for k0 in range(0, nk, kpl):
                src = wb_d[wname][k0 * 128:(k0 + kpl) * 128, nh * 512:(nh + 1) * 512].rearrange("(k p) e -> p k e", p=128)
                wsl, wtok = wload(wname, src, lambda t: t[:, 0:kpl * 512].rearrange("p (k e) -> p k e", k=kpl))
                for kk in range(kpl):
                    kc = k0 + kk
                    for s in range(NS):
                        last = (kc == nk - 1) and not bias
                        S.op("pe", lambda e, kc=kc, s=s, kk=kk, wsl=wsl, last=last, b=banks[s]:
                             e.matmul(ps[b][:], lhsT=actT(kc, s), rhs=wsl[:, kk, :], start=(kc == 0), stop=last),
                             reads=[wtok] + act_toks(kc), writes=[("ps", banks[s])])
            for s in range(NS):
                if bias:
                    S.op("pe", lambda e, s=s, nh=nh, b=banks[s]: e.matmul(ps[b][:], lhsT=ones_row[0:1, :], rhs=bout_row[0:1, nh * 512:(nh + 1) * 512],
                                                                        start=False, stop=True),
                         reads=["ones_row", "bout_row"], writes=[("ps", banks[s])])
                if evac is not None:
                    evac(s, nh, banks[s])
                    continue
                S.op("dve", lambda e, s=s, nh=nh, b=banks[s]: e.tensor_tensor(out=xr_[:, s, nh * 512:(nh + 1) * 512], in0=ps[b][:],
                                                                            in1=xr_[:, s, nh * 512:(nh + 1) * 512], op=ALU.add),
                     reads=[("ps", banks[s]), (xtok, s)], writes=[(xtok, s)])

    def stage_A(xr_, xtok, first_tile):
        yb = LB(0, 4, BF16); xr = LB(4, 8, F32)
        rg = LB(12, 8, F32); ig = LB(20, 8, F32); t1 = LB(28, 8, F32); xrb = LB(28, 4, BF16)
        UP = LB(36, 9, F32)
        upre = UP.flat[:, 0:8 * (TT + 3)].rearrange("p (c t) -> p c t", c=8)
        uptok = lambda cc: UP.tok((cc * (TT + 3)) // 512, ((cc + 1) * (TT + 3) - 1) // 512 - (cc * (TT + 3)) // 512 + 1)
        if first_tile:
            S.op("pool", lambda e: e.memset(ucar[:], 0.0), writes=["ucar"])
            S.op("pool", lambda e: e.memset(hstate[:], 0.0), writes=["hstate"])
        S.op("pool", lambda e: e.tensor_copy(out=upre[:, :, 0:3], in_=ucar[:]), reads=["ucar"], writes=UP.tok())
        rmsnorm_hT(xr_, xtok, "lru_norm", 0)
        for q in range(8):
            src = wb_d["lru_w_in"][:, q * 256:(q + 1) * 256].rearrange("(k p) e -> p k e", p=128)
            wsl, wtok = wload("lru_w_in", src, lambda t: t[:].rearrange("p (k e) -> p k e", k=KC))
            for ee in range(2):
                c = 2 * q + ee
                b = bank()
                for kc in range(KC):
                    S.op("pe", lambda e, kc=kc, ee=ee, wsl=wsl, b=b: e.matmul(ps[b][:], lhsT=wsl[:, kc, ee * 128:(ee + 1) * 128], rhs=hT[:, kc, :],
                                                                          start=(kc == 0), stop=(kc == KC - 1)),
                         reads=[wtok, "hT"], writes=[("ps", b)])
                if c < 8:
                    S.op("act", lambda e, c=c, b=b: e.activation(out=yb.flat[:, c * 512:(c + 1) * 512], in_=ps[b][:], func=AF.Gelu_apprx_tanh,
                                                               bias=V("lru_b_in", c)),
                         reads=[("ps", b), "vecs"], writes=yb.tok(c // 2))
                else:
                    cc = c - 8
                    S.op("act", lambda e, cc=cc, b=b: e.activation(out=upre[:, cc, 3:3 + TT], in_=ps[b][:], func=AF.Identity, bias=V("lru_b_in", 8 + cc)),
                         reads=[("ps", b), "vecs"], writes=uptok(cc))
        for cc in range(8):
            o = xr.blk(cc)
            rd = uptok(cc) + ["vecs"]
            S.op("act", lambda e, cc=cc, o=o: e.activation(out=o, in_=upre[:, cc, 0:TT], func=AF.Identity, scale=V("lru_conv_w", 0 * 8 + cc), bias=V("lru_conv_b", cc)),
                 reads=rd, writes=xr.tok(cc))
            for j in range(1, 4):
                S.op("dve", lambda e, cc=cc, o=o, j=j: e.scalar_tensor_tensor(out=o, in0=upre[:, cc, j:j + TT], scalar=V("lru_conv_w", j * 8 + cc), in1=o,
                                                                             op0=ALU.mult, op1=ALU.add), reads=rd + xr.tok(cc), writes=xr.tok(cc))
            S.op("act", lambda e, cc=cc, o=o: e.activation(out=xrb.flat[:, cc * 512:(cc + 1) * 512], in_=o, func=AF.Copy), reads=xr.tok(cc), writes=xrb.tok(cc // 2))
        S.op("pool", lambda e: e.tensor_copy(out=ucar[:], in_=upre[:, :, TT:TT + 3]), reads=UP.tok(), writes=["ucar"])
        for cc in range(8):
            for g, dst in ((0, rg), (1, ig)):
                b = bank()
                S.op("pe", lambda e, cc=cc, g=g, b=b: e.matmul(ps[b][:], lhsT=gateW[:, g * 8 + cc, :], rhs=xrb.flat[:, cc * 512:(cc + 1) * 512], start=True, stop=True),
                     reads=["gateW"] + xrb.tok(cc // 2), writes=[("ps", b)])
                S.op("act", lambda e, cc=cc, g=g, b=b, dst=dst: e.activation(out=dst.blk(cc), in_=ps[b][:], func=AF.Sigmoid, bias=V("lru_gate_b", g * 8 + cc)),
                     reads=[("ps", b), "vecs"], writes=dst.tok(cc))
        for cc in range(8):
            S.op("act", lambda e, cc=cc: e.activation(out=rg.blk(cc), in_=rg.blk(cc), func=AF.Exp, scale=cneg[:, cc:cc + 1]),
                 reads=rg.tok(cc) + ["cneg"], writes=rg.tok(cc))
            S.op("dve", lambda e, cc=cc: e.tensor_tensor(out=t1.blk(cc), in0=rg.blk(cc), in1=rg.blk(cc), op=ALU.mult), reads=rg.tok(cc), writes=t1.tok(cc))
            S.op("dve", lambda e, cc=cc: e.tensor_tensor(out=ig.blk(cc), in0=ig.blk(cc), in1=xr.blk(cc), op=ALU.mult), reads=ig.tok(cc) + xr.tok(cc), writes=ig.tok(cc))
        for cc in range(8):
            S.op("act", lambda e, cc=cc: e.activation(out=t1.blk(cc), in_=t1.blk(cc), func=AF.Sqrt, scale=-1.0, bias=1.0), reads=t1.tok(cc), writes=t1.tok(cc))
            S.op("dve", lambda e, cc=cc: e.tensor_tensor(out=ig.blk(cc), in0=ig.blk(cc), in1=t1.blk(cc), op=ALU.mult), reads=ig.tok(cc) + t1.tok(cc), writes=ig.tok(cc))
            S.op("dve", lambda e, cc=cc: e.tensor_tensor_scan(out=t1.blk(cc), data0=rg.blk(cc), data1=ig.blk(cc), initial=hstate[:, cc:cc + 1], op0=ALU.mult, op1=ALU.add),
                 reads=rg.tok(cc) + ig.tok(cc) + ["hstate"], writes=t1.tok(cc))
            S.op("dve", lambda e, cc=cc: e.tensor_copy(out=hstate[:, cc:cc + 1], in_=t1.blk(cc)[:, TT - 1:TT]), reads=t1.tok(cc), writes=["hstate"])
            S.op("dve", lambda e, cc=cc: e.tensor_tensor(out=yb.flat[:, cc * 512:(cc + 1) * 512], in0=t1.blk(cc), in1=yb.flat[:, cc * 512:(cc + 1) * 512], op=ALU.mult),
                 reads=t1.tok(cc) + yb.tok(cc // 2), writes=yb.tok(cc // 2))
        out_proj(xr_, xtok, "lru_w_out", 8, lambda kc, s: yb.flat[:, kc * 512 + s * 128: kc * 512 + (s + 1) * 128], lambda kc: yb.tok(kc // 2), bias=True)

    def stage_F(xr_, xtok, l, first_tile):
        hid = LB(0, 12, BF16)
        gpre = [sbuf_gpre[0], sbuf_gpre[1]]
        gc = LB(12, 2, F32); gg = LB(14, 2, F32)
        if first_tile:
            S.op("pool", lambda e: e.memset(fcar[l][:], 0.0), writes=[("fcar", l)])
        rmsnorm_hT(xr_, xtok, "ffn_norm", l * 8)
        up = f"ffn_w_up{l}"
        for jj in range(NJ // 2):
            srcg = wb_d[up][:, jj * 256:(jj + 1) * 256].rearrange("(k p) e -> p k e", p=128)
            srcu = wb_d[up][:, DFF + jj * 256:DFF + (jj + 1) * 256].rearrange("(k p) e -> p k e", p=128)
            wg, wgt = wload(up, srcg, lambda t: t[:].rearrange("p (k e) -> p k e", k=KC))
            wu, wut = wload(up, srcu, lambda t: t[:].rearrange("p (k e) -> p k e", k=KC))
            for ee in range(2):
                j = 2 * jj + ee
                r = j % 2
                bg = bank(); bu = bank()
                for kc in range(KC):
                    S.op("pe", lambda e, kc=kc, ee=ee, wg=wg, bg=bg: e.matmul(ps[bg][:], lhsT=wg[:, kc, ee * 128:(ee + 1) * 128], rhs=hT[:, kc, :], start=(kc == 0), stop=(kc == KC - 1)),
                         reads=[wgt, "hT"], writes=[("ps", bg)])
                for kc in range(KC):
                    S.op("pe", lambda e, kc=kc, ee=ee, wu=wu, bu=bu: e.matmul(ps[bu][:], lhsT=wu[:, kc, ee * 128:(ee + 1) * 128], rhs=hT[:, kc, :], start=(kc == 0), stop=(kc == KC - 1)),
                         reads=[wut, "hT"], writes=[("ps", bu)])
                gp = gpre[r]; gpt = ("gpre", r)
                S.op("pool", lambda e, gp=gp, j=j: e.tensor_copy(out=gp[:, 0:2], in_=fcar[l][:, j, :]), reads=[("fcar", l)], writes=[gpt])
                S.op("act", lambda e, gp=gp, bg=bg: e.activation(out=gp[:, 2:2 + TT], in_=ps[bg][:], func=AF.Identity), reads=[("ps", bg)], writes=[gpt])
                S.op("pool", lambda e, gp=gp, j=j: e.tensor_copy(out=fcar[l][:, j, :], in_=gp[:, TT:TT + 2]), reads=[gpt], writes=[("fcar", l)])
                co = VOFF["ffn_conv_w"] + l * 72
                S.op("act", lambda e, j=j, r=r, co=co, bg=bg: e.activation(out=gc.blk(r), in_=ps[bg][:], func=AF.Copy, scale=vecs[:, co + 2 * 24 + j:co + 2 * 24 + j + 1]),
                     reads=[("ps", bg), "vecs"], writes=gc.tok(r))
                for tap in (0, 1):
                    S.op("dve", lambda e, gp=gp, j=j, r=r, co=co, tap=tap: e.scalar_tensor_tensor(out=gc.blk(r), in0=gp[:, tap:tap + TT], scalar=vecs[:, co + tap * 24 + j:co + tap * 24 + j + 1],
                                                                                               in1=gc.blk(r), op0=ALU.mult, op1=ALU.add),
                         reads=[gpt, "vecs"] + gc.tok(r), writes=gc.tok(r))
                S.op("act", lambda e, j=j, r=r: e.activation(out=gg.blk(r), in_=gc.blk(r), func=AF.Gelu_apprx_tanh, bias=V("ffn_conv_b", l * 24 + j)),
                     reads=gc.tok(r) + ["vecs"], writes=gg.tok(r))
                S.op("dve", lambda e, j=j, r=r, bu=bu: e.tensor_tensor(out=hid.flat[:, j * 512:(j + 1) * 512], in0=ps[bu][:], in1=gg.blk(r), op=ALU.mult),
                     reads=[("ps", bu)] + gg.tok(r), writes=hid.tok(j // 2))
        out_proj(xr_, xtok, f"ffn_w_dn{l}", NJ, lambda kc, s: hid.flat[:, kc * 512 + s * 128: kc * 512 + (s + 1) * 128], lambda kc: hid.tok(kc // 2))

    sbuf_gpre = [sb(f"gpre{i}", [128, TT + 2]) for i in range(2)]

    KAPPA = 0.6065306597126334

    def stage_C(xr_, xtok, first_tile):
        AT = LB(0, 4, BF16); BT = LB(4, 4, BF16); KT = LB(8, 4, BF16); RT = LB(12, 4, BF16)
        Vt = LB(16, 4, BF16); Kh = LB(20, 4, BF16); Bh = LB(24, 4, BF16)
        xx = LB(28, 4, BF16); xs = LB(32, 4, BF16)

        def v3(lb):
            return lb.flat.rearrange("p (k t) -> p k t", k=KC)

        def v4(lb):
            return lb.flat.rearrange("p (c f) -> p c f", c=NS)

        def mm(out, lhsT, rhs, rd, b, start=True, stop=True):
            S.op("pe", lambda e: e.matmul(out, lhsT=lhsT, rhs=rhs, start=start, stop=stop), reads=rd, writes=[("ps", b)])

        if first_tile:
            S.op("pool", lambda e: e.memset(hcar[:], 0.0), writes=["hcar"])
            S.op("pool", lambda e: e.memset(Hst[:], 0.0), writes=["Hst"])
            S.op("pool", lambda e: e.memset(Hbd[:], 0.0), writes=["Hbd"])
        rmsnorm_hT(xr_, xtok, "rwkv_norm", 0)
        xx3 = v3(xx); xs3 = v3(xs)
        S.op("dve", lambda e: e.tensor_tensor(out=xx3[:, :, 1:TT], in0=hT[:, :, 0:TT - 1], in1=hT[:, :, 1:TT], op=ALU.subtract), reads=["hT"], writes=xx.tok())
        S.op("dve", lambda e: e.tensor_tensor(out=xx3[:, :, 0:1], in0=hcar[:].unsqueeze(2), in1=hT[:, :, 0:1], op=ALU.subtract), reads=["hT", "hcar"], writes=xx.tok())
        S.op("pool", lambda e: e.tensor_copy(out=hcar[:].unsqueeze(2), in_=hT[:, :, TT - 1:TT]), reads=["hT"], writes=["hcar"])

        def make_xs(i):
            for kc in range(KC):
                S.op("dve", lambda e, kc=kc: e.scalar_tensor_tensor(out=xs3[:, kc, :], in0=xx3[:, kc, :], scalar=V("rwkv_mix", i * 8 + kc), in1=hT[:, kc, :], op0=ALU.mult, op1=ALU.add),
                     reads=xx.tok(kc // 2) + ["hT", "vecs"], writes=xs.tok(kc // 2))

        def lora1(wt, wtok, c0, c1, func, out_ap, out_tok):
            M = c1 - c0
            b = bank()
            for kc in range(KC):
                mm(ps[b][0:M, :], wt[:, kc, c0:c1], xs3[:, kc, :], [wtok] + xs.tok(kc // 2), b, start=(kc == 0), stop=(kc == KC - 1))
            S.op("act", lambda e: e.activation(out=out_ap, in_=ps[b][0:M, :], func=func), reads=[("ps", b)], writes=[out_tok])

        make_xs(3); lora1(w1b, "w1b", 0, 64, AF.Tanh, tanhw[:], "tanhw")
        make_xs(4); lora1(a1b, "a1b", 0, 64, AF.Copy, a1o[:], "a1o")
        make_xs(5); lora1(g1b, "g1b", 0, 128, AF.Sigmoid, sgT[:, 0, :], "sgT"); lora1(g1b, "g1b", 128, 160, AF.Sigmoid, sgT[0:32, 1, :], "sgT")

        make_xs(2)
        Vt4 = v4(Vt); Kh4 = v4(Kh); Bh4 = v4(Bh)
        out_proj(None, None, "rwkv_w_v", 8, lambda kc, s: xs3[:, kc, s * 128:(s + 1) * 128], lambda kc: xs.tok(kc // 2),
                 evac=lambda s, nh, b: S.op("act", lambda e: e.activation(out=Vt4[:, s, nh * 512:(nh + 1) * 512], in_=ps[b][:], func=AF.Copy),
                                            reads=[("ps", b)], writes=Vt.tok(s)))
        make_xs(0)
        for kc in range(KC):
            S.op("dve", lambda e, kc=kc: e.scalar_tensor_tensor(out=xx3[:, kc, :], in0=xx3[:, kc, :], scalar=V("rwkv_mix", 1 * 8 + kc), in1=hT[:, kc, :], op0=ALU.mult, op1=ALU.add),
                 reads=xx.tok(kc // 2) + ["hT", "vecs"], writes=xx.tok(kc // 2))

        c4 = lambda ap: ap.rearrange("p (c t) -> p c t", t=CH)
        wslabs = {}

        def pair_bufs(p):
            tb = 36 if p % 2 == 0 else 45
            Tt = [LB(tb + i, 1, F32) for i in range(7)]
            return Tt, LB(tb + 7, 1, BF16), LB(tb + 8, 1, BF16)

        def prep1(p):
            Tt, TKB, TRQ = pair_bufs(p)
            T0, T1, T2, T3 = [Tt[i].flat for i in range(4)]
            k0, k1, k2, k3 = [Tt[i].tok() for i in range(4)]
            pc = slice(p * 128, (p + 1) * 128)
            ee = p % 2
            if ee == 0:
                for nm in ("rwkv_w_k", "rwkv_w_r"):
                    src = wb_d[nm][:, (p // 2) * 256:(p // 2 + 1) * 256].rearrange("(k p) e -> p k e", p=128)
                    wslabs[nm] = wload(nm, src, lambda t: t[:].rearrange("p (k e) -> p k e", k=KC))
            b = bank(); b2 = bank()
            mm(ps[b][:], w2b[0:64, pc], tanhw[:], ["w2b", "tanhw"], b)
            mm(ps[b2][:], a2b[0:64, pc], a1o[:], ["a2b", "a1o"], b2)
            S.op("act", lambda e: e.activation(out=T0, in_=ps[b][:], func=AF.Sigmoid, bias=V("rwkv_w0", p)), reads=[("ps", b), "vecs"], writes=k0)
            S.op("act", lambda e: e.activation(out=T3, in_=ps[b2][:], func=AF.Sigmoid, bias=V("rwkv_a0", p)), reads=[("ps", b2), "vecs"], writes=k3)
            S.op("dve", lambda e: e.tensor_tensor_scan(out=T1, data0=rmask[:], data1=T0, initial=0.0, op0=ALU.mult, op1=ALU.add), reads=k0 + ["rmask"], writes=k1)
            S.op("pool", lambda e: e.tensor_tensor(out=T0, in0=T1, in1=T0, op=ALU.subtract), reads=k0 + k1, writes=k0)
            S.op("act", lambda e: e.activation(out=T2, in_=T1, func=AF.Exp, scale=-KAPPA), reads=k1, writes=k2)
            S.op("act", lambda e: e.activation(out=T0, in_=T0, func=AF.Exp, scale=-KAPPA), reads=k0, writes=k0)
            S.op("act", lambda e: e.activation(out=T1, in_=T1, func=AF.Exp, scale=KAPPA), reads=k1, writes=k1)
            S.op("pool", lambda e: e.tensor_copy(out=WC[:, p, :], in_=c4(T2)[:, :, CH - 1]), reads=k2, writes=["WC"])
            bK, bR = (0, 1) if p % 2 == 0 else (2, 3)
            wk, wkt = wslabs["rwkv_w_k"]; wr, wrt = wslabs["rwkv_w_r"]
            for kc in range(KC):
                mm(ps[bK][:], wk[:, kc, ee * 128:(ee + 1) * 128], xx3[:, kc, :], [wkt] + xx.tok(kc // 2), bK, start=(kc == 0), stop=(kc == KC - 1))
            for kc in range(KC):
                mm(ps[bR][:], wr[:, kc, ee * 128:(ee + 1) * 128], xs3[:, kc, :], [wrt] + xs.tok(kc // 2), bR, start=(kc == 0), stop=(kc == KC - 1))
            return bK, bR

        def prep2(p, bK, bR):
            Tt, TKB, TRQ = pair_bufs(p)
            T0, T1, T2, T3, T4, T5, T6 = [Tt[i].flat for i in range(7)]
            k0, k1, k2, k3, k4, k5, k6 = [Tt[i].tok() for i in range(7)]
            TK = TKB.flat[:, 0:512]; TB = TKB.flat[:, 512:1024]; TR = TRQ.flat[:, 0:512]; KSQ = TRQ.flat[:, 512:1024]
            kt = KT.flat[:, p * 512:(p + 1) * 512]; rt = RT.flat[:, p * 512:(p + 1) * 512]
            at = AT.flat[:, p * 512:(p + 1) * 512]; bt = BT.flat[:, p * 512:(p + 1) * 512]
            ktk = KT.tok(p // 2); rtk = RT.tok(p // 2); atk = AT.tok(p // 2); btk = BT.tok(p // 2)
            pc = slice(p * 128, (p + 1) * 128)
            S.op("act", lambda e: e.activation(out=T4, in_=ps[bK][:], func=AF.Copy, scale=V("rwkv_k_k", p)), reads=[("ps", bK), "vecs"], writes=k4)
            S.op("pool", lambda e: e.tensor_tensor(out=KSQ, in0=T4, in1=T4, op=ALU.mult), reads=k4, writes=TRQ.tok())
            b3 = bank()
            mm(ps[b3][:], blkones[:], KSQ, ["blkones"] + TRQ.tok(), b3)
            S.op("act", lambda e: e.activation(out=T6, in_=ps[b3][:], func=AF.Sqrt), reads=[("ps", b3)], writes=k6)
            S.op("act", lambda e: e.activation(out=T5, in_=T3, func=AF.Identity, scale=V("rwkv_k_a", p), bias=omka[:, p:p + 1]), reads=k3 + ["vecs", "omka"], writes=k5)
            S.op("dve", lambda e: e.tensor_tensor(out=T5, in0=ps[bK][:], in1=T5, op=ALU.mult), reads=k5 + [("ps", bK)], writes=k5)
            S.op("dve", lambda e: e.tensor_tensor(out=kt, in0=T5, in1=T1, op=ALU.mult), reads=k5 + k1, writes=ktk)
            wcb = WC[:, p, :].unsqueeze(2).to_broadcast([128, 4, CH])
            S.op("pool", lambda e: e.tensor_tensor(out=c4(TK), in0=c4(kt), in1=wcb, op=ALU.mult), reads=ktk + ["WC"], writes=TKB.tok())
            S.op("dve", lambda e: e.scalar_tensor_tensor(out=TR, in0=ps[bR][:], scalar=V("rwkv_r_k", p), in1=T5, op0=ALU.mult, op1=ALU.mult), reads=[("ps", bR)] + k5 + ["vecs"], writes=TRQ.tok())
            S.op("dve", lambda e: e.tensor_tensor(out=rt, in0=ps[bR][:], in1=T2, op=ALU.mult), reads=[("ps", bR)] + k2, writes=rtk)
            S.op("dve", lambda e: e.tensor_scalar(out=T6, in0=T6, scalar1=1e-12, scalar2=None, op0=ALU.max), reads=k6, writes=k6)
            S.op("dve", lambda e: e.reciprocal(out=T6, in_=T6), reads=k6, writes=k6)
            S.op("dve", lambda e: e.tensor_tensor(out=T4, in0=T4, in1=T6, op=ALU.mult), reads=k4 + k6, writes=k4)
            S.op("dve", lambda e: e.scalar_tensor_tensor(out=at, in0=T4, scalar=-1.0, in1=T0, op0=ALU.mult, op1=ALU.mult), reads=k4 + k0, writes=atk)
            S.op("pool", lambda e: e.tensor_tensor(out=T4, in0=T4, in1=T3, op=ALU.mult), reads=k4 + k3, writes=k4)
            S.op("dve", lambda e: e.tensor_tensor(out=bt, in0=T4, in1=T1, op=ALU.mult), reads=k4 + k1, writes=btk)
            S.op("pool", lambda e: e.tensor_tensor(out=c4(TB), in0=c4(bt), in1=wcb, op=ALU.mult), reads=btk + ["WC"], writes=TKB.tok())
            b4 = bank()
            for c in range(NS):
                mm(ps[b4][:, c * 2:(c + 1) * 2], TR[:, c * 128:(c + 1) * 128], sel2[:], TRQ.tok() + ["sel2"], b4)
            S.op("act", lambda e: e.activation(out=dtok[:, :, 2 * p:2 * p + 2], in_=ps[b4][:, 0:8].rearrange("p (c e) -> p c e", e=2), func=AF.Copy),
                 reads=[("ps", b4)], writes=["dtok"])
            for src, d4, dlb in ((TK, Kh4, Kh), (TB, Bh4, Bh)):
                b5 = bank()
                psb = ps[b5][:].bitcast(BF16)[:, 0:512].rearrange("p (c f) -> p c f", c=NS)
                for c in range(NS):
                    S.op("pe", lambda e, c=c, psb=psb, src=src: e.transpose(psb[:, c, :], src[:, c * 128:(c + 1) * 128], ident[:]), reads=TKB.tok() + ["ident"], writes=[("ps", b5)])
                S.op("act", lambda e, psb=psb, d4=d4: e.activation(out=d4[:, :, pc], in_=psb, func=AF.Copy), reads=[("ps", b5)], writes=dlb.tok())

        def record(build, banks):
            ops = []
            orig = S.op
            saved = bank_list[0]
            bank_list[0] = banks
            S.op = lambda *a, **k: ops.append((a, k))
            try:
                ret = build()
            finally:
                S.op = orig
                bank_list[0] = saved
            return ops, ret

        def emit_merged(A, B):
            i = j = 0
            na, nb = len(A), len(B)
            while i < na or j < nb:
                if j >= nb or (i < na and i * nb <= j * na):
                    a, k = A[i]; i += 1
                else:
                    a, k = B[j]; j += 1
                S.op(*a, **k)

        opsA, ctx_p = record(lambda: prep1(0), [4, 5])
        emit_merged(opsA, [])
        for p_ in range(8):
            opsA, ctx_n = record(lambda: prep1(p_ + 1), [4, 5]) if p_ + 1 < 8 else ([], None)
            opsB, _ = record(lambda: prep2(p_, *ctx_p), [6, 7])
            emit_merged(opsA, opsB)
            ctx_p = ctx_n
        Xb = LB(40, 1, BF16); Ub = LB(41, 1, BF16); yq = LB(42, 2, F32); ysq = LB(44, 2, F32); ygb = LB(44, 1, BF16)
        PAD = LB(40, 4, BF16); BTB = LB(44, 2, BF16); PADALL = LB(40, 6, BF16)
        pad5 = PAD.flat.rearrange("p (q j t) -> p q j t", q=8, j=4)
        btb4 = BTB.flat.rearrange("p (q j t) -> p q j t", q=8, j=2)
        AT3 = v3(AT); BT3 = v3(BT); KT3 = v3(KT); RT3 = v3(RT)
        h8 = lambda ap: ap.rearrange("p (h v) -> p h v", v=64)
        mb2 = lambda m: m[:].unsqueeze(1).to_broadcast([128, 2, 128])
        mb4 = lambda m: m[:].unsqueeze(1).to_broadcast([128, 4, 128])
        def scan_chunk(c):
            cc = slice(c * 128, (c + 1) * 128)
            S.op("dve", lambda e: e.memset(PADALL.flat, 0.0), writes=PADALL.tok())
            for e_ in range(2):
                rows = slice(64 * e_, 64 * e_ + 64)
                S.op("dve", lambda e, rows=rows, e_=e_: e.tensor_copy(out=pad5[rows, :, e_, :], in_=AT3[rows, :, cc]), reads=AT.tok() + PAD.tok(), writes=PAD.tok())
                S.op("act", lambda e, rows=rows, e_=e_: e.activation(out=pad5[rows, :, 2 + e_, :], in_=RT3[rows, :, cc], func=AF.Copy), reads=RT.tok() + PAD.tok(), writes=PAD.tok())
                S.op("dve", lambda e, rows=rows, e_=e_: e.tensor_copy(out=btb4[rows, :, e_, :], in_=BT3[rows, :, cc]), reads=BT.tok() + BTB.tok(), writes=BTB.tok())
            Q = []
            for q in range(4):
                QL = LB(28 + 3 * q, 3, BF16)
                hv = lambda i, QL=QL: QL.flat[:, i * 512:(i + 1) * 512].rearrange("p (h t) -> p h t", h=4)
                Aak, Ark, Arb, P = hv(0), hv(1), hv(2), hv(3)
                PTST = QL.flat[:, 2048:3072].rearrange("p (h x) -> p h x", h=4)
                PT = PTST[:, :, 0:128]; ST = PTST[:, :, 128:256]
                tA, tB, tC = QL.tok(0), QL.tok(1), QL.tok(2)
                Q.append((Aak, Ark, Arb, P, PTST, PT, ST, tA, tB, tC))
                b3 = bank()
                for pp in range(2):
                    p = 2 * q + pp
                    hs2 = slice(2 * pp, 2 * pp + 2)
                    b1 = bank(); b2 = bank()
                    padp = PAD.flat[:, p * 512:(p + 1) * 512]
                    mm(ps[b1][:], BT3[:, p, cc], padp, BT.tok(p // 2) + PAD.tok(), b1)
                    mm(ps[b2][:], KT3[:, p, cc], padp, KT.tok(p // 2) + PAD.tok(), b2)
                    mm(ps[b3][:, pp * 256:(pp + 1) * 256], AT3[:, p, cc], BTB.flat[:, p * 256:(p + 1) * 256], AT.tok(p // 2) + BTB.tok(), b3)
                    v2 = lambda b, half: ps[b][:, half * 256:(half + 1) * 256].rearrange("p (h t) -> p h t", h=2)
                    for (dst, b, half, m, tk) in ((PT, b1, 0, mUs, tC), (Arb, b1, 1, mUi, tB), (Aak, b2, 0, mUs, tA), (Ark, b2, 1, mUi, tA)):
                        S.op("dve", lambda e, dst=dst, b=b, half=half, m=m, hs2=hs2: e.tensor_tensor(out=dst[:, hs2, :], in0=v2(b, half), in1=mb2(m), op=ALU.mult),
                             reads=[("ps", b), "mask%d" % id(m)], writes=tk)
                S.op("dve", lambda e, P=P, b3=b3: e.tensor_tensor(out=P, in0=ps[b3][:].rearrange("p (h t) -> p h t", h=4), in1=mb4(mLs), op=ALU.mult),
                     reads=[("ps", b3), "mask%d" % id(mLs)], writes=tB)
                S.op("pool", lambda e, ST=ST, PT=PT: e.tensor_tensor(out=ST, in0=PT, in1=mb4(ident), op=ALU.add), reads=tC + ["ident"], writes=tC)
            for i in range(7):
                for q in range(4):
                    Aak, Ark, Arb, P, PTST, PT, ST, tA, tB, tC = Q[q]
                    p4 = lambda b: ps[b][:].rearrange("p (h t) -> p h t", h=4)
                    if i < 6:
                        bP = bank()
                        for hq in range(4):
                            mm(ps[bP][:, hq * 128:(hq + 1) * 128], PT[:, hq, :], P[:, hq, :], tB + tC, bP)
                    if i == 0:
                        bT = bank()
                        for hq in range(4):
                            mm(ps[bT][:, hq * 128:(hq + 1) * 128], P[:, hq, :], PT[:, hq, :], tB + tC, bT)
                    elif i < 6:
                        bT2 = [bank(), bank()]
                        for hq in range(4):
                            o = ps[bT2[hq // 2]]
                            c0 = (hq % 2) * 256
                            mm(o[:, c0:c0 + 128], P[:, hq, :], PT[:, hq, :], tB + tC, bT2[hq // 2])
                            mm(o[:, c0 + 128:c0 + 256], P[:, hq, :], ST[:, hq, :], tB + tC, bT2[hq // 2], start=True, stop=False)
                            mm(o[:, c0 + 128:c0 + 256], ident[:, :], ST[:, hq, :], tC + ["ident"], bT2[hq // 2], start=False, stop=True)
                    else:
                        bS = bank()
                        for hq in range(4):
                            mm(ps[bS][:, hq * 128:(hq + 1) * 128], P[:, hq, :], ST[:, hq, :], tB + tC, bS, start=True, stop=False)
                            mm(ps[bS][:, hq * 128:(hq + 1) * 128], ident[:, :], ST[:, hq, :], tC + ["ident"], bS, start=False, stop=True)
                    if i < 6:
                        S.op("act", lambda e, P=P, bP=bP: e.activation(out=P, in_=p4(bP), func=AF.Copy), reads=[("ps", bP)], writes=tB)
                    if i == 0:
                        S.op("dve", lambda e, PT=PT, bT=bT: e.tensor_copy(out=PT, in_=p4(bT)), reads=[("ps", bT)], writes=tC)
                    elif i < 6:
                        for k in range(2):
                            pv = ps[bT2[k]][:].rearrange("p (h x) -> p h x", h=2)
                            if k == 0:
                                S.op("dve", lambda e, PTST=PTST, pv=pv, k=k: e.tensor_copy(out=PTST[:, 2 * k:2 * k + 2, :], in_=pv), reads=[("ps", bT2[k])], writes=tC)
                            else:
                                S.op("act", lambda e, PTST=PTST, pv=pv, k=k: e.activation(out=PTST[:, 2 * k:2 * k + 2, :], in_=pv, func=AF.Copy), reads=[("ps", bT2[k])], writes=tC)
                    else:
                        S.op("dve", lambda e, ST=ST, bS=bS: e.tensor_copy(out=ST, in_=p4(bS)), reads=[("ps", bS)], writes=tC)
            bX = [bank(), bank()]
            for p in range(8):
                bx = bX[p // 4]
                for e_ in range(2):
                    h = 2 * p + e_
                    q, hq = divmod(h, 4)
                    Aak, Ark, Arb, P, PTST, PT, ST, tA, tB, tC = Q[q]
                    o = ps[bx][:, (h % 8) * 64:(h % 8 + 1) * 64]
                    mm(o, AT3[:, p, cc], Hbd[:, p, e_ * 64:(e_ + 1) * 64], AT.tok(p // 2) + ["Hbd"], bx, start=True, stop=False)
                    mm(o, Aak[:, hq, :], Vt4[:, c, h * 64:(h + 1) * 64], tA + Vt.tok(c), bx, start=False, stop=True)
            for k in range(2):
                S.op("act", lambda e, k=k: e.activation(out=Xb.flat[:, k * 512:(k + 1) * 512], in_=ps[bX[k]][:], func=AF.Copy), reads=[("ps", bX[k])], writes=Xb.tok())
            bU = [bank(), bank()]
            for h in range(16):
                q, hq = divmod(h, 4)
                Aak, Ark, Arb, P, PTST, PT, ST, tA, tB, tC = Q[q]
                mm(ps[bU[h // 8]][:, (h % 8) * 64:(h % 8 + 1) * 64], ST[:, hq, :], Xb.flat[:, h * 64:(h + 1) * 64], tC + Xb.tok(), bU[h // 8])
            for k in range(2):
                S.op("act", lambda e, k=k: e.activation(out=Ub.flat[:, k * 512:(k + 1) * 512], in_=ps[bU[k]][:], func=AF.Copy), reads=[("ps", bU[k])], writes=Ub.tok())
            bY = [bank(), bank()]
            for p in range(8):
                by = bY[p // 4]
                for e_ in range(2):
                    h = 2 * p + e_
                    q, hq = divmod(h, 4)
                    Aak, Ark, Arb, P, PTST, PT, ST, tA, tB, tC = Q[q]
                    o = ps[by][:, (h % 8) * 64:(h % 8 + 1) * 64]
                    mm(o, RT3[:, p, cc], Hbd[:, p, e_ * 64:(e_ + 1) * 64], RT.tok(p // 2) + ["Hbd"], by, start=True, stop=False)
                    mm(o, Ark[:, hq, :], Vt4[:, c, h * 64:(h + 1) * 64], tA + Vt.tok(c), by, start=False, stop=False)
                    mm(o, Arb[:, hq, :], Ub.flat[:, h * 64:(h + 1) * 64], tB + Ub.tok(), by, start=False, stop=True)
            bH = [bank(), bank()]
            for p in range(8):
                o = ps[bH[p // 4]][:, (p % 4) * 128:(p % 4 + 1) * 128]
                mm(o, Kh4[:, c, p * 128:(p + 1) * 128], Vt4[:, c, p * 128:(p + 1) * 128], Kh.tok(c) + Vt.tok(c), bH[p // 4], start=True, stop=False)
                mm(o, Bh4[:, c, p * 128:(p + 1) * 128], Ub.flat[:, p * 128:(p + 1) * 128], Bh.tok(c) + Ub.tok(), bH[p // 4], start=False, stop=True)
            for e_ in range(2):
                rows = slice(64 * e_, 64 * e_ + 64)
                S.op("pool", lambda e, rows=rows: e.tensor_tensor(out=Hst[rows, :, :], in0=Hst[rows, :, :], in1=WC[rows, :, c:c + 1].to_broadcast([64, 8, 64]), op=ALU.mult),
                     reads=["Hst", "WC"], writes=["Hst"])
                for k in range(2):
                    pv = ps[bH[k]][rows, :].rearrange("p (q f) -> p q f", q=4)[:, :, e_ * 64:(e_ + 1) * 64]
                    S.op("dve", lambda e, rows=rows, pv=pv, k=k: e.tensor_tensor(out=Hst[rows, 4 * k:4 * k + 4, :], in0=pv, in1=Hst[rows, 4 * k:4 * k + 4, :], op=ALU.add),
                         reads=[("ps", bH[k]), "Hst"], writes=["Hst"])
            for e_ in range(2):
                rows = slice(64 * e_, 64 * e_ + 64)
                S.op("act", lambda e, rows=rows, e_=e_: e.activation(out=Hbd[rows, :, e_ * 64:(e_ + 1) * 64], in_=Hst[rows, :, :], func=AF.Copy), reads=["Hst", "Hbd"], writes=["Hbd"])
            yq3 = yq.flat.rearrange("p (h v) -> p h v", v=64); ysq3 = ysq.flat.rearrange("p (h v) -> p h v", v=64)
            for k in range(2):
                S.op("dve", lambda e, k=k: e.tensor_reduce(out=st_a[:, 8 * k:8 * k + 8], in_=h8(ps[bY[k]][:]), axis=AX.X, op=ALU.add), reads=[("ps", bY[k])], writes=["st_a"])
                S.op("act", lambda e, k=k: e.activation(out=ysq.flat[:, k * 512:(k + 1) * 512], in_=ps[bY[k]][:], func=AF.Square), reads=[("ps", bY[k])], writes=ysq.tok(k))
                S.op("dve", lambda e, k=k: e.tensor_reduce(out=st_b[:, 8 * k:8 * k + 8], in_=ysq3[:, 8 * k:8 * k + 8, :], axis=AX.X, op=ALU.add), reads=ysq.tok(k), writes=["st_b"])
            S.op("dve", lambda e: e.tensor_scalar(out=st_a[:], in0=st_a[:], scalar1=1.0 / 64, scalar2=None, op0=ALU.mult), reads=["st_a"], writes=["st_a"])
            S.op("dve", lambda e: e.tensor_tensor(out=st_c[:], in0=st_a[:], in1=st_a[:], op=ALU.mult), reads=["st_a"], writes=["st_c"])
            S.op("dve", lambda e: e.scalar_tensor_tensor(out=st_b[:], in0=st_b[:], scalar=1.0 / 64, in1=st_c[:], op0=ALU.mult, op1=ALU.subtract), reads=["st_b", "st_c"], writes=["st_b"])
            S.op("act", lambda e: e.activation(out=st_b[:], in_=st_b[:], func=AF.Sqrt, bias=GN_EPS), reads=["st_b"], writes=["st_b"])
            S.op("dve", lambda e: e.reciprocal(out=st_b[:], in_=st_b[:]), reads=["st_b"], writes=["st_b"])
            for k in range(2):
                hs_ = slice(8 * k, 8 * k + 8)
                S.op("dve", lambda e, k=k, hs_=hs_: e.tensor_tensor(out=yq3[:, hs_, :], in0=h8(ps[bY[k]][:]), in1=st_a[:, hs_].unsqueeze(2).to_broadcast([128, 8, 64]), op=ALU.subtract),
                     reads=[("ps", bY[k]), "st_a"], writes=yq.tok(k))
                S.op("dve", lambda e, hs_=hs_: e.tensor_tensor(out=yq3[:, hs_, :], in0=yq3[:, hs_, :], in1=st_b[:, hs_].unsqueeze(2).to_broadcast([128, 8, 64]), op=ALU.mult),
                     reads=yq.tok(k) + ["st_b"], writes=yq.tok(k))
            S.op("pool", lambda e: e.tensor_tensor(out=ysq3, in0=h8(Vt4[:, c, :]), in1=dtok[:, c, :].unsqueeze(2).to_broadcast([128, 16, 64]), op=ALU.mult),
                 reads=Vt.tok(c) + ["dtok"], writes=ysq.tok())
            S.op("pool", lambda e: e.tensor_tensor(out=ysq.flat, in0=ysq.flat, in1=lnb_b[:], op=ALU.add), reads=ysq.tok() + ["lnb_b"], writes=ysq.tok())
            S.op("pool", lambda e: e.tensor_tensor(out=yq.flat, in0=yq.flat, in1=lnw_b[:], op=ALU.mult), reads=yq.tok() + ["lnw_b"], writes=yq.tok())
            S.op("dve", lambda e: e.tensor_tensor(out=yq.flat, in0=yq.flat, in1=ysq.flat, op=ALU.add), reads=yq.tok() + ysq.tok(), writes=yq.tok())
            bG = [bank(), bank()]
            for nh in range(2):
                mm(ps[bG[nh]][:], sgT[:, 0, cc], g2b[:, 0, nh * 512:(nh + 1) * 512], ["sgT", "g2b"], bG[nh], start=True, stop=False)
                mm(ps[bG[nh]][:], sgT[0:32, 1, cc], g2b[0:32, 1, nh * 512:(nh + 1) * 512], ["sgT", "g2b"], bG[nh], start=False, stop=True)
                S.op("dve", lambda e, nh=nh: e.tensor_tensor(out=ygb.flat[:, nh * 512:(nh + 1) * 512], in0=ps[bG[nh]][:], in1=yq.flat[:, nh * 512:(nh + 1) * 512], op=ALU.mult),
                     reads=[("ps", bG[nh])] + yq.tok(nh) + ysq.tok(), writes=ygb.tok())
            b = bank()
            psb = ps[b][:].bitcast(BF16).rearrange("p (k t) -> p k t", k=KC)
            for kc in range(KC):
                S.op("pe", lambda e, kc=kc, psb=psb: e.transpose(psb[:, kc, :], ygb.flat[:, kc * 128:(kc + 1) * 128], ident[:]), reads=ygb.tok() + ["ident"], writes=[("ps", b)])
            S.op("act", lambda e, psb=psb: e.activation(out=hT[:, :, cc], in_=psb, func=AF.Copy), reads=[("ps", b)], writes=["hT"])
        for c_ in range(NS):
            scan_chunk(c_)
        out_proj(xr_, xtok, "rwkv_w_out", 8, lambda kc, s: hT[:, kc, s * 128:(s + 1) * 128], lambda kc: ["hT"])

    def stage_E(xr_, xtok, seq, t0, ost):
        for s in range(NS):
            S.op("act", lambda e, s=s: e.activation(out=sqj[:], in_=xr_[:, s, :], func=AF.Square, accum_out=ss[:, s:s + 1]),
                 reads=[(xtok, s)], writes=[("xn", 0), "ss"])
        S.op("dve", lambda e: e.tensor_scalar(out=rstd[:], in0=ss[:], scalar1=1.0 / D, scalar2=RMS_EPS, op0=ALU.mult, op1=ALU.add), reads=["ss"], writes=["rstd"])
        S.op("act", lambda e: e.activation(out=rstd[:], in_=rstd[:], func=AF.Sqrt), reads=["rstd"], writes=["rstd"])
        S.op("dve", lambda e: e.reciprocal(out=rstd[:], in_=rstd[:]), reads=["rstd"], writes=["rstd"])
        for s in range(NS):
            S.op("dve", lambda e, s=s: e.scalar_tensor_tensor(out=xr_[:, s, :], in0=xr_[:, s, :], scalar=rstd[:, s:s + 1], in1=gfin_b[:], op0=ALU.mult, op1=ALU.mult),
                 reads=[(xtok, s), "rstd", "gfin_b"], writes=[(xtok, s)])
        dst = out_d[seq, t0:t0 + TT, :].rearrange("(s p) d -> p s d", p=128)
        return S.op("sp", lambda e: e.dma_start(out=dst, in_=xr_[:]), reads=[(xtok, s) for s in range(NS)], writes=[], dsem=ost)

    xsem = [S.new_dsem("x") for _ in range(2)]
    osem = [S.new_dsem("o") for _ in range(2)]
    tiles = [(q, t) for q in range(n_seq) for t in range(n_tiles)]

    def xload(i):
        q, t = tiles[i]
        buf = xres[0]
        src = x_d[q, t * TT:(t + 1) * TT, :].rearrange("(s p) d -> p s d", p=128)
        S.op("sp", lambda e: e.dma_start(out=buf[:], in_=src), reads=[], writes=[("x0", s) for s in range(NS)], dsem=xsem[0])

    last_out = []
    for i, (q, t) in enumerate(tiles):
        xload(i)
        xr_ = xres[0]
        xtok = "x0"
        first = (t == 0)
        if "A" in stages:
            if i == 0:
                cast_next()
            stage_A(xr_, xtok, first)
        if "B" in stages:
            if i == 0:
                cast_next()
            stage_F(xr_, xtok, 0, first)
        if "C" in stages:
            if i == 0:
                cast_next()
            stage_C(xr_, xtok, first)
        if "D" in stages:
            if i == 0:
                cast_next()
            stage_F(xr_, xtok, 1, first)
        c = stage_E(xr_, xtok, q, t * TT, osem[0])
        last_out.append(c)
    for c in last_out[-2:]:
        S.final_wait("sp", c)

    with nc.Block() as block:
        S.emit(block)
    es.close()
    return nc, S


def kernel(**inputs):
    inp = {k: np.asarray(v) for k, v in inputs.items()}
    x = np.ascontiguousarray(inp["x"], dtype=np.float32)
    B, T, _ = x.shape
    n_seq = B // N_CORES
    nc, _ = build(n_seq=n_seq, T=T)
    shared = host_shared(inp)
    in_maps = []
    for c in range(N_CORES):
        m = dict(shared)
        m["x"] = np.ascontiguousarray(x[c * n_seq:(c + 1) * n_seq])
        in_maps.append(m)
    res = run_bass_kernel_spmd(nc, in_maps, core_ids=list(range(N_CORES)))
    return np.concatenate([r["out"] for r in res.results], axis=0)


def host_shared(inp):
    f = lambda a: np.ascontiguousarray(np.asarray(a, dtype=np.float32))
    return {
        "vecs": make_vecs(inp),
        "lru_w_in": f(inp["lru_w_in"][0]), "lru_w_out": f(inp["lru_w_out"][0]),
        "ffn_w_up0": f(inp["ffn_w_up"][0]), "ffn_w_up1": f(inp["ffn_w_up"][1]),
        "ffn_w_dn0": f(inp["ffn_w_down"][0]), "ffn_w_dn1": f(inp["ffn_w_down"][1]),
        "rwkv_w_r": f(inp["rwkv_w_rkv"][0, 0]), "rwkv_w_k": f(inp["rwkv_w_rkv"][0, 1]), "rwkv_w_v": f(inp["rwkv_w_rkv"][0, 2]),
        "rwkv_w_out": f(inp["rwkv_w_out"][0]),
        "lru_gate_w": f(inp["lru_gate_w"][0]), "lru_b_out": f(inp["lru_b_out"]),
        "rwkv_w1": f(inp["rwkv_w1"][0]), "rwkv_a1": f(inp["rwkv_a1"][0]), "rwkv_g1": f(inp["rwkv_g1"][0]),
        "rwkv_w2": f(inp["rwkv_w2"][0]), "rwkv_a2": f(inp["rwkv_a2"][0]), "rwkv_g2": f(inp["rwkv_g2"][0]),
        "rwkv_ln_w": f(inp["rwkv_ln_w"][0]), "rwkv_ln_b": f(inp["rwkv_ln_b"][0]), "final_norm": f(inp["final_norm"]),
    }
```

```python
import numpy as np
from contextlib import ExitStack
import concourse.bass as bass
import concourse.mybir as mybir
from concourse.bass_utils import run_bass_kernel_spmd

F32 = mybir.dt.float32
BF16 = mybir.dt.bfloat16
AF = mybir.ActivationFunctionType
ALU = mybir.AluOpType
AX = mybir.AxisListType

D = 1024
KC = 8
TT = 512
NS = 4
DFF = 3072
NJ = 24
NW = 5
CH = 128
RMS_EPS = 1e-6
GN_EPS = 64e-5
N_CORES = 8
SAME_ENGINE_SYNC = True

VEC_SPECS = [
    ("lru_norm", 8), ("lru_b_in", 16), ("lru_conv_w", 32), ("lru_conv_b", 8), ("lru_gate_b", 16), ("lru_lambda", 8),
    ("rwkv_norm", 8), ("rwkv_mix", 48), ("rwkv_w0", 8), ("rwkv_a0", 8), ("rwkv_k_k", 8), ("rwkv_k_a", 8), ("rwkv_r_k", 8),
    ("ffn_norm", 16), ("ffn_conv_w", 144), ("ffn_conv_b", 48),
]
VOFF = {}
_o = 0
for _n, _w in VEC_SPECS:
    VOFF[_n] = _o
    _o += _w
NV = _o


def _fm(v):
    v = np.asarray(v, dtype=np.float32)
    return np.ascontiguousarray(v.reshape(-1, 128).T)


def make_vecs(inp):
    cols = [_fm(inp[n]) for n, _ in VEC_SPECS]
    out = np.concatenate(cols, axis=1)
    assert out.shape == (128, NV), out.shape
    return np.ascontiguousarray(out)


class DmaSem:
    def __init__(self, sem):
        self.sem = sem
        self.val = 0


class Sched:
    ENG = ("pe", "act", "dve", "pool", "sp")
    EPOCH = 30000

    def __init__(self, nc, es):
        self.nc = nc
        self.es = es
        self.nsem = 0
        self.prog = {e: [] for e in self.ENG}
        self.esem = {}
        self.cnt = {}
        self.own = {e: set() for e in self.ENG}
        for e in self.ENG:
            self._new_epoch(e)
        self.waited = {e: {} for e in self.ENG}
        self.lastw = {}
        self.readers = {}
        self.nops = 0

    def new_sem(self, name):
        self.nsem += 1
        return self.es.enter_context(self.nc.semaphore(f"{name}{self.nsem}"))

    def new_dsem(self, name="d"):
        return DmaSem(self.new_sem(name))

    def _new_epoch(self, e):
        s = self.new_sem("e" + e)
        self.esem[e] = s
        self.cnt[e] = 0
        self.own[e].add(id(s))

    def op(self, eng, fn, reads=(), writes=(), dsem=None):
        deps = {}

        def add(c):
            if c is None:
                return
            k = id(c[0])
            if k not in deps or deps[k][1] < c[1]:
                deps[k] = c

        for t in reads:
            add(self.lastw.get(t))
        for t in writes:
            add(self.lastw.get(t))
            for c in self.readers.get(t, {}).values():
                add(c)
        for k, (s, v) in deps.items():
            if k in self.own[eng] and dsem is None and (eng == "pe" or not SAME_ENGINE_SYNC):
                continue
            if self.waited[eng].get(k, 0) < v:
                self.prog[eng].append(("w", s, v))
                self.waited[eng][k] = v
        if dsem is None:
            if self.cnt[eng] >= self.EPOCH:
                self._new_epoch(eng)
            self.cnt[eng] += 1
            comp = (self.esem[eng], self.cnt[eng])
            inc = 1
        else:
            dsem.val += 16
            comp = (dsem.sem, dsem.val)
            inc = 16
        self.prog[eng].append(("o", fn, comp[0], inc))
        for t in reads:
            self.readers.setdefault(t, {})[id(comp[0])] = comp
        for t in writes:
            self.lastw[t] = comp
            self.readers[t] = {}
        self.nops += 1
        return comp

    def drain(self, eng):
        if self.cnt[eng] > 0:
            self.prog[eng].append(("w", self.esem[eng], self.cnt[eng]))

    def final_wait(self, eng, comp):
        self.prog[eng].append(("w", comp[0], comp[1]))

    def emit(self, block):
        def mk(e):
            def f(h):
                for it in self.prog[e]:
                    if it[0] == "w":
                        h.wait_ge(it[1], it[2])
                    else:
                        it[1](h).then_inc(it[2], it[3])
            return f

        block.tensor(mk("pe"))
        block.scalar(mk("act"))
        block.vector(mk("dve"))
        block.gpsimd(mk("pool"))
        block.sync(mk("sp"))


def build(n_seq=4, T=2048, stages="ABCDE"):
    assert T % TT == 0
    n_tiles = T // TT
    nc = bass.Bass("TRN2", target_bir_lowering=False)
    es = ExitStack()
    S = Sched(nc, es)

    def dram(name, shape, dt=F32, kind="ExternalInput"):
        return nc.dram_tensor(name, list(shape), dt, kind=kind).ap()

    def sb(name, shape, dt=F32):
        return es.enter_context(nc.sbuf_tensor("s_" + name, list(shape), dt))

    x_d = dram("x", [n_seq, T, D])
    out_d = dram("out", [n_seq, T, D], kind="ExternalOutput")
    vecs_d = dram("vecs", [128, NV])
    bigw = {
        "lru_w_in": (D, 2048), "lru_w_out": (D, D),
        "ffn_w_up0": (D, 2 * DFF), "ffn_w_up1": (D, 2 * DFF), "ffn_w_dn0": (DFF, D), "ffn_w_dn1": (DFF, D),
        "rwkv_w_r": (D, D), "rwkv_w_k": (D, D), "rwkv_w_v": (D, D), "rwkv_w_out": (D, D),
    }
    w_d = {n: dram(n, s) for n, s in bigw.items()}
    wb_d = {n: dram("b_" + n, s, BF16, kind="Internal") for n, s in bigw.items()}
    gate_w_d = dram("lru_gate_w", [2, 16, 64, 64])
    b_out_d = dram("lru_b_out", [1, D])
    w1_d = dram("rwkv_w1", [D, 64]); a1_d = dram("rwkv_a1", [D, 64]); g1_d = dram("rwkv_g1", [D, 160])
    w2_d = dram("rwkv_w2", [64, D]); a2_d = dram("rwkv_a2", [64, D]); g2_d = dram("rwkv_g2", [160, D])
    lnw_d = dram("rwkv_ln_w", [D]); lnb_d = dram("rwkv_ln_b", [D]); fin_d = dram("final_norm", [D])

    vecs = sb("vecs", [128, NV])
    ident = sb("ident", [128, 128], BF16)
    ones_row = sb("ones_row", [1, 128], BF16)
    bout_row = sb("bout_row", [1, D], BF16)
    gateW = sb("gateW", [128, 16, 128], BF16)
    cneg = sb("cneg", [128, 8])
    gfin_b = sb("gfin_b", [128, D])
    HAS_C = "C" in stages
    if HAS_C:
        lnw_b = sb("lnw_b", [128, D]); lnb_b = sb("lnb_b", [128, D])
        w1b = sb("w1b", [128, KC, 64], BF16); a1b = sb("a1b", [128, KC, 64], BF16); g1b = sb("g1b", [128, KC, 160], BF16)
        w2b = sb("w2b", [64, D], BF16); a2b = sb("a2b", [64, D], BF16); g2b = sb("g2b", [128, 2, D], BF16)
        mUs = sb("mUs", [128, 128], BF16); mUi = sb("mUi", [128, 128], BF16); mLs = sb("mLs", [128, 128], BF16)
        sel2 = sb("sel2", [128, 2], BF16); blkones = sb("blkones", [128, 128], BF16)
        rmask = sb("rmask", [128, TT]); omka = sb("omka", [128, 8])
        tanhw = sb("tanhw", [64, TT], BF16); a1o = sb("a1o", [64, TT], BF16); sgT = sb("sgT", [128, 2, TT], BF16)
        WC = sb("WC", [128, 8, 4]); dtok = sb("dtok", [128, 4, 16]); hcar = sb("hcar", [128, 8], BF16)
        Hst = sb("Hst", [128, 8, 64]); Hbd = sb("Hbd", [128, 8, 128], BF16)
        st_a = sb("st_a", [128, 16]); st_b = sb("st_b", [128, 16]); st_c = sb("st_c", [128, 16])
    xres = [sb("xres0", [128, NS, D])]
    hT = sb("hT", [128, KC, TT], BF16)
    xn = [sb(f"xn{i}", [128, D], BF16) for i in range(2)]
    sqj = xn[0]
    ss = sb("ss", [128, NS]); rstd = sb("rstd", [128, NS])
    wring = [sb(f"wring{i}", [128, 2048], BF16) for i in range(NW)]
    wsem = [S.new_dsem("w") for _ in range(NW)]
    hstate = sb("hstate", [128, 8])
    ucar = sb("ucar", [128, 8, 3])
    fcar = [sb(f"fcar{l}", [128, NJ, 2]) for l in range(2)]
    NBLK = 54
    arena = sb("arena", [128, NBLK * 1024], BF16)
    ps = [es.enter_context(nc.psum_tensor(f"ps{i}", [128, 512], F32)) for i in range(8)]

    def V(name, c, n=1):
        o = VOFF[name] + c
        return vecs[:, o:o + n]

    class LB:
        def __init__(self, b0, nb, dt, inner=None):
            self.b0, self.nb, self.dt = b0, nb, dt
            a = arena[:, b0 * 1024:(b0 + nb) * 1024]
            if dt == F32:
                a = a.bitcast(F32)
            self.flat = a
            self.per = 1024 if dt == BF16 else 512
        def blk(self, i, n=1):
            return self.flat[:, i * self.per:(i + n) * self.per]
        def tok(self, i=None, n=1):
            if i is None:
                return [("ar", self.b0 + k) for k in range(self.nb)]
            return [("ar", self.b0 + i + k) for k in range(n)]

    bank_ctr = [0]

    bank_list = [list(range(8))]

    def bank():
        bl = bank_list[0]
        b = bl[bank_ctr[0] % len(bl)]
        bank_ctr[0] += 1
        return b

    wi = [0]

    def wload(name, src_ap, view):
        i = wi[0] % NW
        wi[0] += 1
        dst = view(wring[i])
        S.op("sp", lambda e, dst=dst, src_ap=src_ap: e.dma_start(out=dst, in_=src_ap),
             reads=[("wd", name)], writes=[("w", i)], dsem=wsem[i])
        return dst, ("w", i)

    cs_by_eng = {"sp": S.new_dsem("c"), "pool": S.new_dsem("cp")}
    const_toks = []

    def cload(eng, out_ap, in_ap, tok):
        S.op(eng, lambda e: e.dma_start(out=out_ap, in_=in_ap), reads=[], writes=[tok], dsem=cs_by_eng[eng])
        const_toks.append((tok, eng))

    cload("sp", vecs[:], vecs_d[:, :], "vecs")
    cload("sp", gfin_b[:], fin_d.partition_broadcast(128), "gfin_b")
    cload("pool", bout_row[:], b_out_d[:, :], "bout_row")
    S.op("pool", lambda e: e.memset(gateW[:], 0.0), writes=["gateW"])
    for g in range(2):
        for par in range(2):
            src = gate_w_d[g, par::2].rearrange("n c d -> c n d")
            dst = gateW[par * 64:(par + 1) * 64, g * 8:(g + 1) * 8, par * 64:(par + 1) * 64]
            cload("pool", dst, src, "gateW")
    if HAS_C:
        cload("sp", lnw_b[:], lnw_d.partition_broadcast(128), "lnw_b")
        cload("sp", lnb_b[:], lnb_d.partition_broadcast(128), "lnb_b")
        cload("pool", w1b[:], w1_d.rearrange("(k p) r -> p k r", p=128), "w1b")
        cload("pool", a1b[:], a1_d.rearrange("(k p) r -> p k r", p=128), "a1b")
        cload("pool", g1b[:], g1_d.rearrange("(k p) r -> p k r", p=128), "g1b")
        cload("pool", w2b[:], w2_d[:, :], "w2b")
        cload("pool", a2b[:], a2_d[:, :], "a2b")
        cload("pool", g2b[:, 0, :], g2_d[0:128, :], "g2b")
        cload("pool", g2b[0:32, 1, :], g2_d[128:160, :], "g2b")
    for t, eng in set(const_toks):
        S.lastw[t] = (cs_by_eng[eng].sem, cs_by_eng[eng].val)
    if HAS_C:
        for m, cmp_, cm, pat in ((mUs, ALU.is_gt, -1, 1), (mUi, ALU.is_ge, -1, 1), (mLs, ALU.is_gt, 1, -1)):
            tk = "mask%d" % id(m)
            S.op("pool", lambda e, m=m: e.memset(m[:], 1.0), writes=[tk])
            S.op("pool", lambda e, m=m, cmp_=cmp_, cm=cm, pat=pat: e.affine_select(out=m[:], in_=m[:], pattern=[[pat, 128]], compare_op=cmp_,
                                                                               fill=0.0, base=0, channel_multiplier=cm), reads=[tk], writes=[tk])
        S.op("pool", lambda e: e.memset(sel2[:], 0.0), writes=["sel2"])
        S.op("pool", lambda e: e.memset(sel2[0:64, 0:1], 1.0), reads=["sel2"], writes=["sel2"])
        S.op("pool", lambda e: e.memset(sel2[64:128, 1:2], 1.0), reads=["sel2"], writes=["sel2"])
        S.op("pool", lambda e: e.memset(blkones[:], 0.0), writes=["blkones"])
        S.op("pool", lambda e: e.memset(blkones[0:64, 0:64], 1.0), reads=["blkones"], writes=["blkones"])
        S.op("pool", lambda e: e.memset(blkones[64:128, 64:128], 1.0), reads=["blkones"], writes=["blkones"])
        S.op("pool", lambda e: e.memset(rmask[:], 1.0), writes=["rmask"])
        S.op("pool", lambda e: e.memset(rmask[:].rearrange("p (c t) -> p c t", t=CH)[:, :, 0:1], 0.0), reads=["rmask"], writes=["rmask"])
        S.op("dve", lambda e: e.tensor_scalar(out=omka[:], in0=V("rwkv_k_a", 0, 8), scalar1=-1.0, scalar2=1.0, op0=ALU.mult, op1=ALU.add),
             reads=["vecs"], writes=["omka"])
    S.op("pool", lambda e: e.memset(ident[:], 0.0), writes=["ident"])
    S.op("pool", lambda e: e.affine_select(out=ident[:], in_=ident[:], pattern=[[-1, 128]], compare_op=ALU.not_equal,
                                           fill=1.0, base=0, channel_multiplier=1), reads=["ident"], writes=["ident"])
    S.op("pool", lambda e: e.memset(ones_row[:], 1.0), writes=["ones_row"])
    S.op("act", lambda e: e.activation(out=cneg[:], in_=V("lru_lambda", 0, 8), func=AF.Exp, scale=-1.0), reads=["vecs"], writes=["cneg"])
    S.op("act", lambda e: e.activation(out=cneg[:], in_=cneg[:], func=AF.Ln, bias=1.0), reads=["cneg"], writes=["cneg"])
    S.op("act", lambda e: e.mul(out=cneg[:], in_=cneg[:], mul=-8.0), reads=["cneg"], writes=["cneg"])

    cast_groups = {"A": ["lru_w_in", "lru_w_out"], "B": ["ffn_w_up0", "ffn_w_dn0"],
                   "C": ["rwkv_w_k", "rwkv_w_r", "rwkv_w_v", "rwkv_w_out"], "D": ["ffn_w_up1", "ffn_w_dn1"]}
    cast_order = [g for g in "ABCD" if g in stages]
    cast_done = []

    def cast_group(g):
        prev = [("wd", n) for n in cast_done]
        for n in cast_groups[g]:
            rows, cols = bigw[n]
            ds = S.new_dsem("k")
            step = max(32, (1 << 18) // cols)
            for r0 in range(0, rows, step):
                r1 = min(rows, r0 + step)
                S.op("pool", lambda e, n=n, r0=r0, r1=r1: e.dma_start(out=wb_d[n][r0:r1, :], in_=w_d[n][r0:r1, :]),
                     reads=prev, writes=[("wd", n)], dsem=ds)
            S.lastw[("wd", n)] = (ds.sem, ds.val)
        cast_done.extend(cast_groups[g])

    def cast_next():
        if len(cast_done) < sum(len(cast_groups[g]) for g in cast_order):
            k = 0
            for g in cast_order:
                if cast_groups[g][0] not in cast_done:
                    cast_group(g)
                    return

    cast_next()

    def rmsnorm_hT(xr_, xtok, gname, goff):
        for s in range(NS):
            S.op("act", lambda e, s=s: e.activation(out=sqj[:], in_=xr_[:, s, :], func=AF.Square, accum_out=ss[:, s:s + 1]),
                 reads=[(xtok, s)], writes=[("xn", 0), "ss"])
        S.op("dve", lambda e: e.tensor_scalar(out=rstd[:], in0=ss[:], scalar1=1.0 / D, scalar2=RMS_EPS, op0=ALU.mult, op1=ALU.add),
             reads=["ss"], writes=["rstd"])
        S.op("act", lambda e: e.activation(out=rstd[:], in_=rstd[:], func=AF.Sqrt), reads=["rstd"], writes=["rstd"])
        S.op("dve", lambda e: e.reciprocal(out=rstd[:], in_=rstd[:]), reads=["rstd"], writes=["rstd"])
        gb = V(gname, goff, 8).unsqueeze(2).to_broadcast([128, KC, 128])
        for s in range(NS):
            xb = xn[s % 2]
            xbt = ("xn", s % 2)
            S.op("act", lambda e, s=s, xb=xb: e.activation(out=xb[:], in_=xr_[:, s, :], func=AF.Copy, scale=rstd[:, s:s + 1]),
                 reads=[(xtok, s), "rstd"], writes=[xbt])
            b = bank()
            psb = ps[b][:].bitcast(BF16).rearrange("p (k t) -> p k t", k=KC)
            for kc in range(KC):
                S.op("pe", lambda e, kc=kc, xb=xb, psb=psb: e.transpose(psb[:, kc, :], xb[:, kc * 128:(kc + 1) * 128], ident[:]),
                     reads=[xbt, "ident"], writes=[("ps", b)])
            S.op("dve", lambda e, s=s, psb=psb: e.tensor_tensor(out=hT[:, :, s * 128:(s + 1) * 128], in0=psb, in1=gb, op=ALU.mult),
                 reads=[("ps", b), "vecs"], writes=["hT"])

    def out_proj(xr_, xtok, wname, nk, actT, act_toks, bias=False, evac=None):
        kpl = 4 if nk >= 4 else nk
        for nh in range(2):
            banks = [bank() for _ in range(NS)]
            for k0 in range(0, nk, kpl):
                src = wb_d[wname][k0 * 128:(k0 + kpl) * 128, nh * 512:(nh + 1) * 512].rearrange("(k p) e -> p k e", p=128)
                wsl, wtok = wload(wname, src, lambda t: t[:, 0:kpl * 512].rearrange("p (k e) -> p k e", k=kpl))
                for kk in range(kpl):
                    kc = k0 + kk
                    for s in range(NS):
                        last = (kc == nk - 1) and not bias
                        S.op("pe", lambda e, kc=kc, s=s, kk=kk, wsl=wsl, last=last, b=banks[s]:
                             e.matmul(ps[b][:], lhsT=actT(kc, s), rhs=wsl[:, kk, :], start=(kc == 0), stop=last),
                             reads=[wtok] + act_toks(kc), writes=[("ps", banks[s])])
            for s in range(NS):
                if bias:
                    S.op("pe", lambda e, s=s, nh=nh, b=banks[s]: e.matmul(ps[b][:], lhsT=ones_row[0:1, :], rhs=bout_row[0:1, nh * 512:(nh + 1) * 512],
                                                                        start=False, stop=True),
                         reads=["ones_row", "bout_row"], writes=[("ps", banks[s])])
                if evac is not None:
                    evac(s, nh, banks[s])
                    continue
                S.op("dve", lambda e, s=s, nh=nh, b=banks[s]: e.tensor_tensor(out=xr_[:, s, nh * 512:(nh + 1) * 512], in0=ps[b][:],
                                                                            in1=xr_[:, s, nh * 512:(nh + 1) * 512], op=ALU.add),
                     reads=[("ps", banks[s]), (xtok, s)], writes=[(xtok, s)])

    def stage_A(xr_, xtok, first_tile):
        yb = LB(0, 4, BF16); xr = LB(4, 8, F32)
        rg = LB(12, 8, F32); ig = LB(20, 8, F32); t1 = LB(28, 8, F32); xrb = LB(28, 4, BF16)
        UP = LB(36, 9, F32)
        upre = UP.flat[:, 0:8 * (TT + 3)].rearrange("p (c t) -> p c t", c=8)
        uptok = lambda cc: UP.tok((cc * (TT + 3)) // 512, ((cc + 1) * (TT + 3) - 1) // 512 - (cc * (TT + 3)) // 512 + 1)
        if first_tile:
            S.op("pool", lambda e: e.memset(ucar[:], 0.0), writes=["ucar"])
            S.op("pool", lambda e: e.memset(hstate[:], 0.0), writes=["hstate"])
        S.op("pool", lambda e: e.tensor_copy(out=upre[:, :, 0:3], in_=ucar[:]), reads=["ucar"], writes=UP.tok())
        rmsnorm_hT(xr_, xtok, "lru_norm", 0)
        for q in range(8):
            src = wb_d["lru_w_in"][:, q * 256:(q + 1) * 256].rearrange("(k p) e -> p k e", p=128)
            wsl, wtok = wload("lru_w_in", src, lambda t: t[:].rearrange("p (k e) -> p k e", k=KC))
            for ee in range(2):
                c = 2 * q + ee
                b = bank()
                for kc in range(KC):
                    S.op("pe", lambda e, kc=kc, ee=ee, wsl=wsl, b=b: e.matmul(ps[b][:], lhsT=wsl[:, kc, ee * 128:(ee + 1) * 128], rhs=hT[:, kc, :],
                                                                          start=(kc == 0), stop=(kc == KC - 1)),
                         reads=[wtok, "hT"], writes=[("ps", b)])
                if c < 8:
                    S.op("act", lambda e, c=c, b=b: e.activation(out=yb.flat[:, c * 512:(c + 1) * 512], in_=ps[b][:], func=AF.Gelu_apprx_tanh,
                                                               bias=V("lru_b_in", c)),
                         reads=[("ps", b), "vecs"], writes=yb.tok(c // 2))
                else:
                    cc = c - 8
                    S.op("act", lambda e, cc=cc, b=b: e.activation(out=upre[:, cc, 3:3 + TT], in_=ps[b][:], func=AF.Identity, bias=V("lru_b_in", 8 + cc)),
                         reads=[("ps", b), "vecs"], writes=uptok(cc))
        for cc in range(8):
            o = xr.blk(cc)
            rd = uptok(cc) + ["vecs"]
            S.op("act", lambda e, cc=cc, o=o: e.activation(out=o, in_=upre[:, cc, 0:TT], func=AF.Identity, scale=V("lru_conv_w", 0 * 8 + cc), bias=V("lru_conv_b", cc)),
                 reads=rd, writes=xr.tok(cc))
            for j in range(1, 4):
                S.op("dve", lambda e, cc=cc, o=o, j=j: e.scalar_tensor_tensor(out=o, in0=upre[:, cc, j:j + TT], scalar=V("lru_conv_w", j * 8 + cc), in1=o,
                                                                             op0=ALU.mult, op1=ALU.add), reads=rd + xr.tok(cc), writes=xr.tok(cc))
            S.op("act", lambda e, cc=cc, o=o: e.activation(out=xrb.flat[:, cc * 512:(cc + 1) * 512], in_=o, func=AF.Copy), reads=xr.tok(cc), writes=xrb.tok(cc // 2))
        S.op("pool", lambda e: e.tensor_copy(out=ucar[:], in_=upre[:, :, TT:TT + 3]), reads=UP.tok(), writes=["ucar"])
        for cc in range(8):
            for g, dst in ((0, rg), (1, ig)):
                b = bank()
                S.op("pe", lambda e, cc=cc, g=g, b=b: e.matmul(ps[b][:], lhsT=gateW[:, g * 8 + cc, :], rhs=xrb.flat[:, cc * 512:(cc + 1) * 512], start=True, stop=True),
                     reads=["gateW"] + xrb.tok(cc // 2), writes=[("ps", b)])
                S.op("act", lambda e, cc=cc, g=g, b=b, dst=dst: e.activation(out=dst.blk(cc), in_=ps[b][:], func=AF.Sigmoid, bias=V("lru_gate_b", g * 8 + cc)),
                     reads=[("ps", b), "vecs"], writes=dst.tok(cc))
        for cc in range(8):
            S.op("act", lambda e, cc=cc: e.activation(out=rg.blk(cc), in_=rg.blk(cc), func=AF.Exp, scale=cneg[:, cc:cc + 1]),
                 reads=rg.tok(cc) + ["cneg"], writes=rg.tok(cc))
            S.op("dve", lambda e, cc=cc: e.tensor_tensor(out=t1.blk(cc), in0=rg.blk(cc), in1=rg.blk(cc), op=ALU.mult), reads=rg.tok(cc), writes=t1.tok(cc))
            S.op("dve", lambda e, cc=cc: e.tensor_tensor(out=ig.blk(cc), in0=ig.blk(cc), in1=xr.blk(cc), op=ALU.mult), reads=ig.tok(cc) + xr.tok(cc), writes=ig.tok(cc))
        for cc in range(8):
            S.op("act", lambda e, cc=cc: e.activation(out=t1.blk(cc), in_=t1.blk(cc), func=AF.Sqrt, scale=-1.0, bias=1.0), reads=t1.tok(cc), writes=t1.tok(cc))
            S.op("dve", lambda e, cc=cc: e.tensor_tensor(out=ig.blk(cc), in0=ig.blk(cc), in1=t1.blk(cc), op=ALU.mult), reads=ig.tok(cc) + t1.tok(cc), writes=ig.tok(cc))
            S.op("dve", lambda e, cc=cc: e.tensor_tensor_scan(out=t1.blk(cc), data0=rg.blk(cc), data1=ig.blk(cc), initial=hstate[:, cc:cc + 1], op0=ALU.mult, op1=ALU.add),
                 reads=rg.tok(cc) + ig.tok(cc) + ["hstate"], writes=t1.tok(cc))
            S.op("dve", lambda e, cc=cc: e.tensor_copy(out=hstate[:, cc:cc + 1], in_=t1.blk(cc)[:, TT - 1:TT]), reads=t1.tok(cc), writes=["hstate"])
            S.op("dve", lambda e, cc=cc: e.tensor_tensor(out=yb.flat[:, cc * 512:(cc + 1) * 512], in0=t1.blk(cc), in1=yb.flat[:, cc * 512:(cc + 1) * 512], op=ALU.mult),
                 reads=t1.tok(cc) + yb.tok(cc // 2), writes=yb.tok(cc // 2))
        out_proj(xr_, xtok, "lru_w_out", 8, lambda kc, s: yb.flat[:, kc * 512 + s * 128: kc * 512 + (s + 1) * 128], lambda kc: yb.tok(kc // 2), bias=True)

    def stage_F(xr_, xtok, l, first_tile):
        hid = LB(0, 12, BF16)
        gpre = [sbuf_gpre[0], sbuf_gpre[1]]
        gc = LB(12, 2, F32); gg = LB(14, 2, F32)
        if first_tile:
            S.op("pool", lambda e: e.memset(fcar[l][:], 0.0), writes=[("fcar", l)])
        rmsnorm_hT(xr_, xtok, "ffn_norm", l * 8)
        up = f"ffn_w_up{l}"
        for jj in range(NJ // 2):
            srcg = wb_d[up][:, jj * 256:(jj + 1) * 256].rearrange("(k p) e -> p k e", p=128)
            srcu = wb_d[up][:, DFF + jj * 256:DFF + (jj + 1) * 256].rearrange("(k p) e -> p k e", p=128)
            wg, wgt = wload(up, srcg, lambda t: t[:].rearrange("p (k e) -> p k e", k=KC))
            wu, wut = wload(up, srcu, lambda t: t[:].rearrange("p (k e) -> p k e", k=KC))
            for ee in range(2):
                j = 2 * jj + ee
                r = j % 2
                bg = bank(); bu = bank()
                for kc in range(KC):
                    S.op("pe", lambda e, kc=kc, ee=ee, wg=wg, bg=bg: e.matmul(ps[bg][:], lhsT=wg[:, kc, ee * 128:(ee + 1) * 128], rhs=hT[:, kc, :], start=(kc == 0), stop=(kc == KC - 1)),
                         reads=[wgt, "hT"], writes=[("ps", bg)])
                for kc in range(KC):
                    S.op("pe", lambda e, kc=kc, ee=ee, wu=wu, bu=bu: e.matmul(ps[bu][:], lhsT=wu[:, kc, ee * 128:(ee + 1) * 128], rhs=hT[:, kc, :], start=(kc == 0), stop=(kc == KC - 1)),
                         reads=[wut, "hT"], writes=[("ps", bu)])
                gp = gpre[r]; gpt = ("gpre", r)
                S.op("pool", lambda e, gp=gp, j=j: e.tensor_copy(out=gp[:, 0:2], in_=fcar[l][:, j, :]), reads=[("fcar", l)], writes=[gpt])
                S.op("act", lambda e, gp=gp, bg=bg: e.activation(out=gp[:, 2:2 + TT], in_=ps[bg][:], func=AF.Identity), reads=[("ps", bg)], writes=[gpt])
                S.op("pool", lambda e, gp=gp, j=j: e.tensor_copy(out=fcar[l][:, j, :], in_=gp[:, TT:TT + 2]), reads=[gpt], writes=[("fcar", l)])
                co = VOFF["ffn_conv_w"] + l * 72
                S.op("act", lambda e, j=j, r=r, co=co, bg=bg: e.activation(out=gc.blk(r), in_=ps[bg][:], func=AF.Copy, scale=vecs[:, co + 2 * 24 + j:co + 2 * 24 + j + 1]),
                     reads=[("ps", bg), "vecs"], writes=gc.tok(r))
                for tap in (0, 1):
                    S.op("dve", lambda e, gp=gp, j=j, r=r, co=co, tap=tap: e.scalar_tensor_tensor(out=gc.blk(r), in0=gp[:, tap:tap + TT], scalar=vecs[:, co + tap * 24 + j:co + tap * 24 + j + 1],
                                                                                               in1=gc.blk(r), op0=ALU.mult, op1=ALU.add),
                         reads=[gpt, "vecs"] + gc.tok(r), writes=gc.tok(r))
                S.op("act", lambda e, j=j, r=r: e.activation(out=gg.blk(r), in_=gc.blk(r), func=AF.Gelu_apprx_tanh, bias=V("ffn_conv_b", l * 24 + j)),
                     reads=gc.tok(r) + ["vecs"], writes=gg.tok(r))
                S.op("dve", lambda e, j=j, r=r, bu=bu: e.tensor_tensor(out=hid.flat[:, j * 512:(j + 1) * 512], in0=ps[bu][:], in1=gg.blk(r), op=ALU.mult),
                     reads=[("ps", bu)] + gg.tok(r), writes=hid.tok(j // 2))
        out_proj(xr_, xtok, f"ffn_w_dn{l}", NJ, lambda kc, s: hid.flat[:, kc * 512 + s * 128: kc * 512 + (s + 1) * 128], lambda kc: hid.tok(kc // 2))

    sbuf_gpre = [sb(f"gpre{i}", [128, TT + 2]) for i in range(2)]

    KAPPA = 0.6065306597126334

    def stage_C(xr_, xtok, first_tile):
        AT = LB(0, 4, BF16); BT = LB(4, 4, BF16); KT = LB(8, 4, BF16); RT = LB(12, 4, BF16)
        Vt = LB(16, 4, BF16); Kh = LB(20, 4, BF16); Bh = LB(24, 4, BF16)
        xx = LB(28, 4, BF16); xs = LB(32, 4, BF16)

        def v3(lb):
            return lb.flat.rearrange("p (k t) -> p k t", k=KC)

        def v4(lb):
            return lb.flat.rearrange("p (c f) -> p c f", c=NS)

        def mm(out, lhsT, rhs, rd, b, start=True, stop=True):
            S.op("pe", lambda e: e.matmul(out, lhsT=lhsT, rhs=rhs, start=start, stop=stop), reads=rd, writes=[("ps", b)])

        if first_tile:
            S.op("pool", lambda e: e.memset(hcar[:], 0.0), writes=["hcar"])
            S.op("pool", lambda e: e.memset(Hst[:], 0.0), writes=["Hst"])
            S.op("pool", lambda e: e.memset(Hbd[:], 0.0), writes=["Hbd"])
        rmsnorm_hT(xr_, xtok, "rwkv_norm", 0)
        xx3 = v3(xx); xs3 = v3(xs)
        S.op("dve", lambda e: e.tensor_tensor(out=xx3[:, :, 1:TT], in0=hT[:, :, 0:TT - 1], in1=hT[:, :, 1:TT], op=ALU.subtract), reads=["hT"], writes=xx.tok())
        S.op("dve", lambda e: e.tensor_tensor(out=xx3[:, :, 0:1], in0=hcar[:].unsqueeze(2), in1=hT[:, :, 0:1], op=ALU.subtract), reads=["hT", "hcar"], writes=xx.tok())
        S.op("pool", lambda e: e.tensor_copy(out=hcar[:].unsqueeze(2), in_=hT[:, :, TT - 1:TT]), reads=["hT"], writes=["hcar"])

        def make_xs(i):
            for kc in range(KC):
                S.op("dve", lambda e, kc=kc: e.scalar_tensor_tensor(out=xs3[:, kc, :], in0=xx3[:, kc, :], scalar=V("rwkv_mix", i * 8 + kc), in1=hT[:, kc, :], op0=ALU.mult, op1=ALU.add),
                     reads=xx.tok(kc // 2) + ["hT", "vecs"], writes=xs.tok(kc // 2))

        def lora1(wt, wtok, c0, c1, func, out_ap, out_tok):
            M = c1 - c0
            b = bank()
            for kc in range(KC):
                mm(ps[b][0:M, :], wt[:, kc, c0:c1], xs3[:, kc, :], [wtok] + xs.tok(kc // 2), b, start=(kc == 0), stop=(kc == KC - 1))
            S.op("act", lambda e: e.activation(out=out_ap, in_=ps[b][0:M, :], func=func), reads=[("ps", b)], writes=[out_tok])

        make_xs(3); lora1(w1b, "w1b", 0, 64, AF.Tanh, tanhw[:], "tanhw")
        make_xs(4); lora1(a1b, "a1b", 0, 64, AF.Copy, a1o[:], "a1o")
        make_xs(5); lora1(g1b, "g1b", 0, 128, AF.Sigmoid, sgT[:, 0, :], "sgT"); lora1(g1b, "g1b", 128, 160, AF.Sigmoid, sgT[0:32, 1, :], "sgT")

        make_xs(2)
        Vt4 = v4(Vt); Kh4 = v4(Kh); Bh4 = v4(Bh)
        out_proj(None, None, "rwkv_w_v", 8, lambda kc, s: xs3[:, kc, s * 128:(s + 1) * 128], lambda kc: xs.tok(kc // 2),
                 evac=lambda s, nh, b: S.op("act", lambda e: e.activation(out=Vt4[:, s, nh * 512:(nh + 1) * 512], in_=ps[b][:], func=AF.Copy),
                                            reads=[("ps", b)], writes=Vt.tok(s)))
        make_xs(0)
        for kc in range(KC):
            S.op("dve", lambda e, kc=kc: e.scalar_tensor_tensor(out=xx3[:, kc, :], in0=xx3[:, kc, :], scalar=V("rwkv_mix", 1 * 8 + kc), in1=hT[:, kc, :], op0=ALU.mult, op1=ALU.add),
                 reads=xx.tok(kc // 2) + ["hT", "vecs"], writes=xx.tok(kc // 2))

        c4 = lambda ap: ap.rearrange("p (c t) -> p c t", t=CH)
        wslabs = {}

        def pair_bufs(p):
            tb = 36 if p % 2 == 0 else 45
            Tt = [LB(tb + i, 1, F32) for i in range(7)]
            return Tt, LB(tb + 7, 1, BF16), LB(tb + 8, 1, BF16)

        def prep1(p):
            Tt, TKB, TRQ = pair_bufs(p)
            T0, T1, T2, T3 = [Tt[i].flat for i in range(4)]
            k0, k1, k2, k3 = [Tt[i].tok() for i in range(4)]
            pc = slice(p * 128, (p + 1) * 128)
            ee = p % 2
            if ee == 0:
                for nm in ("rwkv_w_k", "rwkv_w_r"):
                    src = wb_d[nm][:, (p // 2) * 256:(p // 2 + 1) * 256].rearrange("(k p) e -> p k e", p=128)
                    wslabs[nm] = wload(nm, src, lambda t: t[:].rearrange("p (k e) -> p k e", k=KC))
            b = bank(); b2 = bank()
            mm(ps[b][:], w2b[0:64, pc], tanhw[:], ["w2b", "tanhw"], b)
            mm(ps[b2][:], a2b[0:64, pc], a1o[:], ["a2b", "a1o"], b2)
            S.op("act", lambda e: e.activation(out=T0, in_=ps[b][:], func=AF.Sigmoid, bias=V("rwkv_w0", p)), reads=[("ps", b), "vecs"], writes=k0)
            S.op("act", lambda e: e.activation(out=T3, in_=ps[b2][:], func=AF.Sigmoid, bias=V("rwkv_a0", p)), reads=[("ps", b2), "vecs"], writes=k3)
            S.op("dve", lambda e: e.tensor_tensor_scan(out=T1, data0=rmask[:], data1=T0, initial=0.0, op0=ALU.mult, op1=ALU.add), reads=k0 + ["rmask"], writes=k1)
            S.op("pool", lambda e: e.tensor_tensor(out=T0, in0=T1, in1=T0, op=ALU.subtract), reads=k0 + k1, writes=k0)
            S.op("act", lambda e: e.activation(out=T2, in_=T1, func=AF.Exp, scale=-KAPPA), reads=k1, writes=k2)
            S.op("act", lambda e: e.activation(out=T0, in_=T0, func=AF.Exp, scale=-KAPPA), reads=k0, writes=k0)
            S.op("act", lambda e: e.activation(out=T1, in_=T1, func=AF.Exp, scale=KAPPA), reads=k1, writes=k1)
            S.op("pool", lambda e: e.tensor_copy(out=WC[:, p, :], in_=c4(T2)[:, :, CH - 1]), reads=k2, writes=["WC"])
            bK, bR = (0, 1) if p % 2 == 0 else (2, 3)
            wk, wkt = wslabs["rwkv_w_k"]; wr, wrt = wslabs["rwkv_w_r"]
            for kc in range(KC):
                mm(ps[bK][:], wk[:, kc, ee * 128:(ee + 1) * 128], xx3[:, kc, :], [wkt] + xx.tok(kc // 2), bK, start=(kc == 0), stop=(kc == KC - 1))
            for kc in range(KC):
                mm(ps[bR][:], wr[:, kc, ee * 128:(ee + 1) * 128], xs3[:, kc, :], [wrt] + xs.tok(kc // 2), bR, start=(kc == 0), stop=(kc == KC - 1))
            return bK, bR

        def prep2(p, bK, bR):
            Tt, TKB, TRQ = pair_bufs(p)
            T0, T1, T2, T3, T4, T5, T6 = [Tt[i].flat for i in range(7)]
            k0, k1, k2, k3, k4, k5, k6 = [Tt[i].tok() for i in range(7)]
            TK = TKB.flat[:, 0:512]; TB = TKB.flat[:, 512:1024]; TR = TRQ.flat[:, 0:512]; KSQ = TRQ.flat[:, 512:1024]
            kt = KT.flat[:, p * 512:(p + 1) * 512]; rt = RT.flat[:, p * 512:(p + 1) * 512]
            at = AT.flat[:, p * 512:(p + 1) * 512]; bt = BT.flat[:, p * 512:(p + 1) * 512]
            ktk = KT.tok(p // 2); rtk = RT.tok(p // 2); atk = AT.tok(p // 2); btk = BT.tok(p // 2)
            pc = slice(p * 128, (p + 1) * 128)
            S.op("act", lambda e: e.activation(out=T4, in_=ps[bK][:], func=AF.Copy, scale=V("rwkv_k_k", p)), reads=[("ps", bK), "vecs"], writes=k4)
            S.op("pool", lambda e: e.tensor_tensor(out=KSQ, in0=T4, in1=T4, op=ALU.mult), reads=k4, writes=TRQ.tok())
            b3 = bank()
            mm(ps[b3][:], blkones[:], KSQ, ["blkones"] + TRQ.tok(), b3)
            S.op("act", lambda e: e.activation(out=T6, in_=ps[b3][:], func=AF.Sqrt), reads=[("ps", b3)], writes=k6)
            S.op("act", lambda e: e.activation(out=T5, in_=T3, func=AF.Identity, scale=V("rwkv_k_a", p), bias=omka[:, p:p + 1]), reads=k3 + ["vecs", "omka"], writes=k5)
            S.op("dve", lambda e: e.tensor_tensor(out=T5, in0=ps[bK][:], in1=T5, op=ALU.mult), reads=k5 + [("ps", bK)], writes=k5)
            S.op("dve", lambda e: e.tensor_tensor(out=kt, in0=T5, in1=T1, op=ALU.mult), reads=k5 + k1, writes=ktk)
            wcb = WC[:, p, :].unsqueeze(2).to_broadcast([128, 4, CH])
            S.op("pool", lambda e: e.tensor_tensor(out=c4(TK), in0=c4(kt), in1=wcb, op=ALU.mult), reads=ktk + ["WC"], writes=TKB.tok())
            S.op("dve", lambda e: e.scalar_tensor_tensor(out=TR, in0=ps[bR][:], scalar=V("rwkv_r_k", p), in1=T5, op0=ALU.mult, op1=ALU.mult), reads=[("ps", bR)] + k5 + ["vecs"], writes=TRQ.tok())
            S.op("dve", lambda e: e.tensor_tensor(out=rt, in0=ps[bR][:], in1=T2, op=ALU.mult), reads=[("ps", bR)] + k2, writes=rtk)
            S.op("dve", lambda e: e.tensor_scalar(out=T6, in0=T6, scalar1=1e-12, scalar2=None, op0=ALU.max), reads=k6, writes=k6)
            S.op("dve", lambda e: e.reciprocal(out=T6, in_=T6), reads=k6, writes=k6)
            S.op("dve", lambda e: e.tensor_tensor(out=T4, in0=T4, in1=T6, op=ALU.mult), reads=k4 + k6, writes=k4)
            S.op("dve", lambda e: e.scalar_tensor_tensor(out=at, in0=T4, scalar=-1.0, in1=T0, op0=ALU.mult, op1=ALU.mult), reads=k4 + k0, writes=atk)
            S.op("pool", lambda e: e.tensor_tensor(out=T4, in0=T4, in1=T3, op=ALU.mult), reads=k4 + k3, writes=k4)
            S.op("dve", lambda e: e.tensor_tensor(out=bt, in0=T4, in1=T1, op=ALU.mult), reads=k4 + k1, writes=btk)
            S.op("pool", lambda e: e.tensor_tensor(out=c4(TB), in0=c4(bt), in1=wcb, op=ALU.mult), reads=btk + ["WC"], writes=TKB.tok())
            b4 = bank()
            for c in range(NS):
                mm(ps[b4][:, c * 2:(c + 1) * 2], TR[:, c * 128:(c + 1) * 128], sel2[:], TRQ.tok() + ["sel2"], b4)
            S.op("act", lambda e: e.activation(out=dtok[:, :, 2 * p:2 * p + 2], in_=ps[b4][:, 0:8].rearrange("p (c e) -> p c e", e=2), func=AF.Copy),
                 reads=[("ps", b4)], writes=["dtok"])
            for src, d4, dlb in ((TK, Kh4, Kh), (TB, Bh4, Bh)):
                b5 = bank()
                psb = ps[b5][:].bitcast(BF16)[:, 0:512].rearrange("p (c f) -> p c f", c=NS)
                for c in range(NS):
                    S.op("pe", lambda e, c=c, psb=psb, src=src: e.transpose(psb[:, c, :], src[:, c * 128:(c + 1) * 128], ident[:]), reads=TKB.tok() + ["ident"], writes=[("ps", b5)])
                S.op("act", lambda e, psb=psb, d4=d4: e.activation(out=d4[:, :, pc], in_=psb, func=AF.Copy), reads=[("ps", b5)], writes=dlb.tok())

        def record(build, banks):
            ops = []
            orig = S.op
            saved = bank_list[0]
            bank_list[0] = banks
            S.op = lambda *a, **k: ops.append((a, k))
            try:
                ret = build()
            finally:
                S.op = orig
                bank_list[0] = saved
            return ops, ret

        def emit_merged(A, B):
            i = j = 0
            na, nb = len(A), len(B)
            while i < na or j < nb:
                if j >= nb or (i < na and i * nb <= j * na):
                    a, k = A[i]; i += 1
                else:
                    a, k = B[j]; j += 1
                S.op(*a, **k)

        opsA, ctx_p = record(lambda: prep1(0), [4, 5])
        emit_merged(opsA, [])
        for p_ in range(8):
            opsA, ctx_n = record(lambda: prep1(p_ + 1), [4, 5]) if p_ + 1 < 8 else ([], None)
            opsB, _ = record(lambda: prep2(p_, *ctx_p), [6, 7])
            emit_merged(opsA, opsB)
            ctx_p = ctx_n
        Xb = LB(40, 1, BF16); Ub = LB(41, 1, BF16); yq = LB(42, 2, F32); ysq = LB(44, 2, F32); ygb = LB(44, 1, BF16)
        PAD = LB(46, 4, BF16); BTB = LB(50, 2, BF16); PADALL = LB(46, 6, BF16)
        pad5 = PAD.flat.rearrange("p (q j t) -> p q j t", q=8, j=4)
        btb4 = BTB.flat.rearrange("p (q j t) -> p q j t", q=8, j=2)
        AT3 = v3(AT); BT3 = v3(BT); KT3 = v3(KT); RT3 = v3(RT)
        h8 = lambda ap: ap.rearrange("p (h v) -> p h v", v=64)
        mb2 = lambda m: m[:].unsqueeze(1).to_broadcast([128, 2, 128])
        mb4 = lambda m: m[:].unsqueeze(1).to_broadcast([128, 4, 128])
        Qs = {}; bYs = {}

        def phase1(c):
            cc = slice(c * 128, (c + 1) * 128)
            for e_ in range(2):
                rows = slice(64 * e_, 64 * e_ + 64)
                S.op("act", lambda e, rows=rows, e_=e_: e.activation(out=pad5[rows, :, e_, :], in_=AT3[rows, :, cc], func=AF.Copy), reads=AT.tok() + PAD.tok(), writes=PAD.tok())
                S.op("act", lambda e, rows=rows, e_=e_: e.activation(out=pad5[rows, :, 2 + e_, :], in_=RT3[rows, :, cc], func=AF.Copy), reads=RT.tok() + PAD.tok(), writes=PAD.tok())
                S.op("act", lambda e, rows=rows, e_=e_: e.activation(out=btb4[rows, :, e_, :], in_=BT3[rows, :, cc], func=AF.Copy), reads=BT.tok() + BTB.tok(), writes=BTB.tok())
            Q = []
            Qs[c] = Q
            for q in range(4):
                QL = LB(28 + 3 * q, 3, BF16)
                hv = lambda i, QL=QL: QL.flat[:, i * 512:(i + 1) * 512].rearrange("p (h t) -> p h t", h=4)
                Aak, Ark, Arb, P = hv(0), hv(1), hv(2), hv(3)
                PTST = QL.flat[:, 2048:3072].rearrange("p (h x) -> p h x", h=4)
                PT = PTST[:, :, 0:128]; ST = PTST[:, :, 128:256]
                tA, tB, tC = QL.tok(0), QL.tok(1), QL.tok(2)
                Q.append((Aak, Ark, Arb, P, PTST, PT, ST, tA, tB, tC))
                for pp in range(2):
                    p = 2 * q + pp
                    hs2 = slice(2 * pp, 2 * pp + 2)
                    b1 = bank(); b2 = bank(); b3 = bank()
                    padp = PAD.flat[:, p * 512:(p + 1) * 512]
                    mm(ps[b1][:], BT3[:, p, cc], padp, BT.tok(p // 2) + PAD.tok(), b1)
                    mm(ps[b2][:], KT3[:, p, cc], padp, KT.tok(p // 2) + PAD.tok(), b2)
                    mm(ps[b3][:, 0:256], AT3[:, p, cc], BTB.flat[:, p * 256:(p + 1) * 256], AT.tok(p // 2) + BTB.tok(), b3)
                    v2 = lambda b, half: ps[b][:, half * 256:(half + 1) * 256].rearrange("p (h t) -> p h t", h=2)
                    for (dst, b, half, m, tk) in ((PT, b1, 0, mUs, tC), (Arb, b1, 1, mUi, tB), (Aak, b2, 0, mUs, tA), (Ark, b2, 1, mUi, tA), (P, b3, 0, mLs, tB)):
                        S.op("dve", lambda e, dst=dst, b=b, half=half, m=m, hs2=hs2: e.tensor_tensor(out=dst[:, hs2, :], in0=v2(b, half), in1=mb2(m), op=ALU.mult),
                             reads=[("ps", b), "mask%d" % id(m)], writes=tk)
                S.op("pool", lambda e, ST=ST, PT=PT: e.tensor_tensor(out=ST, in0=PT, in1=mb4(ident), op=ALU.add), reads=tC + ["ident"], writes=tC)
            for i in range(7):
                for q in range(4):
                    Aak, Ark, Arb, P, PTST, PT, ST, tA, tB, tC = Q[q]
                    p4 = lambda b: ps[b][:].rearrange("p (h t) -> p h t", h=4)
                    if i < 6:
                        bP = bank()
                        for hq in range(4):
                            mm(ps[bP][:, hq * 128:(hq + 1) * 128], PT[:, hq, :], P[:, hq, :], tB + tC, bP)
                    if i == 0:
                        bT = bank()
                        for hq in range(4):
                            mm(ps[bT][:, hq * 128:(hq + 1) * 128], P[:, hq, :], PT[:, hq, :], tB + tC, bT)
                    elif i < 6:
                        bT2 = [bank(), bank()]
                        for hq in range(4):
                            o = ps[bT2[hq // 2]]
                            c0 = (hq % 2) * 256
                            mm(o[:, c0:c0 + 128], P[:, hq, :], PT[:, hq, :], tB + tC, bT2[hq // 2])
                            mm(o[:, c0 + 128:c0 + 256], P[:, hq, :], ST[:, hq, :], tB + tC, bT2[hq // 2], start=True, stop=False)
                            mm(o[:, c0 + 128:c0 + 256], ident[:, :], ST[:, hq, :], tC + ["ident"], bT2[hq // 2], start=False, stop=True)
                    else:
                        bS = bank()
                        for hq in range(4):
                            mm(ps[bS][:, hq * 128:(hq + 1) * 128], P[:, hq, :], ST[:, hq, :], tB + tC, bS, start=True, stop=False)
                            mm(ps[bS][:, hq * 128:(hq + 1) * 128], ident[:, :], ST[:, hq, :], tC + ["ident"], bS, start=False, stop=True)
                    if i < 6:
                        S.op("act", lambda e, P=P, bP=bP: e.activation(out=P, in_=p4(bP), func=AF.Copy), reads=[("ps", bP)], writes=tB)
                    if i == 0:
                        S.op("act", lambda e, PT=PT, bT=bT: e.activation(out=PT, in_=p4(bT), func=AF.Copy), reads=[("ps", bT)], writes=tC)
                    elif i < 6:
                        for k in range(2):
                            pv = ps[bT2[k]][:].rearrange("p (h x) -> p h x", h=2)
                            if k == 0:
                                S.op("dve", lambda e, PTST=PTST, pv=pv, k=k: e.tensor_copy(out=PTST[:, 2 * k:2 * k + 2, :], in_=pv), reads=[("ps", bT2[k])], writes=tC)
                            else:
                                S.op("act", lambda e, PTST=PTST, pv=pv, k=k: e.activation(out=PTST[:, 2 * k:2 * k + 2, :], in_=pv, func=AF.Copy), reads=[("ps", bT2[k])], writes=tC)
                    else:
                        S.op("act", lambda e, ST=ST, bS=bS: e.activation(out=ST, in_=p4(bS), func=AF.Copy), reads=[("ps", bS)], writes=tC)

        def phase2(c):
            cc = slice(c * 128, (c + 1) * 128)
            Q = Qs[c]
            bX = [2, 3]
            for p in range(8):
                bx = bX[p // 4]
                for e_ in range(2):
                    h = 2 * p + e_
                    q, hq = divmod(h, 4)
                    Aak, Ark, Arb, P, PTST, PT, ST, tA, tB, tC = Q[q]
                    o = ps[bx][:, (h % 8) * 64:(h % 8 + 1) * 64]
                    mm(o, AT3[:, p, cc], Hbd[:, p, e_ * 64:(e_ + 1) * 64], AT.tok(p // 2) + ["Hbd"], bx, start=True, stop=False)
                    mm(o, Aak[:, hq, :], Vt4[:, c, h * 64:(h + 1) * 64], tA + Vt.tok(c), bx, start=False, stop=True)
            for k in range(2):
                S.op("act", lambda e, k=k: e.activation(out=Xb.flat[:, k * 512:(k + 1) * 512], in_=ps[bX[k]][:], func=AF.Copy), reads=[("ps", bX[k])], writes=Xb.tok())
            bU = [4, 5]
            for h in range(16):
                q, hq = divmod(h, 4)
                Aak, Ark, Arb, P, PTST, PT, ST, tA, tB, tC = Q[q]
                mm(ps[bU[h // 8]][:, (h % 8) * 64:(h % 8 + 1) * 64], ST[:, hq, :], Xb.flat[:, h * 64:(h + 1) * 64], tC + Xb.tok(), bU[h // 8])
            for k in range(2):
                S.op("act", lambda e, k=k: e.activation(out=Ub.flat[:, k * 512:(k + 1) * 512], in_=ps[bU[k]][:], func=AF.Copy), reads=[("ps", bU[k])], writes=Ub.tok())
            bY = [0, 1]
            bYs[c] = bY
            for p in range(8):
                by = bY[p // 4]
                for e_ in range(2):
                    h = 2 * p + e_
                    q, hq = divmod(h, 4)
                    Aak, Ark, Arb, P, PTST, PT, ST, tA, tB, tC = Q[q]
                    o = ps[by][:, (h % 8) * 64:(h % 8 + 1) * 64]
                    mm(o, RT3[:, p, cc], Hbd[:, p, e_ * 64:(e_ + 1) * 64], RT.tok(p // 2) + ["Hbd"], by, start=True, stop=False)
                    mm(o, Ark[:, hq, :], Vt4[:, c, h * 64:(h + 1) * 64], tA + Vt.tok(c), by, start=False, stop=False)
                    mm(o, Arb[:, hq, :], Ub.flat[:, h * 64:(h + 1) * 64], tB + Ub.tok(), by, start=False, stop=True)

        def phase3(c):
            cc = slice(c * 128, (c + 1) * 128)
            bY = bYs[c]
            bH = 2
            for e_ in range(2):
                rows = slice(64 * e_, 64 * e_ + 64)
                S.op("pool", lambda e, rows=rows: e.tensor_tensor(out=Hst[rows, :, :], in0=Hst[rows, :, :], in1=WC[rows, :, c:c + 1].to_broadcast([64, 8, 64]), op=ALU.mult),
                     reads=["Hst", "WC"], writes=["Hst"])
            for k in range(2):
                for p in range(4 * k, 4 * k + 4):
                    o = ps[bH][:, (p % 4) * 128:(p % 4 + 1) * 128]
                    mm(o, Kh4[:, c, p * 128:(p + 1) * 128], Vt4[:, c, p * 128:(p + 1) * 128], Kh.tok(c) + Vt.tok(c), bH, start=True, stop=False)
                    mm(o, Bh4[:, c, p * 128:(p + 1) * 128], Ub.flat[:, p * 128:(p + 1) * 128], Bh.tok(c) + Ub.tok(), bH, start=False, stop=True)
                for e_ in range(2):
                    rows = slice(64 * e_, 64 * e_ + 64)
                    pv = ps[bH][rows, :].rearrange("p (q f) -> p q f", q=4)[:, :, e_ * 64:(e_ + 1) * 64]
                    S.op("dve", lambda e, rows=rows, pv=pv, k=k: e.tensor_tensor(out=Hst[rows, 4 * k:4 * k + 4, :], in0=pv, in1=Hst[rows, 4 * k:4 * k + 4, :], op=ALU.add),
                         reads=[("ps", bH), "Hst"], writes=["Hst"])
            for e_ in range(2):
                rows = slice(64 * e_, 64 * e_ + 64)
                S.op("act", lambda e, rows=rows, e_=e_: e.activation(out=Hbd[rows, :, e_ * 64:(e_ + 1) * 64], in_=Hst[rows, :, :], func=AF.Copy), reads=["Hst", "Hbd"], writes=["Hbd"])
            yq3 = yq.flat.rearrange("p (h v) -> p h v", v=64); ysq3 = ysq.flat.rearrange("p (h v) -> p h v", v=64)
            for k in range(2):
                S.op("dve", lambda e, k=k: e.tensor_reduce(out=st_a[:, 8 * k:8 * k + 8], in_=h8(ps[bY[k]][:]), axis=AX.X, op=ALU.add), reads=[("ps", bY[k])], writes=["st_a"])
                S.op("act", lambda e, k=k: e.activation(out=ysq.flat[:, k * 512:(k + 1) * 512], in_=ps[bY[k]][:], func=AF.Square), reads=[("ps", bY[k])], writes=ysq.tok(k))
                S.op("dve", lambda e, k=k: e.tensor_reduce(out=st_b[:, 8 * k:8 * k + 8], in_=ysq3[:, 8 * k:8 * k + 8, :], axis=AX.X, op=ALU.add), reads=ysq.tok(k), writes=["st_b"])
            S.op("dve", lambda e: e.tensor_scalar(out=st_a[:], in0=st_a[:], scalar1=1.0 / 64, scalar2=None, op0=ALU.mult), reads=["st_a"], writes=["st_a"])
            S.op("dve", lambda e: e.tensor_tensor(out=st_c[:], in0=st_a[:], in1=st_a[:], op=ALU.mult), reads=["st_a"], writes=["st_c"])
            S.op("dve", lambda e: e.scalar_tensor_tensor(out=st_b[:], in0=st_b[:], scalar=1.0 / 64, in1=st_c[:], op0=ALU.mult, op1=ALU.subtract), reads=["st_b", "st_c"], writes=["st_b"])
            S.op("act", lambda e: e.activation(out=st_b[:], in_=st_b[:], func=AF.Sqrt, bias=GN_EPS), reads=["st_b"], writes=["st_b"])
            S.op("dve", lambda e: e.reciprocal(out=st_b[:], in_=st_b[:]), reads=["st_b"], writes=["st_b"])
            for k in range(2):
                hs_ = slice(8 * k, 8 * k + 8)
                S.op("dve", lambda e, k=k, hs_=hs_: e.tensor_tensor(out=yq3[:, hs_, :], in0=h8(ps[bY[k]][:]), in1=st_a[:, hs_].unsqueeze(2).to_broadcast([128, 8, 64]), op=ALU.subtract),
                     reads=[("ps", bY[k]), "st_a"], writes=yq.tok(k))
                S.op("dve", lambda e, hs_=hs_: e.tensor_tensor(out=yq3[:, hs_, :], in0=yq3[:, hs_, :], in1=st_b[:, hs_].unsqueeze(2).to_broadcast([128, 8, 64]), op=ALU.mult),
                     reads=yq.tok(k) + ["st_b"], writes=yq.tok(k))
            S.op("pool", lambda e: e.tensor_tensor(out=ysq3, in0=h8(Vt4[:, c, :]), in1=dtok[:, c, :].unsqueeze(2).to_broadcast([128, 16, 64]), op=ALU.mult),
                 reads=Vt.tok(c) + ["dtok"], writes=ysq.tok())
            S.op("pool", lambda e: e.tensor_tensor(out=ysq.flat, in0=ysq.flat, in1=lnb_b[:], op=ALU.add), reads=ysq.tok() + ["lnb_b"], writes=ysq.tok())
            S.op("pool", lambda e: e.tensor_tensor(out=yq.flat, in0=yq.flat, in1=lnw_b[:], op=ALU.mult), reads=yq.tok() + ["lnw_b"], writes=yq.tok())
            S.op("dve", lambda e: e.tensor_tensor(out=yq.flat, in0=yq.flat, in1=ysq.flat, op=ALU.add), reads=yq.tok() + ysq.tok(), writes=yq.tok())
            bG = [2, 2]
            for nh in range(2):
                mm(ps[bG[nh]][:], sgT[:, 0, cc], g2b[:, 0, nh * 512:(nh + 1) * 512], ["sgT", "g2b"], bG[nh], start=True, stop=False)
                mm(ps[bG[nh]][:], sgT[0:32, 1, cc], g2b[0:32, 1, nh * 512:(nh + 1) * 512], ["sgT", "g2b"], bG[nh], start=False, stop=True)
                S.op("dve", lambda e, nh=nh: e.tensor_tensor(out=ygb.flat[:, nh * 512:(nh + 1) * 512], in0=ps[bG[nh]][:], in1=yq.flat[:, nh * 512:(nh + 1) * 512], op=ALU.mult),
                     reads=[("ps", bG[nh])] + yq.tok(nh) + ysq.tok(), writes=ygb.tok())
            b = 2
            psb = ps[b][:].bitcast(BF16).rearrange("p (k t) -> p k t", k=KC)
            for kc in range(KC):
                S.op("pe", lambda e, kc=kc, psb=psb: e.transpose(psb[:, kc, :], ygb.flat[:, kc * 128:(kc + 1) * 128], ident[:]), reads=ygb.tok() + ["ident"], writes=[("ps", b)])
            S.op("act", lambda e, psb=psb: e.activation(out=hT[:, :, cc], in_=psb, func=AF.Copy), reads=[("ps", b)], writes=["hT"])
        S.op("pool", lambda e: e.memset(PADALL.flat, 0.0), writes=PADALL.tok())
        opsP, _ = record(lambda: phase1(0), [3, 4, 5, 6, 7])
        emit_merged(opsP, [])
        for c_ in range(NS):
            phase2(c_)
            ops3, _ = record(lambda: phase3(c_), [2])
            ops1, _ = record(lambda: phase1(c_ + 1), [3, 4, 5, 6, 7]) if c_ + 1 < NS else ([], None)
            emit_merged(ops3, ops1)
        out_proj(xr_, xtok, "rwkv_w_out", 8, lambda kc, s: hT[:, kc, s * 128:(s + 1) * 128], lambda kc: ["hT"])

    def stage_E(xr_, xtok, seq, t0, ost):
        for s in range(NS):
            S.op("act", lambda e, s=s: e.activation(out=sqj[:], in_=xr_[:, s, :], func=AF.Square, accum_out=ss[:, s:s + 1]),
                 reads=[(xtok, s)], writes=[("xn", 0), "ss"])
        S.op("dve", lambda e: e.tensor_scalar(out=rstd[:], in0=ss[:], scalar1=1.0 / D, scalar2=RMS_EPS, op0=ALU.mult, op1=ALU.add), reads=["ss"], writes=["rstd"])
        S.op("act", lambda e: e.activation(out=rstd[:], in_=rstd[:], func=AF.Sqrt), reads=["rstd"], writes=["rstd"])
        S.op("dve", lambda e: e.reciprocal(out=rstd[:], in_=rstd[:]), reads=["rstd"], writes=["rstd"])
        for s in range(NS):
            S.op("dve", lambda e, s=s: e.scalar_tensor_tensor(out=xr_[:, s, :], in0=xr_[:, s, :], scalar=rstd[:, s:s + 1], in1=gfin_b[:], op0=ALU.mult, op1=ALU.mult),
                 reads=[(xtok, s), "rstd", "gfin_b"], writes=[(xtok, s)])
        dst = out_d[seq, t0:t0 + TT, :].rearrange("(s p) d -> p s d", p=128)
        return S.op("sp", lambda e: e.dma_start(out=dst, in_=xr_[:]), reads=[(xtok, s) for s in range(NS)], writes=[], dsem=ost)

    xsem = [S.new_dsem("x") for _ in range(2)]
    osem = [S.new_dsem("o") for _ in range(2)]
    tiles = [(q, t) for q in range(n_seq) for t in range(n_tiles)]

    def xload(i):
        q, t = tiles[i]
        buf = xres[0]
        src = x_d[q, t * TT:(t + 1) * TT, :].rearrange("(s p) d -> p s d", p=128)
        S.op("sp", lambda e: e.dma_start(out=buf[:], in_=src), reads=[], writes=[("x0", s) for s in range(NS)], dsem=xsem[0])

    last_out = []
    for i, (q, t) in enumerate(tiles):
        xload(i)
        xr_ = xres[0]
        xtok = "x0"
        first = (t == 0)
        if "A" in stages:
            if i == 0:
                cast_next()
            stage_A(xr_, xtok, first)
        if "B" in stages:
            if i == 0:
                cast_next()
            stage_F(xr_, xtok, 0, first)
        if "C" in stages:
            if i == 0:
                cast_next()
            stage_C(xr_, xtok, first)
        if "D" in stages:
            if i == 0:
                cast_next()
            stage_F(xr_, xtok, 1, first)
        c = stage_E(xr_, xtok, q, t * TT, osem[0])
        last_out.append(c)
    for c in last_out[-2:]:
        S.final_wait("sp", c)

    with nc.Block() as block:
        S.emit(block)
    es.close()
    return nc, S


def kernel(**inputs):
    inp = {k: np.asarray(v) for k, v in inputs.items()}
    x = np.ascontiguousarray(inp["x"], dtype=np.float32)
    B, T, _ = x.shape
    n_seq = B // N_CORES
    nc, _ = build(n_seq=n_seq, T=T)
    shared = host_shared(inp)
    in_maps = []
    for c in range(N_CORES):
        m = dict(shared)
        m["x"] = np.ascontiguousarray(x[c * n_seq:(c + 1) * n_seq])
        in_maps.append(m)
    res = run_bass_kernel_spmd(nc, in_maps, core_ids=list(range(N_CORES)))
    return np.concatenate([r["out"] for r in res.results], axis=0)


def host_shared(inp):
    f = lambda a: np.ascontiguousarray(np.asarray(a, dtype=np.float32))
    return {
        "vecs": make_vecs(inp),
        "lru_w_in": f(inp["lru_w_in"][0]), "lru_w_out": f(inp["lru_w_out"][0]),
        "ffn_w_up0": f(inp["ffn_w_up"][0]), "ffn_w_up1": f(inp["ffn_w_up"][1]),
        "ffn_w_dn0": f(inp["ffn_w_down"][0]), "ffn_w_dn1": f(inp["ffn_w_down"][1]),
        "rwkv_w_r": f(inp["rwkv_w_rkv"][0, 0]), "rwkv_w_k": f(inp["rwkv_w_rkv"][0, 1]), "rwkv_w_v": f(inp["rwkv_w_rkv"][0, 2]),
        "rwkv_w_out": f(inp["rwkv_w_out"][0]),
        "lru_gate_w": f(inp["lru_gate_w"][0]), "lru_b_out": f(inp["lru_b_out"]),
        "rwkv_w1": f(inp["rwkv_w1"][0]), "rwkv_a1": f(inp["rwkv_a1"][0]), "rwkv_g1": f(inp["rwkv_g1"][0]),
        "rwkv_w2": f(inp["rwkv_w2"][0]), "rwkv_a2": f(inp["rwkv_a2"][0]), "rwkv_g2": f(inp["rwkv_g2"][0]),
        "rwkv_ln_w": f(inp["rwkv_ln_w"][0]), "rwkv_ln_b": f(inp["rwkv_ln_b"][0]), "final_norm": f(inp["final_norm"]),
    }
```

```python
import numpy as np
from contextlib import ExitStack
import concourse.bass as bass
import concourse.mybir as mybir
from concourse.bass_utils import run_bass_kernel_spmd

F32 = mybir.dt.float32
BF16 = mybir.dt.bfloat16
AF = mybir.ActivationFunctionType
ALU = mybir.AluOpType
AX = mybir.AxisListType

D = 1024
KC = 8
TT = 512
NS = 4
DFF = 3072
NJ = 24
NW = 5
CH = 128
RMS_EPS = 1e-6
GN_EPS = 64e-5
N_CORES = 8
SAME_ENGINE_SYNC = True

VEC_SPECS = [
    ("lru_norm", 8), ("lru_b_in", 16), ("lru_conv_w", 32), ("lru_conv_b", 8), ("lru_gate_b", 16), ("lru_lambda", 8),
    ("rwkv_norm", 8), ("rwkv_mix", 48), ("rwkv_w0", 8), ("rwkv_a0", 8), ("rwkv_k_k", 8), ("rwkv_k_a", 8), ("rwkv_r_k", 8),
    ("ffn_norm", 16), ("ffn_conv_w", 144), ("ffn_conv_b", 48),
]
VOFF = {}
_o = 0
for _n, _w in VEC_SPECS:
    VOFF[_n] = _o
    _o += _w
NV = _o


def _fm(v):
    v = np.asarray(v, dtype=np.float32)
    return np.ascontiguousarray(v.reshape(-1, 128).T)


def make_vecs(inp):
    cols = [_fm(inp[n]) for n, _ in VEC_SPECS]
    out = np.concatenate(cols, axis=1)
    assert out.shape == (128, NV), out.shape
    return np.ascontiguousarray(out)


class DmaSem:
    def __init__(self, sem):
        self.sem = sem
        self.val = 0


class Sched:
    ENG = ("pe", "act", "dve", "pool", "sp")
    EPOCH = 30000

    def __init__(self, nc, es):
        self.nc = nc
        self.es = es
        self.nsem = 0
        self.prog = {e: [] for e in self.ENG}
        self.esem = {}
        self.cnt = {}
        self.own = {e: set() for e in self.ENG}
        for e in self.ENG:
            self._new_epoch(e)
        self.waited = {e: {} for e in self.ENG}
        self.lastw = {}
        self.readers = {}
        self.nops = 0

    def new_sem(self, name):
        self.nsem += 1
        return self.es.enter_context(self.nc.semaphore(f"{name}{self.nsem}"))

    def new_dsem(self, name="d"):
        return DmaSem(self.new_sem(name))

    def _new_epoch(self, e):
        s = self.new_sem("e" + e)
        self.esem[e] = s
        self.cnt[e] = 0
        self.own[e].add(id(s))

    def op(self, eng, fn, reads=(), writes=(), dsem=None):
        deps = {}

        def add(c):
            if c is None:
                return
            k = id(c[0])
            if k not in deps or deps[k][1] < c[1]:
                deps[k] = c

        for t in reads:
            add(self.lastw.get(t))
        for t in writes:
            add(self.lastw.get(t))
            for c in self.readers.get(t, {}).values():
                add(c)
        for k, (s, v) in deps.items():
            if k in self.own[eng] and dsem is None and (eng == "pe" or not SAME_ENGINE_SYNC):
                continue
            if self.waited[eng].get(k, 0) < v:
                self.prog[eng].append(("w", s, v))
                self.waited[eng][k] = v
        if dsem is None:
            if self.cnt[eng] >= self.EPOCH:
                self._new_epoch(eng)
            self.cnt[eng] += 1
            comp = (self.esem[eng], self.cnt[eng])
            inc = 1
        else:
            dsem.val += 16
            comp = (dsem.sem, dsem.val)
            inc = 16
        self.prog[eng].append(("o", fn, comp[0], inc))
        for t in reads:
            self.readers.setdefault(t, {})[id(comp[0])] = comp
        for t in writes:
            self.lastw[t] = comp
            self.readers[t] = {}
        self.nops += 1
        return comp

    def drain(self, eng):
        if self.cnt[eng] > 0:
            self.prog[eng].append(("w", self.esem[eng], self.cnt[eng]))

    def final_wait(self, eng, comp):
        self.prog[eng].append(("w", comp[0], comp[1]))

    def emit(self, block):
        def mk(e):
            def f(h):
                for it in self.prog[e]:
                    if it[0] == "w":
                        h.wait_ge(it[1], it[2])
                    else:
                        it[1](h).then_inc(it[2], it[3])
            return f

        block.tensor(mk("pe"))
        block.scalar(mk("act"))
        block.vector(mk("dve"))
        block.gpsimd(mk("pool"))
        block.sync(mk("sp"))


def build(n_seq=4, T=2048, stages="ABCDE"):
    assert T % TT == 0
    n_tiles = T // TT
    nc = bass.Bass("TRN2", target_bir_lowering=False)
    es = ExitStack()
    S = Sched(nc, es)

    def dram(name, shape, dt=F32, kind="ExternalInput"):
        return nc.dram_tensor(name, list(shape), dt, kind=kind).ap()

    def sb(name, shape, dt=F32):
        return es.enter_context(nc.sbuf_tensor("s_" + name, list(shape), dt))

    x_d = dram("x", [n_seq, T, D])
    out_d = dram("out", [n_seq, T, D], kind="ExternalOutput")
    vecs_d = dram("vecs", [128, NV])
    bigw = {
        "lru_w_in": (D, 2048), "lru_w_out": (D, D),
        "ffn_w_up0": (D, 2 * DFF), "ffn_w_up1": (D, 2 * DFF), "ffn_w_dn0": (DFF, D), "ffn_w_dn1": (DFF, D),
        "rwkv_w_r": (D, D), "rwkv_w_k": (D, D), "rwkv_w_v": (D, D), "rwkv_w_out": (D, D),
    }
    w_d = {n: dram(n, s) for n, s in bigw.items()}
    wb_d = {n: dram("b_" + n, s, BF16, kind="Internal") for n, s in bigw.items()}
    gate_w_d = dram("lru_gate_w", [2, 16, 64, 64])
    b_out_d = dram("lru_b_out", [1, D])
    w1_d = dram("rwkv_w1", [D, 64]); a1_d = dram("rwkv_a1", [D, 64]); g1_d = dram("rwkv_g1", [D, 160])
    w2_d = dram("rwkv_w2", [64, D]); a2_d = dram("rwkv_a2", [64, D]); g2_d = dram("rwkv_g2", [160, D])
    lnw_d = dram("rwkv_ln_w", [D]); lnb_d = dram("rwkv_ln_b", [D]); fin_d = dram("final_norm", [D])

    vecs = sb("vecs", [128, NV])
    ident = sb("ident", [128, 128], BF16)
    ones_row = sb("ones_row", [1, 128], BF16)
    bout_row = sb("bout_row", [1, D], BF16)
    gateW = sb("gateW", [128, 16, 128], BF16)
    cneg = sb("cneg", [128, 8])
    gfin_b = sb("gfin_b", [128, D])
    HAS_C = "C" in stages
    if HAS_C:
        lnw_b = sb("lnw_b", [128, D]); lnb_b = sb("lnb_b", [128, D])
        w1b = sb("w1b", [128, KC, 64], BF16); a1b = sb("a1b", [128, KC, 64], BF16); g1b = sb("g1b", [128, KC, 160], BF16)
        w2b = sb("w2b", [64, D], BF16); a2b = sb("a2b", [64, D], BF16); g2b = sb("g2b", [128, 2, D], BF16)
        mUs = sb("mUs", [128, 128], BF16); mUi = sb("mUi", [128, 128], BF16); mLs = sb("mLs", [128, 128], BF16)
        sel2 = sb("sel2", [128, 2], BF16); blkones = sb("blkones", [128, 128], BF16)
        rmask = sb("rmask", [128, TT]); omka = sb("omka", [128, 8])
        tanhw = sb("tanhw", [64, TT], BF16); a1o = sb("a1o", [64, TT], BF16); sgT = sb("sgT", [128, 2, TT], BF16)
        WC = sb("WC", [128, 8, 4]); dtok = sb("dtok", [128, 4, 16]); hcar = sb("hcar", [128, 8], BF16)
        Hst = sb("Hst", [128, 8, 64]); Hbd = sb("Hbd", [128, 8, 128], BF16)
        st_a = sb("st_a", [128, 16]); st_b = sb("st_b", [128, 16]); st_c = sb("st_c", [128, 16])
    xres = [sb("xres0", [128, NS, D])]
    hT = sb("hT", [128, KC, TT], BF16)
    xn = [sb(f"xn{i}", [128, D], BF16) for i in range(2)]
    sqj = xn[0]
    ss = sb("ss", [128, NS]); rstd = sb("rstd", [128, NS])
    wring = [sb(f"wring{i}", [128, 2048], BF16) for i in range(NW)]
    wsem = [S.new_dsem("w") for _ in range(NW)]
    hstate = sb("hstate", [128, 8])
    ucar = sb("ucar", [128, 8, 3])
    fcar = [sb(f"fcar{l}", [128, NJ, 2]) for l in range(2)]
    NBLK = 54
    arena = sb("arena", [128, NBLK * 1024], BF16)
    ps = [es.enter_context(nc.psum_tensor(f"ps{i}", [128, 512], F32)) for i in range(8)]

    def V(name, c, n=1):
        o = VOFF[name] + c
        return vecs[:, o:o + n]

    class LB:
        def __init__(self, b0, nb, dt, inner=None):
            self.b0, self.nb, self.dt = b0, nb, dt
            a = arena[:, b0 * 1024:(b0 + nb) * 1024]
            if dt == F32:
                a = a.bitcast(F32)
            self.flat = a
            self.per = 1024 if dt == BF16 else 512
        def blk(self, i, n=1):
            return self.flat[:, i * self.per:(i + n) * self.per]
        def tok(self, i=None, n=1):
            if i is None:
                return [("ar", self.b0 + k) for k in range(self.nb)]
            return [("ar", self.b0 + i + k) for k in range(n)]

    bank_ctr = [0]

    bank_list = [list(range(8))]

    def bank():
        bl = bank_list[0]
        b = bl[bank_ctr[0] % len(bl)]
        bank_ctr[0] += 1
        return b

    wi = [0]

    def wload(name, src_ap, view):
        i = wi[0] % NW
        wi[0] += 1
        dst = view(wring[i])
        S.op("sp", lambda e, dst=dst, src_ap=src_ap: e.dma_start(out=dst, in_=src_ap),
             reads=[("wd", name)], writes=[("w", i)], dsem=wsem[i])
        return dst, ("w", i)

    cs_by_eng = {"sp": S.new_dsem("c"), "pool": S.new_dsem("cp")}
    const_toks = []

    def cload(eng, out_ap, in_ap, tok):
        S.op(eng, lambda e: e.dma_start(out=out_ap, in_=in_ap), reads=[], writes=[tok], dsem=cs_by_eng[eng])
        const_toks.append((tok, eng))

    cload("sp", vecs[:], vecs_d[:, :], "vecs")
    cload("sp", gfin_b[:], fin_d.partition_broadcast(128), "gfin_b")
    cload("pool", bout_row[:], b_out_d[:, :], "bout_row")
    S.op("pool", lambda e: e.memset(gateW[:], 0.0), writes=["gateW"])
    for g in range(2):
        for par in range(2):
            src = gate_w_d[g, par::2].rearrange("n c d -> c n d")
            dst = gateW[par * 64:(par + 1) * 64, g * 8:(g + 1) * 8, par * 64:(par + 1) * 64]
            cload("pool", dst, src, "gateW")
    if HAS_C:
        cload("sp", lnw_b[:], lnw_d.partition_broadcast(128), "lnw_b")
        cload("sp", lnb_b[:], lnb_d.partition_broadcast(128), "lnb_b")
        cload("pool", w1b[:], w1_d.rearrange("(k p) r -> p k r", p=128), "w1b")
        cload("pool", a1b[:], a1_d.rearrange("(k p) r -> p k r", p=128), "a1b")
        cload("pool", g1b[:], g1_d.rearrange("(k p) r -> p k r", p=128), "g1b")
        cload("pool", w2b[:], w2_d[:, :], "w2b")
        cload("pool", a2b[:], a2_d[:, :], "a2b")
        cload("pool", g2b[:, 0, :], g2_d[0:128, :], "g2b")
        cload("pool", g2b[0:32, 1, :], g2_d[128:160, :], "g2b")
    for t, eng in set(const_toks):
        S.lastw[t] = (cs_by_eng[eng].sem, cs_by_eng[eng].val)
    if HAS_C:
        for m, cmp_, cm, pat in ((mUs, ALU.is_gt, -1, 1), (mUi, ALU.is_ge, -1, 1), (mLs, ALU.is_gt, 1, -1)):
            tk = "mask%d" % id(m)
            S.op("pool", lambda e, m=m: e.memset(m[:], 1.0), writes=[tk])
            S.op("pool", lambda e, m=m, cmp_=cmp_, cm=cm, pat=pat: e.affine_select(out=m[:], in_=m[:], pattern=[[pat, 128]], compare_op=cmp_,
                                                                               fill=0.0, base=0, channel_multiplier=cm), reads=[tk], writes=[tk])
        S.op("pool", lambda e: e.memset(sel2[:], 0.0), writes=["sel2"])
        S.op("pool", lambda e: e.memset(sel2[0:64, 0:1], 1.0), reads=["sel2"], writes=["sel2"])
        S.op("pool", lambda e: e.memset(sel2[64:128, 1:2], 1.0), reads=["sel2"], writes=["sel2"])
        S.op("pool", lambda e: e.memset(blkones[:], 0.0), writes=["blkones"])
        S.op("pool", lambda e: e.memset(blkones[0:64, 0:64], 1.0), reads=["blkones"], writes=["blkones"])
        S.op("pool", lambda e: e.memset(blkones[64:128, 64:128], 1.0), reads=["blkones"], writes=["blkones"])
        S.op("pool", lambda e: e.memset(rmask[:], 1.0), writes=["rmask"])
        S.op("pool", lambda e: e.memset(rmask[:].rearrange("p (c t) -> p c t", t=CH)[:, :, 0:1], 0.0), reads=["rmask"], writes=["rmask"])
        S.op("dve", lambda e: e.tensor_scalar(out=omka[:], in0=V("rwkv_k_a", 0, 8), scalar1=-1.0, scalar2=1.0, op0=ALU.mult, op1=ALU.add),
             reads=["vecs"], writes=["omka"])
    S.op("pool", lambda e: e.memset(ident[:], 0.0), writes=["ident"])
    S.op("pool", lambda e: e.affine_select(out=ident[:], in_=ident[:], pattern=[[-1, 128]], compare_op=ALU.not_equal,
                                           fill=1.0, base=0, channel_multiplier=1), reads=["ident"], writes=["ident"])
    S.op("pool", lambda e: e.memset(ones_row[:], 1.0), writes=["ones_row"])
    S.op("act", lambda e: e.activation(out=cneg[:], in_=V("lru_lambda", 0, 8), func=AF.Exp, scale=-1.0), reads=["vecs"], writes=["cneg"])
    S.op("act", lambda e: e.activation(out=cneg[:], in_=cneg[:], func=AF.Ln, bias=1.0), reads=["cneg"], writes=["cneg"])
    S.op("act", lambda e: e.mul(out=cneg[:], in_=cneg[:], mul=-8.0), reads=["cneg"], writes=["cneg"])

    cast_groups = {"A": ["lru_w_in", "lru_w_out"], "B": ["ffn_w_up0", "ffn_w_dn0"],
                   "C": ["rwkv_w_k", "rwkv_w_r", "rwkv_w_v", "rwkv_w_out"], "D": ["ffn_w_up1", "ffn_w_dn1"]}
    cast_order = [g for g in "ABCD" if g in stages]
    cast_done = []

    def cast_group(g):
        prev = [("wd", n) for n in cast_done]
        for n in cast_groups[g]:
            rows, cols = bigw[n]
            ds = S.new_dsem("k")
            step = max(32, (1 << 18) // cols)
            for r0 in range(0, rows, step):
                r1 = min(rows, r0 + step)
                S.op("pool", lambda e, n=n, r0=r0, r1=r1: e.dma_start(out=wb_d[n][r0:r1, :], in_=w_d[n][r0:r1, :]),
                     reads=prev, writes=[("wd", n)], dsem=ds)
            S.lastw[("wd", n)] = (ds.sem, ds.val)
        cast_done.extend(cast_groups[g])

    def cast_next():
        if len(cast_done) < sum(len(cast_groups[g]) for g in cast_order):
            k = 0
            for g in cast_order:
                if cast_groups[g][0] not in cast_done:
                    cast_group(g)
                    return

    cast_next()

    def rmsnorm_hT(xr_, xtok, gname, goff):
        for s in range(NS):
            S.op("act", lambda e, s=s: e.activation(out=sqj[:], in_=xr_[:, s, :], func=AF.Square, accum_out=ss[:, s:s + 1]),
                 reads=[(xtok, s)], writes=[("xn", 0), "ss"])
        S.op("dve", lambda e: e.tensor_scalar(out=rstd[:], in0=ss[:], scalar1=1.0 / D, scalar2=RMS_EPS, op0=ALU.mult, op1=ALU.add),
             reads=["ss"], writes=["rstd"])
        S.op("act", lambda e: e.activation(out=rstd[:], in_=rstd[:], func=AF.Sqrt), reads=["rstd"], writes=["rstd"])
        S.op("dve", lambda e: e.reciprocal(out=rstd[:], in_=rstd[:]), reads=["rstd"], writes=["rstd"])
        gb = V(gname, goff, 8).unsqueeze(2).to_broadcast([128, KC, 128])
        for s in range(NS):
            xb = xn[s % 2]
            xbt = ("xn", s % 2)
            S.op("act", lambda e, s=s, xb=xb: e.activation(out=xb[:], in_=xr_[:, s, :], func=AF.Copy, scale=rstd[:, s:s + 1]),
                 reads=[(xtok, s), "rstd"], writes=[xbt])
            b = bank()
            psb = ps[b][:].bitcast(BF16).rearrange("p (k t) -> p k t", k=KC)
            for kc in range(KC):
                S.op("pe", lambda e, kc=kc, xb=xb, psb=psb: e.transpose(psb[:, kc, :], xb[:, kc * 128:(kc + 1) * 128], ident[:]),
                     reads=[xbt, "ident"], writes=[("ps", b)])
            S.op("dve", lambda e, s=s, psb=psb: e.tensor_tensor(out=hT[:, :, s * 128:(s + 1) * 128], in0=psb, in1=gb, op=ALU.mult),
                 reads=[("ps", b), "vecs"], writes=["hT"])

    def out_proj(xr_, xtok, wname, nk, actT, act_toks, bias=False, evac=None):
        kpl = 4 if nk >= 4 else nk
        for nh in range(2):
            banks = [bank() for _ in range(NS)]
            for k0 in range(0, nk, kpl):
                src = wb_d[wname][k0 * 128:(k0 + kpl) * 128, nh * 512:(nh + 1) * 512].rearrange("(k p) e -> p k e", p=128)
                wsl, wtok = wload(wname, src, lambda t: t[:, 0:kpl * 512].rearrange("p (k e) -> p k e", k=kpl))
                for kk in range(kpl):
                    kc = k0 + kk
                    for s in range(NS):
                        last = (kc == nk - 1) and not bias
                        S.op("pe", lambda e, kc=kc, s=s, kk=kk, wsl=wsl, last=last, b=banks[s]:
                             e.matmul(ps[b][:], lhsT=actT(kc, s), rhs=wsl[:, kk, :], start=(kc == 0), stop=last),
                             reads=[wtok] + act_toks(kc), writes=[("ps", banks[s])])
            for s in range(NS):
                if bias:
                    S.op("pe", lambda e, s=s, nh=nh, b=banks[s]: e.matmul(ps[b][:], lhsT=ones_row[0:1, :], rhs=bout_row[0:1, nh * 512:(nh + 1) * 512],
                                                                        start=False, stop=True),
                         reads=["ones_row", "bout_row"], writes=[("ps", banks[s])])
                if evac is not None:
                    evac(s, nh, banks[s])
                    continue
                S.op("dve", lambda e, s=s, nh=nh, b=banks[s]: e.tensor_tensor(out=xr_[:, s, nh * 512:(nh + 1) * 512], in0=ps[b][:],
                                                                            in1=xr_[:, s, nh * 512:(nh + 1) * 512], op=ALU.add),
                     reads=[("ps", banks[s]), (xtok, s)], writes=[(xtok, s)])

    def stage_A(xr_, xtok, first_tile):
        yb = LB(0, 4, BF16); xr = LB(4, 8, F32)
        rg = LB(12, 8, F32); ig = LB(20, 8, F32); t1 = LB(28, 8, F32); xrb = LB(28, 4, BF16)
        UP = LB(36, 9, F32)
        upre = UP.flat[:, 0:8 * (TT + 3)].rearrange("p (c t) -> p c t", c=8)
        uptok = lambda cc: UP.tok((cc * (TT + 3)) // 512, ((cc + 1) * (TT + 3) - 1) // 512 - (cc * (TT + 3)) // 512 + 1)
        if first_tile:
            S.op("pool", lambda e: e.memset(ucar[:], 0.0), writes=["ucar"])
            S.op("pool", lambda e: e.memset(hstate[:], 0.0), writes=["hstate"])
        S.op("pool", lambda e: e.tensor_copy(out=upre[:, :, 0:3], in_=ucar[:]), reads=["ucar"], writes=UP.tok())
        rmsnorm_hT(xr_, xtok, "lru_norm", 0)
        for q in range(8):
            src = wb_d["lru_w_in"][:, q * 256:(q + 1) * 256].rearrange("(k p) e -> p k e", p=128)
            wsl, wtok = wload("lru_w_in", src, lambda t: t[:].rearrange("p (k e) -> p k e", k=KC))
            for ee in range(2):
                c = 2 * q + ee
                b = bank()
                for kc in range(KC):
                    S.op("pe", lambda e, kc=kc, ee=ee, wsl=wsl, b=b: e.matmul(ps[b][:], lhsT=wsl[:, kc, ee * 128:(ee + 1) * 128], rhs=hT[:, kc, :],
                                                                          start=(kc == 0), stop=(kc == KC - 1)),
                         reads=[wtok, "hT"], writes=[("ps", b)])
                if c < 8:
                    S.op("act", lambda e, c=c, b=b: e.activation(out=yb.flat[:, c * 512:(c + 1) * 512], in_=ps[b][:], func=AF.Gelu_apprx_tanh,
                                                               bias=V("lru_b_in", c)),
                         reads=[("ps", b), "vecs"], writes=yb.tok(c // 2))
                else:
                    cc = c - 8
                    S.op("act", lambda e, cc=cc, b=b: e.activation(out=upre[:, cc, 3:3 + TT], in_=ps[b][:], func=AF.Identity, bias=V("lru_b_in", 8 + cc)),
                         reads=[("ps", b), "vecs"], writes=uptok(cc))
        for cc in range(8):
            o = xr.blk(cc)
            rd = uptok(cc) + ["vecs"]
            S.op("act", lambda e, cc=cc, o=o: e.activation(out=o, in_=upre[:, cc, 0:TT], func=AF.Identity, scale=V("lru_conv_w", 0 * 8 + cc), bias=V("lru_conv_b", cc)),
                 reads=rd, writes=xr.tok(cc))
            for j in range(1, 4):
                S.op("dve", lambda e, cc=cc, o=o, j=j: e.scalar_tensor_tensor(out=o, in0=upre[:, cc, j:j + TT], scalar=V("lru_conv_w", j * 8 + cc), in1=o,
                                                                             op0=ALU.mult, op1=ALU.add), reads=rd + xr.tok(cc), writes=xr.tok(cc))
            S.op("act", lambda e, cc=cc, o=o: e.activation(out=xrb.flat[:, cc * 512:(cc + 1) * 512], in_=o, func=AF.Copy), reads=xr.tok(cc), writes=xrb.tok(cc // 2))
        S.op("pool", lambda e: e.tensor_copy(out=ucar[:], in_=upre[:, :, TT:TT + 3]), reads=UP.tok(), writes=["ucar"])
        for cc in range(8):
            for g, dst in ((0, rg), (1, ig)):
                b = bank()
                S.op("pe", lambda e, cc=cc, g=g, b=b: e.matmul(ps[b][:], lhsT=gateW[:, g * 8 + cc, :], rhs=xrb.flat[:, cc * 512:(cc + 1) * 512], start=True, stop=True),
                     reads=["gateW"] + xrb.tok(cc // 2), writes=[("ps", b)])
                S.op("act", lambda e, cc=cc, g=g, b=b, dst=dst: e.activation(out=dst.blk(cc), in_=ps[b][:], func=AF.Sigmoid, bias=V("lru_gate_b", g * 8 + cc)),
                     reads=[("ps", b), "vecs"], writes=dst.tok(cc))
        for cc in range(8):
            S.op("act", lambda e, cc=cc: e.activation(out=rg.blk(cc), in_=rg.blk(cc), func=AF.Exp, scale=cneg[:, cc:cc + 1]),
                 reads=rg.tok(cc) + ["cneg"], writes=rg.tok(cc))
            S.op("dve", lambda e, cc=cc: e.tensor_tensor(out=t1.blk(cc), in0=rg.blk(cc), in1=rg.blk(cc), op=ALU.mult), reads=rg.tok(cc), writes=t1.tok(cc))
            S.op("dve", lambda e, cc=cc: e.tensor_tensor(out=ig.blk(cc), in0=ig.blk(cc), in1=xr.blk(cc), op=ALU.mult), reads=ig.tok(cc) + xr.tok(cc), writes=ig.tok(cc))
        for cc in range(8):
            S.op("act", lambda e, cc=cc: e.activation(out=t1.blk(cc), in_=t1.blk(cc), func=AF.Sqrt, scale=-1.0, bias=1.0), reads=t1.tok(cc), writes=t1.tok(cc))
            S.op("dve", lambda e, cc=cc: e.tensor_tensor(out=ig.blk(cc), in0=ig.blk(cc), in1=t1.blk(cc), op=ALU.mult), reads=ig.tok(cc) + t1.tok(cc), writes=ig.tok(cc))
            S.op("dve", lambda e, cc=cc: e.tensor_tensor_scan(out=t1.blk(cc), data0=rg.blk(cc), data1=ig.blk(cc), initial=hstate[:, cc:cc + 1], op0=ALU.mult, op1=ALU.add),
                 reads=rg.tok(cc) + ig.tok(cc) + ["hstate"], writes=t1.tok(cc))
            S.op("dve", lambda e, cc=cc: e.tensor_copy(out=hstate[:, cc:cc + 1], in_=t1.blk(cc)[:, TT - 1:TT]), reads=t1.tok(cc), writes=["hstate"])
            S.op("dve", lambda e, cc=cc: e.tensor_tensor(out=yb.flat[:, cc * 512:(cc + 1) * 512], in0=t1.blk(cc), in1=yb.flat[:, cc * 512:(cc + 1) * 512], op=ALU.mult),
                 reads=t1.tok(cc) + yb.tok(cc // 2), writes=yb.tok(cc // 2))
        out_proj(xr_, xtok, "lru_w_out", 8, lambda kc, s: yb.flat[:, kc * 512 + s * 128: kc * 512 + (s + 1) * 128], lambda kc: yb.tok(kc // 2), bias=True)

    def stage_F(xr_, xtok, l, first_tile):
        hid = LB(0, 12, BF16)
        gpre = [sbuf_gpre[0], sbuf_gpre[1]]
        gc = LB(12, 2, F32); gg = LB(14, 2, F32)
        if first_tile:
            S.op("pool", lambda e: e.memset(fcar[l][:], 0.0), writes=[("fcar", l)])
        rmsnorm_hT(xr_, xtok, "ffn_norm", l * 8)
        up = f"ffn_w_up{l}"
        for jj in range(NJ // 2):
            srcg = wb_d[up][:, jj * 256:(jj + 1) * 256].rearrange("(k p) e -> p k e", p=128)
            srcu = wb_d[up][:, DFF + jj * 256:DFF + (jj + 1) * 256].rearrange("(k p) e -> p k e", p=128)
            wg, wgt = wload(up, srcg, lambda t: t[:].rearrange("p (k e) -> p k e", k=KC))
            wu, wut = wload(up, srcu, lambda t: t[:].rearrange("p (k e) -> p k e", k=KC))
            for ee in range(2):
                j = 2 * jj + ee
                r = j % 2
                bg = bank(); bu = bank()
                for kc in range(KC):
                    S.op("pe", lambda e, kc=kc, ee=ee, wg=wg, bg=bg: e.matmul(ps[bg][:], lhsT=wg[:, kc, ee * 128:(ee + 1) * 128], rhs=hT[:, kc, :], start=(kc == 0), stop=(kc == KC - 1)),
                         reads=[wgt, "hT"], writes=[("ps", bg)])
                for kc in range(KC):
                    S.op("pe", lambda e, kc=kc, ee=ee, wu=wu, bu=bu: e.matmul(ps[bu][:], lhsT=wu[:, kc, ee * 128:(ee + 1) * 128], rhs=hT[:, kc, :], start=(kc == 0), stop=(kc == KC - 1)),
                         reads=[wut, "hT"], writes=[("ps", bu)])
                gp = gpre[r]; gpt = ("gpre", r)
                S.op("pool", lambda e, gp=gp, j=j: e.tensor_copy(out=gp[:, 0:2], in_=fcar[l][:, j, :]), reads=[("fcar", l)], writes=[gpt])
                S.op("act", lambda e, gp=gp, bg=bg: e.activation(out=gp[:, 2:2 + TT], in_=ps[bg][:], func=AF.Identity), reads=[("ps", bg)], writes=[gpt])
                S.op("pool", lambda e, gp=gp, j=j: e.tensor_copy(out=fcar[l][:, j, :], in_=gp[:, TT:TT + 2]), reads=[gpt], writes=[("fcar", l)])
                co = VOFF["ffn_conv_w"] + l * 72
                S.op("act", lambda e, j=j, r=r, co=co, bg=bg: e.activation(out=gc.blk(r), in_=ps[bg][:], func=AF.Copy, scale=vecs[:, co + 2 * 24 + j:co + 2 * 24 + j + 1]),
                     reads=[("ps", bg), "vecs"], writes=gc.tok(r))
                for tap in (0, 1):
                    S.op("dve", lambda e, gp=gp, j=j, r=r, co=co, tap=tap: e.scalar_tensor_tensor(out=gc.blk(r), in0=gp[:, tap:tap + TT], scalar=vecs[:, co + tap * 24 + j:co + tap * 24 + j + 1],
                                                                                               in1=gc.blk(r), op0=ALU.mult, op1=ALU.add),
                         reads=[gpt, "vecs"] + gc.tok(r), writes=gc.tok(r))
                S.op("act", lambda e, j=j, r=r: e.activation(out=gg.blk(r), in_=gc.blk(r), func=AF.Gelu_apprx_tanh, bias=V("ffn_conv_b", l * 24 + j)),
                     reads=gc.tok(r) + ["vecs"], writes=gg.tok(r))
                S.op("dve", lambda e, j=j, r=r, bu=bu: e.tensor_tensor(out=hid.flat[:, j * 512:(j + 1) * 512], in0=ps[bu][:], in1=gg.blk(r), op=ALU.mult),
                     reads=[("ps", bu)] + gg.tok(r), writes=hid.tok(j // 2))
        out_proj(xr_, xtok, f"ffn_w_dn{l}", NJ, lambda kc, s: hid.flat[:, kc * 512 + s * 128: kc * 512 + (s + 1) * 128], lambda kc: hid.tok(kc // 2))

    sbuf_gpre = [sb(f"gpre{i}", [128, TT + 2]) for i in range(2)]

    KAPPA = 0.6065306597126334

    def stage_C(xr_, xtok, first_tile):
        AT = LB(0, 4, BF16); BT = LB(4, 4, BF16); KT = LB(8, 4, BF16); RT = LB(12, 4, BF16)
        Vt = LB(16, 4, BF16); Kh = LB(20, 4, BF16); Bh = LB(24, 4, BF16)
        xx = LB(28, 4, BF16); xs = LB(32, 4, BF16)

        def v3(lb):
            return lb.flat.rearrange("p (k t) -> p k t", k=KC)

        def v4(lb):
            return lb.flat.rearrange("p (c f) -> p c f", c=NS)

        def mm(out, lhsT, rhs, rd, b, start=True, stop=True):
            S.op("pe", lambda e: e.matmul(out, lhsT=lhsT, rhs=rhs, start=start, stop=stop), reads=rd, writes=[("ps", b)])

        if first_tile:
            S.op("pool", lambda e: e.memset(hcar[:], 0.0), writes=["hcar"])
            S.op("pool", lambda e: e.memset(Hst[:], 0.0), writes=["Hst"])
            S.op("pool", lambda e: e.memset(Hbd[:], 0.0), writes=["Hbd"])
        rmsnorm_hT(xr_, xtok, "rwkv_norm", 0)
        xx3 = v3(xx); xs3 = v3(xs)
        S.op("dve", lambda e: e.tensor_tensor(out=xx3[:, :, 1:TT], in0=hT[:, :, 0:TT - 1], in1=hT[:, :, 1:TT], op=ALU.subtract), reads=["hT"], writes=xx.tok())
        S.op("dve", lambda e: e.tensor_tensor(out=xx3[:, :, 0:1], in0=hcar[:].unsqueeze(2), in1=hT[:, :, 0:1], op=ALU.subtract), reads=["hT", "hcar"], writes=xx.tok())
        S.op("pool", lambda e: e.tensor_copy(out=hcar[:].unsqueeze(2), in_=hT[:, :, TT - 1:TT]), reads=["hT"], writes=["hcar"])

        def make_xs(i):
            for kc in range(KC):
                S.op("dve", lambda e, kc=kc: e.scalar_tensor_tensor(out=xs3[:, kc, :], in0=xx3[:, kc, :], scalar=V("rwkv_mix", i * 8 + kc), in1=hT[:, kc, :], op0=ALU.mult, op1=ALU.add),
                     reads=xx.tok(kc // 2) + ["hT", "vecs"], writes=xs.tok(kc // 2))

        def lora1(wt, wtok, c0, c1, func, out_ap, out_tok):
            M = c1 - c0
            b = bank()
            for kc in range(KC):
                mm(ps[b][0:M, :], wt[:, kc, c0:c1], xs3[:, kc, :], [wtok] + xs.tok(kc // 2), b, start=(kc == 0), stop=(kc == KC - 1))
            S.op("act", lambda e: e.activation(out=out_ap, in_=ps[b][0:M, :], func=func), reads=[("ps", b)], writes=[out_tok])

        make_xs(3); lora1(w1b, "w1b", 0, 64, AF.Tanh, tanhw[:], "tanhw")
        make_xs(4); lora1(a1b, "a1b", 0, 64, AF.Copy, a1o[:], "a1o")
        make_xs(5); lora1(g1b, "g1b", 0, 128, AF.Sigmoid, sgT[:, 0, :], "sgT"); lora1(g1b, "g1b", 128, 160, AF.Sigmoid, sgT[0:32, 1, :], "sgT")

        make_xs(2)
        Vt4 = v4(Vt); Kh4 = v4(Kh); Bh4 = v4(Bh)
        out_proj(None, None, "rwkv_w_v", 8, lambda kc, s: xs3[:, kc, s * 128:(s + 1) * 128], lambda kc: xs.tok(kc // 2),
                 evac=lambda s, nh, b: S.op("act", lambda e: e.activation(out=Vt4[:, s, nh * 512:(nh + 1) * 512], in_=ps[b][:], func=AF.Copy),
                                            reads=[("ps", b)], writes=Vt.tok(s)))
        make_xs(0)
        for kc in range(KC):
            S.op("dve", lambda e, kc=kc: e.scalar_tensor_tensor(out=xx3[:, kc, :], in0=xx3[:, kc, :], scalar=V("rwkv_mix", 1 * 8 + kc), in1=hT[:, kc, :], op0=ALU.mult, op1=ALU.add),
                 reads=xx.tok(kc // 2) + ["hT", "vecs"], writes=xx.tok(kc // 2))

        c4 = lambda ap: ap.rearrange("p (c t) -> p c t", t=CH)
        wslabs = {}

        def pair_bufs(p):
            tb = 36 if p % 2 == 0 else 45
            Tt = [LB(tb + i, 1, F32) for i in range(7)]
            return Tt, LB(tb + 7, 1, BF16), LB(tb + 8, 1, BF16)

        def prep1(p):
            Tt, TKB, TRQ = pair_bufs(p)
            T0, T1, T2, T3 = [Tt[i].flat for i in range(4)]
            k0, k1, k2, k3 = [Tt[i].tok() for i in range(4)]
            pc = slice(p * 128, (p + 1) * 128)
            ee = p % 2
            if ee == 0:
                for nm in ("rwkv_w_k", "rwkv_w_r"):
                    src = wb_d[nm][:, (p // 2) * 256:(p // 2 + 1) * 256].rearrange("(k p) e -> p k e", p=128)
                    wslabs[nm] = wload(nm, src, lambda t: t[:].rearrange("p (k e) -> p k e", k=KC))
            b = bank(); b2 = bank()
            mm(ps[b][:], w2b[0:64, pc], tanhw[:], ["w2b", "tanhw"], b)
            mm(ps[b2][:], a2b[0:64, pc], a1o[:], ["a2b", "a1o"], b2)
            S.op("act", lambda e: e.activation(out=T0, in_=ps[b][:], func=AF.Sigmoid, bias=V("rwkv_w0", p)), reads=[("ps", b), "vecs"], writes=k0)
            S.op("act", lambda e: e.activation(out=T3, in_=ps[b2][:], func=AF.Sigmoid, bias=V("rwkv_a0", p)), reads=[("ps", b2), "vecs"], writes=k3)
            S.op("dve", lambda e: e.tensor_tensor_scan(out=T1, data0=rmask[:], data1=T0, initial=0.0, op0=ALU.mult, op1=ALU.add), reads=k0 + ["rmask"], writes=k1)
            S.op("pool", lambda e: e.tensor_tensor(out=T0, in0=T1, in1=T0, op=ALU.subtract), reads=k0 + k1, writes=k0)
            S.op("act", lambda e: e.activation(out=T2, in_=T1, func=AF.Exp, scale=-KAPPA), reads=k1, writes=k2)
            S.op("act", lambda e: e.activation(out=T0, in_=T0, func=AF.Exp, scale=-KAPPA), reads=k0, writes=k0)
            S.op("act", lambda e: e.activation(out=T1, in_=T1, func=AF.Exp, scale=KAPPA), reads=k1, writes=k1)
            S.op("pool", lambda e: e.tensor_copy(out=WC[:, p, :], in_=c4(T2)[:, :, CH - 1]), reads=k2, writes=["WC"])
            bK, bR = (0, 1) if p % 2 == 0 else (2, 3)
            wk, wkt = wslabs["rwkv_w_k"]; wr, wrt = wslabs["rwkv_w_r"]
            for kc in range(KC):
                mm(ps[bK][:], wk[:, kc, ee * 128:(ee + 1) * 128], xx3[:, kc, :], [wkt] + xx.tok(kc // 2), bK, start=(kc == 0), stop=(kc == KC - 1))
            for kc in range(KC):
                mm(ps[bR][:], wr[:, kc, ee * 128:(ee + 1) * 128], xs3[:, kc, :], [wrt] + xs.tok(kc // 2), bR, start=(kc == 0), stop=(kc == KC - 1))
            return bK, bR

        def prep2(p, bK, bR):
            Tt, TKB, TRQ = pair_bufs(p)
            T0, T1, T2, T3, T4, T5, T6 = [Tt[i].flat for i in range(7)]
            k0, k1, k2, k3, k4, k5, k6 = [Tt[i].tok() for i in range(7)]
            TK = TKB.flat[:, 0:512]; TB = TKB.flat[:, 512:1024]; TR = TRQ.flat[:, 0:512]; KSQ = TRQ.flat[:, 512:1024]
            kt = KT.flat[:, p * 512:(p + 1) * 512]; rt = RT.flat[:, p * 512:(p + 1) * 512]
            at = AT.flat[:, p * 512:(p + 1) * 512]; bt = BT.flat[:, p * 512:(p + 1) * 512]
            ktk = KT.tok(p // 2); rtk = RT.tok(p // 2); atk = AT.tok(p // 2); btk = BT.tok(p // 2)
            pc = slice(p * 128, (p + 1) * 128)
            S.op("act", lambda e: e.activation(out=T4, in_=ps[bK][:], func=AF.Copy, scale=V("rwkv_k_k", p)), reads=[("ps", bK), "vecs"], writes=k4)
            S.op("pool", lambda e: e.tensor_tensor(out=KSQ, in0=T4, in1=T4, op=ALU.mult), reads=k4, writes=TRQ.tok())
            b3 = bank()
            mm(ps[b3][:], blkones[:], KSQ, ["blkones"] + TRQ.tok(), b3)
            S.op("act", lambda e: e.activation(out=T6, in_=ps[b3][:], func=AF.Sqrt), reads=[("ps", b3)], writes=k6)
            S.op("act", lambda e: e.activation(out=T5, in_=T3, func=AF.Identity, scale=V("rwkv_k_a", p), bias=omka[:, p:p + 1]), reads=k3 + ["vecs", "omka"], writes=k5)
            S.op("dve", lambda e: e.tensor_tensor(out=T5, in0=ps[bK][:], in1=T5, op=ALU.mult), reads=k5 + [("ps", bK)], writes=k5)
            S.op("dve", lambda e: e.tensor_tensor(out=kt, in0=T5, in1=T1, op=ALU.mult), reads=k5 + k1, writes=ktk)
            wcb = WC[:, p, :].unsqueeze(2).to_broadcast([128, 4, CH])
            S.op("pool", lambda e: e.tensor_tensor(out=c4(TK), in0=c4(kt), in1=wcb, op=ALU.mult), reads=ktk + ["WC"], writes=TKB.tok())
            S.op("dve", lambda e: e.scalar_tensor_tensor(out=TR, in0=ps[bR][:], scalar=V("rwkv_r_k", p), in1=T5, op0=ALU.mult, op1=ALU.mult), reads=[("ps", bR)] + k5 + ["vecs"], writes=TRQ.tok())
            S.op("dve", lambda e: e.tensor_tensor(out=rt, in0=ps[bR][:], in1=T2, op=ALU.mult), reads=[("ps", bR)] + k2, writes=rtk)
            S.op("dve", lambda e: e.tensor_scalar(out=T6, in0=T6, scalar1=1e-12, scalar2=None, op0=ALU.max), reads=k6, writes=k6)
            S.op("dve", lambda e: e.reciprocal(out=T6, in_=T6), reads=k6, writes=k6)
            S.op("dve", lambda e: e.tensor_tensor(out=T4, in0=T4, in1=T6, op=ALU.mult), reads=k4 + k6, writes=k4)
            S.op("dve", lambda e: e.scalar_tensor_tensor(out=at, in0=T4, scalar=-1.0, in1=T0, op0=ALU.mult, op1=ALU.mult), reads=k4 + k0, writes=atk)
            S.op("pool", lambda e: e.tensor_tensor(out=T4, in0=T4, in1=T3, op=ALU.mult), reads=k4 + k3, writes=k4)
            S.op("dve", lambda e: e.tensor_tensor(out=bt, in0=T4, in1=T1, op=ALU.mult), reads=k4 + k1, writes=btk)
            S.op("pool", lambda e: e.tensor_tensor(out=c4(TB), in0=c4(bt), in1=wcb, op=ALU.mult), reads=btk + ["WC"], writes=TKB.tok())
            b4 = bank()
            for c in range(NS):
                mm(ps[b4][:, c * 2:(c + 1) * 2], TR[:, c * 128:(c + 1) * 128], sel2[:], TRQ.tok() + ["sel2"], b4)
            S.op("act", lambda e: e.activation(out=dtok[:, :, 2 * p:2 * p + 2], in_=ps[b4][:, 0:8].rearrange("p (c e) -> p c e", e=2), func=AF.Copy),
                 reads=[("ps", b4)], writes=["dtok"])
            for src, d4, dlb in ((TK, Kh4, Kh), (TB, Bh4, Bh)):
                b5 = bank()
                psb = ps[b5][:].bitcast(BF16)[:, 0:512].rearrange("p (c f) -> p c f", c=NS)
                for c in range(NS):
                    S.op("pe", lambda e, c=c, psb=psb, src=src: e.transpose(psb[:, c, :], src[:, c * 128:(c + 1) * 128], ident[:]), reads=TKB.tok() + ["ident"], writes=[("ps", b5)])
                S.op("act", lambda e, psb=psb, d4=d4: e.activation(out=d4[:, :, pc], in_=psb, func=AF.Copy), reads=[("ps", b5)], writes=dlb.tok())

        def record(build, banks):
            ops = []
            orig = S.op
            saved = bank_list[0]
            bank_list[0] = banks
            S.op = lambda *a, **k: ops.append((a, k))
            try:
                ret = build()
            finally:
                S.op = orig
                bank_list[0] = saved
            return ops, ret

        def emit_merged(A, B):
            i = j = 0
            na, nb = len(A), len(B)
            while i < na or j < nb:
                if j >= nb or (i < na and i * nb <= j * na):
                    a, k = A[i]; i += 1
                else:
                    a, k = B[j]; j += 1
                S.op(*a, **k)

        opsA, ctx_p = record(lambda: prep1(0), [4, 5])
        emit_merged(opsA, [])
        for p_ in range(8):
            opsA, ctx_n = record(lambda: prep1(p_ + 1), [4, 5]) if p_ + 1 < 8 else ([], None)
            opsB, _ = record(lambda: prep2(p_, *ctx_p), [6, 7])
            emit_merged(opsA, opsB)
            ctx_p = ctx_n
        Xb = LB(40, 1, BF16); Ub = LB(41, 1, BF16); yq = LB(42, 2, F32); ysq = LB(44, 2, F32); ygb = LB(44, 1, BF16)
        PAD = LB(46, 4, BF16); BTB = LB(50, 2, BF16); PADALL = LB(46, 6, BF16)
        pad5 = PAD.flat.rearrange("p (q j t) -> p q j t", q=8, j=4)
        btb4 = BTB.flat.rearrange("p (q j t) -> p q j t", q=8, j=2)
        AT3 = v3(AT); BT3 = v3(BT); KT3 = v3(KT); RT3 = v3(RT)
        h8 = lambda ap: ap.rearrange("p (h v) -> p h v", v=64)
        mb2 = lambda m: m[:].unsqueeze(1).to_broadcast([128, 2, 128])
        mb4 = lambda m: m[:].unsqueeze(1).to_broadcast([128, 4, 128])
        Qs = {}; bYs = {}

        def phase1(c):
            cc = slice(c * 128, (c + 1) * 128)
            for e_ in range(2):
                rows = slice(64 * e_, 64 * e_ + 64)
                S.op("act", lambda e, rows=rows, e_=e_: e.activation(out=pad5[rows, :, e_, :], in_=AT3[rows, :, cc], func=AF.Copy), reads=AT.tok() + PAD.tok(), writes=PAD.tok())
                S.op("act", lambda e, rows=rows, e_=e_: e.activation(out=pad5[rows, :, 2 + e_, :], in_=RT3[rows, :, cc], func=AF.Copy), reads=RT.tok() + PAD.tok(), writes=PAD.tok())
                S.op("act", lambda e, rows=rows, e_=e_: e.activation(out=btb4[rows, :, e_, :], in_=BT3[rows, :, cc], func=AF.Copy), reads=BT.tok() + BTB.tok(), writes=BTB.tok())
            Q = []
            Qs[c] = Q
            for q in range(4):
                QL = LB(28 + 3 * q, 3, BF16)
                hv = lambda i, QL=QL: QL.flat[:, i * 512:(i + 1) * 512].rearrange("p (h t) -> p h t", h=4)
                Aak, Ark, Arb, P = hv(0), hv(1), hv(2), hv(3)
                PTST = QL.flat[:, 2048:3072].rearrange("p (h x) -> p h x", h=4)
                PT = PTST[:, :, 0:128]; ST = PTST[:, :, 128:256]
                tA, tB, tC = QL.tok(0), QL.tok(1), QL.tok(2)
                Q.append((Aak, Ark, Arb, P, PTST, PT, ST, tA, tB, tC))
                for pp in range(2):
                    p = 2 * q + pp
                    hs2 = slice(2 * pp, 2 * pp + 2)
                    b1 = bank(); b2 = bank(); b3 = bank()
                    padp = PAD.flat[:, p * 512:(p + 1) * 512]
                    mm(ps[b1][:], BT3[:, p, cc], padp, BT.tok(p // 2) + PAD.tok(), b1)
                    mm(ps[b2][:], KT3[:, p, cc], padp, KT.tok(p // 2) + PAD.tok(), b2)
                    mm(ps[b3][:, 0:256], AT3[:, p, cc], BTB.flat[:, p * 256:(p + 1) * 256], AT.tok(p // 2) + BTB.tok(), b3)
                    v2 = lambda b, half: ps[b][:, half * 256:(half + 1) * 256].rearrange("p (h t) -> p h t", h=2)
                    for (dst, b, half, m, tk) in ((PT, b1, 0, mUs, tC), (Arb, b1, 1, mUi, tB), (Aak, b2, 0, mUs, tA), (Ark, b2, 1, mUi, tA), (P, b3, 0, mLs, tB)):
                        S.op("dve", lambda e, dst=dst, b=b, half=half, m=m, hs2=hs2: e.tensor_tensor(out=dst[:, hs2, :], in0=v2(b, half), in1=mb2(m), op=ALU.mult),
                             reads=[("ps", b), "mask%d" % id(m)], writes=tk)
                S.op("pool", lambda e, ST=ST, PT=PT: e.tensor_tensor(out=ST, in0=PT, in1=mb4(ident), op=ALU.add), reads=tC + ["ident"], writes=tC)
            for i in range(7):
                for q in range(4):
                    Aak, Ark, Arb, P, PTST, PT, ST, tA, tB, tC = Q[q]
                    p4 = lambda b: ps[b][:].rearrange("p (h t) -> p h t", h=4)
                    if i < 6:
                        bP = bank()
                        for hq in range(4):
                            mm(ps[bP][:, hq * 128:(hq + 1) * 128], PT[:, hq, :], P[:, hq, :], tB + tC, bP)
                    if i == 0:
                        bT = bank()
                        for hq in range(4):
                            mm(ps[bT][:, hq * 128:(hq + 1) * 128], P[:, hq, :], PT[:, hq, :], tB + tC, bT)
                    elif i < 6:
                        bT2 = [bank(), bank()]
                        for hq in range(4):
                            o = ps[bT2[hq // 2]]
                            c0 = (hq % 2) * 256
                            mm(o[:, c0:c0 + 128], P[:, hq, :], PT[:, hq, :], tB + tC, bT2[hq // 2])
                            mm(o[:, c0 + 128:c0 + 256], P[:, hq, :], ST[:, hq, :], tB + tC, bT2[hq // 2], start=True, stop=False)
                            mm(o[:, c0 + 128:c0 + 256], ident[:, :], ST[:, hq, :], tC + ["ident"], bT2[hq // 2], start=False, stop=True)
                    else:
                        bS = bank()
                        for hq in range(4):
                            mm(ps[bS][:, hq * 128:(hq + 1) * 128], P[:, hq, :], ST[:, hq, :], tB + tC, bS, start=True, stop=False)
                            mm(ps[bS][:, hq * 128:(hq + 1) * 128], ident[:, :], ST[:, hq, :], tC + ["ident"], bS, start=False, stop=True)
                    if i < 6:
                        S.op("act", lambda e, P=P, bP=bP: e.activation(out=P, in_=p4(bP), func=AF.Copy), reads=[("ps", bP)], writes=tB)
                    if i == 0:
                        S.op("act", lambda e, PT=PT, bT=bT: e.activation(out=PT, in_=p4(bT), func=AF.Copy), reads=[("ps", bT)], writes=tC)
                    elif i < 6:
                        for k in range(2):
                            pv = ps[bT2[k]][:].rearrange("p (h x) -> p h x", h=2)
                            if k == 0:
                                S.op("dve", lambda e, PTST=PTST, pv=pv, k=k: e.tensor_copy(out=PTST[:, 2 * k:2 * k + 2, :], in_=pv), reads=[("ps", bT2[k])], writes=tC)
                            else:
                                S.op("act", lambda e, PTST=PTST, pv=pv, k=k: e.activation(out=PTST[:, 2 * k:2 * k + 2, :], in_=pv, func=AF.Copy), reads=[("ps", bT2[k])], writes=tC)
                    else:
                        S.op("act", lambda e, ST=ST, bS=bS: e.activation(out=ST, in_=p4(bS), func=AF.Copy), reads=[("ps", bS)], writes=tC)

        def phase2(c):
            cc = slice(c * 128, (c + 1) * 128)
            Q = Qs[c]
            bX = [2, 3]
            for p in range(8):
                bx = bX[p // 4]
                for e_ in range(2):
                    h = 2 * p + e_
                    q, hq = divmod(h, 4)
                    Aak, Ark, Arb, P, PTST, PT, ST, tA, tB, tC = Q[q]
                    o = ps[bx][:, (h % 8) * 64:(h % 8 + 1) * 64]
                    mm(o, AT3[:, p, cc], Hbd[:, p, e_ * 64:(e_ + 1) * 64], AT.tok(p // 2) + ["Hbd"], bx, start=True, stop=False)
                    mm(o, Aak[:, hq, :], Vt4[:, c, h * 64:(h + 1) * 64], tA + Vt.tok(c), bx, start=False, stop=True)
            for k in range(2):
                S.op("act", lambda e, k=k: e.activation(out=Xb.flat[:, k * 512:(k + 1) * 512], in_=ps[bX[k]][:], func=AF.Copy), reads=[("ps", bX[k])], writes=Xb.tok())
            bU = [4, 5]
            for h in range(16):
                q, hq = divmod(h, 4)
                Aak, Ark, Arb, P, PTST, PT, ST, tA, tB, tC = Q[q]
                mm(ps[bU[h // 8]][:, (h % 8) * 64:(h % 8 + 1) * 64], ST[:, hq, :], Xb.flat[:, h * 64:(h + 1) * 64], tC + Xb.tok(), bU[h // 8])
            for k in range(2):
                S.op("act", lambda e, k=k: e.activation(out=Ub.flat[:, k * 512:(k + 1) * 512], in_=ps[bU[k]][:], func=AF.Copy), reads=[("ps", bU[k])], writes=Ub.tok())
            bY = [0, 1]
            bYs[c] = bY
            for p in range(8):
                by = bY[p // 4]
                for e_ in range(2):
                    h = 2 * p + e_
                    q, hq = divmod(h, 4)
                    Aak, Ark, Arb, P, PTST, PT, ST, tA, tB, tC = Q[q]
                    o = ps[by][:, (h % 8) * 64:(h % 8 + 1) * 64]
                    mm(o, RT3[:, p, cc], Hbd[:, p, e_ * 64:(e_ + 1) * 64], RT.tok(p // 2) + ["Hbd"], by, start=True, stop=False)
                    mm(o, Ark[:, hq, :], Vt4[:, c, h * 64:(h + 1) * 64], tA + Vt.tok(c), by, start=False, stop=False)
                    mm(o, Arb[:, hq, :], Ub.flat[:, h * 64:(h + 1) * 64], tB + Ub.tok(), by, start=False, stop=True)

        def phase3(c):
            cc = slice(c * 128, (c + 1) * 128)
            bY = bYs[c]
            bH = 2
            for e_ in range(2):
                rows = slice(64 * e_, 64 * e_ + 64)
                S.op("pool", lambda e, rows=rows: e.tensor_tensor(out=Hst[rows, :, :], in0=Hst[rows, :, :], in1=WC[rows, :, c:c + 1].to_broadcast([64, 8, 64]), op=ALU.mult),
                     reads=["Hst", "WC"], writes=["Hst"])
            for k in range(2):
                for p in range(4 * k, 4 * k + 4):
                    o = ps[bH][:, (p % 4) * 128:(p % 4 + 1) * 128]
                    mm(o, Kh4[:, c, p * 128:(p + 1) * 128], Vt4[:, c, p * 128:(p + 1) * 128], Kh.tok(c) + Vt.tok(c), bH, start=True, stop=False)
                    mm(o, Bh4[:, c, p * 128:(p + 1) * 128], Ub.flat[:, p * 128:(p + 1) * 128], Bh.tok(c) + Ub.tok(), bH, start=False, stop=True)
                for e_ in range(2):
                    rows = slice(64 * e_, 64 * e_ + 64)
                    pv = ps[bH][rows, :].rearrange("p (q f) -> p q f", q=4)[:, :, e_ * 64:(e_ + 1) * 64]
                    S.op("dve", lambda e, rows=rows, pv=pv, k=k: e.tensor_tensor(out=Hst[rows, 4 * k:4 * k + 4, :], in0=pv, in1=Hst[rows, 4 * k:4 * k + 4, :], op=ALU.add),
                         reads=[("ps", bH), "Hst"], writes=["Hst"])
            for e_ in range(2):
                rows = slice(64 * e_, 64 * e_ + 64)
                S.op("act", lambda e, rows=rows, e_=e_: e.activation(out=Hbd[rows, :, e_ * 64:(e_ + 1) * 64], in_=Hst[rows, :, :], func=AF.Copy), reads=["Hst", "Hbd"], writes=["Hbd"])
            yq3 = yq.flat.rearrange("p (h v) -> p h v", v=64); ysq3 = ysq.flat.rearrange("p (h v) -> p h v", v=64)
            for k in range(2):
                S.op("dve", lambda e, k=k: e.tensor_reduce(out=st_a[:, 8 * k:8 * k + 8], in_=h8(ps[bY[k]][:]), axis=AX.X, op=ALU.add), reads=[("ps", bY[k])], writes=["st_a"])
                S.op("act", lambda e, k=k: e.activation(out=ysq.flat[:, k * 512:(k + 1) * 512], in_=ps[bY[k]][:], func=AF.Square), reads=[("ps", bY[k])], writes=ysq.tok(k))
                S.op("dve", lambda e, k=k: e.tensor_reduce(out=st_b[:, 8 * k:8 * k + 8], in_=ysq3[:, 8 * k:8 * k + 8, :], axis=AX.X, op=ALU.add), reads=ysq.tok(k), writes=["st_b"])
            S.op("dve", lambda e: e.tensor_scalar(out=st_a[:], in0=st_a[:], scalar1=1.0 / 64, scalar2=None, op0=ALU.mult), reads=["st_a"], writes=["st_a"])
            S.op("dve", lambda e: e.tensor_tensor(out=st_c[:], in0=st_a[:], in1=st_a[:], op=ALU.mult), reads=["st_a"], writes=["st_c"])
            S.op("dve", lambda e: e.scalar_tensor_tensor(out=st_b[:], in0=st_b[:], scalar=1.0 / 64, in1=st_c[:], op0=ALU.mult, op1=ALU.subtract), reads=["st_b", "st_c"], writes=["st_b"])
            S.op("act", lambda e: e.activation(out=st_b[:], in_=st_b[:], func=AF.Sqrt, bias=GN_EPS), reads=["st_b"], writes=["st_b"])
            S.op("dve", lambda e: e.reciprocal(out=st_b[:], in_=st_b[:]), reads=["st_b"], writes=["st_b"])
            for k in range(2):
                hs_ = slice(8 * k, 8 * k + 8)
                S.op("dve", lambda e, k=k, hs_=hs_: e.tensor_tensor(out=yq3[:, hs_, :], in0=h8(ps[bY[k]][:]), in1=st_a[:, hs_].unsqueeze(2).to_broadcast([128, 8, 64]), op=ALU.subtract),
                     reads=[("ps", bY[k]), "st_a"], writes=yq.tok(k))
                S.op("dve", lambda e, hs_=hs_: e.tensor_tensor(out=yq3[:, hs_, :], in0=yq3[:, hs_, :], in1=st_b[:, hs_].unsqueeze(2).to_broadcast([128, 8, 64]), op=ALU.mult),
                     reads=yq.tok(k) + ["st_b"], writes=yq.tok(k))
            S.op("pool", lambda e: e.tensor_tensor(out=ysq3, in0=h8(Vt4[:, c, :]), in1=dtok[:, c, :].unsqueeze(2).to_broadcast([128, 16, 64]), op=ALU.mult),
                 reads=Vt.tok(c) + ["dtok"], writes=ysq.tok())
            S.op("pool", lambda e: e.tensor_tensor(out=ysq.flat, in0=ysq.flat, in1=lnb_b[:], op=ALU.add), reads=ysq.tok() + ["lnb_b"], writes=ysq.tok())
            S.op("pool", lambda e: e.tensor_tensor(out=yq.flat, in0=yq.flat, in1=lnw_b[:], op=ALU.mult), reads=yq.tok() + ["lnw_b"], writes=yq.tok())
            S.op("dve", lambda e: e.tensor_tensor(out=yq.flat, in0=yq.flat, in1=ysq.flat, op=ALU.add), reads=yq.tok() + ysq.tok(), writes=yq.tok())
            bG = [2, 2]
            for nh in range(2):
                mm(ps[bG[nh]][:], sgT[:, 0, cc], g2b[:, 0, nh * 512:(nh + 1) * 512], ["sgT", "g2b"], bG[nh], start=True, stop=False)
                mm(ps[bG[nh]][:], sgT[0:32, 1, cc], g2b[0:32, 1, nh * 512:(nh + 1) * 512], ["sgT", "g2b"], bG[nh], start=False, stop=True)
                S.op("dve", lambda e, nh=nh: e.tensor_tensor(out=ygb.flat[:, nh * 512:(nh + 1) * 512], in0=ps[bG[nh]][:], in1=yq.flat[:, nh * 512:(nh + 1) * 512], op=ALU.mult),
                     reads=[("ps", bG[nh])] + yq.tok(nh) + ysq.tok(), writes=ygb.tok())
            b = 2
            psb = ps[b][:].bitcast(BF16).rearrange("p (k t) -> p k t", k=KC)
            for kc in range(KC):
                S.op("pe", lambda e, kc=kc, psb=psb: e.transpose(psb[:, kc, :], ygb.flat[:, kc * 128:(kc + 1) * 128], ident[:]), reads=ygb.tok() + ["ident"], writes=[("ps", b)])
            S.op("act", lambda e, psb=psb: e.activation(out=hT[:, :, cc], in_=psb, func=AF.Copy), reads=[("ps", b)], writes=["hT"])
        S.op("pool", lambda e: e.memset(PADALL.flat, 0.0), writes=PADALL.tok())
        opsP, _ = record(lambda: phase1(0), [3, 4, 5, 6, 7])
        emit_merged(opsP, [])
        for c_ in range(NS):
            phase2(c_)
            ops3, _ = record(lambda: phase3(c_), [2])
            ops1, _ = record(lambda: phase1(c_ + 1), [3, 4, 5, 6, 7]) if c_ + 1 < NS else ([], None)
            emit_merged(ops3, ops1)
        out_proj(xr_, xtok, "rwkv_w_out", 8, lambda kc, s: hT[:, kc, s * 128:(s + 1) * 128], lambda kc: ["hT"])

    def stage_E(xr_, xtok, seq, t0, ost):
        for s in range(NS):
            S.op("act", lambda e, s=s: e.activation(out=sqj[:], in_=xr_[:, s, :], func=AF.Square, accum_out=ss[:, s:s + 1]),
                 reads=[(xtok, s)], writes=[("xn", 0), "ss"])
        S.op("dve", lambda e: e.tensor_scalar(out=rstd[:], in0=ss[:], scalar1=1.0 / D, scalar2=RMS_EPS, op0=ALU.mult, op1=ALU.add), reads=["ss"], writes=["rstd"])
        S.op("act", lambda e: e.activation(out=rstd[:], in_=rstd[:], func=AF.Sqrt), reads=["rstd"], writes=["rstd"])
        S.op("dve", lambda e: e.reciprocal(out=rstd[:], in_=rstd[:]), reads=["rstd"], writes=["rstd"])
        xo = LB(0, 8, F32)
        xo3 = xo.flat.rearrange("p (s d) -> p s d", s=NS)
        comps = []
        for s in range(NS):
            S.op("dve", lambda e, s=s: e.scalar_tensor_tensor(out=xo3[:, s, :], in0=xr_[:, s, :], scalar=rstd[:, s:s + 1], in1=gfin_b[:], op0=ALU.mult, op1=ALU.mult),
                 reads=[(xtok, s), "rstd", "gfin_b"], writes=xo.tok(2 * s, 2))
            dst = out_d[seq, t0 + s * 128:t0 + (s + 1) * 128, :]
            comps.append(S.op("sp", lambda e, s=s, dst=dst: e.dma_start(out=dst, in_=xo3[:, s, :]), reads=xo.tok(2 * s, 2), writes=[], dsem=ost[s]))
        return comps

    xsem = [S.new_dsem("x") for _ in range(NS)]
    osem = [S.new_dsem("o") for _ in range(NS)]
    tiles = [(q, t) for q in range(n_seq) for t in range(n_tiles)]

    def xload(i):
        q, t = tiles[i]
        buf = xres[0]
        for s in range(NS):
            src = x_d[q, t * TT + s * 128:t * TT + (s + 1) * 128, :]
            S.op("sp", lambda e, s=s, src=src: e.dma_start(out=buf[:, s, :], in_=src), reads=[], writes=[("x0", s)], dsem=xsem[s])

    last_out = []
    for i, (q, t) in enumerate(tiles):
        xload(i)
        xr_ = xres[0]
        xtok = "x0"
        first = (t == 0)
        if "A" in stages:
            if i == 0:
                cast_next()
            stage_A(xr_, xtok, first)
        if "B" in stages:
            if i == 0:
                cast_next()
            stage_F(xr_, xtok, 0, first)
        if "C" in stages:
            if i == 0:
                cast_next()
            stage_C(xr_, xtok, first)
        if "D" in stages:
            if i == 0:
                cast_next()
            stage_F(xr_, xtok, 1, first)
        last_out = stage_E(xr_, xtok, q, t * TT, osem)
    for c in last_out:
        S.final_wait("sp", c)

    with nc.Block() as block:
        S.emit(block)
    es.close()
    return nc, S


def kernel(**inputs):
    inp = {k: np.asarray(v) for k, v in inputs.items()}
    x = np.ascontiguousarray(inp["x"], dtype=np.float32)
    B, T, _ = x.shape
    n_seq = B // N_CORES
    nc, _ = build(n_seq=n_seq, T=T)
    shared = host_shared(inp)
    in_maps = []
    for c in range(N_CORES):
        m = dict(shared)
        m["x"] = np.ascontiguousarray(x[c * n_seq:(c + 1) * n_seq])
        in_maps.append(m)
    res = run_bass_kernel_spmd(nc, in_maps, core_ids=list(range(N_CORES)))
    return np.concatenate([r["out"] for r in res.results], axis=0)


def host_shared(inp):
    f = lambda a: np.ascontiguousarray(np.asarray(a, dtype=np.float32))
    return {
        "vecs": make_vecs(inp),
        "lru_w_in": f(inp["lru_w_in"][0]), "lru_w_out": f(inp["lru_w_out"][0]),
        "ffn_w_up0": f(inp["ffn_w_up"][0]), "ffn_w_up1": f(inp["ffn_w_up"][1]),
        "ffn_w_dn0": f(inp["ffn_w_down"][0]), "ffn_w_dn1": f(inp["ffn_w_down"][1]),
        "rwkv_w_r": f(inp["rwkv_w_rkv"][0, 0]), "rwkv_w_k": f(inp["rwkv_w_rkv"][0, 1]), "rwkv_w_v": f(inp["rwkv_w_rkv"][0, 2]),
        "rwkv_w_out": f(inp["rwkv_w_out"][0]),
        "lru_gate_w": f(inp["lru_gate_w"][0]), "lru_b_out": f(inp["lru_b_out"]),
        "rwkv_w1": f(inp["rwkv_w1"][0]), "rwkv_a1": f(inp["rwkv_a1"][0]), "rwkv_g1": f(inp["rwkv_g1"][0]),
        "rwkv_w2": f(inp["rwkv_w2"][0]), "rwkv_a2": f(inp["rwkv_a2"][0]), "rwkv_g2": f(inp["rwkv_g2"][0]),
        "rwkv_ln_w": f(inp["rwkv_ln_w"][0]), "rwkv_ln_b": f(inp["rwkv_ln_b"][0]), "final_norm": f(inp["final_norm"]),
    }
```
